# Optimizing a Trainium2 kernel written in Bass

```python
import jax, jax.numpy as jnp
from jax import lax
import numpy as np

D_MODEL = 1024
BATCH = 8
SEQ = 2048
DEPTH = 2
DEC_BATCH = 128
DEC_SEQ = 8
PAST_LEN = 16384
PAGE_SIZE = 128

N_EVEN = (DEPTH + 1) // 2
N_ODD = DEPTH // 2
EPS = 1e-6

SGU_CHUNK = 128
SGU_GROUPS = 4
SGU_WIDTH = D_MODEL
SGU_GROUP_DIM = SGU_WIDTH // SGU_GROUPS
RET_HEADS = 4
RET_DK = D_MODEL // 8
RET_DV = D_MODEL // 4
RET_CHUNK = 128
RET_ROPE_BASE = 10000.0
C_HEADS = 16
C_KV_HEADS = 4
C_HEAD_DIM = D_MODEL // C_HEADS
C_GROUP = C_HEADS // C_KV_HEADS
WINDOW = 128
C_ROPE_BASE = 150000.0
N_MEM = 256
MEM_HEADS = 4
MEM_HEAD_DIM = D_MODEL // MEM_HEADS
D_FF = 4 * D_MODEL

EVEN_SPLITS = (SGU_WIDTH, 2 * SGU_WIDTH, 2 * SGU_WIDTH + RET_HEADS * RET_DK,
               2 * SGU_WIDTH + 2 * RET_HEADS * RET_DK,
               2 * SGU_WIDTH + 2 * RET_HEADS * RET_DK + RET_HEADS * RET_DV)
EVEN_IN = 2 * SGU_WIDTH + 2 * RET_HEADS * RET_DK + 2 * RET_HEADS * RET_DV
EVEN_OUT = SGU_WIDTH + RET_HEADS * RET_DV
ODD_SPLITS = (C_HEADS * C_HEAD_DIM, (C_HEADS + C_KV_HEADS) * C_HEAD_DIM)
ODD_IN = (C_HEADS + 2 * C_KV_HEADS) * C_HEAD_DIM

kernel_name = 'hybrid_sgu_retention_swa_decoder_step'


def rms_norm(x, g):
    xf = x.astype(jnp.float32)
    y = xf * lax.rsqrt(jnp.mean(xf * xf, axis=-1, keepdims=True) + EPS)
    return (y * g.astype(jnp.float32)).astype(x.dtype)


def layer_norm(x, g, b):
    xf = x.astype(jnp.float32)
    mu = jnp.mean(xf, axis=-1, keepdims=True)
    xc = xf - mu
    y = xc * lax.rsqrt(jnp.mean(xc * xc, axis=-1, keepdims=True) + EPS)
    return (y * g.astype(jnp.float32) + b.astype(jnp.float32)).astype(x.dtype)


def rotary(x, pos, base):
    half = x.shape[-1] // 2
    inv = base ** (-jnp.arange(half, dtype=jnp.float32) / half)
    ang = pos.astype(jnp.float32)[:, None] * inv[None, :]
    cos = jnp.cos(ang)[None, :, None, :]
    sin = jnp.sin(ang)[None, :, None, :]
    xf = x.astype(jnp.float32)
    x1, x2 = xf[..., :half], xf[..., half:]
    return jnp.concatenate([x1 * cos - x2 * sin, x1 * sin + x2 * cos], axis=-1).astype(x.dtype)


def sgu_mix(u, v, ln_g, ln_b, w_s, b_s):
    b, l, _ = u.shape
    vn = layer_norm(v, ln_g, ln_b)
    c = SGU_CHUNK if l % SGU_CHUNK == 0 else l
    nc = l // c
    mask = jnp.tril(jnp.ones((c, c), dtype=bool))
    w = jnp.where(mask[None], w_s[:, :c, :c], 0).astype(v.dtype)
    vc = vn.reshape(b, nc, c, SGU_GROUPS, SGU_GROUP_DIM)
    mixed = jnp.einsum('gts,bnsgd->bntgd', w, vc) + b_s[:, :c].T[None, None, :, :, None].astype(v.dtype)
    out = u * mixed.reshape(b, l, SGU_WIDTH)
    return out, vc.reshape(b, l, SGU_GROUPS, SGU_GROUP_DIM)


def retention(q, k, v, s0):
    b, l, h, _ = q.shape
    c = RET_CHUNK if l % RET_CHUNK == 0 else l
    nc = l // c
    lg = jnp.log1p(-jnp.exp2(-5.0 - jnp.arange(h, dtype=jnp.float32)))
    idx = jnp.arange(c, dtype=jnp.float32)
    diff = idx[:, None] - idx[None, :]
    decay = jnp.where(diff >= 0, jnp.exp(jnp.maximum(diff, 0.0)[None] * lg[:, None, None]), 0.0)
    xi = jnp.exp((idx[:, None] + 1.0) * lg[None, :])
    zeta = jnp.exp((c - 1.0 - idx)[:, None] * lg[None, :])
    chunk_decay = jnp.exp(c * lg)

    def to_chunks(t):
        return t.astype(jnp.float32).reshape(b, nc, c, h, t.shape[-1]).transpose(1, 0, 2, 3, 4)

    def step(s, inp):
        qc, kc, vc = inp
        sc = jnp.einsum('bihd,bjhd->bhij', qc, kc) * decay
        o = (jnp.einsum('bhij,bjhe->bihe', sc, vc)
             + jnp.einsum('bihd,bhde->bihe', qc, s) * xi[None, :, :, None])
        s = s * chunk_decay[:, None, None] + jnp.einsum('bjhd,bjhe->bhde', kc * zeta[None, :, :, None], vc)
        return s, o

    s, o = lax.scan(step, s0.astype(jnp.float32), (to_chunks(q), to_chunks(k), to_chunks(v)))
    o = o.transpose(1, 0, 2, 3, 4).reshape(b, l, h, v.shape[-1])
    return o, s


def even_mixer(h, pos, s0, w_in, ln_g, ln_b, w_s, b_s, w_out):
    b, l, _ = h.shape
    z = h @ w_in
    u, v, q, k, vr, gate = jnp.split(z, EVEN_SPLITS, axis=-1)
    a_out, v_rows = sgu_mix(jax.nn.gelu(u), jax.nn.gelu(v), ln_g, ln_b, w_s, b_s)
    q = rotary(q.reshape(b, l, RET_HEADS, RET_DK), pos, RET_ROPE_BASE)
    k = rotary(k.reshape(b, l, RET_HEADS, RET_DK), pos, RET_ROPE_BASE) * (RET_DK ** -0.5)
    o, s = retention(q, k, vr.reshape(b, l, RET_HEADS, RET_DV), s0)
    o = o * lax.rsqrt(jnp.mean(o * o, axis=-1, keepdims=True) + EPS)
    b_out = (o.reshape(b, l, RET_HEADS * RET_DV) * jax.nn.silu(gate.astype(jnp.float32))).astype(h.dtype)
    y = jnp.concatenate([a_out, b_out], axis=-1) @ w_out
    return y, v_rows, s.astype(h.dtype)


def odd_qkv(h, pos, w_qkv, b_qkv):
    b, l, _ = h.shape
    q, k, v = jnp.split(h @ w_qkv + b_qkv, ODD_SPLITS, axis=-1)
    q = rotary(q.reshape(b, l, C_HEADS, C_HEAD_DIM), pos, C_ROPE_BASE)
    k = rotary(k.reshape(b, l, C_KV_HEADS, C_HEAD_DIM), pos, C_ROPE_BASE)
    return q, k, v.reshape(b, l, C_KV_HEADS, C_HEAD_DIM)


def sink_attention(q, k, v, valid, sinks):
    s = jnp.einsum('...qkgd,...skd->...kgqs', q, k).astype(jnp.float32) * (C_HEAD_DIM ** -0.5)
    s = jnp.where(valid, s, -jnp.inf)
    sink = jnp.broadcast_to(sinks.astype(jnp.float32).reshape(C_KV_HEADS, C_GROUP, 1, 1), s.shape[:-1] + (1,))
    p = jax.nn.softmax(jnp.concatenate([s, sink], axis=-1), axis=-1)[..., :-1]
    return jnp.einsum('...kgqs,...skd->...qkgd', p.astype(v.dtype), v)


def window_attn_prompt(q, k, v, sinks):
    b, l = q.shape[:2]
    nb = l // WINDOW
    qb = q.reshape(b, nb, WINDOW, C_KV_HEADS, C_GROUP, C_HEAD_DIM)

    def with_prev(t):
        tb = t.reshape(b, nb, WINDOW, C_KV_HEADS, C_HEAD_DIM)
        prev = jnp.concatenate([jnp.zeros_like(tb[:, :1]), tb[:, :-1]], axis=1)
        return jnp.concatenate([prev, tb], axis=2)

    qpos = jnp.arange(l, dtype=jnp.int32).reshape(nb, WINDOW)
    kpos = qpos[:, :1] - WINDOW + jnp.arange(2 * WINDOW, dtype=jnp.int32)[None, :]
    dist = qpos[:, :, None] - kpos[:, None, :]
    valid = (dist >= 0) & (dist < WINDOW) & (kpos[:, None, :] >= 0)
    o = sink_attention(qb, with_prev(k), with_prev(v), valid[:, None, None], sinks)
    return o.reshape(b, l, C_HEADS * C_HEAD_DIM)


def window_attn_sample(q, k, v, buf_k, buf_v, sinks):
    b, l = q.shape[:2]
    kk = jnp.concatenate([buf_k.astype(k.dtype), k], axis=1)
    vv = jnp.concatenate([buf_v.astype(v.dtype), v], axis=1)
    qpos = jnp.arange(l, dtype=jnp.int32)[:, None] + WINDOW
    kpos = jnp.arange(WINDOW + l, dtype=jnp.int32)[None, :]
    dist = qpos - kpos
    valid = (dist >= 0) & (dist < WINDOW)
    o = sink_attention(q.reshape(b, l, C_KV_HEADS, C_GROUP, C_HEAD_DIM), kk, vv, valid, sinks)
    return o.reshape(b, l, C_HEADS * C_HEAD_DIM), kk[:, -WINDOW:], vv[:, -WINDOW:]


def mem_kv(mem, g_mem, w_mk, w_mv):
    b = mem.shape[0]
    m = rms_norm(mem, g_mem)
    k = (m @ w_mk).reshape(b, N_MEM, MEM_HEADS, MEM_HEAD_DIM)
    v = (m @ w_mv).reshape(b, N_MEM, MEM_HEADS, MEM_HEAD_DIM)
    return k, v


def cross_attend(h, mk, mv, w_mq, w_mo):
    b, l, _ = h.shape
    q = (h @ w_mq).reshape(b, l, MEM_HEADS, MEM_HEAD_DIM)
    s = jnp.einsum('blhd,bmhd->bhlm', q, mk.astype(q.dtype)).astype(jnp.float32) * (MEM_HEAD_DIM ** -0.5)
    p = jax.nn.softmax(s, axis=-1)
    o = jnp.einsum('bhlm,bmhd->blhd', p.astype(q.dtype), mv.astype(q.dtype))
    return o.reshape(b, l, MEM_HEADS * MEM_HEAD_DIM) @ w_mo


def squared_relu_mlp(h, w_up, w_down):
    return jnp.square(jax.nn.relu(h @ w_up)) @ w_down


def _normal(k, shape, scale):
    return scale * jax.random.normal(k, shape, jnp.float32)


def _gain(k, shape):
    return 1.0 + _normal(k, shape, 0.02)


def setup_inputs(seed: int = 0) -> dict:
    key = jax.random.key(seed)
    ks = jax.random.split(key, 32)
    return {
        'x_prompt': _normal(ks[0], (BATCH, SEQ, D_MODEL), 1.0),
        'x_sample': _normal(ks[1], (DEC_BATCH, DEC_SEQ, D_MODEL), 1.0),
        'state_ret': _normal(ks[2], (N_EVEN, DEC_BATCH, RET_HEADS, RET_DK, RET_DV), 0.5),
        'cache_win_k': _normal(ks[3], (N_ODD, DEC_BATCH, WINDOW, C_KV_HEADS, C_HEAD_DIM), 1.0),
        'cache_win_v': _normal(ks[4], (N_ODD, DEC_BATCH, WINDOW, C_KV_HEADS, C_HEAD_DIM), 1.0),
        'cache_mem_k': _normal(ks[5], (DEPTH, DEC_BATCH, N_MEM, MEM_HEADS, MEM_HEAD_DIM), 1.0),
        'cache_mem_v': _normal(ks[6], (DEPTH, DEC_BATCH, N_MEM, MEM_HEADS, MEM_HEAD_DIM), 1.0),
        'mem_prompt': _normal(ks[7], (BATCH, N_MEM, D_MODEL), 1.0),
        'g_mix': _gain(ks[8], (DEPTH, D_MODEL)),
        'w_in_e': _normal(ks[9], (N_EVEN, D_MODEL, EVEN_IN), D_MODEL ** -0.5),
        'sgu_ln_g': _gain(ks[10], (N_EVEN, SGU_WIDTH)),
        'sgu_ln_b': _normal(ks[11], (N_EVEN, SGU_WIDTH), 0.02),
        'w_spatial': _normal(ks[12], (N_EVEN, SGU_GROUPS, SGU_CHUNK, SGU_CHUNK), SGU_CHUNK ** -0.5),
        'b_spatial': _gain(ks[13], (N_EVEN, SGU_GROUPS, SGU_CHUNK)),
        'w_out_e': _normal(ks[14], (N_EVEN, EVEN_OUT, D_MODEL), EVEN_OUT ** -0.5),
        'w_qkv_o': _normal(ks[15], (N_ODD, D_MODEL, ODD_IN), D_MODEL ** -0.5),
        'b_qkv_o': _normal(ks[16], (N_ODD, ODD_IN), 0.02),
        'sinks': _normal(ks[17], (N_ODD, C_HEADS), 0.5),
        'w_out_o': _normal(ks[18], (N_ODD, C_HEADS * C_HEAD_DIM, D_MODEL), (C_HEADS * C_HEAD_DIM) ** -0.5),
        'b_out_o': _normal(ks[19], (N_ODD, D_MODEL), 0.02),
        'g_cross': _gain(ks[20], (DEPTH, D_MODEL)),
        'g_mem': _gain(ks[21], (DEPTH, D_MODEL)),
        'w_mq': _normal(ks[22], (DEPTH, D_MODEL, MEM_HEADS * MEM_HEAD_DIM), D_MODEL ** -0.5),
        'w_mk': _normal(ks[23], (DEPTH, D_MODEL, MEM_HEADS * MEM_HEAD_DIM), D_MODEL ** -0.5),
        'w_mv': _normal(ks[24], (DEPTH, D_MODEL, MEM_HEADS * MEM_HEAD_DIM), D_MODEL ** -0.5),
        'w_mo': _normal(ks[25], (DEPTH, MEM_HEADS * MEM_HEAD_DIM, D_MODEL), (MEM_HEADS * MEM_HEAD_DIM) ** -0.5),
        'g_ffn': _gain(ks[26], (DEPTH, D_MODEL)),
        'w_up': _normal(ks[27], (DEPTH, D_MODEL, D_FF), D_MODEL ** -0.5),
        'w_down': _normal(ks[28], (DEPTH, D_FF, D_MODEL), D_FF ** -0.5),
        'g_final': _gain(ks[29], (D_MODEL,)),
    }


def reference(x_prompt, x_sample, state_ret, cache_win_k, cache_win_v, cache_mem_k, cache_mem_v,
              mem_prompt, g_mix, w_in_e, sgu_ln_g, sgu_ln_b, w_spatial, b_spatial, w_out_e,
              w_qkv_o, b_qkv_o, sinks, w_out_o, b_out_o, g_cross, g_mem, w_mq, w_mk, w_mv, w_mo,
              g_ffn, w_up, w_down, g_final):
    bp, lp, _ = x_prompt.shape
    ls = x_sample.shape[1]
    pos_p = jnp.arange(lp, dtype=jnp.int32)
    pos_s = PAST_LEN + jnp.arange(ls, dtype=jnp.int32)
    xp, xs = x_prompt, x_sample
    mem_k_p, mem_v_p = [], []
    ret_p, ret_s, sgu_v_s = [], [], []
    wk_p, wv_p, wk_s, wv_s = [], [], [], []
    for layer in range(DEPTH):
        j = layer // 2
        hp = rms_norm(xp, g_mix[layer])
        hs = rms_norm(xs, g_mix[layer])
        if layer % 2 == 0:
            s0 = jnp.zeros((bp, RET_HEADS, RET_DK, RET_DV), jnp.float32)
            op, _, sp = even_mixer(hp, pos_p, s0, w_in_e[j], sgu_ln_g[j], sgu_ln_b[j],
                                   w_spatial[j], b_spatial[j], w_out_e[j])
            o_s, v_rows, ss = even_mixer(hs, pos_s, state_ret[j], w_in_e[j], sgu_ln_g[j], sgu_ln_b[j],
                                         w_spatial[j], b_spatial[j], w_out_e[j])
            ret_p.append(sp)
            ret_s.append(ss)
            sgu_v_s.append(v_rows)
        else:
            qp, kp, vp = odd_qkv(hp, pos_p, w_qkv_o[j], b_qkv_o[j])
            op = window_attn_prompt(qp, kp, vp, sinks[j]) @ w_out_o[j] + b_out_o[j]
            qs, ks_, vs_ = odd_qkv(hs, pos_s, w_qkv_o[j], b_qkv_o[j])
            att_s, kbuf, vbuf = window_attn_sample(qs, ks_, vs_, cache_win_k[j], cache_win_v[j], sinks[j])
            o_s = att_s @ w_out_o[j] + b_out_o[j]
            wk_p.append(kp[:, -WINDOW:])
            wv_p.append(vp[:, -WINDOW:])
            wk_s.append(kbuf)
            wv_s.append(vbuf)
        xp = xp + op
        xs = xs + o_s
        mk, mv = mem_kv(mem_prompt, g_mem[layer], w_mk[layer], w_mv[layer])
        mem_k_p.append(mk)
        mem_v_p.append(mv)
        xp = xp + cross_attend(rms_norm(xp, g_cross[layer]), mk, mv, w_mq[layer], w_mo[layer])
        xs = xs + cross_attend(rms_norm(xs, g_cross[layer]), cache_mem_k[layer], cache_mem_v[layer],
                               w_mq[layer], w_mo[layer])
        xp = xp + squared_relu_mlp(rms_norm(xp, g_ffn[layer]), w_up[layer], w_down[layer])
        xs = xs + squared_relu_mlp(rms_norm(xs, g_ffn[layer]), w_up[layer], w_down[layer])
    y_prompt = rms_norm(xp, g_final)
    y_sample = rms_norm(xs, g_final)
    return (y_prompt, y_sample, jnp.stack(mem_k_p), jnp.stack(mem_v_p), jnp.stack(ret_p), jnp.stack(ret_s),
            jnp.stack(sgu_v_s), jnp.stack(wk_p), jnp.stack(wv_p), jnp.stack(wk_s), jnp.stack(wv_s))
```

```python
import math
from contextlib import ExitStack
import numpy as np
import concourse.bass as bass
import concourse.mybir as mybir
from concourse.bass_utils import run_bass_kernel_spmd

F32 = mybir.dt.float32
BF16 = mybir.dt.bfloat16
AF = mybir.ActivationFunctionType
ALU = mybir.AluOpType

NCORES = 8
D = 1024
TP = 2048
TS = 128
T = TP + TS
BLKS = [(0, 512), (512, 512), (1024, 512), (1536, 512), (2048, 128)]
BLK2 = [(i * 256, 256) for i in range(8)] + [(2048, 128)]
EPS = 1e-6
SLOT = 8192
NRING = 3
LIM = 8000
PERM_HEADS = []
for _c in range(8):
    PERM_HEADS += ([_c, 4 + _c] if _c < 4 else [4 + _c, 8 + _c])


class Tok:
    __slots__ = ("w", "r", "const")

    def __init__(self, const=False):
        self.w = None
        self.r = {}
        self.const = const


class SemC:
    def __init__(self, h, owner=None, base=0):
        self.h = h
        self.count = 0
        self.owner = owner
        self.base = base


class Eng:
    def __init__(self, name, h):
        self.name = name
        self.h = h
        self.sems = []
        self.n = 0
        self.seen = {}


class KB:
    def __init__(self, nc):
        self.nc = nc
        self.es = ExitStack()
        self.eng = {n: Eng(n, h) for n, h in [("pe", nc.tensor), ("act", nc.scalar), ("dve", nc.vector),
                                               ("pool", nc.gpsimd), ("sp", nc.sync)]}
        self.nsem = 0
        self.rings = {q: [self.newsem() for _ in range(16)] for q in ("sp", "pool")}
        self.ri = {"sp": 0, "pool": 0}
        self.bar = self.newsem()
        self.psum = []
        self.pi = 0
        for i in range(8):
            t = self.es.enter_context(nc.psum_tensor(f"ps{i}", [128, 512], F32))
            self.psum.append((t, Tok()))

    def newsem(self, owner=None, base=0):
        self.nsem += 1
        h = self.es.enter_context(self.nc.semaphore(f"s{self.nsem}"))
        return SemC(h, owner, base)

    def sb(self, scope, name, shape, dt):
        self.nsb = getattr(self, "nsb", 0) + 1
        return scope.enter_context(self.nc.sbuf_tensor(f"sb{self.nsb}_{name}", shape, dt))

    def ps(self, hold=False):
        held = getattr(self, "held", None)
        if held is None:
            held = self.held = set()
        assert len(held) < 8, "all PSUM banks held"
        while (self.pi % 8) in held:
            self.pi += 1
        i = self.pi % 8
        if hold:
            held.add(i)
        self.pi += 1
        return self.psum[i]

    def ps_release_all(self):
        self.held = set()

    def pfree(self, t):
        for i, (tt_, _) in enumerate(self.psum):
            if tt_ is t:
                self.held.discard(i)

    def pipeline(self, gens, depth):
        gens = list(gens)
        active = []
        nxt = 0
        while nxt < len(gens) or active:
            while len(active) < depth and nxt < len(gens):
                active.append(gens[nxt])
                nxt += 1
            for g in list(active):
                try:
                    next(g)
                except StopIteration:
                    active.remove(g)

    def tick(self, e):
        ep = e.n // LIM
        if ep >= len(e.sems):
            e.sems.append(self.newsem(owner=e, base=ep * LIM))
        s = e.sems[ep]
        v = e.n % LIM + 1
        e.n += 1
        return s, v

    def sync(self, e, deps):
        for d in deps:
            if d is None:
                continue
            s, v = d
            if s.owner is e:
                if e.name == "pe":
                    continue
                if s.base + v < e.n - 1:
                    continue
            if e.seen.get(s, 0) >= v:
                continue
            e.h.wait_ge(s.h, v)
            e.seen[s] = v

    def _deps(self, reads, writes):
        deps = []
        for t in reads:
            deps.append(t.w)
        for t in writes:
            deps.append(t.w)
            deps.extend(t.r.items())
        return deps

    def _update(self, d, reads, writes):
        for t in reads:
            if not t.const:
                if t.r.get(d[0], 0) < d[1]:
                    t.r[d[0]] = d[1]
        for t in writes:
            t.w = d
            t.r = {}

    def op(self, en, reads, writes, fn):
        e = self.eng[en]
        self.sync(e, self._deps(reads, writes))
        ins = fn(e.h)
        d = self.tick(e)
        ins.then_inc(d[0].h, 1)
        self._update(d, reads, writes)

    def dma(self, q, out, in_, reads, writes):
        e = self.eng[q]
        ring = self.rings[q]
        s = ring[self.ri[q] % len(ring)]
        self.ri[q] += 1
        deps = self._deps(reads, writes)
        if s.count:
            deps.append((s, s.count))
        self.sync(e, deps)
        e.h.dma_start(out=out, in_=in_).then_inc(s.h, 16)
        s.count += 16
        self._update((s, s.count), reads, writes)

    def barrier(self):
        sp = self.eng["sp"]
        deps = []
        for q in self.rings:
            for s in self.rings[q]:
                if s.count:
                    deps.append((s, s.count))
        for n in ("pe", "act", "dve"):
            e = self.eng[n]
            if e.n:
                s = e.sems[(e.n - 1) // LIM]
                deps.append((s, (e.n - 1) % LIM + 1))
        self.sync(sp, deps)
        sp.h.sem_inc(self.bar.h, 1)
        self.bar.count += 1
        for n in ("pe", "act", "dve", "pool"):
            self.sync(self.eng[n], [(self.bar, self.bar.count)])

    def mm(self, out, pairs, reads, writes):
        def fn(h):
            ins = None
            n = len(pairs)
            for i, (l, r) in enumerate(pairs):
                ins = h.matmul(out, l, r, start=(i == 0), stop=(i == n - 1))
            return ins
        self.op("pe", reads, writes, fn)

    def act(self, out, in_, func, reads, writes, **kw):
        self.op("act", reads, writes, lambda h: h.activation(out=out, in_=in_, func=func, **kw))

    def tt(self, out, a, b, op, reads, writes, en="dve"):
        self.op(en, reads, writes, lambda h: h.tensor_tensor(out=out, in0=a, in1=b, op=op))

    def ts(self, out, a, s1, s2, op0, op1, reads, writes, en="dve"):
        self.op(en, reads, writes,
                lambda h: h.tensor_scalar(out=out, in0=a, scalar1=s1, scalar2=s2, op0=op0, op1=op1))

    def stt(self, out, a, s, b, op0, op1, reads, writes, en="dve"):
        self.op(en, reads, writes,
                lambda h: h.scalar_tensor_tensor(out=out, in0=a, scalar=s, in1=b, op0=op0, op1=op1))

    def recip(self, out, in_, reads, writes):
        self.op("dve", reads, writes, lambda h: h.reciprocal(out=out, in_=in_))

    def copy(self, out, in_, reads, writes, en="dve"):
        if en == "act":
            self.act(out, in_, AF.Copy, reads, writes)
        else:
            self.op(en, reads, writes, lambda h: h.tensor_copy(out=out, in_=in_))


def build_program():
    nc = bass.Bass("TRN2", target_bir_lowering=False)
    kb = KB(nc)
    es = kb.es

    def din(name, shape):
        return nc.dram_tensor(name, list(shape), F32, kind="ExternalInput").ap()

    def dout(name, shape):
        return nc.dram_tensor(name, list(shape), F32, kind="ExternalOutput").ap()

    d_xT = din("xT", [D, T])
    d_memT = din("memT", [D, 256])
    d_w = din("wslots", [36, 128, SLOT])
    d_cbf = din("cbf", [128, 1792])
    d_cf = din("cf", [128, 110])
    d_crow = din("crow", [1, 1408])
    d_ln = din("lngb", [2, 128, 1024])
    d_wst = din("wst", [128, 8, 128])
    d_rt0 = din("rt0", [4, 128, 4, T])
    d_rt1 = din("rt1", [128, 2, T])
    d_state = din("state", [16, 4, 128, 256])
    d_cmkT = din("cmkT", [2, 16, D, 256])
    d_cmv = din("cmv", [2, 16, 256, D])
    d_cwkT = din("cwkT", [16, 128, 2, 128])
    d_cwv = din("cwv", [16, 128, 256])
    d_cwk_raw = din("cwk_raw", [16, 128, 256])
    d_cwv_raw = din("cwv_raw", [16, 128, 256])

    o_yT = dout("yT", [D, T])
    o_memk = dout("memk", [2, 256, D])
    o_memv = dout("memv", [2, 256, D])
    o_retp = dout("retp", [4, 128, 256])
    o_rets = dout("rets", [16, 4, 128, 256])
    o_sguv = dout("sguv", [128, 1024])
    o_wkT = dout("wkT", [128, 2, 256])
    o_wvp = dout("wvp", [128, 256])
    o_wvs = dout("wvs", [128, 256])
    o_wk_old = dout("wk_old", [16, 120, 256])
    o_wv_old = dout("wv_old", [16, 120, 256])
    import os
    DBG = bool(os.environ.get("KDBG"))
    if DBG:
        o_dbg = dout("dbg", [8, 128, 512])

    def dbgdump(i, ap, n, toks):
        if DBG:
            kb.dma("pool", o_dbg[i, :, 0:n], ap, toks, [])

    X = kb.sb(es, "X", [128, 8, T], F32)
    H = kb.sb(es, "H", [128, 8, T], BF16)
    WR = [kb.sb(es, f"WR{i}", [128, SLOT], BF16) for i in range(NRING)]
    cbf = kb.sb(es, "cbf", [128, 1792], BF16)
    cf = kb.sb(es, "cf", [128, 110], F32)
    crow = kb.sb(es, "crow", [1, 1408], BF16)
    Xt = [Tok() for _ in BLKS]
    Ht = [Tok() for _ in BLKS]
    WRt = [Tok() for _ in range(NRING)]
    Ct = Tok(const=True)
    sqt, rst = Tok(), Tok()

    ident = cbf[:, 0:128]
    o1024 = cbf[:, 128:256]
    o256 = cbf[:, 256:384]
    o1 = cbf[:, 384:512]
    olo = cbf[:, 512:640]
    ohi = cbf[:, 640:768]
    M1 = cbf[:, 768:896]
    M2 = cbf[:, 896:1024]
    mP = cbf[:, 1024:1280]
    mP0 = cbf[:, 1280:1536]
    mS = cbf[:, 1536:1792]

    def gv(i, kc):
        return cf[:, i * 8 + kc:i * 8 + kc + 1]
    BQ, BQS, BK, BKS, BO, ESK, EPSC, MHALF = 72, 80, 88, 90, 92, 100, 108, 109
    bsP = crow[0:1, 0:512]
    bsS = crow[0:1, 512:1024]
    bvrow = crow[0:1, 1024:1280]
    onerow = crow[0:1, 1280:1408]

    xT_v = d_xT.rearrange("(kc p) t -> p kc t", p=128)
    for bi, (c0, n) in enumerate(BLKS):
        kb.dma("sp", X[:, :, c0:c0 + n], xT_v[:, :, c0:c0 + n], [], [Xt[bi]])
    cft = Tok()
    kb.dma("sp", cf[:], d_cf, [], [cft])
    kb.dma("pool", cbf[:], d_cbf, [], [Ct])
    kb.dma("pool", crow[:], d_crow, [], [Ct])
    bsel = kb.sb(es, "bsel", [128, 16], BF16)
    d_bsel = din("bsel", [128, 16])
    kb.dma("pool", bsel[:], d_bsel, [], [Ct])
    kb.act(cf[:, ESK:ESK + 8], cf[:, ESK:ESK + 8], AF.Exp, [cft], [cft])
    kb.barrier()
    Ct.w = None

    def window_passthrough(scope):
        pas = kb.sb(scope, "pas", [120, 16, 256], F32)
        past = Tok()
        for src, dst in ((d_cwk_raw, o_wk_old), (d_cwv_raw, o_wv_old)):
            kb.dma("sp", pas[:], src[:, 8:128, :].rearrange("b s e -> s b e"), [], [past])
            kb.dma("sp", dst.rearrange("b s e -> s b e"), pas[:], [past], [past])

    wstate = {"next": 0, "free": list(range(NRING)), "loaded": {}}

    def wprefetch():
        while wstate["free"] and wstate["next"] < 36:
            k = wstate["next"]
            ph = wstate["free"].pop(0)
            kb.dma("pool", WR[ph][:], d_w[k], [], [WRt[ph]])
            wstate["loaded"][k] = ph
            wstate["next"] += 1

    def wuse(k):
        while k not in wstate["loaded"]:
            assert wstate["free"], "weight ring exhausted"
            wprefetch()
        ph = wstate["loaded"][k]
        return WR[ph], WRt[ph]

    def wrelease(k):
        ph = wstate["loaded"].pop(k)
        wstate["free"].append(ph)
        wprefetch()

    def w3(W, off, kc, n):
        return W[:, off:off + kc * n].rearrange("p (k n) -> p k n", k=kc)

    def rmsnorm(Xs, Xtok, c0, n, gi, Hs, Htok, hc0, sq, rs):
        kb.act(sq[:, :, 0:n], Xs[:, :, c0:c0 + n], AF.Square, [Xtok], [sqt])
        pt, ptk = kb.ps()
        kb.mm(pt[:, 0:n], [(o1024, sq[:, kc, 0:n]) for kc in range(8)], [sqt, Ct], [ptk])
        kb.act(rs[:, 0:n], pt[:, 0:n], AF.Ln, [ptk, Ct], [rst], bias=cf[:, EPSC:EPSC + 1], scale=1.0)
        kb.act(rs[:, 0:n], rs[:, 0:n], AF.Exp, [rst], [rst], scale=-0.5)
        for kc in range(8):
            kb.stt(Hs[:, kc, hc0:hc0 + n], Xs[:, kc, c0:c0 + n], gv(gi, kc), rs[:, 0:n],
                   ALU.mult, ALU.mult, [Xtok, rst, Ct], [Htok])

    def norm_all(gi):
        with ExitStack() as scn:
            sq = kb.sb(scn, "sq", [128, 8, 512], BF16)
            rs = kb.sb(scn, "rs", [128, 512], F32)
            for bi, (c0, n) in enumerate(BLKS):
                rmsnorm(X, Xt[bi], c0, n, gi, H, Ht[bi], c0, sq, rs)
            kb.barrier()

    def xadd(oc, c0, n, pt, ptk, bi):
        kb.tt(X[:, oc, c0:c0 + n], X[:, oc, c0:c0 + n], pt[:, 0:n], ALU.add, [ptk, Xt[bi]], [Xt[bi]])

    def layer0_mixer():
        norm_all(0)
        wprefetch()
        with ExitStack() as sc:
            lng = kb.sb(sc, "lng", [128, 1024], F32)
            lnb = kb.sb(sc, "lnb", [128, 1024], F32)
            wst = kb.sb(sc, "wst", [128, 8, 128], BF16)
            GU = kb.sb(sc, "GU", [128, 8, 512], BF16)
            AO = kb.sb(sc, "AO", [128, 8, 512], BF16)
            zv2 = [kb.sb(sc, f"zv{i}", [128, 1024], F32) for i in range(2)]
            vnb2 = [kb.sb(sc, f"vnb{i}", [128, 1024], BF16) for i in range(2)]
            st62 = [kb.sb(sc, f"st6{i}", [128, 2, 6], F32) for i in range(2)]
            mv2 = [kb.sb(sc, f"mv{i}", [128, 4], F32) for i in range(2)]
            zvt2, vnbt2, stt2, mvt2 = [[Tok(), Tok()] for _ in range(4)]
            lt, wstt, GUt, AOt = [Tok() for _ in range(4)]
            kb.dma("sp", lng[:], d_ln[0], [], [lt])
            kb.dma("sp", lnb[:], d_ln[1], [], [lt])
            kb.dma("pool", wst[:], d_wst, [], [wstt])
            for g in range(4):
                kb.tt(wst[:, g, :], wst[:, g, :], M1, ALU.mult, [wstt, Ct], [wstt])
                kb.tt(wst[:, 4 + g, :], wst[:, 4 + g, :], M2, ALU.mult, [wstt, Ct], [wstt])
            Wu, Wut = wuse(0)
            Wv, Wvt = wuse(1)
            Wo, Wot = wuse(2)
            Wu3, Wv3, Wo3 = w3(Wu, 0, 8, 1024), w3(Wv, 0, 8, 1024), w3(Wo, 0, 8, 1024)
            def uproj(bi):
                c0, n = BLKS[bi]
                for oc in range(8):
                    pt, ptk = kb.ps()
                    kb.mm(pt[:, 0:n], [(Wu3[:, kc, oc * 128:(oc + 1) * 128], H[:, kc, c0:c0 + n]) for kc in range(8)],
                          [Wut, Ht[bi]], [ptk])
                    kb.act(GU[:, oc, 0:n], pt[:, 0:n], AF.Gelu_apprx_tanh, [ptk], [GUt])

            uproj(0)
            for bi, (c0, n) in enumerate(BLKS):
                smp = bi == 4
                def sgu_chunk(bi, c0, ch, smp):
                    t0 = c0 + ch * 128
                    db = ch % 2
                    zvb, zvtb, vnbb, vnbtb = zv2[db], zvt2[db], vnb2[db], vnbt2[db]
                    st6b, sttb, mvb, mvtb = st62[db], stt2[db], mv2[db], mvt2[db]
                    pv = []
                    for hf in range(2):
                        pt, ptk = kb.ps(hold=True)
                        pv.append((pt, ptk))
                        kb.mm(pt[:, :], [(H[:, kc, t0:t0 + 128], Wv3[:, kc, hf * 512:(hf + 1) * 512]) for kc in range(8)],
                              [Wvt, Ht[bi]], [ptk])
                    yield
                    for hf in range(2):
                        pt, ptk = pv[hf]
                        kb.act(zvb[:, hf * 512:(hf + 1) * 512], pt[:, :], AF.Gelu_apprx_tanh, [ptk], [zvtb])
                        kb.pfree(pt)
                    for hf in range(2):
                        kb.op("dve", [zvtb], [sttb],
                              lambda h, hf=hf: h.bn_stats(out=st6b[:, hf, :], in_=zvb[:, hf * 512:(hf + 1) * 512]))
                    kb.op("dve", [sttb], [mvtb],
                          lambda h: h.bn_aggr(out=mvb[:, 0:2], in_=st6b[:].rearrange("p a b -> p (a b)")))
                    kb.act(mvb[:, 2:3], mvb[:, 1:2], AF.Sqrt, [mvtb], [mvtb], bias=EPS, scale=1.0)
                    kb.recip(mvb[:, 2:3], mvb[:, 2:3], [mvtb], [mvtb])
                    kb.stt(mvb[:, 3:4], mvb[:, 0:1], -1.0, mvb[:, 2:3], ALU.mult, ALU.mult, [mvtb], [mvtb])
                    kb.ts(zvb[:], zvb[:], mvb[:, 2:3], mvb[:, 3:4], ALU.mult, ALU.add, [zvtb, mvtb], [zvtb])
                    kb.tt(zvb[:], zvb[:], lng[:], ALU.mult, [zvtb, lt], [zvtb], en="pool")
                    if smp:
                        kb.tt(zvb[:], zvb[:], lnb[:], ALU.add, [zvtb, lt], [zvtb], en="pool")
                        kb.dma("sp", o_sguv, zvb[:], [zvtb], [])
                        kb.copy(vnbb[:], zvb[:], [zvtb], [vnbtb])
                    else:
                        kb.tt(vnbb[:], zvb[:], lnb[:], ALU.add, [zvtb, lt], [vnbtb], en="pool")
                    yield
                    pA, pAt = kb.ps(hold=True)
                    pB, pBt = kb.ps(hold=True)
                    for oc in range(8):
                        g = oc // 2
                        pp, ppt = (pA, pAt) if oc < 4 else (pB, pBt)
                        wsel = wst[:, (4 + g) if smp else g, :]
                        brow = (bsS if smp else bsP)[0:1, g * 128:(g + 1) * 128]
                        kb.mm(pp[:, (oc % 4) * 128:(oc % 4 + 1) * 128],
                              [(vnbb[:, oc * 128:(oc + 1) * 128], wsel), (onerow, brow)],
                              [vnbtb, wstt, Ct], [ppt])
                    yield
                    kb.tt(AO[:, 0:4, ch * 128:(ch + 1) * 128], GU[:, 0:4, ch * 128:(ch + 1) * 128],
                          pA[:, :].rearrange("p (a b) -> p a b", a=4), ALU.mult, [GUt, pAt], [AOt])
                    kb.tt(AO[:, 4:8, ch * 128:(ch + 1) * 128], GU[:, 4:8, ch * 128:(ch + 1) * 128],
                          pB[:, :].rearrange("p (a b) -> p a b", a=4), ALU.mult, [GUt, pBt], [AOt])
                    kb.pfree(pA)
                    kb.pfree(pB)

                kb.pipeline([sgu_chunk(bi, c0, ch, smp) for ch in range(n // 128)], 2)
                if bi + 1 < len(BLKS):
                    uproj(bi + 1)
                for oc in range(8):
                    pt, ptk = kb.ps()
                    kb.mm(pt[:, 0:n], [(Wo3[:, kc, oc * 128:(oc + 1) * 128], AO[:, kc, 0:n]) for kc in range(8)],
                          [Wot, AOt], [ptk])
                    xadd(oc, c0, n, pt, ptk, bi)
            kb.barrier()
        wrelease(0)
        wrelease(1)
        wrelease(2)
        import os
        if os.environ.get("KSKIPB"):
            for k in range(3, 8):
                wuse(k)
                wrelease(k)
            return
        Wob, Wobt = wuse(3)
        Wob3 = w3(Wob, 0, 8, 1024)
        with ExitStack() as sc:
            tab = kb.sb(sc, "tab", [128, 4, 512], F32)
            QK = kb.sb(sc, "QK", [128, 2, 512], BF16)
            t1 = kb.sb(sc, "t1", [128, 512], F32)
            t2 = kb.sb(sc, "t2", [128, 512], F32)
            SG = kb.sb(sc, "SG", [128, 2, 512], BF16)
            Vt = kb.sb(sc, "Vt", [128, 4, 256], BF16)
            S32 = kb.sb(sc, "S32", [128, 256], F32)
            st32 = kb.sb(sc, "st32", [128, 8, 256], F32)
            stbf = kb.sb(sc, "stbf", [128, 8, 256], BF16)
            Vblk = kb.sb(sc, "Vblk", [128, 8, 256], BF16)
            (tabt, QKt, t1t, t2t, SGt, Vtt, scTt, kTMt, S32t, Sbft, osqt, r2t, BOt, st32t, stbft,
             Vblkt) = [Tok() for _ in range(16)]
            scT4 = kb.sb(sc, "scT4", [128, 4, 128], BF16)
            kTM4 = kb.sb(sc, "kTM4", [128, 4, 128], BF16)
            Sb5 = kb.sb(sc, "Sb5", [128, 5, 256], BF16)
            osq4 = kb.sb(sc, "osq4", [128, 4, 2, 128], BF16)
            osq4t = [Tok() for _ in range(4)]
            osq2 = osq4
            BO2 = kb.sb(sc, "BO2", [128, 2, 2, 512], BF16)
            scT4t, kTM4t = [Tok() for _ in range(4)], [Tok() for _ in range(4)]
            Sb5t = [Tok() for _ in range(5)]
            osq2t, r22t, BO2t = [Tok(), Tok()], [Tok(), Tok()], [Tok(), Tok()]
            r24 = kb.sb(sc, "r24", [128, 512], F32)
            SGr = kb.sb(sc, "SGr", [128, 2, 512], BF16)
            r24t, SGrt = Tok(), Tok()
            scT, kTM, osq, r2, BO = scT4[:, 0, :], kTM4[:, 0, :], osq4[:, 0], r24[:, 0:128], BO2[:, 0]
            scTt, kTMt, osqt, r2t, BOt = scT4t[0], kTM4t[0], osq4t[0], r24t, BO2t[0]
            Sbf, Sbft = Sb5[:, 0, :], Sb5t[0]

            st32q, stbfq, Vblkq = [Tok(), Tok()], [Tok(), Tok()], [Tok(), Tok()]

            def load_state(hd, hq):
                qb2 = hq % 2
                src = d_state[hq * 4:(hq + 1) * 4, hd].rearrange("b p e -> p b e")
                kb.dma("sp", st32[:, qb2 * 4:qb2 * 4 + 4, :], src, [], [st32q[qb2]])
                kb.dma("pool", stbf[:, qb2 * 4:qb2 * 4 + 4, :], src, [], [stbfq[qb2]])

            def outproj(hd, bi, c0, n, BOb, BObt):
                for oc in range(8):
                    pt, ptk = kb.ps()
                    kb.mm(pt[:, 0:n], [(Wob3[:, 2 * hd + ec, oc * 128:(oc + 1) * 128], BOb[:, ec, 0:n]) for ec in range(2)],
                          [Wobt, BObt], [ptk])
                    xadd(oc, c0, n, pt, ptk, bi)

            for hd in range(4):
                lg = math.log1p(-2.0 ** (-5.0 - hd))
                gP = math.exp(128.0 * lg)
                gS = math.exp(8.0 * lg)
                Wh, Wht = wuse(4 + hd)
                Wh3 = w3(Wh, 0, 8, 1024)
                kb.op("dve", [], [S32t], lambda h: h.memset(S32[:], 0.0))
                kb.op("dve", [], [Sb5t[0]], lambda h: h.memset(Sb5[:, 0, :], 0.0))
                pending = None
                load_state(hd, 0)
                load_state(hd, 1)

                def p0_qk(bi, hd=hd, Wh3=Wh3, Wht=Wht):
                    c0, n = BLKS[bi]
                    kb.dma("sp", tab[:, :, 0:n], d_rt0[hd, :, :, c0:c0 + n], [], [tabt])
                    for qk in range(2):
                        pa, pat = kb.ps()
                        pb, pbt = kb.ps()
                        kb.mm(pa[:, 0:n], [(Wh3[:, kc, qk * 256:qk * 256 + 128], H[:, kc, c0:c0 + n]) for kc in range(8)],
                              [Wht, Ht[bi]], [pat])
                        kb.mm(pb[:, 0:n], [(Wh3[:, kc, qk * 256 + 128:qk * 256 + 256], H[:, kc, c0:c0 + n]) for kc in range(8)],
                              [Wht, Ht[bi]], [pbt])
                        kb.tt(t1[:, 0:n], pa[:, 0:n], tab[:, 2 * qk, 0:n], ALU.mult, [pat, tabt], [t1t])
                        kb.tt(t2[:, 0:n], pb[:, 0:n], tab[:, 2 * qk + 1, 0:n], ALU.mult, [pbt, tabt], [t2t])
                        kb.tt(QK[:, qk, 0:n], t1[:, 0:n], t2[:, 0:n], ALU.add, [t1t, t2t], [QKt])

                def p0_v(bi, Wh3=Wh3, Wht=Wht):
                    c0, n = BLKS[bi]
                    for ch in range(n // 128):
                        t0 = c0 + ch * 128
                        pt, ptk = kb.ps()
                        kb.mm(pt[:, 0:256], [(H[:, kc, t0:t0 + 128], Wh3[:, kc, 512:768]) for kc in range(8)],
                              [Wht, Ht[bi]], [ptk])
                        kb.copy(Vt[:, ch, :], pt[:, 0:256], [ptk], [Vtt], en="act")

                for bi, (c0, n) in enumerate(BLKS):
                    smp = bi == 4
                    p0_qk(bi)
                    p0_v(bi)

                    def gate_proj():
                        for ec in range(2):
                            pt, ptk = kb.ps()
                            kb.mm(pt[:, 0:n], [(Wh3[:, kc, 768 + ec * 128:768 + (ec + 1) * 128], H[:, kc, c0:c0 + n]) for kc in range(8)],
                                  [Wht, Ht[bi]], [ptk])
                            kb.act(SG[:, ec, 0:n], pt[:, 0:n], AF.Silu, [ptk], [SGt])

                    if not smp:
                        bb = bi % 2
                        BOb, BObt = BO2[:, bb], BO2t[bb]
                        nch = 4
                        for ch in range(nch):
                            cs = slice(ch * 128, (ch + 1) * 128)
                            pt, ptk = kb.ps()
                            kb.mm(pt[:, 0:128], [(QK[:, 1, cs], QK[:, 0, cs])], [QKt], [ptk])
                            kb.tt(scT4[:, ch, :], pt[:, 0:128], M1, ALU.mult, [ptk, Ct], [scT4t[ch]])
                            pk, pkt = kb.ps()
                            kb.mm(pk[:, 0:128], [(QK[:, 1, cs], ident)], [QKt, Ct], [pkt])
                            kb.copy(kTM4[:, ch, :], pk[:, 0:128], [pkt], [kTM4t[ch]], en="act")
                        gate_proj()
                        puA, puAt = kb.ps(hold=True)
                        puB, puBt = kb.ps(hold=True)
                        PU = [(puA, puAt, 0), (puA, puAt, 256), (puB, puBt, 0), (puB, puBt, 256)]
                        for ch in range(nch):
                            pu, put, o = PU[ch]
                            kb.mm(pu[:, o:o + 256], [(kTM4[:, ch, :], Vt[:, ch, :])], [kTM4t[ch], Vtt], [put])
                        if pending is not None:
                            outproj(*pending)
                            pending = None
                        for ch in range(nch):
                            g = bi * 4 + ch
                            pu, put, o = PU[ch]
                            kb.stt(S32[:], S32[:], gP, pu[:, o:o + 256], ALU.mult, ALU.add, [put, S32t], [S32t])
                            kb.act(Sb5[:, (g + 1) % 5, :], S32[:], AF.Copy, [S32t], [Sb5t[(g + 1) % 5]], scale=gP)
                        kb.pfree(puA)
                        kb.pfree(puB)
                        POs = []
                        for ch in range(nch):
                            g = bi * 4 + ch
                            cs = slice(ch * 128, (ch + 1) * 128)
                            po, pot = kb.ps(hold=True)
                            POs.append((po, pot))
                            for ec in range(2):
                                kb.mm(po[:, ec * 128:(ec + 1) * 128],
                                      [(Vt[:, ch, ec * 128:(ec + 1) * 128], scT4[:, ch, :]),
                                       (Sb5[:, g % 5, ec * 128:(ec + 1) * 128], QK[:, 0, cs])],
                                      [Vtt, scT4t[ch], Sb5t[g % 5], QKt], [pot])
                        for ch in range(nch):
                            po, pot = POs[ch]
                            kb.act(osq4[:, ch].rearrange("p a b -> p (a b)"), po[:, 0:256], AF.Square, [pot], [osq4t[ch]])
                        pn, pnt = kb.ps(hold=True)
                        for ch in range(nch):
                            kb.mm(pn[:, ch * 128:(ch + 1) * 128], [(o256, osq4[:, ch, ec, :]) for ec in range(2)], [osq4t[ch], Ct], [pnt])
                        kb.act(r24[:], pn[:, :], AF.Copy, [pnt], [r24t], bias=EPS, scale=1.0)
                        kb.pfree(pn)
                        kb.tt(r24[:], r24[:], cf[:, MHALF:MHALF + 1].to_broadcast([128, 512]), ALU.pow, [r24t, Ct], [r24t], en="pool")
                        kb.tt(SGr[:], SG[:], r24[:].unsqueeze(1).to_broadcast([128, 2, 512]), ALU.mult, [SGt, r24t], [SGrt])
                        for ch in range(nch):
                            po, pot = POs[ch]
                            cs = slice(ch * 128, (ch + 1) * 128)
                            kb.tt(BOb[:, :, cs], po[:, 0:256].rearrange("p (a b) -> p a b", a=2), SGr[:, :, cs], ALU.mult,
                                  [pot, SGrt], [BObt])
                            kb.pfree(po)
                        pending = (hd, bi, c0, n, BOb, BObt)
                        if bi == 3:
                            outproj(*pending)
                            pending = None
                            kb.act(S32[:], S32[:], AF.Copy, [S32t], [S32t], scale=gP)
                            kb.dma("sp", o_retp[hd], S32[:], [S32t], [S32t])
                        continue
                    gate_proj()
                    for ch in range(n // 128):
                        cs = slice(ch * 128, (ch + 1) * 128)
                        pt, ptk = kb.ps()
                        kb.mm(pt[:, 0:128], [(QK[:, 1, cs], QK[:, 0, cs])], [QKt], [ptk])
                        kb.tt(scT[:], pt[:, 0:128], M2 if smp else M1, ALU.mult, [ptk, Ct], [scTt])
                        pk, pkt = kb.ps()
                        kb.mm(pk[:, 0:128], [(QK[:, 1, cs], ident)], [QKt, Ct], [pkt])
                        kb.copy(kTM[:], pk[:, 0:128], [pkt], [kTMt], en="act")
                        if not smp:
                            po, pot = kb.ps()
                            PO = [(po, pot, 0), (po, pot, 128)]
                        else:
                            poA, poAt = kb.ps(hold=True)
                            poB, poBt = kb.ps(hold=True)
                            PO = [(poA, poAt, 0), (poB, poBt, 0)]
                        if not smp:
                            for ec in range(2):
                                kb.mm(po[:, ec * 128:(ec + 1) * 128],
                                      [(Vt[:, ch, ec * 128:(ec + 1) * 128], scT[:]),
                                       (Sbf[:, ec * 128:(ec + 1) * 128], QK[:, 0, cs])],
                                      [Vtt, scTt, Sbft, QKt], [pot])
                        else:
                            for hq in range(4):
                                qb2 = hq % 2
                                s32q = st32[:, qb2 * 4:qb2 * 4 + 4, :]
                                sbfq = stbf[:, qb2 * 4:qb2 * 4 + 4, :]
                                vbq = Vblk[:, qb2 * 4:qb2 * 4 + 4, :]
                                if hq == 0:
                                    for ec in range(2):
                                        kb.mm(PO[ec][0][:, 0:128],
                                              [(Vt[:, ch, ec * 128:(ec + 1) * 128], scT[:])],
                                              [Vtt, scTt], [PO[ec][1]])
                                for b4 in range(4):
                                    b = hq * 4 + b4
                                    for ec in range(2):
                                        kb.op("pe", [stbfq[qb2], QKt], [PO[ec][1]],
                                              lambda h, b=b, b4=b4, ec=ec, PO=PO, sbfq=sbfq: h.matmul(
                                                  PO[ec][0][:, b * 8:b * 8 + 8],
                                                  sbfq[:, b4, ec * 128:(ec + 1) * 128],
                                                  QK[:, 0, b * 8:b * 8 + 8], start=False, stop=True,
                                                  skip_group_check=True))
                                kb.op("dve", [Vtt, Ct], [Vblkq[qb2]],
                                      lambda h, hq=hq, vbq=vbq: h.tensor_tensor(
                                          out=vbq,
                                          in0=Vt[:, ch:ch + 1, :].to_broadcast([128, 4, 256]),
                                          in1=bsel[:, hq * 4:(hq + 1) * 4].unsqueeze(2).to_broadcast([128, 4, 256]), op=ALU.mult))
                                for b2 in range(2):
                                    pu, put = kb.ps()
                                    kb.mm(pu[:, :], [(kTM[:], vbq[:, 2 * b2:2 * b2 + 2, :].rearrange("p a b -> p (a b)"))],
                                          [kTMt, Vblkq[qb2]], [put])
                                    kb.tt(s32q[:, 2 * b2:2 * b2 + 2, :].rearrange("p a b -> p (a b)"),
                                          s32q[:, 2 * b2:2 * b2 + 2, :].rearrange("p a b -> p (a b)"),
                                          pu[:, :], ALU.add, [put, st32q[qb2]], [st32q[qb2]])
                                kb.act(s32q, s32q, AF.Copy, [st32q[qb2]], [st32q[qb2]], scale=gS)
                                kb.dma("sp", o_rets[hq * 4:(hq + 1) * 4, hd].rearrange("b p e -> p b e"), s32q,
                                       [st32q[qb2]], [st32q[qb2]])
                                if hq + 2 < 4:
                                    load_state(hd, hq + 2)
                        if not smp:
                            kb.act(osq[:].rearrange("p a b -> p (a b)"), po[:, 0:256], AF.Square, [pot], [osqt])
                        else:
                            for ec in range(2):
                                kb.act(osq[:, ec, :], PO[ec][0][:, 0:128], AF.Square, [PO[ec][1]], [osqt])
                        pn, pnt = kb.ps()
                        kb.mm(pn[:, 0:128], [(o256, osq[:, ec, :]) for ec in range(2)], [osqt, Ct], [pnt])
                        kb.act(r2[:], pn[:, 0:128], AF.Sqrt, [pnt], [r2t], bias=EPS, scale=1.0)
                        kb.recip(r2[:], r2[:], [r2t], [r2t])
                        for ec in range(2):
                            kb.tt(t1[:, 0:128], PO[ec][0][:, PO[ec][2]:PO[ec][2] + 128], r2[:], ALU.mult, [PO[ec][1], r2t], [t1t])
                            kb.tt(BO[:, ec, cs], t1[:, 0:128], SG[:, ec, cs], ALU.mult, [t1t, SGt], [BOt])
                        kb.ps_release_all()
                        if not smp:
                            pu, put = kb.ps()
                            kb.mm(pu[:, 0:256], [(kTM[:], Vt[:, ch, :])], [kTMt, Vtt], [put])
                            kb.stt(S32[:], S32[:], gP, pu[:, 0:256], ALU.mult, ALU.add, [put, S32t], [S32t])
                            kb.act(Sbf[:], S32[:], AF.Copy, [S32t], [Sbft], scale=gP)
                    if smp and hd == 0:
                        dbgdump(0, QK[:, 0, 0:128], 128, [QKt])
                        dbgdump(1, QK[:, 1, 0:128], 128, [QKt])
                        dbgdump(2, scT[:], 128, [scTt])
                        dbgdump(3, BO[:, 0, 0:128], 128, [BOt])
                        dbgdump(4, BO[:, 1, 0:128], 128, [BOt])
                        dbgdump(5, Vt[:, 0, :], 256, [Vtt])
                        dbgdump(6, SG[:, 0, 0:128], 128, [SGt])
                        dbgdump(7, r2[:], 128, [r2t])
                    for oc in range(8):
                        pt, ptk = kb.ps()
                        kb.mm(pt[:, 0:n], [(Wob3[:, 2 * hd + ec, oc * 128:(oc + 1) * 128], BO[:, ec, 0:n]) for ec in range(2)],
                              [Wobt, BOt], [ptk])
                        xadd(oc, c0, n, pt, ptk, bi)
                wrelease(4 + hd)
            kb.barrier()
        wrelease(3)

    def cross(l):
        s_mk, s_mv, s_mq, s_mo = (8, 9, 10, 11) if l == 0 else (24, 25, 26, 27)
        norm_all(2 + l)
        with ExitStack() as sc:
            KT = kb.sb(sc, "KT", [128, 8, 256], BF16)
            Vm = kb.sb(sc, "Vm", [128, 2, 1024], BF16)
            KTt, Vmt = Tok(), Tok()
            with ExitStack() as sc1:
                MT = kb.sb(sc1, "MT", [128, 8, 256], F32)
                Mh = kb.sb(sc1, "Mh", [128, 8, 256], BF16)
                kvo = kb.sb(sc1, "kvo", [128, 1024], F32)
                MTt, Mht, kvot = Tok(), Tok(), Tok()
                kb.dma("sp", MT[:], d_memT.rearrange("(kc p) m -> p kc m", p=128), [], [MTt])
                sqm = kb.sb(sc1, "sqm", [128, 8, 256], BF16)
                rsm = kb.sb(sc1, "rsm", [128, 256], F32)
                rmsnorm(MT, MTt, 0, 256, 4 + l, Mh, Mht, 0, sqm, rsm)
                Wk, Wkt = wuse(s_mk)
                Wk3 = w3(Wk, 0, 8, 1024)
                for oc in range(8):
                    pt, ptk = kb.ps()
                    kb.mm(pt[:, 0:256], [(Wk3[:, kc, oc * 128:(oc + 1) * 128], Mh[:, kc, :]) for kc in range(8)],
                          [Wkt, Mht], [ptk])
                    kb.copy(KT[:, oc, :], pt[:, 0:256], [ptk], [KTt], en="act")
                for which, (sl, dst) in enumerate([(s_mk, o_memk), (s_mv, o_memv)]):
                    Wx, Wxt = wuse(sl)
                    Wx3 = w3(Wx, 0, 8, 1024)
                    for mc in range(2):
                        for hf in range(2):
                            pt, ptk = kb.ps()
                            kb.mm(pt[:, :], [(Mh[:, kc, mc * 128:(mc + 1) * 128], Wx3[:, kc, hf * 512:(hf + 1) * 512]) for kc in range(8)],
                                  [Wxt, Mht], [ptk])
                            kb.copy(kvo[:, hf * 512:(hf + 1) * 512], pt[:, :], [ptk], [kvot], en="act")
                            if which == 1:
                                kb.copy(Vm[:, mc, hf * 512:(hf + 1) * 512], kvo[:, hf * 512:(hf + 1) * 512], [kvot], [Vmt], en="dve")
                        kb.dma("sp", dst[l, mc * 128:(mc + 1) * 128, :], kvo[:], [kvot], [kvot])
                    wrelease(sl)
                kb.barrier()
            with ExitStack() as sc2:
                QTh = kb.sb(sc2, "QTh", [128, 2, 2, 512], BF16)
                E = kb.sb(sc2, "E", [128, 2, 2, 512], BF16)
                rden = kb.sb(sc2, "rden", [128, 512], F32)
                AT = kb.sb(sc2, "AT", [128, 8, 512], BF16)
                KbT = kb.sb(sc2, "KbT", [128, 2, 8, 256], BF16)
                Vb = kb.sb(sc2, "Vb", [128, 2, 2, 1024], BF16)
                QTs = kb.sb(sc2, "QTs", [128, 8, 128], BF16)
                Eb = kb.sb(sc2, "Eb", [128, 64], BF16)
                rd = kb.sb(sc2, "rd", [128, 32], F32)
                QTht, Et = [Tok(), Tok()], [Tok(), Tok()]
                rdent, ATt, QTst, Ebt, rdt = [Tok() for _ in range(5)]
                KbTt, Vbt = [Tok(), Tok()], [Tok(), Tok()]
                Wq, Wqt = wuse(s_mq)
                Wo, Wot = wuse(s_mo)
                Wq3, Wo3 = w3(Wq, 0, 8, 1024), w3(Wo, 0, 8, 1024)
                ATs = kb.sb(sc2, "ATs", [128, 8, 128], BF16)
                ATst = Tok()

                def outproj(ATx, ATxt, bi, c0, n):
                    for oc in range(8):
                        pt, ptk = kb.ps()
                        kb.mm(pt[:, 0:n], [(Wo3[:, kc, oc * 128:(oc + 1) * 128], ATx[:, kc, 0:n]) for kc in range(8)],
                              [Wot, ATxt], [ptk])
                        xadd(oc, c0, n, pt, ptk, bi)

                def load_batch(b):
                    pb = b % 2
                    kb.dma("pool", KbT[:, pb], d_cmkT[l, b].rearrange("(kc p) m -> p kc m", p=128), [], [KbTt[pb]])
                    kb.dma("pool", Vb[:, pb], d_cmv[l, b].rearrange("(mc p) e -> p mc e", p=128), [], [Vbt[pb]])

                def head_A(bi, c0, n, hd):
                    hb = hd % 2
                    for dc in range(2):
                        pt, ptk = kb.ps()
                        kb.mm(pt[:, 0:n], [(Wq3[:, kc, (2 * hd + dc) * 128:(2 * hd + dc + 1) * 128], H[:, kc, c0:c0 + n]) for kc in range(8)],
                              [Wqt, Ht[bi]], [ptk])
                        kb.copy(QTh[:, hb, dc, 0:n], pt[:, 0:n], [ptk], [QTht[hb]], en="act")

                def head_B(bi, c0, n, hd):
                    hb = hd % 2
                    for mc in range(2):
                        pt, ptk = kb.ps()
                        kb.mm(pt[:, 0:n], [(KT[:, 2 * hd + dc, mc * 128:(mc + 1) * 128], QTh[:, hb, dc, 0:n]) for dc in range(2)],
                              [KTt, QTht[hb]], [ptk])
                        kb.act(E[:, hb, mc, 0:n], pt[:, 0:n], AF.Exp, [ptk], [Et[hb]], scale=1.0 / 16.0)

                def head_C(bi, c0, n, hd):
                    hb = hd % 2
                    pd, pdt = kb.ps()
                    kb.mm(pd[:, 0:n], [(o1, E[:, hb, mc, 0:n]) for mc in range(2)], [Et[hb], Ct], [pdt])
                    kb.act(rden[:, 0:n], pd[:, 0:n], AF.Ln, [pdt], [rdent])
                    kb.act(rden[:, 0:n], rden[:, 0:n], AF.Exp, [rdent], [rdent], scale=-1.0)
                    for ec in range(2):
                        pt, ptk = kb.ps()
                        kb.mm(pt[:, 0:n], [(Vm[:, mc, hd * 256 + ec * 128:hd * 256 + (ec + 1) * 128], E[:, hb, mc, 0:n]) for mc in range(2)],
                              [Vmt, Et[hb]], [ptk])
                        kb.tt(AT[:, 2 * hd + ec, 0:n], pt[:, 0:n], rden[:, 0:n], ALU.mult, [ptk, rdent], [ATt])

                def sample_S1(b):
                    pb = b % 2
                    pS, pSt = kb.ps()
                    for hd in range(4):
                        for mc in range(2):
                            r0 = (hd * 2 + mc) * 8
                            kb.mm(pS[:, r0:r0 + 8],
                                  [(KbT[:, pb, 2 * hd + dc, mc * 128:(mc + 1) * 128], QTs[:, 2 * hd + dc, b * 8:b * 8 + 8]) for dc in range(2)],
                                  [KbTt[pb], QTst], [pSt])
                    kb.act(Eb[:], pS[:, 0:64], AF.Exp, [pSt], [Ebt], scale=1.0 / 16.0)

                def sample_S2(b):
                    pb = b % 2
                    pO, pOt = kb.ps()
                    for hd in range(4):
                        for ec in range(2):
                            r0 = (hd * 2 + ec) * 8
                            kb.mm(pO[:, r0:r0 + 8],
                                  [(Vb[:, pb, mc, hd * 256 + ec * 128:hd * 256 + (ec + 1) * 128], Eb[:, (hd * 2 + mc) * 8:(hd * 2 + mc) * 8 + 8]) for mc in range(2)],
                                  [Vbt[pb], Ebt], [pOt])
                        kb.mm(pO[:, 64 + hd * 8:64 + hd * 8 + 8],
                              [(o1, Eb[:, (hd * 2 + mc) * 8:(hd * 2 + mc) * 8 + 8]) for mc in range(2)],
                              [Ebt, Ct], [pOt])
                    kb.act(rd[:], pO[:, 64:96], AF.Ln, [pOt], [rdt])
                    kb.act(rd[:], rd[:], AF.Exp, [rdt], [rdt], scale=-1.0)
                    for ec in range(2):
                        kb.tt(ATs[:, :, b * 8:b * 8 + 8].rearrange("p (h e) c -> p h e c", e=2)[:, :, ec, :],
                              pO[:, 0:64].rearrange("p (h e c) -> p h e c", e=2, c=8)[:, :, ec, :],
                              rd[:].rearrange("p (h c) -> p h c", c=8), ALU.mult, [pOt, rdt], [ATst])

                sc0, sn = BLKS[4]
                for oc in range(8):
                    pt, ptk = kb.ps()
                    kb.mm(pt[:, 0:sn], [(Wq3[:, kc, oc * 128:(oc + 1) * 128], H[:, kc, sc0:sc0 + sn]) for kc in range(8)],
                          [Wqt, Ht[4]], [ptk])
                    kb.copy(QTs[:, oc, :], pt[:, 0:sn], [ptk], [QTst], en="act")
                load_batch(0)
                load_batch(1)
                def unit(u):
                    bi, hd = u // 4, u % 4
                    c0, n = BLKS[bi]
                    return bi, c0, n, hd
                head_A(*unit(0))
                head_B(*unit(0))
                for u in range(16):
                    bi, c0, n, hd = unit(u)
                    if u + 1 < 16:
                        head_A(*unit(u + 1))
                    sample_S1(u)
                    head_C(bi, c0, n, hd)
                    if u + 1 < 16:
                        head_B(*unit(u + 1))
                    if hd == 3:
                        outproj(AT, ATt, bi, c0, n)
                    sample_S2(u)
                    if u + 2 < 16:
                        load_batch(u + 2)
                outproj(ATs, ATst, 4, sc0, sn)
                kb.barrier()
            wrelease(s_mq)
            wrelease(s_mo)

    def mlp(l):
        base = 12 if l == 0 else 28
        norm_all(6 + l)
        with ExitStack() as sc:
            r32 = kb.sb(sc, "r32", [128, 2, 512], F32)
            hid = kb.sb(sc, "hid", [128, 2, 4, 512], BF16)
            r32t, hidt = [Tok(), Tok()], [Tok(), Tok()]
            if l == 0:
                window_passthrough(sc)
            its = [(fb, bi) for fb in range(8) for bi in range(len(BLKS))]
            wcache = {}

            def getw(fb):
                if fb not in wcache:
                    Wf, Wft = wuse(base + fb)
                    wcache[fb] = (w3(Wf, 0, 8, 512), w3(Wf, 4096, 4, 1024), Wft)
                return wcache[fb]

            def up(i):
                fb, bi = its[i]
                c0, n = BLKS[bi]
                Wup, Wdn, Wft = getw(fb)
                hbuf = i % 2
                for hc in range(4):
                    pt, ptk = kb.ps()
                    kb.mm(pt[:, 0:n], [(Wup[:, kc, hc * 128:(hc + 1) * 128], H[:, kc, c0:c0 + n]) for kc in range(8)],
                          [Wft, Ht[bi]], [ptk])
                    rb = hc % 2
                    kb.act(r32[:, rb, 0:n], pt[:, 0:n], AF.Relu, [ptk], [r32t[rb]])
                    kb.tt(hid[:, hbuf, hc, 0:n], r32[:, rb, 0:n], r32[:, rb, 0:n], ALU.mult, [r32t[rb]], [hidt[hbuf]])

            def down(i):
                fb, bi = its[i]
                c0, n = BLKS[bi]
                Wup, Wdn, Wft = getw(fb)
                hbuf = i % 2
                for oc in range(8):
                    pt, ptk = kb.ps()
                    kb.mm(pt[:, 0:n], [(Wdn[:, hc, oc * 128:(oc + 1) * 128], hid[:, hbuf, hc, 0:n]) for hc in range(4)],
                          [Wft, hidt[hbuf]], [ptk])
                    xadd(oc, c0, n, pt, ptk, bi)
                if bi == len(BLKS) - 1:
                    wrelease(base + fb)

            up(0)
            for i in range(len(its)):
                if i + 1 < len(its):
                    up(i + 1)
                down(i)
            kb.barrier()

    def layer1_mixer():
        norm_all(1)
        with ExitStack() as sc:
            KTa = kb.sb(sc, "KTa", [128, 2, T], BF16)
            Va = kb.sb(sc, "Va", [128, 17, 256], BF16)
            tb = kb.sb(sc, "tb", [128, 2, 256], F32)
            t1 = kb.sb(sc, "t1", [128, 256], F32)
            t2 = kb.sb(sc, "t2", [128, 256], F32)
            scp1 = ExitStack()
            k32 = kb.sb(scp1, "k32", [128, 2, 128], F32)
            v32 = kb.sb(scp1, "v32", [128, 256], F32)
            KTat = [Tok() for _ in BLKS]
            Vat = [Tok() for _ in BLKS]
            tbt, t1t, t2t, k32t, v32t = [Tok() for _ in range(5)]
            Wkv, Wkvt = wuse(20)
            Wkv3 = w3(Wkv, 0, 8, 768)
            for (c0, n) in BLK2:
                bi = min(c0 // 512, 4)
                kb.dma("sp", tb[:, :, 0:n], d_rt1[:, :, c0:c0 + n], [], [tbt])
                for kc in range(2):
                    pa, pat = kb.ps()
                    pb, pbt = kb.ps()
                    kb.mm(pa[:, 0:n], [(Wkv3[:, k, kc * 128:(kc + 1) * 128], H[:, k, c0:c0 + n]) for k in range(8)],
                          [Wkvt, Ht[bi]], [pat])
                    kb.mm(pb[:, 0:n], [(Wkv3[:, k, 256 + kc * 128:256 + (kc + 1) * 128], H[:, k, c0:c0 + n]) for k in range(8)],
                          [Wkvt, Ht[bi]], [pbt])
                    kb.stt(t1[:, 0:n], pa[:, 0:n], cf[:, BK + kc:BK + kc + 1], tb[:, 0, 0:n], ALU.add, ALU.mult, [pat, tbt, Ct], [t1t])
                    kb.stt(t2[:, 0:n], pb[:, 0:n], cf[:, BKS + kc:BKS + kc + 1], tb[:, 1, 0:n], ALU.add, ALU.mult, [pbt, tbt, Ct], [t2t])
                    kb.tt(KTa[:, kc, c0:c0 + n], t1[:, 0:n], t2[:, 0:n], ALU.add, [t1t, t2t], [KTat[bi]])
                    if c0 >= 1792:
                        lo = n - 128
                        kb.tt(k32[:, kc, :], t1[:, lo:n], t2[:, lo:n], ALU.add, [t1t, t2t], [k32t])
                if c0 >= 1792:
                    kb.dma("sp", o_wkT[:, (c0 - 1792) // 256, :].rearrange("p (k t) -> p k t", k=2), k32[:], [k32t], [k32t])
                for ch in range(n // 128):
                    t0 = c0 + ch * 128
                    ci = t0 // 128
                    pt, ptk = kb.ps()
                    kb.mm(pt[:, 0:256], [(H[:, k, t0:t0 + 128], Wkv3[:, k, 512:768]) for k in range(8)] + [(onerow, bvrow)],
                          [Wkvt, Ht[bi], Ct], [ptk])
                    kb.copy(Va[:, ci, :], pt[:, 0:256], [ptk], [Vat[bi]], en="act")
                    if ci == 15 or ci == 16:
                        kb.copy(v32[:], pt[:, 0:256], [ptk, Vat[bi]], [v32t], en="dve")
                        kb.dma("sp", o_wvp if ci == 15 else o_wvs, v32[:], [v32t], [v32t])
            wrelease(20)
            kb.barrier()
            scp1.close()
            QTc = kb.sb(sc, "QTc", [128, 3, 2, 256], BF16)
            Ee = kb.sb(sc, "Ee", [128, 6, 512], BF16)
            rr = kb.sb(sc, "rr", [128, 128], F32)
            AT = kb.sb(sc, "AT", [128, 8, 256], BF16)
            KcT = kb.sb(sc, "KcT", [128, 16, 256], BF16)
            Vc = kb.sb(sc, "Vc", [128, 16, 256], BF16)
            QTct, Eet = [Tok(), Tok(), Tok()], [Tok() for _ in range(6)]
            kb.op("dve", [], QTct, lambda h: h.memset(QTc[:], 0.0))
            rrt, ATt, KcTt, Vct = [Tok() for _ in range(4)]
            kb.dma("pool", KcT[:], d_cwkT.rearrange("b p k s -> p b (k s)"), [], [KcTt])
            kb.dma("pool", Vc[:], d_cwv.rearrange("b s e -> s b e"), [], [Vct])
            Wq, Wqt = wuse(21)
            Wqs, Wqst = wuse(22)
            Wo, Wot = wuse(23)
            Wq3, Wqs3, Wo3 = w3(Wq, 0, 8, 1024), w3(Wqs, 0, 8, 1024), w3(Wo, 0, 8, 1024)
            def pair_gen(c0, n, bi, c, smp):
                kcx = c // 4
                qb = c % 3
                pa, pat = kb.ps(hold=True)
                pb, pbt = kb.ps(hold=True)
                kb.mm(pa[:, 0:n], [(Wq3[:, k, c * 128:(c + 1) * 128], H[:, k, c0:c0 + n]) for k in range(8)],
                      [Wqt, Ht[bi]], [pat])
                kb.mm(pb[:, 0:n], [(Wqs3[:, k, c * 128:(c + 1) * 128], H[:, k, c0:c0 + n]) for k in range(8)],
                      [Wqst, Ht[bi]], [pbt])
                yield
                kb.stt(t1[:, 0:n], pa[:, 0:n], cf[:, BQ + c:BQ + c + 1], tb[:, 0, 0:n], ALU.add, ALU.mult, [pat, tbt, Ct], [t1t])
                kb.stt(t2[:, 0:n], pb[:, 0:n], cf[:, BQS + c:BQS + c + 1], tb[:, 1, 0:n], ALU.add, ALU.mult, [pbt, tbt, Ct], [t2t])
                kb.pfree(pa)
                kb.pfree(pb)
                kb.tt(QTc[0:64, qb, 0, 0:n], t1[0:64, 0:n], t2[0:64, 0:n], ALU.add, [t1t, t2t], [QTct[qb]])
                kb.tt(QTc[64:128, qb, 1, 0:n], t1[64:128, 0:n], t2[64:128, 0:n], ALU.add, [t1t, t2t], [QTct[qb]])
                for ch in range(n // 128):
                    t0 = c0 + ch * 128
                    ci = t0 // 128
                    cs = slice(ch * 128, (ch + 1) * 128)
                    eb = qb * 2 + ch
                    pci = max(ci - 1, 0)
                    pbi = min(pci // 4, 4)
                    mask = mS if smp else (mP0 if ci == 0 else mP)
                    pS, pSt = kb.ps(hold=True)
                    rd_toks = [QTct[qb], KTat[bi], Ct] + ([KcTt] if smp else [KTat[pbi]])

                    def emit_scores(h, pS=pS, qb=qb, kcx=kcx, pci=pci, t0=t0, cs=cs, mask=mask, smp=smp):
                        ins = None
                        for e2 in range(2):
                            h.matmul(pS[:, e2 * 256:(e2 + 1) * 256], ident, mask, start=True, stop=False, skip_group_check=True)
                            if not smp:
                                h.matmul(pS[:, e2 * 256:e2 * 256 + 128], KTa[:, kcx, pci * 128:(pci + 1) * 128],
                                         QTc[:, qb, e2, cs], start=False, stop=False, skip_group_check=True)
                            else:
                                for b in range(16):
                                    h.matmul(pS[:, e2 * 256 + b * 8:e2 * 256 + b * 8 + 8], KcT[:, b, kcx * 128:(kcx + 1) * 128],
                                             QTc[:, qb, e2, b * 8:b * 8 + 8], start=False, stop=False, skip_group_check=True)
                            ins = h.matmul(pS[:, e2 * 256 + 128:e2 * 256 + 256], KTa[:, kcx, t0:t0 + 128],
                                           QTc[:, qb, e2, cs], start=False, stop=True, skip_group_check=True)
                        return ins
                    kb.op("pe", rd_toks, [pSt], emit_scores)
                    yield
                    kb.act(Ee[:, eb, :], pS[:, :], AF.Exp, [pSt], [Eet[eb]], scale=0.125)
                    kb.pfree(pS)
                    pO, pOt = kb.ps(hold=True)
                    for e2 in range(2):
                        if not smp:
                            kb.mm(pO[:, e2 * 128:(e2 + 1) * 128],
                                  [(Va[:, pci, kcx * 128:(kcx + 1) * 128], Ee[:, eb, e2 * 256:e2 * 256 + 128]),
                                   (Va[:, ci, kcx * 128:(kcx + 1) * 128], Ee[:, eb, e2 * 256 + 128:e2 * 256 + 256])],
                                  [Vat[pbi], Vat[bi], Eet[eb]], [pOt])
                        else:
                            kb.mm(pO[:, e2 * 128:(e2 + 1) * 128],
                                  [(Va[:, ci, kcx * 128:(kcx + 1) * 128], Ee[:, eb, e2 * 256 + 128:e2 * 256 + 256])],
                                  [Vat[bi], Eet[eb]], [pOt])
                            for b in range(16):
                                kb.op("pe", [Vct, Eet[eb]], [pOt],
                                      lambda h, b=b, e2=e2, eb=eb, kcx=kcx, pO=pO: h.matmul(
                                          pO[:, e2 * 128 + b * 8:e2 * 128 + b * 8 + 8],
                                          Vc[:, b, kcx * 128:(kcx + 1) * 128],
                                          Ee[:, eb, e2 * 256 + b * 8:e2 * 256 + b * 8 + 8],
                                          start=False, stop=True, skip_group_check=True))
                    kb.mm(pO[:, 256:384],
                          [(olo, Ee[:, eb, 0:128]), (olo, Ee[:, eb, 128:256]),
                           (ohi, Ee[:, eb, 256:384]), (ohi, Ee[:, eb, 384:512])],
                          [Eet[eb], Ct], [pOt])
                    yield
                    kb.act(rr[:], pO[:, 256:384], AF.Ln, [pOt, Ct], [rrt], bias=cf[:, ESK + c:ESK + c + 1], scale=1.0)
                    kb.act(rr[:], rr[:], AF.Exp, [rrt], [rrt], scale=-1.0)
                    for e2 in range(2):
                        rows = slice(e2 * 64, (e2 + 1) * 64)
                        kb.tt(AT[rows, c, cs], pO[rows, e2 * 128:(e2 + 1) * 128], rr[rows, :], ALU.mult, [pOt, rrt], [ATt])
                    kb.pfree(pO)

            for (c0, n) in BLK2:
                bi = min(c0 // 512, 4)
                smp = bi == 4
                kb.dma("sp", tb[:, :, 0:n], d_rt1[:, :, c0:c0 + n], [], [tbt])
                kb.pipeline([pair_gen(c0, n, bi, c, smp) for c in range(8)], 3)
                for oc in range(8):
                    pt, ptk = kb.ps()
                    kb.mm(pt[:, 0:n], [(Wo3[:, k, oc * 128:(oc + 1) * 128], AT[:, k, 0:n]) for k in range(8)],
                          [Wot, ATt], [ptk])
                    kb.stt(X[:, oc, c0:c0 + n], pt[:, 0:n], cf[:, BO + oc:BO + oc + 1], X[:, oc, c0:c0 + n],
                           ALU.add, ALU.add, [ptk, Xt[bi], Ct], [Xt[bi]])
            kb.barrier()
        wrelease(21)
        wrelease(22)
        wrelease(23)

    import os
    KST = int(os.environ.get("KSTAGES", "99"))
    if KST >= 1:
        layer0_mixer()
    if KST >= 2:
        cross(0)
    if KST >= 3:
        mlp(0)
    if KST >= 4:
        layer1_mixer()
    if KST >= 5:
        cross(1)
        mlp(1)

    with ExitStack() as sc:
        yo = kb.sb(sc, "yo", [128, 8, 512], F32)
        sq = kb.sb(sc, "sq", [128, 8, 512], BF16)
        rs = kb.sb(sc, "rs", [128, 512], F32)
        yot = Tok()
        yT_v = o_yT.rearrange("(kc p) t -> p kc t", p=128)
        for bi, (c0, n) in enumerate(BLKS):
            rmsnorm(X, Xt[bi], c0, n, 8, yo, yot, 0, sq, rs)
            kb.dma("sp", yT_v[:, :, c0:c0 + n], yo[:, :, 0:n], [yot], [yot])
        sp = kb.eng["sp"]
        deps = []
        for q in kb.rings:
            for s in kb.rings[q]:
                if s.count:
                    deps.append((s, s.count))
        kb.sync(sp, deps)
        kb.barrier()
    es.close()
    return nc


def _slot(w):
    K, N = w.shape
    return np.ascontiguousarray(w.reshape(K // 128, 128, N).transpose(1, 0, 2).reshape(128, -1))


def _pad_slot(a):
    out = np.zeros((128, SLOT), np.float32)
    out[:, :a.shape[1]] = a
    return out


def _const_tables():
    f32 = np.float32
    cbf = np.zeros((128, 1792), f32)
    cbf[:, 0:128] = np.eye(128)
    cbf[:, 128:256] = 1.0 / 1024.0
    cbf[:, 256:384] = 1.0 / 256.0
    cbf[:, 384:512] = 1.0
    cbf[:, 512:576] = 1.0
    cbf[:, 704:768] = 1.0
    j = np.arange(128)[:, None]
    i = np.arange(128)[None, :]
    A = (j <= i).astype(f32)
    cbf[:, 768:896] = A
    cbf[:, 896:1024] = ((j // 8 == i // 8) & (j <= i)).astype(f32)
    cbf[:, 1024:1152] = 1.0 - A
    cbf[:, 1152:1280] = A
    cbf[:, 1408:1536] = A
    cbf[:, 1536:1664] = (j > (i % 8)).astype(f32)
    cbf[:, 1664:1792] = cbf[:, 896:1024]
    cbf[:, 1024:1792] = (cbf[:, 1024:1792] - 1.0) * 30000.0
    bsel = np.zeros((128, 16), f32)
    for b in range(16):
        bsel[b * 8:(b + 1) * 8, b] = 1.0
    pos = np.concatenate([np.arange(TP), np.tile(16384 + np.arange(8), 16)]).astype(np.int32)
    ci = np.concatenate([np.arange(TP) % 128, np.tile(np.arange(8), 16)]).astype(np.float64)
    half = 64
    inv = (f32(10000.0) ** (-np.arange(half, dtype=f32) / f32(half))).astype(f32)
    ang = pos.astype(f32)[:, None] * inv[None, :]
    cs = np.cos(ang).astype(np.float64).T
    sn = np.sin(ang).astype(np.float64).T
    cosd = np.concatenate([cs, cs], 0)
    sind = np.concatenate([-sn, sn], 0)
    rt0 = np.zeros((4, 128, 4, T), f32)
    for h in range(4):
        lg = math.log1p(-2.0 ** (-5.0 - h))
        xi = np.exp((ci + 1.0) * lg)[None, :]
        kk = (1.0 / xi) * (128.0 ** -0.5)
        rt0[h, :, 0] = cosd * xi
        rt0[h, :, 1] = sind * xi
        rt0[h, :, 2] = cosd * kk
        rt0[h, :, 3] = sind * kk
    half = 32
    inv1 = (f32(150000.0) ** (-np.arange(half, dtype=f32) / f32(half))).astype(f32)
    ang1 = pos.astype(f32)[:, None] * inv1[None, :]
    c1 = np.cos(ang1).astype(f32).T
    s1 = np.sin(ang1).astype(f32).T
    rt1 = np.zeros((128, 2, T), f32)
    rt1[:, 0] = np.concatenate([c1, c1, c1, c1], 0)
    rt1[:, 1] = np.concatenate([-s1, s1, -s1, s1], 0)
    return cbf, bsel, rt0, rt1


def _prep_shared(inp):
    f32 = np.float32
    g = lambda k: np.asarray(inp[k], f32)
    w_in = g("w_in_e")[0]
    slots = []
    slots.append(_slot(w_in[:, 0:1024]))
    slots.append(_slot(w_in[:, 1024:2048]))
    w_out = g("w_out_e")[0]
    slots.append(_slot(w_out[0:1024]))
    slots.append(_slot(w_out[1024:2048]))
    sw = np.concatenate([np.arange(64, 128), np.arange(0, 64)])
    for h in range(4):
        q = w_in[:, 2048 + h * 128:2048 + (h + 1) * 128]
        k = w_in[:, 2560 + h * 128:2560 + (h + 1) * 128]
        v = w_in[:, 3072 + h * 256:3072 + (h + 1) * 256]
        gt = w_in[:, 4096 + h * 256:4096 + (h + 1) * 256]
        slots.append(_slot(np.concatenate([q, q[:, sw], k, k[:, sw], v, gt], 1)))
    def cross_slots(l):
        return [_slot(g("w_mk")[l]), _slot(g("w_mv")[l]), _slot(g("w_mq")[l]), _slot(g("w_mo")[l])]
    def mlp_slots(l):
        up, dn = g("w_up")[l], g("w_down")[l]
        return [np.concatenate([_slot(up[:, fb * 512:(fb + 1) * 512]), _slot(dn[fb * 512:(fb + 1) * 512, :])], 1)
                for fb in range(8)]
    slots += cross_slots(0) + mlp_slots(0)
    wqkv = g("w_qkv_o")[0]
    bqkv = g("b_qkv_o")[0]
    sw64 = np.concatenate([np.arange(32, 64), np.arange(0, 32)])
    qcols = np.concatenate([np.arange(h * 64, (h + 1) * 64) for h in PERM_HEADS])
    qscols = np.concatenate([h * 64 + sw64 for h in PERM_HEADS])
    kcols = 1024 + np.arange(256)
    kscols = 1024 + np.concatenate([h * 64 + sw64 for h in range(4)])
    vcols = 1280 + np.arange(256)
    slots.append(_pad_slot(_slot(wqkv[:, np.concatenate([kcols, kscols, vcols])])))
    slots.append(_slot(wqkv[:, qcols]))
    slots.append(_slot(wqkv[:, qscols]))
    slots.append(_slot(g("w_out_o")[0][qcols, :]))
    slots += cross_slots(1)
    slots += mlp_slots(1)
    assert len(slots) == 36
    wslots = np.stack(slots, 0)
    cf = np.zeros((128, 110), f32)
    cf[:, 108] = EPS
    cf[:, 109] = -0.5
    gl = [g("g_mix")[0], g("g_mix")[1], g("g_cross")[0], g("g_cross")[1], g("g_mem")[0], g("g_mem")[1],
          g("g_ffn")[0], g("g_ffn")[1], g("g_final")]
    for i, v in enumerate(gl):
        cf[:, i * 8:(i + 1) * 8] = v.reshape(8, 128).T
    cf[:, 72:80] = bqkv[qcols].reshape(8, 128).T
    cf[:, 80:88] = bqkv[qscols].reshape(8, 128).T
    cf[:, 88:90] = bqkv[kcols].reshape(2, 128).T
    cf[:, 90:92] = bqkv[kscols].reshape(2, 128).T
    cf[:, 92:100] = g("b_out_o")[0].reshape(8, 128).T
    sk = g("sinks")[0]
    for c in range(8):
        cf[0:64, 100 + c] = sk[PERM_HEADS[2 * c]]
        cf[64:128, 100 + c] = sk[PERM_HEADS[2 * c + 1]]
    crow = np.zeros((1, 1408), f32)
    bs = g("b_spatial")[0]
    crow[0, 0:512] = bs.reshape(-1)
    crow[0, 512:1024] = np.tile(bs[:, 0:8], (1, 16)).reshape(-1)
    crow[0, 1024:1280] = bqkv[vcols]
    crow[0, 1280:1408] = 1.0
    lngb = np.stack([np.broadcast_to(g("sgu_ln_g")[0], (128, 1024)), np.broadcast_to(g("sgu_ln_b")[0], (128, 1024))], 0)
    ws = g("w_spatial")[0]
    wst = np.zeros((128, 8, 128), f32)
    for gg in range(4):
        wst[:, gg, :] = ws[gg].T
        for b in range(16):
            wst[b * 8:(b + 1) * 8, 4 + gg, b * 8:(b + 1) * 8] = ws[gg, 0:8, 0:8].T
    return wslots, cf, crow, np.ascontiguousarray(lngb), wst


def make_in_maps(inp, cores=range(NCORES)):
    f32 = np.float32
    cbf, bsel, rt0, rt1 = _const_tables()
    wslots, cf, crow, lngb, wst = _prep_shared(inp)
    in_maps = []
    for c in cores:
        bs = slice(16 * c, 16 * c + 16)
        xs = np.asarray(inp["x_sample"][bs], f32).reshape(128, D)
        xT = np.ascontiguousarray(np.concatenate([np.asarray(inp["x_prompt"][c], f32), xs], 0).T)
        cmk = np.asarray(inp["cache_mem_k"][:, bs], f32).reshape(2, 16, 256, D)
        cmv = np.asarray(inp["cache_mem_v"][:, bs], f32).reshape(2, 16, 256, D)
        cwk = np.asarray(inp["cache_win_k"][0, bs], f32).reshape(16, 128, 256)
        cwv = np.asarray(inp["cache_win_v"][0, bs], f32).reshape(16, 128, 256)
        cwkT = np.ascontiguousarray(cwk.reshape(16, 128, 2, 128).transpose(0, 3, 2, 1))
        in_maps.append({
            "xT": xT,
            "memT": np.ascontiguousarray(np.asarray(inp["mem_prompt"][c], f32).T),
            "wslots": wslots,
            "cbf": cbf, "cf": cf, "crow": crow, "lngb": lngb, "wst": wst, "rt0": rt0, "rt1": rt1, "bsel": bsel,
            "state": np.ascontiguousarray(np.asarray(inp["state_ret"][0, bs], f32)),
            "cmkT": np.ascontiguousarray(cmk.transpose(0, 1, 3, 2)),
            "cmv": np.ascontiguousarray(cmv),
            "cwkT": cwkT, "cwv": np.ascontiguousarray(cwv),
            "cwk_raw": np.ascontiguousarray(cwk), "cwv_raw": np.ascontiguousarray(cwv),
        })
    return in_maps


def kernel(**inp):
    f32 = np.float32
    nc = build_program()
    in_maps = make_in_maps(inp)
    res = run_bass_kernel_spmd(nc, in_maps, core_ids=list(range(NCORES)))
    R = res.results
    y_p = np.stack([R[c]["yT"][:, :TP].T for c in range(NCORES)], 0).astype(f32)
    y_s = np.concatenate([R[c]["yT"][:, TP:].T.reshape(16, 8, D) for c in range(NCORES)], 0).astype(f32)
    mem_k = np.stack([R[c]["memk"] for c in range(NCORES)], 1).reshape(2, 8, 256, 4, 256).astype(f32)
    mem_v = np.stack([R[c]["memv"] for c in range(NCORES)], 1).reshape(2, 8, 256, 4, 256).astype(f32)
    ret_p = np.stack([R[c]["retp"] for c in range(NCORES)], 0)[None].astype(f32)
    ret_s = np.concatenate([R[c]["rets"] for c in range(NCORES)], 0)[None].astype(f32)
    sgu_v = np.concatenate([R[c]["sguv"].reshape(16, 8, 4, 256) for c in range(NCORES)], 0)[None].astype(f32)

    def kT_to_tm(a):
        return a.reshape(2, 64, 2, 128).transpose(3, 2, 0, 1).reshape(128, 4, 64)

    wk_p = np.stack([kT_to_tm(R[c]["wkT"][:, 0, :].reshape(128, 2, 128)) for c in range(NCORES)], 0)[None].astype(f32)
    wv_p = np.stack([R[c]["wvp"].reshape(128, 4, 64) for c in range(NCORES)], 0)[None].astype(f32)
    wk_s_l, wv_s_l = [], []
    for c in range(NCORES):
        knew = kT_to_tm(R[c]["wkT"][:, 1, :].reshape(128, 2, 128)).reshape(16, 8, 4, 64)
        vnew = R[c]["wvs"].reshape(16, 8, 4, 64)
        wk_s_l.append(np.concatenate([R[c]["wk_old"].reshape(16, 120, 4, 64), knew], 1))
        wv_s_l.append(np.concatenate([R[c]["wv_old"].reshape(16, 120, 4, 64), vnew], 1))
    wk_s = np.concatenate(wk_s_l, 0)[None].astype(f32)
    wv_s = np.concatenate(wv_s_l, 0)[None].astype(f32)
    return (y_p, y_s, mem_k, mem_v, ret_p, ret_s, sgu_v, wk_p, wv_p, wk_s, wv_s)
```

```python
import math
from contextlib import ExitStack
import numpy as np
import concourse.bass as bass
import concourse.mybir as mybir
from concourse.bass_utils import run_bass_kernel_spmd

F32 = mybir.dt.float32
BF16 = mybir.dt.bfloat16
AF = mybir.ActivationFunctionType
ALU = mybir.AluOpType

NCORES = 8
D = 1024
TP = 2048
TS = 128
T = TP + TS
BLKS = [(0, 512), (512, 512), (1024, 512), (1536, 512), (2048, 128)]
BLK2 = [(i * 256, 256) for i in range(8)] + [(2048, 128)]
EPS = 1e-6
SLOT = 8192
NRING = 3
LIM = 8000
PERM_HEADS = []
for _c in range(8):
    PERM_HEADS += ([_c, 4 + _c] if _c < 4 else [4 + _c, 8 + _c])


class Tok:
    __slots__ = ("w", "r", "const")

    def __init__(self, const=False):
        self.w = None
        self.r = {}
        self.const = const


class SemC:
    def __init__(self, h, owner=None, base=0):
        self.h = h
        self.count = 0
        self.owner = owner
        self.base = base


class Eng:
    def __init__(self, name, h):
        self.name = name
        self.h = h
        self.sems = []
        self.n = 0
        self.seen = {}


class KB:
    def __init__(self, nc):
        self.nc = nc
        self.es = ExitStack()
        self.eng = {n: Eng(n, h) for n, h in [("pe", nc.tensor), ("act", nc.scalar), ("dve", nc.vector),
                                               ("pool", nc.gpsimd), ("sp", nc.sync)]}
        self.nsem = 0
        self.rings = {q: [self.newsem() for _ in range(16)] for q in ("sp", "pool")}
        self.ri = {"sp": 0, "pool": 0}
        self.bar = self.newsem()
        self.psum = []
        self.pi = 0
        for i in range(8):
            t = self.es.enter_context(nc.psum_tensor(f"ps{i}", [128, 512], F32))
            self.psum.append((t, Tok()))

    def newsem(self, owner=None, base=0):
        self.nsem += 1
        h = self.es.enter_context(self.nc.semaphore(f"s{self.nsem}"))
        return SemC(h, owner, base)

    def sb(self, scope, name, shape, dt):
        self.nsb = getattr(self, "nsb", 0) + 1
        return scope.enter_context(self.nc.sbuf_tensor(f"sb{self.nsb}_{name}", shape, dt))

    def ps(self, hold=False):
        held = getattr(self, "held", None)
        if held is None:
            held = self.held = set()
        assert len(held) < 8, "all PSUM banks held"
        while (self.pi % 8) in held:
            self.pi += 1
        i = self.pi % 8
        if hold:
            held.add(i)
        self.pi += 1
        return self.psum[i]

    def ps_release_all(self):
        self.held = set()

    def pfree(self, t):
        for i, (tt_, _) in enumerate(self.psum):
            if tt_ is t:
                self.held.discard(i)

    def pipeline(self, gens, depth):
        gens = list(gens)
        active = []
        nxt = 0
        while nxt < len(gens) or active:
            while len(active) < depth and nxt < len(gens):
                active.append(gens[nxt])
                nxt += 1
            for g in list(active):
                try:
                    next(g)
                except StopIteration:
                    active.remove(g)

    def tick(self, e):
        ep = e.n // LIM
        if ep >= len(e.sems):
            e.sems.append(self.newsem(owner=e, base=ep * LIM))
        s = e.sems[ep]
        v = e.n % LIM + 1
        e.n += 1
        return s, v

    def sync(self, e, deps):
        for d in deps:
            if d is None:
                continue
            s, v = d
            if s.owner is e:
                if e.name == "pe":
                    continue
                if s.base + v < e.n - 1:
                    continue
            if e.seen.get(s, 0) >= v:
                continue
            e.h.wait_ge(s.h, v)
            e.seen[s] = v

    def _deps(self, reads, writes):
        deps = []
        for t in reads:
            deps.append(t.w)
        for t in writes:
            deps.append(t.w)
            deps.extend(t.r.items())
        return deps

    def _update(self, d, reads, writes):
        for t in reads:
            if not t.const:
                if t.r.get(d[0], 0) < d[1]:
                    t.r[d[0]] = d[1]
        for t in writes:
            t.w = d
            t.r = {}

    def op(self, en, reads, writes, fn):
        e = self.eng[en]
        self.sync(e, self._deps(reads, writes))
        ins = fn(e.h)
        d = self.tick(e)
        ins.then_inc(d[0].h, 1)
        self._update(d, reads, writes)

    def dma(self, q, out, in_, reads, writes):
        e = self.eng[q]
        ring = self.rings[q]
        s = ring[self.ri[q] % len(ring)]
        self.ri[q] += 1
        deps = self._deps(reads, writes)
        if s.count:
            deps.append((s, s.count))
        self.sync(e, deps)
        e.h.dma_start(out=out, in_=in_).then_inc(s.h, 16)
        s.count += 16
        self._update((s, s.count), reads, writes)

    def barrier(self):
        sp = self.eng["sp"]
        deps = []
        for q in self.rings:
            for s in self.rings[q]:
                if s.count:
                    deps.append((s, s.count))
        for n in ("pe", "act", "dve"):
            e = self.eng[n]
            if e.n:
                s = e.sems[(e.n - 1) // LIM]
                deps.append((s, (e.n - 1) % LIM + 1))
        self.sync(sp, deps)
        sp.h.sem_inc(self.bar.h, 1)
        self.bar.count += 1
        for n in ("pe", "act", "dve", "pool"):
            self.sync(self.eng[n], [(self.bar, self.bar.count)])

    def mm(self, out, pairs, reads, writes):
        def fn(h):
            ins = None
            n = len(pairs)
            for i, (l, r) in enumerate(pairs):
                ins = h.matmul(out, l, r, start=(i == 0), stop=(i == n - 1))
            return ins
        self.op("pe", reads, writes, fn)

    def act(self, out, in_, func, reads, writes, **kw):
        self.op("act", reads, writes, lambda h: h.activation(out=out, in_=in_, func=func, **kw))

    def tt(self, out, a, b, op, reads, writes, en="dve"):
        self.op(en, reads, writes, lambda h: h.tensor_tensor(out=out, in0=a, in1=b, op=op))

    def ts(self, out, a, s1, s2, op0, op1, reads, writes, en="dve"):
        self.op(en, reads, writes,
                lambda h: h.tensor_scalar(out=out, in0=a, scalar1=s1, scalar2=s2, op0=op0, op1=op1))

    def stt(self, out, a, s, b, op0, op1, reads, writes, en="dve"):
        self.op(en, reads, writes,
                lambda h: h.scalar_tensor_tensor(out=out, in0=a, scalar=s, in1=b, op0=op0, op1=op1))

    def recip(self, out, in_, reads, writes):
        self.op("dve", reads, writes, lambda h: h.reciprocal(out=out, in_=in_))

    def copy(self, out, in_, reads, writes, en="dve"):
        if en == "act":
            self.act(out, in_, AF.Copy, reads, writes)
        else:
            self.op(en, reads, writes, lambda h: h.tensor_copy(out=out, in_=in_))


def build_program():
    nc = bass.Bass("TRN2", target_bir_lowering=False)
    kb = KB(nc)
    es = kb.es

    def din(name, shape):
        return nc.dram_tensor(name, list(shape), F32, kind="ExternalInput").ap()

    def dout(name, shape):
        return nc.dram_tensor(name, list(shape), F32, kind="ExternalOutput").ap()

    d_xT = din("xT", [D, T])
    d_memT = din("memT", [D, 256])
    d_w = din("wslots", [36, 128, SLOT])
    d_cbf = din("cbf", [128, 1792])
    d_cf = din("cf", [128, 109])
    d_crow = din("crow", [1, 1408])
    d_ln = din("lngb", [2, 128, 1024])
    d_wst = din("wst", [128, 8, 128])
    d_rt0 = din("rt0", [4, 128, 4, T])
    d_rt1 = din("rt1", [128, 2, T])
    d_state = din("state", [16, 4, 128, 256])
    d_cmkT = din("cmkT", [2, 16, D, 256])
    d_cmv = din("cmv", [2, 16, 256, D])
    d_cwkT = din("cwkT", [16, 128, 2, 128])
    d_cwv = din("cwv", [16, 128, 256])
    d_cwk_raw = din("cwk_raw", [16, 128, 256])
    d_cwv_raw = din("cwv_raw", [16, 128, 256])

    o_yT = dout("yT", [D, T])
    o_memk = dout("memk", [2, 256, D])
    o_memv = dout("memv", [2, 256, D])
    o_retp = dout("retp", [4, 128, 256])
    o_rets = dout("rets", [16, 4, 128, 256])
    o_sguv = dout("sguv", [128, 1024])
    o_wkT = dout("wkT", [128, 2, 256])
    o_wvp = dout("wvp", [128, 256])
    o_wvs = dout("wvs", [128, 256])
    o_wk_old = dout("wk_old", [16, 120, 256])
    o_wv_old = dout("wv_old", [16, 120, 256])
    import os
    DBG = bool(os.environ.get("KDBG"))
    if DBG:
        o_dbg = dout("dbg", [8, 128, 512])

    def dbgdump(i, ap, n, toks):
        if DBG:
            kb.dma("pool", o_dbg[i, :, 0:n], ap, toks, [])

    X = kb.sb(es, "X", [128, 8, T], F32)
    H = kb.sb(es, "H", [128, 8, T], BF16)
    WR = [kb.sb(es, f"WR{i}", [128, SLOT], BF16) for i in range(NRING)]
    cbf = kb.sb(es, "cbf", [128, 1792], BF16)
    cf = kb.sb(es, "cf", [128, 109], F32)
    crow = kb.sb(es, "crow", [1, 1408], BF16)
    Xt = [Tok() for _ in BLKS]
    Ht = [Tok() for _ in BLKS]
    WRt = [Tok() for _ in range(NRING)]
    Ct = Tok(const=True)
    sqt, rst = Tok(), Tok()

    ident = cbf[:, 0:128]
    o1024 = cbf[:, 128:256]
    o256 = cbf[:, 256:384]
    o1 = cbf[:, 384:512]
    olo = cbf[:, 512:640]
    ohi = cbf[:, 640:768]
    M1 = cbf[:, 768:896]
    M2 = cbf[:, 896:1024]
    mP = cbf[:, 1024:1280]
    mP0 = cbf[:, 1280:1536]
    mS = cbf[:, 1536:1792]

    def gv(i, kc):
        return cf[:, i * 8 + kc:i * 8 + kc + 1]
    BQ, BQS, BK, BKS, BO, ESK, EPSC = 72, 80, 88, 90, 92, 100, 108
    bsP = crow[0:1, 0:512]
    bsS = crow[0:1, 512:1024]
    bvrow = crow[0:1, 1024:1280]
    onerow = crow[0:1, 1280:1408]

    xT_v = d_xT.rearrange("(kc p) t -> p kc t", p=128)
    for bi, (c0, n) in enumerate(BLKS):
        kb.dma("sp", X[:, :, c0:c0 + n], xT_v[:, :, c0:c0 + n], [], [Xt[bi]])
    cft = Tok()
    kb.dma("sp", cf[:], d_cf, [], [cft])
    kb.dma("pool", cbf[:], d_cbf, [], [Ct])
    kb.dma("pool", crow[:], d_crow, [], [Ct])
    bsel = kb.sb(es, "bsel", [128, 16], BF16)
    d_bsel = din("bsel", [128, 16])
    kb.dma("pool", bsel[:], d_bsel, [], [Ct])
    kb.act(cf[:, ESK:ESK + 8], cf[:, ESK:ESK + 8], AF.Exp, [cft], [cft])
    kb.barrier()
    Ct.w = None

    def window_passthrough(scope):
        pas = kb.sb(scope, "pas", [120, 16, 256], F32)
        past = Tok()
        for src, dst in ((d_cwk_raw, o_wk_old), (d_cwv_raw, o_wv_old)):
            kb.dma("sp", pas[:], src[:, 8:128, :].rearrange("b s e -> s b e"), [], [past])
            kb.dma("sp", dst.rearrange("b s e -> s b e"), pas[:], [past], [past])

    wstate = {"next": 0, "free": list(range(NRING)), "loaded": {}}

    def wprefetch():
        while wstate["free"] and wstate["next"] < 36:
            k = wstate["next"]
            ph = wstate["free"].pop(0)
            kb.dma("pool", WR[ph][:], d_w[k], [], [WRt[ph]])
            wstate["loaded"][k] = ph
            wstate["next"] += 1

    def wuse(k):
        while k not in wstate["loaded"]:
            assert wstate["free"], "weight ring exhausted"
            wprefetch()
        ph = wstate["loaded"][k]
        return WR[ph], WRt[ph]

    def wrelease(k):
        ph = wstate["loaded"].pop(k)
        wstate["free"].append(ph)
        wprefetch()

    def w3(W, off, kc, n):
        return W[:, off:off + kc * n].rearrange("p (k n) -> p k n", k=kc)

    def rmsnorm(Xs, Xtok, c0, n, gi, Hs, Htok, hc0, sq, rs):
        kb.act(sq[:, :, 0:n], Xs[:, :, c0:c0 + n], AF.Square, [Xtok], [sqt])
        pt, ptk = kb.ps()
        kb.mm(pt[:, 0:n], [(o1024, sq[:, kc, 0:n]) for kc in range(8)], [sqt, Ct], [ptk])
        kb.act(rs[:, 0:n], pt[:, 0:n], AF.Ln, [ptk, Ct], [rst], bias=cf[:, EPSC:EPSC + 1], scale=1.0)
        kb.act(rs[:, 0:n], rs[:, 0:n], AF.Exp, [rst], [rst], scale=-0.5)
        for kc in range(8):
            kb.stt(Hs[:, kc, hc0:hc0 + n], Xs[:, kc, c0:c0 + n], gv(gi, kc), rs[:, 0:n],
                   ALU.mult, ALU.mult, [Xtok, rst, Ct], [Htok])

    def norm_all(gi):
        with ExitStack() as scn:
            sq = kb.sb(scn, "sq", [128, 8, 512], BF16)
            rs = kb.sb(scn, "rs", [128, 512], F32)
            for bi, (c0, n) in enumerate(BLKS):
                rmsnorm(X, Xt[bi], c0, n, gi, H, Ht[bi], c0, sq, rs)
            kb.barrier()

    def xadd(oc, c0, n, pt, ptk, bi):
        kb.tt(X[:, oc, c0:c0 + n], X[:, oc, c0:c0 + n], pt[:, 0:n], ALU.add, [ptk, Xt[bi]], [Xt[bi]])

    def layer0_mixer():
        norm_all(0)
        wprefetch()
        with ExitStack() as sc:
            lng = kb.sb(sc, "lng", [128, 1024], F32)
            lnb = kb.sb(sc, "lnb", [128, 1024], F32)
            wst = kb.sb(sc, "wst", [128, 8, 128], BF16)
            GU = kb.sb(sc, "GU", [128, 8, 512], BF16)
            AO = kb.sb(sc, "AO", [128, 8, 512], BF16)
            zv2 = [kb.sb(sc, f"zv{i}", [128, 1024], F32) for i in range(2)]
            vnb2 = [kb.sb(sc, f"vnb{i}", [128, 1024], BF16) for i in range(2)]
            st62 = [kb.sb(sc, f"st6{i}", [128, 2, 6], F32) for i in range(2)]
            mv2 = [kb.sb(sc, f"mv{i}", [128, 4], F32) for i in range(2)]
            zvt2, vnbt2, stt2, mvt2 = [[Tok(), Tok()] for _ in range(4)]
            lt, wstt, GUt, AOt = [Tok() for _ in range(4)]
            kb.dma("sp", lng[:], d_ln[0], [], [lt])
            kb.dma("sp", lnb[:], d_ln[1], [], [lt])
            kb.dma("pool", wst[:], d_wst, [], [wstt])
            for g in range(4):
                kb.tt(wst[:, g, :], wst[:, g, :], M1, ALU.mult, [wstt, Ct], [wstt])
                kb.tt(wst[:, 4 + g, :], wst[:, 4 + g, :], M2, ALU.mult, [wstt, Ct], [wstt])
            Wu, Wut = wuse(0)
            Wv, Wvt = wuse(1)
            Wo, Wot = wuse(2)
            Wu3, Wv3, Wo3 = w3(Wu, 0, 8, 1024), w3(Wv, 0, 8, 1024), w3(Wo, 0, 8, 1024)
            def uproj(bi):
                c0, n = BLKS[bi]
                for oc in range(8):
                    pt, ptk = kb.ps()
                    kb.mm(pt[:, 0:n], [(Wu3[:, kc, oc * 128:(oc + 1) * 128], H[:, kc, c0:c0 + n]) for kc in range(8)],
                          [Wut, Ht[bi]], [ptk])
                    kb.act(GU[:, oc, 0:n], pt[:, 0:n], AF.Gelu_apprx_tanh, [ptk], [GUt])

            uproj(0)
            for bi, (c0, n) in enumerate(BLKS):
                smp = bi == 4
                def sgu_chunk(bi, c0, ch, smp):
                    t0 = c0 + ch * 128
                    db = ch % 2
                    zvb, zvtb, vnbb, vnbtb = zv2[db], zvt2[db], vnb2[db], vnbt2[db]
                    st6b, sttb, mvb, mvtb = st62[db], stt2[db], mv2[db], mvt2[db]
                    pv = []
                    for hf in range(2):
                        pt, ptk = kb.ps(hold=True)
                        pv.append((pt, ptk))
                        kb.mm(pt[:, :], [(H[:, kc, t0:t0 + 128], Wv3[:, kc, hf * 512:(hf + 1) * 512]) for kc in range(8)],
                              [Wvt, Ht[bi]], [ptk])
                    yield
                    for hf in range(2):
                        pt, ptk = pv[hf]
                        kb.act(zvb[:, hf * 512:(hf + 1) * 512], pt[:, :], AF.Gelu_apprx_tanh, [ptk], [zvtb])
                        kb.pfree(pt)
                    for hf in range(2):
                        kb.op("dve", [zvtb], [sttb],
                              lambda h, hf=hf: h.bn_stats(out=st6b[:, hf, :], in_=zvb[:, hf * 512:(hf + 1) * 512]))
                    kb.op("dve", [sttb], [mvtb],
                          lambda h: h.bn_aggr(out=mvb[:, 0:2], in_=st6b[:].rearrange("p a b -> p (a b)")))
                    kb.act(mvb[:, 2:3], mvb[:, 1:2], AF.Sqrt, [mvtb], [mvtb], bias=EPS, scale=1.0)
                    kb.recip(mvb[:, 2:3], mvb[:, 2:3], [mvtb], [mvtb])
                    kb.stt(mvb[:, 3:4], mvb[:, 0:1], -1.0, mvb[:, 2:3], ALU.mult, ALU.mult, [mvtb], [mvtb])
                    kb.ts(zvb[:], zvb[:], mvb[:, 2:3], mvb[:, 3:4], ALU.mult, ALU.add, [zvtb, mvtb], [zvtb])
                    kb.tt(zvb[:], zvb[:], lng[:], ALU.mult, [zvtb, lt], [zvtb], en="pool")
                    if smp:
                        kb.tt(zvb[:], zvb[:], lnb[:], ALU.add, [zvtb, lt], [zvtb], en="pool")
                        kb.dma("sp", o_sguv, zvb[:], [zvtb], [])
                        kb.copy(vnbb[:], zvb[:], [zvtb], [vnbtb])
                    else:
                        kb.tt(vnbb[:], zvb[:], lnb[:], ALU.add, [zvtb, lt], [vnbtb], en="pool")
                    yield
                    pA, pAt = kb.ps(hold=True)
                    pB, pBt = kb.ps(hold=True)
                    for oc in range(8):
                        g = oc // 2
                        pp, ppt = (pA, pAt) if oc < 4 else (pB, pBt)
                        wsel = wst[:, (4 + g) if smp else g, :]
                        brow = (bsS if smp else bsP)[0:1, g * 128:(g + 1) * 128]
                        kb.mm(pp[:, (oc % 4) * 128:(oc % 4 + 1) * 128],
                              [(vnbb[:, oc * 128:(oc + 1) * 128], wsel), (onerow, brow)],
                              [vnbtb, wstt, Ct], [ppt])
                    yield
                    kb.tt(AO[:, 0:4, ch * 128:(ch + 1) * 128], GU[:, 0:4, ch * 128:(ch + 1) * 128],
                          pA[:, :].rearrange("p (a b) -> p a b", a=4), ALU.mult, [GUt, pAt], [AOt])
                    kb.tt(AO[:, 4:8, ch * 128:(ch + 1) * 128], GU[:, 4:8, ch * 128:(ch + 1) * 128],
                          pB[:, :].rearrange("p (a b) -> p a b", a=4), ALU.mult, [GUt, pBt], [AOt])
                    kb.pfree(pA)
                    kb.pfree(pB)

                kb.pipeline([sgu_chunk(bi, c0, ch, smp) for ch in range(n // 128)], 2)
                if bi + 1 < len(BLKS):
                    uproj(bi + 1)
                for oc in range(8):
                    pt, ptk = kb.ps()
                    kb.mm(pt[:, 0:n], [(Wo3[:, kc, oc * 128:(oc + 1) * 128], AO[:, kc, 0:n]) for kc in range(8)],
                          [Wot, AOt], [ptk])
                    xadd(oc, c0, n, pt, ptk, bi)
            kb.barrier()
        wrelease(0)
        wrelease(1)
        wrelease(2)
        import os
        if os.environ.get("KSKIPB"):
            for k in range(3, 8):
                wuse(k)
                wrelease(k)
            return
        Wob, Wobt = wuse(3)
        Wob3 = w3(Wob, 0, 8, 1024)
        with ExitStack() as sc:
            tab = kb.sb(sc, "tab", [128, 4, 512], F32)
            QK = kb.sb(sc, "QK", [128, 2, 512], BF16)
            t1 = kb.sb(sc, "t1", [128, 512], F32)
            t2 = kb.sb(sc, "t2", [128, 512], F32)
            SG = kb.sb(sc, "SG", [128, 2, 512], BF16)
            Vt = kb.sb(sc, "Vt", [128, 4, 256], BF16)
            S32 = kb.sb(sc, "S32", [128, 256], F32)
            st32 = kb.sb(sc, "st32", [128, 8, 256], F32)
            stbf = kb.sb(sc, "stbf", [128, 8, 256], BF16)
            Vblk = kb.sb(sc, "Vblk", [128, 8, 256], BF16)
            (tabt, QKt, t1t, t2t, SGt, Vtt, scTt, kTMt, S32t, Sbft, osqt, r2t, BOt, st32t, stbft,
             Vblkt) = [Tok() for _ in range(16)]
            scT4 = kb.sb(sc, "scT4", [128, 4, 128], BF16)
            kTM4 = kb.sb(sc, "kTM4", [128, 4, 128], BF16)
            Sb5 = kb.sb(sc, "Sb5", [128, 5, 256], BF16)
            osq4 = kb.sb(sc, "osq4", [128, 4, 2, 128], BF16)
            osq4t = [Tok() for _ in range(4)]
            osq2 = osq4
            BO2 = kb.sb(sc, "BO2", [128, 2, 2, 512], BF16)
            scT4t, kTM4t = [Tok() for _ in range(4)], [Tok() for _ in range(4)]
            Sb5t = [Tok() for _ in range(5)]
            osq2t, r22t, BO2t = [Tok(), Tok()], [Tok(), Tok()], [Tok(), Tok()]
            r24 = kb.sb(sc, "r24", [128, 512], F32)
            SGr = kb.sb(sc, "SGr", [128, 2, 512], BF16)
            r24t, SGrt = Tok(), Tok()
            scT, kTM, osq, r2, BO = scT4[:, 0, :], kTM4[:, 0, :], osq4[:, 0], r24[:, 0:128], BO2[:, 0]
            scTt, kTMt, osqt, r2t, BOt = scT4t[0], kTM4t[0], osq4t[0], r24t, BO2t[0]
            Sbf, Sbft = Sb5[:, 0, :], Sb5t[0]

            st32q, stbfq, Vblkq = [Tok(), Tok()], [Tok(), Tok()], [Tok(), Tok()]

            def load_state(hd, hq):
                qb2 = hq % 2
                src = d_state[hq * 4:(hq + 1) * 4, hd].rearrange("b p e -> p b e")
                kb.dma("sp", st32[:, qb2 * 4:qb2 * 4 + 4, :], src, [], [st32q[qb2]])
                kb.dma("pool", stbf[:, qb2 * 4:qb2 * 4 + 4, :], src, [], [stbfq[qb2]])

            def outproj(hd, bi, c0, n, BOb, BObt):
                for oc in range(8):
                    pt, ptk = kb.ps()
                    kb.mm(pt[:, 0:n], [(Wob3[:, 2 * hd + ec, oc * 128:(oc + 1) * 128], BOb[:, ec, 0:n]) for ec in range(2)],
                          [Wobt, BObt], [ptk])
                    xadd(oc, c0, n, pt, ptk, bi)

            for hd in range(4):
                lg = math.log1p(-2.0 ** (-5.0 - hd))
                gP = math.exp(128.0 * lg)
                gS = math.exp(8.0 * lg)
                Wh, Wht = wuse(4 + hd)
                Wh3 = w3(Wh, 0, 8, 1024)
                kb.op("dve", [], [S32t], lambda h: h.memset(S32[:], 0.0))
                kb.op("dve", [], [Sb5t[0]], lambda h: h.memset(Sb5[:, 0, :], 0.0))
                pending = None
                load_state(hd, 0)
                load_state(hd, 1)

                def p0_qk(bi, hd=hd, Wh3=Wh3, Wht=Wht):
                    c0, n = BLKS[bi]
                    kb.dma("sp", tab[:, :, 0:n], d_rt0[hd, :, :, c0:c0 + n], [], [tabt])
                    for qk in range(2):
                        pa, pat = kb.ps()
                        pb, pbt = kb.ps()
                        kb.mm(pa[:, 0:n], [(Wh3[:, kc, qk * 256:qk * 256 + 128], H[:, kc, c0:c0 + n]) for kc in range(8)],
                              [Wht, Ht[bi]], [pat])
                        kb.mm(pb[:, 0:n], [(Wh3[:, kc, qk * 256 + 128:qk * 256 + 256], H[:, kc, c0:c0 + n]) for kc in range(8)],
                              [Wht, Ht[bi]], [pbt])
                        kb.tt(t1[:, 0:n], pa[:, 0:n], tab[:, 2 * qk, 0:n], ALU.mult, [pat, tabt], [t1t])
                        kb.tt(t2[:, 0:n], pb[:, 0:n], tab[:, 2 * qk + 1, 0:n], ALU.mult, [pbt, tabt], [t2t])
                        kb.tt(QK[:, qk, 0:n], t1[:, 0:n], t2[:, 0:n], ALU.add, [t1t, t2t], [QKt])

                def p0_v(bi, Wh3=Wh3, Wht=Wht):
                    c0, n = BLKS[bi]
                    for ch in range(n // 128):
                        t0 = c0 + ch * 128
                        pt, ptk = kb.ps()
                        kb.mm(pt[:, 0:256], [(H[:, kc, t0:t0 + 128], Wh3[:, kc, 512:768]) for kc in range(8)],
                              [Wht, Ht[bi]], [ptk])
                        kb.copy(Vt[:, ch, :], pt[:, 0:256], [ptk], [Vtt], en="act")

                for bi, (c0, n) in enumerate(BLKS):
                    smp = bi == 4
                    p0_qk(bi)
                    p0_v(bi)

                    def gate_proj():
                        for ec in range(2):
                            pt, ptk = kb.ps()
                            kb.mm(pt[:, 0:n], [(Wh3[:, kc, 768 + ec * 128:768 + (ec + 1) * 128], H[:, kc, c0:c0 + n]) for kc in range(8)],
                                  [Wht, Ht[bi]], [ptk])
                            kb.act(SG[:, ec, 0:n], pt[:, 0:n], AF.Silu, [ptk], [SGt])

                    if not smp:
                        bb = bi % 2
                        BOb, BObt = BO2[:, bb], BO2t[bb]
                        nch = 4
                        for ch in range(nch):
                            cs = slice(ch * 128, (ch + 1) * 128)
                            pt, ptk = kb.ps()
                            kb.mm(pt[:, 0:128], [(QK[:, 1, cs], QK[:, 0, cs])], [QKt], [ptk])
                            kb.tt(scT4[:, ch, :], pt[:, 0:128], M1, ALU.mult, [ptk, Ct], [scT4t[ch]])
                            pk, pkt = kb.ps()
                            kb.mm(pk[:, 0:128], [(QK[:, 1, cs], ident)], [QKt, Ct], [pkt])
                            kb.copy(kTM4[:, ch, :], pk[:, 0:128], [pkt], [kTM4t[ch]], en="act")
                        gate_proj()
                        puA, puAt = kb.ps(hold=True)
                        puB, puBt = kb.ps(hold=True)
                        PU = [(puA, puAt, 0), (puA, puAt, 256), (puB, puBt, 0), (puB, puBt, 256)]
                        for ch in range(nch):
                            pu, put, o = PU[ch]
                            kb.mm(pu[:, o:o + 256], [(kTM4[:, ch, :], Vt[:, ch, :])], [kTM4t[ch], Vtt], [put])
                        if pending is not None:
                            outproj(*pending)
                            pending = None
                        for ch in range(nch):
                            g = bi * 4 + ch
                            pu, put, o = PU[ch]
                            kb.stt(S32[:], S32[:], gP, pu[:, o:o + 256], ALU.mult, ALU.add, [put, S32t], [S32t])
                            kb.act(Sb5[:, (g + 1) % 5, :], S32[:], AF.Copy, [S32t], [Sb5t[(g + 1) % 5]], scale=gP)
                        kb.pfree(puA)
                        kb.pfree(puB)
                        POs = []
                        for ch in range(nch):
                            g = bi * 4 + ch
                            cs = slice(ch * 128, (ch + 1) * 128)
                            po, pot = kb.ps(hold=True)
                            POs.append((po, pot))
                            for ec in range(2):
                                kb.mm(po[:, ec * 128:(ec + 1) * 128],
                                      [(Vt[:, ch, ec * 128:(ec + 1) * 128], scT4[:, ch, :]),
                                       (Sb5[:, g % 5, ec * 128:(ec + 1) * 128], QK[:, 0, cs])],
                                      [Vtt, scT4t[ch], Sb5t[g % 5], QKt], [pot])
                        for ch in range(nch):
                            po, pot = POs[ch]
                            kb.act(osq4[:, ch].rearrange("p a b -> p (a b)"), po[:, 0:256], AF.Square, [pot], [osq4t[ch]])
                        pn, pnt = kb.ps(hold=True)
                        for ch in range(nch):
                            kb.mm(pn[:, ch * 128:(ch + 1) * 128], [(o256, osq4[:, ch, ec, :]) for ec in range(2)], [osq4t[ch], Ct], [pnt])
                        kb.act(r24[:], pn[:, :], AF.Ln, [pnt], [r24t], bias=cf[:, EPSC:EPSC + 1], scale=1.0)
                        kb.pfree(pn)
                        kb.act(r24[:], r24[:], AF.Exp, [r24t], [r24t], scale=-0.5)
                        kb.tt(SGr[:], SG[:], r24[:].unsqueeze(1).to_broadcast([128, 2, 512]), ALU.mult, [SGt, r24t], [SGrt])
                        for ch in range(nch):
                            po, pot = POs[ch]
                            cs = slice(ch * 128, (ch + 1) * 128)
                            kb.tt(BOb[:, :, cs], po[:, 0:256].rearrange("p (a b) -> p a b", a=2), SGr[:, :, cs], ALU.mult,
                                  [pot, SGrt], [BObt])
                            kb.pfree(po)
                        pending = (hd, bi, c0, n, BOb, BObt)
                        if bi == 3:
                            outproj(*pending)
                            pending = None
                            kb.act(S32[:], S32[:], AF.Copy, [S32t], [S32t], scale=gP)
                            kb.dma("sp", o_retp[hd], S32[:], [S32t], [S32t])
                        continue
                    gate_proj()
                    for ch in range(n // 128):
                        cs = slice(ch * 128, (ch + 1) * 128)
                        pt, ptk = kb.ps()
                        kb.mm(pt[:, 0:128], [(QK[:, 1, cs], QK[:, 0, cs])], [QKt], [ptk])
                        kb.tt(scT[:], pt[:, 0:128], M2 if smp else M1, ALU.mult, [ptk, Ct], [scTt])
                        pk, pkt = kb.ps()
                        kb.mm(pk[:, 0:128], [(QK[:, 1, cs], ident)], [QKt, Ct], [pkt])
                        kb.copy(kTM[:], pk[:, 0:128], [pkt], [kTMt], en="act")
                        if not smp:
                            po, pot = kb.ps()
                            PO = [(po, pot, 0), (po, pot, 128)]
                        else:
                            poA, poAt = kb.ps(hold=True)
                            poB, poBt = kb.ps(hold=True)
                            PO = [(poA, poAt, 0), (poB, poBt, 0)]
                        if not smp:
                            for ec in range(2):
                                kb.mm(po[:, ec * 128:(ec + 1) * 128],
                                      [(Vt[:, ch, ec * 128:(ec + 1) * 128], scT[:]),
                                       (Sbf[:, ec * 128:(ec + 1) * 128], QK[:, 0, cs])],
                                      [Vtt, scTt, Sbft, QKt], [pot])
                        else:
                            for hq in range(4):
                                qb2 = hq % 2
                                s32q = st32[:, qb2 * 4:qb2 * 4 + 4, :]
                                sbfq = stbf[:, qb2 * 4:qb2 * 4 + 4, :]
                                vbq = Vblk[:, qb2 * 4:qb2 * 4 + 4, :]
                                if hq == 0:
                                    for ec in range(2):
                                        kb.mm(PO[ec][0][:, 0:128],
                                              [(Vt[:, ch, ec * 128:(ec + 1) * 128], scT[:])],
                                              [Vtt, scTt], [PO[ec][1]])
                                for b4 in range(4):
                                    b = hq * 4 + b4
                                    for ec in range(2):
                                        kb.op("pe", [stbfq[qb2], QKt], [PO[ec][1]],
                                              lambda h, b=b, b4=b4, ec=ec, PO=PO, sbfq=sbfq: h.matmul(
                                                  PO[ec][0][:, b * 8:b * 8 + 8],
                                                  sbfq[:, b4, ec * 128:(ec + 1) * 128],
                                                  QK[:, 0, b * 8:b * 8 + 8], start=False, stop=True,
                                                  skip_group_check=True))
                                kb.op("dve", [Vtt, Ct], [Vblkq[qb2]],
                                      lambda h, hq=hq, vbq=vbq: h.tensor_tensor(
                                          out=vbq,
                                          in0=Vt[:, ch:ch + 1, :].to_broadcast([128, 4, 256]),
                                          in1=bsel[:, hq * 4:(hq + 1) * 4].unsqueeze(2).to_broadcast([128, 4, 256]), op=ALU.mult))
                                for b2 in range(2):
                                    pu, put = kb.ps()
                                    kb.mm(pu[:, :], [(kTM[:], vbq[:, 2 * b2:2 * b2 + 2, :].rearrange("p a b -> p (a b)"))],
                                          [kTMt, Vblkq[qb2]], [put])
                                    kb.tt(s32q[:, 2 * b2:2 * b2 + 2, :].rearrange("p a b -> p (a b)"),
                                          s32q[:, 2 * b2:2 * b2 + 2, :].rearrange("p a b -> p (a b)"),
                                          pu[:, :], ALU.add, [put, st32q[qb2]], [st32q[qb2]])
                                kb.act(s32q, s32q, AF.Copy, [st32q[qb2]], [st32q[qb2]], scale=gS)
                                kb.dma("sp", o_rets[hq * 4:(hq + 1) * 4, hd].rearrange("b p e -> p b e"), s32q,
                                       [st32q[qb2]], [st32q[qb2]])
                                if hq + 2 < 4:
                                    load_state(hd, hq + 2)
                        if not smp:
                            kb.act(osq[:].rearrange("p a b -> p (a b)"), po[:, 0:256], AF.Square, [pot], [osqt])
                        else:
                            for ec in range(2):
                                kb.act(osq[:, ec, :], PO[ec][0][:, 0:128], AF.Square, [PO[ec][1]], [osqt])
                        pn, pnt = kb.ps()
                        kb.mm(pn[:, 0:128], [(o256, osq[:, ec, :]) for ec in range(2)], [osqt, Ct], [pnt])
                        kb.act(r2[:], pn[:, 0:128], AF.Sqrt, [pnt], [r2t], bias=EPS, scale=1.0)
                        kb.recip(r2[:], r2[:], [r2t], [r2t])
                        for ec in range(2):
                            kb.tt(t1[:, 0:128], PO[ec][0][:, PO[ec][2]:PO[ec][2] + 128], r2[:], ALU.mult, [PO[ec][1], r2t], [t1t])
                            kb.tt(BO[:, ec, cs], t1[:, 0:128], SG[:, ec, cs], ALU.mult, [t1t, SGt], [BOt])
                        kb.ps_release_all()
                        if not smp:
                            pu, put = kb.ps()
                            kb.mm(pu[:, 0:256], [(kTM[:], Vt[:, ch, :])], [kTMt, Vtt], [put])
                            kb.stt(S32[:], S32[:], gP, pu[:, 0:256], ALU.mult, ALU.add, [put, S32t], [S32t])
                            kb.act(Sbf[:], S32[:], AF.Copy, [S32t], [Sbft], scale=gP)
                    if smp and hd == 0:
                        dbgdump(0, QK[:, 0, 0:128], 128, [QKt])
                        dbgdump(1, QK[:, 1, 0:128], 128, [QKt])
                        dbgdump(2, scT[:], 128, [scTt])
                        dbgdump(3, BO[:, 0, 0:128], 128, [BOt])
                        dbgdump(4, BO[:, 1, 0:128], 128, [BOt])
                        dbgdump(5, Vt[:, 0, :], 256, [Vtt])
                        dbgdump(6, SG[:, 0, 0:128], 128, [SGt])
                        dbgdump(7, r2[:], 128, [r2t])
                    for oc in range(8):
                        pt, ptk = kb.ps()
                        kb.mm(pt[:, 0:n], [(Wob3[:, 2 * hd + ec, oc * 128:(oc + 1) * 128], BO[:, ec, 0:n]) for ec in range(2)],
                              [Wobt, BOt], [ptk])
                        xadd(oc, c0, n, pt, ptk, bi)
                wrelease(4 + hd)
            kb.barrier()
        wrelease(3)

    def cross(l):
        s_mk, s_mv, s_mq, s_mo = (8, 9, 10, 11) if l == 0 else (24, 25, 26, 27)
        norm_all(2 + l)
        with ExitStack() as sc:
            KT = kb.sb(sc, "KT", [128, 8, 256], BF16)
            Vm = kb.sb(sc, "Vm", [128, 2, 1024], BF16)
            KTt, Vmt = Tok(), Tok()
            with ExitStack() as sc1:
                MT = kb.sb(sc1, "MT", [128, 8, 256], F32)
                Mh = kb.sb(sc1, "Mh", [128, 8, 256], BF16)
                kvo = kb.sb(sc1, "kvo", [128, 1024], F32)
                MTt, Mht, kvot = Tok(), Tok(), Tok()
                kb.dma("sp", MT[:], d_memT.rearrange("(kc p) m -> p kc m", p=128), [], [MTt])
                sqm = kb.sb(sc1, "sqm", [128, 8, 256], BF16)
                rsm = kb.sb(sc1, "rsm", [128, 256], F32)
                rmsnorm(MT, MTt, 0, 256, 4 + l, Mh, Mht, 0, sqm, rsm)
                Wk, Wkt = wuse(s_mk)
                Wk3 = w3(Wk, 0, 8, 1024)
                for oc in range(8):
                    pt, ptk = kb.ps()
                    kb.mm(pt[:, 0:256], [(Wk3[:, kc, oc * 128:(oc + 1) * 128], Mh[:, kc, :]) for kc in range(8)],
                          [Wkt, Mht], [ptk])
                    kb.copy(KT[:, oc, :], pt[:, 0:256], [ptk], [KTt], en="act")
                for which, (sl, dst) in enumerate([(s_mk, o_memk), (s_mv, o_memv)]):
                    Wx, Wxt = wuse(sl)
                    Wx3 = w3(Wx, 0, 8, 1024)
                    for mc in range(2):
                        for hf in range(2):
                            pt, ptk = kb.ps()
                            kb.mm(pt[:, :], [(Mh[:, kc, mc * 128:(mc + 1) * 128], Wx3[:, kc, hf * 512:(hf + 1) * 512]) for kc in range(8)],
                                  [Wxt, Mht], [ptk])
                            kb.copy(kvo[:, hf * 512:(hf + 1) * 512], pt[:, :], [ptk], [kvot], en="act")
                            if which == 1:
                                kb.copy(Vm[:, mc, hf * 512:(hf + 1) * 512], kvo[:, hf * 512:(hf + 1) * 512], [kvot], [Vmt], en="dve")
                        kb.dma("sp", dst[l, mc * 128:(mc + 1) * 128, :], kvo[:], [kvot], [kvot])
                    wrelease(sl)
                kb.barrier()
            with ExitStack() as sc2:
                QTh = kb.sb(sc2, "QTh", [128, 2, 2, 512], BF16)
                E = kb.sb(sc2, "E", [128, 2, 2, 512], BF16)
                rden = kb.sb(sc2, "rden", [128, 512], F32)
                AT = kb.sb(sc2, "AT", [128, 8, 512], BF16)
                KbT = kb.sb(sc2, "KbT", [128, 2, 8, 256], BF16)
                Vb = kb.sb(sc2, "Vb", [128, 2, 2, 1024], BF16)
                QTs = kb.sb(sc2, "QTs", [128, 8, 128], BF16)
                Eb = kb.sb(sc2, "Eb", [128, 64], BF16)
                rd = kb.sb(sc2, "rd", [128, 32], F32)
                QTht, Et = [Tok(), Tok()], [Tok(), Tok()]
                rdent, ATt, QTst, Ebt, rdt = [Tok() for _ in range(5)]
                KbTt, Vbt = [Tok(), Tok()], [Tok(), Tok()]
                Wq, Wqt = wuse(s_mq)
                Wo, Wot = wuse(s_mo)
                Wq3, Wo3 = w3(Wq, 0, 8, 1024), w3(Wo, 0, 8, 1024)
                ATs = kb.sb(sc2, "ATs", [128, 8, 128], BF16)
                ATst = Tok()

                def outproj(ATx, ATxt, bi, c0, n):
                    for oc in range(8):
                        pt, ptk = kb.ps()
                        kb.mm(pt[:, 0:n], [(Wo3[:, kc, oc * 128:(oc + 1) * 128], ATx[:, kc, 0:n]) for kc in range(8)],
                              [Wot, ATxt], [ptk])
                        xadd(oc, c0, n, pt, ptk, bi)

                def load_batch(b):
                    pb = b % 2
                    kb.dma("pool", KbT[:, pb], d_cmkT[l, b].rearrange("(kc p) m -> p kc m", p=128), [], [KbTt[pb]])
                    kb.dma("pool", Vb[:, pb], d_cmv[l, b].rearrange("(mc p) e -> p mc e", p=128), [], [Vbt[pb]])

                def head_A(bi, c0, n, hd):
                    hb = hd % 2
                    for dc in range(2):
                        pt, ptk = kb.ps()
                        kb.mm(pt[:, 0:n], [(Wq3[:, kc, (2 * hd + dc) * 128:(2 * hd + dc + 1) * 128], H[:, kc, c0:c0 + n]) for kc in range(8)],
                              [Wqt, Ht[bi]], [ptk])
                        kb.copy(QTh[:, hb, dc, 0:n], pt[:, 0:n], [ptk], [QTht[hb]], en="act")

                def head_B(bi, c0, n, hd):
                    hb = hd % 2
                    for mc in range(2):
                        pt, ptk = kb.ps()
                        kb.mm(pt[:, 0:n], [(KT[:, 2 * hd + dc, mc * 128:(mc + 1) * 128], QTh[:, hb, dc, 0:n]) for dc in range(2)],
                              [KTt, QTht[hb]], [ptk])
                        kb.act(E[:, hb, mc, 0:n], pt[:, 0:n], AF.Exp, [ptk], [Et[hb]], scale=1.0 / 16.0)

                def head_C(bi, c0, n, hd):
                    hb = hd % 2
                    pd, pdt = kb.ps()
                    kb.mm(pd[:, 0:n], [(o1, E[:, hb, mc, 0:n]) for mc in range(2)], [Et[hb], Ct], [pdt])
                    kb.act(rden[:, 0:n], pd[:, 0:n], AF.Ln, [pdt], [rdent])
                    kb.act(rden[:, 0:n], rden[:, 0:n], AF.Exp, [rdent], [rdent], scale=-1.0)
                    for ec in range(2):
                        pt, ptk = kb.ps()
                        kb.mm(pt[:, 0:n], [(Vm[:, mc, hd * 256 + ec * 128:hd * 256 + (ec + 1) * 128], E[:, hb, mc, 0:n]) for mc in range(2)],
                              [Vmt, Et[hb]], [ptk])
                        kb.tt(AT[:, 2 * hd + ec, 0:n], pt[:, 0:n], rden[:, 0:n], ALU.mult, [ptk, rdent], [ATt])

                def sample_S1(b):
                    pb = b % 2
                    pS, pSt = kb.ps()
                    for hd in range(4):
                        for mc in range(2):
                            r0 = (hd * 2 + mc) * 8
                            kb.mm(pS[:, r0:r0 + 8],
                                  [(KbT[:, pb, 2 * hd + dc, mc * 128:(mc + 1) * 128], QTs[:, 2 * hd + dc, b * 8:b * 8 + 8]) for dc in range(2)],
                                  [KbTt[pb], QTst], [pSt])
                    kb.act(Eb[:], pS[:, 0:64], AF.Exp, [pSt], [Ebt], scale=1.0 / 16.0)

                def sample_S2(b):
                    pb = b % 2
                    pO, pOt = kb.ps()
                    for hd in range(4):
                        for ec in range(2):
                            r0 = (hd * 2 + ec) * 8
                            kb.mm(pO[:, r0:r0 + 8],
                                  [(Vb[:, pb, mc, hd * 256 + ec * 128:hd * 256 + (ec + 1) * 128], Eb[:, (hd * 2 + mc) * 8:(hd * 2 + mc) * 8 + 8]) for mc in range(2)],
                                  [Vbt[pb], Ebt], [pOt])
                        kb.mm(pO[:, 64 + hd * 8:64 + hd * 8 + 8],
                              [(o1, Eb[:, (hd * 2 + mc) * 8:(hd * 2 + mc) * 8 + 8]) for mc in range(2)],
                              [Ebt, Ct], [pOt])
                    kb.act(rd[:], pO[:, 64:96], AF.Ln, [pOt], [rdt])
                    kb.act(rd[:], rd[:], AF.Exp, [rdt], [rdt], scale=-1.0)
                    for ec in range(2):
                        kb.tt(ATs[:, :, b * 8:b * 8 + 8].rearrange("p (h e) c -> p h e c", e=2)[:, :, ec, :],
                              pO[:, 0:64].rearrange("p (h e c) -> p h e c", e=2, c=8)[:, :, ec, :],
                              rd[:].rearrange("p (h c) -> p h c", c=8), ALU.mult, [pOt, rdt], [ATst])

                sc0, sn = BLKS[4]
                for oc in range(8):
                    pt, ptk = kb.ps()
                    kb.mm(pt[:, 0:sn], [(Wq3[:, kc, oc * 128:(oc + 1) * 128], H[:, kc, sc0:sc0 + sn]) for kc in range(8)],
                          [Wqt, Ht[4]], [ptk])
                    kb.copy(QTs[:, oc, :], pt[:, 0:sn], [ptk], [QTst], en="act")
                load_batch(0)
                load_batch(1)
                def unit(u):
                    bi, hd = u // 4, u % 4
                    c0, n = BLKS[bi]
                    return bi, c0, n, hd
                head_A(*unit(0))
                head_B(*unit(0))
                for u in range(16):
                    bi, c0, n, hd = unit(u)
                    if u + 1 < 16:
                        head_A(*unit(u + 1))
                    sample_S1(u)
                    head_C(bi, c0, n, hd)
                    if u + 1 < 16:
                        head_B(*unit(u + 1))
                    if hd == 3:
                        outproj(AT, ATt, bi, c0, n)
                    sample_S2(u)
                    if u + 2 < 16:
                        load_batch(u + 2)
                outproj(ATs, ATst, 4, sc0, sn)
                kb.barrier()
            wrelease(s_mq)
            wrelease(s_mo)

    def mlp(l):
        base = 12 if l == 0 else 28
        with ExitStack() as sc:
            sqn = kb.sb(sc, "sqn", [128, 8, 512], BF16)
            rsn = kb.sb(sc, "rsn", [128, 512], F32)
            for bi_, (c0_, n_) in enumerate(BLKS):
                rmsnorm(X, Xt[bi_], c0_, n_, 6 + l, H, Ht[bi_], c0_, sqn, rsn)
            if l == 1:
                yo = kb.sb(sc, "yo", [128, 8, 512], F32)
                yot = Tok()
                yT_v = o_yT.rearrange("(kc p) t -> p kc t", p=128)
            r32 = kb.sb(sc, "r32", [128, 2, 512], F32)
            hid = kb.sb(sc, "hid", [128, 2, 4, 512], BF16)
            r32t, hidt = [Tok(), Tok()], [Tok(), Tok()]
            if l == 0:
                window_passthrough(sc)
            its = [(fb, bi) for fb in range(8) for bi in range(len(BLKS))]
            wcache = {}

            def getw(fb):
                if fb not in wcache:
                    Wf, Wft = wuse(base + fb)
                    wcache[fb] = (w3(Wf, 0, 8, 512), w3(Wf, 4096, 4, 1024), Wft)
                return wcache[fb]

            def up(i):
                fb, bi = its[i]
                c0, n = BLKS[bi]
                Wup, Wdn, Wft = getw(fb)
                hbuf = i % 2
                for hc in range(4):
                    pt, ptk = kb.ps()
                    kb.mm(pt[:, 0:n], [(Wup[:, kc, hc * 128:(hc + 1) * 128], H[:, kc, c0:c0 + n]) for kc in range(8)],
                          [Wft, Ht[bi]], [ptk])
                    rb = hc % 2
                    kb.act(r32[:, rb, 0:n], pt[:, 0:n], AF.Relu, [ptk], [r32t[rb]])
                    kb.tt(hid[:, hbuf, hc, 0:n], r32[:, rb, 0:n], r32[:, rb, 0:n], ALU.mult, [r32t[rb]], [hidt[hbuf]])

            def down(i):
                fb, bi = its[i]
                c0, n = BLKS[bi]
                Wup, Wdn, Wft = getw(fb)
                hbuf = i % 2
                for oc in range(8):
                    pt, ptk = kb.ps()
                    kb.mm(pt[:, 0:n], [(Wdn[:, hc, oc * 128:(oc + 1) * 128], hid[:, hbuf, hc, 0:n]) for hc in range(4)],
                          [Wft, hidt[hbuf]], [ptk])
                    xadd(oc, c0, n, pt, ptk, bi)
                if l == 1 and fb == 7:
                    rmsnorm(X, Xt[bi], c0, n, 8, yo, yot, 0, sqn, rsn)
                    kb.dma("sp", yT_v[:, :, c0:c0 + n], yo[:, :, 0:n], [yot], [yot])
                if bi == len(BLKS) - 1:
                    wrelease(base + fb)

            up(0)
            for i in range(len(its)):
                if i + 1 < len(its):
                    up(i + 1)
                down(i)
            kb.barrier()

    def layer1_mixer():
        norm_all(1)
        with ExitStack() as sc:
            KTa = kb.sb(sc, "KTa", [128, 2, T], BF16)
            Va = kb.sb(sc, "Va", [128, 17, 256], BF16)
            tb = kb.sb(sc, "tb", [128, 2, 256], F32)
            t1 = kb.sb(sc, "t1", [128, 256], F32)
            t2 = kb.sb(sc, "t2", [128, 256], F32)
            scp1 = ExitStack()
            k32 = kb.sb(scp1, "k32", [128, 2, 128], F32)
            v32 = kb.sb(scp1, "v32", [128, 256], F32)
            KTat = [Tok() for _ in BLKS]
            Vat = [Tok() for _ in BLKS]
            tbt, t1t, t2t, k32t, v32t = [Tok() for _ in range(5)]
            Wkv, Wkvt = wuse(20)
            Wkv3 = w3(Wkv, 0, 8, 768)
            for (c0, n) in BLK2:
                bi = min(c0 // 512, 4)
                kb.dma("sp", tb[:, :, 0:n], d_rt1[:, :, c0:c0 + n], [], [tbt])
                for kc in range(2):
                    pa, pat = kb.ps()
                    pb, pbt = kb.ps()
                    kb.mm(pa[:, 0:n], [(Wkv3[:, k, kc * 128:(kc + 1) * 128], H[:, k, c0:c0 + n]) for k in range(8)],
                          [Wkvt, Ht[bi]], [pat])
                    kb.mm(pb[:, 0:n], [(Wkv3[:, k, 256 + kc * 128:256 + (kc + 1) * 128], H[:, k, c0:c0 + n]) for k in range(8)],
                          [Wkvt, Ht[bi]], [pbt])
                    kb.stt(t1[:, 0:n], pa[:, 0:n], cf[:, BK + kc:BK + kc + 1], tb[:, 0, 0:n], ALU.add, ALU.mult, [pat, tbt, Ct], [t1t])
                    kb.stt(t2[:, 0:n], pb[:, 0:n], cf[:, BKS + kc:BKS + kc + 1], tb[:, 1, 0:n], ALU.add, ALU.mult, [pbt, tbt, Ct], [t2t])
                    kb.tt(KTa[:, kc, c0:c0 + n], t1[:, 0:n], t2[:, 0:n], ALU.add, [t1t, t2t], [KTat[bi]])
                    if c0 >= 1792:
                        lo = n - 128
                        kb.tt(k32[:, kc, :], t1[:, lo:n], t2[:, lo:n], ALU.add, [t1t, t2t], [k32t])
                if c0 >= 1792:
                    kb.dma("sp", o_wkT[:, (c0 - 1792) // 256, :].rearrange("p (k t) -> p k t", k=2), k32[:], [k32t], [k32t])
                for ch in range(n // 128):
                    t0 = c0 + ch * 128
                    ci = t0 // 128
                    pt, ptk = kb.ps()
                    kb.mm(pt[:, 0:256], [(H[:, k, t0:t0 + 128], Wkv3[:, k, 512:768]) for k in range(8)] + [(onerow, bvrow)],
                          [Wkvt, Ht[bi], Ct], [ptk])
                    kb.copy(Va[:, ci, :], pt[:, 0:256], [ptk], [Vat[bi]], en="act")
                    if ci == 15 or ci == 16:
                        kb.copy(v32[:], pt[:, 0:256], [ptk, Vat[bi]], [v32t], en="dve")
                        kb.dma("sp", o_wvp if ci == 15 else o_wvs, v32[:], [v32t], [v32t])
            wrelease(20)
            kb.barrier()
            scp1.close()
            QTc = kb.sb(sc, "QTc", [128, 3, 2, 256], BF16)
            Ee = kb.sb(sc, "Ee", [128, 6, 512], BF16)
            rr = kb.sb(sc, "rr", [128, 128], F32)
            AT = kb.sb(sc, "AT", [128, 8, 256], BF16)
            KcT = kb.sb(sc, "KcT", [128, 16, 256], BF16)
            Vc = kb.sb(sc, "Vc", [128, 16, 256], BF16)
            QTct, Eet = [Tok(), Tok(), Tok()], [Tok() for _ in range(6)]
            kb.op("dve", [], QTct, lambda h: h.memset(QTc[:], 0.0))
            rrt, ATt, KcTt, Vct = [Tok() for _ in range(4)]
            kb.dma("pool", KcT[:], d_cwkT.rearrange("b p k s -> p b (k s)"), [], [KcTt])
            kb.dma("pool", Vc[:], d_cwv.rearrange("b s e -> s b e"), [], [Vct])
            Wq, Wqt = wuse(21)
            Wqs, Wqst = wuse(22)
            Wo, Wot = wuse(23)
            Wq3, Wqs3, Wo3 = w3(Wq, 0, 8, 1024), w3(Wqs, 0, 8, 1024), w3(Wo, 0, 8, 1024)
            def pair_gen(c0, n, bi, c, smp):
                kcx = c // 4
                qb = c % 3
                pa, pat = kb.ps(hold=True)
                pb, pbt = kb.ps(hold=True)
                kb.mm(pa[:, 0:n], [(Wq3[:, k, c * 128:(c + 1) * 128], H[:, k, c0:c0 + n]) for k in range(8)],
                      [Wqt, Ht[bi]], [pat])
                kb.mm(pb[:, 0:n], [(Wqs3[:, k, c * 128:(c + 1) * 128], H[:, k, c0:c0 + n]) for k in range(8)],
                      [Wqst, Ht[bi]], [pbt])
                yield
                kb.stt(t1[:, 0:n], pa[:, 0:n], cf[:, BQ + c:BQ + c + 1], tb[:, 0, 0:n], ALU.add, ALU.mult, [pat, tbt, Ct], [t1t])
                kb.stt(t2[:, 0:n], pb[:, 0:n], cf[:, BQS + c:BQS + c + 1], tb[:, 1, 0:n], ALU.add, ALU.mult, [pbt, tbt, Ct], [t2t])
                kb.pfree(pa)
                kb.pfree(pb)
                kb.tt(QTc[0:64, qb, 0, 0:n], t1[0:64, 0:n], t2[0:64, 0:n], ALU.add, [t1t, t2t], [QTct[qb]])
                kb.tt(QTc[64:128, qb, 1, 0:n], t1[64:128, 0:n], t2[64:128, 0:n], ALU.add, [t1t, t2t], [QTct[qb]])
                for ch in range(n // 128):
                    t0 = c0 + ch * 128
                    ci = t0 // 128
                    cs = slice(ch * 128, (ch + 1) * 128)
                    eb = qb * 2 + ch
                    pci = max(ci - 1, 0)
                    pbi = min(pci // 4, 4)
                    mask = mS if smp else (mP0 if ci == 0 else mP)
                    pS, pSt = kb.ps(hold=True)
                    rd_toks = [QTct[qb], KTat[bi], Ct] + ([KcTt] if smp else [KTat[pbi]])

                    def emit_scores(h, pS=pS, qb=qb, kcx=kcx, pci=pci, t0=t0, cs=cs, mask=mask, smp=smp):
                        ins = None
                        for e2 in range(2):
                            h.matmul(pS[:, e2 * 256:(e2 + 1) * 256], ident, mask, start=True, stop=False, skip_group_check=True)
                            if not smp:
                                h.matmul(pS[:, e2 * 256:e2 * 256 + 128], KTa[:, kcx, pci * 128:(pci + 1) * 128],
                                         QTc[:, qb, e2, cs], start=False, stop=False, skip_group_check=True)
                            else:
                                for b in range(16):
                                    h.matmul(pS[:, e2 * 256 + b * 8:e2 * 256 + b * 8 + 8], KcT[:, b, kcx * 128:(kcx + 1) * 128],
                                             QTc[:, qb, e2, b * 8:b * 8 + 8], start=False, stop=False, skip_group_check=True)
                            ins = h.matmul(pS[:, e2 * 256 + 128:e2 * 256 + 256], KTa[:, kcx, t0:t0 + 128],
                                           QTc[:, qb, e2, cs], start=False, stop=True, skip_group_check=True)
                        return ins
                    kb.op("pe", rd_toks, [pSt], emit_scores)
                    yield
                    kb.act(Ee[:, eb, :], pS[:, :], AF.Exp, [pSt], [Eet[eb]], scale=0.125)
                    kb.pfree(pS)
                    pO, pOt = kb.ps(hold=True)
                    for e2 in range(2):
                        if not smp:
                            kb.mm(pO[:, e2 * 128:(e2 + 1) * 128],
                                  [(Va[:, pci, kcx * 128:(kcx + 1) * 128], Ee[:, eb, e2 * 256:e2 * 256 + 128]),
                                   (Va[:, ci, kcx * 128:(kcx + 1) * 128], Ee[:, eb, e2 * 256 + 128:e2 * 256 + 256])],
                                  [Vat[pbi], Vat[bi], Eet[eb]], [pOt])
                        else:
                            kb.mm(pO[:, e2 * 128:(e2 + 1) * 128],
                                  [(Va[:, ci, kcx * 128:(kcx + 1) * 128], Ee[:, eb, e2 * 256 + 128:e2 * 256 + 256])],
                                  [Vat[bi], Eet[eb]], [pOt])
                            for b in range(16):
                                kb.op("pe", [Vct, Eet[eb]], [pOt],
                                      lambda h, b=b, e2=e2, eb=eb, kcx=kcx, pO=pO: h.matmul(
                                          pO[:, e2 * 128 + b * 8:e2 * 128 + b * 8 + 8],
                                          Vc[:, b, kcx * 128:(kcx + 1) * 128],
                                          Ee[:, eb, e2 * 256 + b * 8:e2 * 256 + b * 8 + 8],
                                          start=False, stop=True, skip_group_check=True))
                    kb.mm(pO[:, 256:384],
                          [(olo, Ee[:, eb, 0:128]), (olo, Ee[:, eb, 128:256]),
                           (ohi, Ee[:, eb, 256:384]), (ohi, Ee[:, eb, 384:512])],
                          [Eet[eb], Ct], [pOt])
                    yield
                    kb.act(rr[:], pO[:, 256:384], AF.Ln, [pOt, Ct], [rrt], bias=cf[:, ESK + c:ESK + c + 1], scale=1.0)
                    kb.act(rr[:], rr[:], AF.Exp, [rrt], [rrt], scale=-1.0)
                    for e2 in range(2):
                        rows = slice(e2 * 64, (e2 + 1) * 64)
                        kb.tt(AT[rows, c, cs], pO[rows, e2 * 128:(e2 + 1) * 128], rr[rows, :], ALU.mult, [pOt, rrt], [ATt])
                    kb.pfree(pO)

            for (c0, n) in BLK2:
                bi = min(c0 // 512, 4)
                smp = bi == 4
                kb.dma("sp", tb[:, :, 0:n], d_rt1[:, :, c0:c0 + n], [], [tbt])
                kb.pipeline([pair_gen(c0, n, bi, c, smp) for c in range(8)], 3)
                for oc in range(8):
                    pt, ptk = kb.ps()
                    kb.mm(pt[:, 0:n], [(Wo3[:, k, oc * 128:(oc + 1) * 128], AT[:, k, 0:n]) for k in range(8)],
                          [Wot, ATt], [ptk])
                    kb.stt(X[:, oc, c0:c0 + n], pt[:, 0:n], cf[:, BO + oc:BO + oc + 1], X[:, oc, c0:c0 + n],
                           ALU.add, ALU.add, [ptk, Xt[bi], Ct], [Xt[bi]])
            kb.barrier()
        wrelease(21)
        wrelease(22)
        wrelease(23)

    import os
    KST = int(os.environ.get("KSTAGES", "99"))
    if KST >= 1:
        layer0_mixer()
    if KST >= 2:
        cross(0)
    if KST >= 3:
        mlp(0)
    if KST >= 4:
        layer1_mixer()
    if KST >= 5:
        cross(1)
        mlp(1)

    with ExitStack() as sc:
        if KST < 5:
            yo = kb.sb(sc, "yo", [128, 8, 512], F32)
            sq = kb.sb(sc, "sq", [128, 8, 512], BF16)
            rs = kb.sb(sc, "rs", [128, 512], F32)
            yot = Tok()
            yT_v = o_yT.rearrange("(kc p) t -> p kc t", p=128)
            for bi, (c0, n) in enumerate(BLKS):
                rmsnorm(X, Xt[bi], c0, n, 8, yo, yot, 0, sq, rs)
                kb.dma("sp", yT_v[:, :, c0:c0 + n], yo[:, :, 0:n], [yot], [yot])
        sp = kb.eng["sp"]
        deps = []
        for q in kb.rings:
            for s in kb.rings[q]:
                if s.count:
                    deps.append((s, s.count))
        kb.sync(sp, deps)
        kb.barrier()
    es.close()
    return nc


def _slot(w):
    K, N = w.shape
    return np.ascontiguousarray(w.reshape(K // 128, 128, N).transpose(1, 0, 2).reshape(128, -1))


def _pad_slot(a):
    out = np.zeros((128, SLOT), np.float32)
    out[:, :a.shape[1]] = a
    return out


def _const_tables():
    f32 = np.float32
    cbf = np.zeros((128, 1792), f32)
    cbf[:, 0:128] = np.eye(128)
    cbf[:, 128:256] = 1.0 / 1024.0
    cbf[:, 256:384] = 1.0 / 256.0
    cbf[:, 384:512] = 1.0
    cbf[:, 512:576] = 1.0
    cbf[:, 704:768] = 1.0
    j = np.arange(128)[:, None]
    i = np.arange(128)[None, :]
    A = (j <= i).astype(f32)
    cbf[:, 768:896] = A
    cbf[:, 896:1024] = ((j // 8 == i // 8) & (j <= i)).astype(f32)
    cbf[:, 1024:1152] = 1.0 - A
    cbf[:, 1152:1280] = A
    cbf[:, 1408:1536] = A
    cbf[:, 1536:1664] = (j > (i % 8)).astype(f32)
    cbf[:, 1664:1792] = cbf[:, 896:1024]
    cbf[:, 1024:1792] = (cbf[:, 1024:1792] - 1.0) * 30000.0
    bsel = np.zeros((128, 16), f32)
    for b in range(16):
        bsel[b * 8:(b + 1) * 8, b] = 1.0
    pos = np.concatenate([np.arange(TP), np.tile(16384 + np.arange(8), 16)]).astype(np.int32)
    ci = np.concatenate([np.arange(TP) % 128, np.tile(np.arange(8), 16)]).astype(np.float64)
    half = 64
    inv = (f32(10000.0) ** (-np.arange(half, dtype=f32) / f32(half))).astype(f32)
    ang = pos.astype(f32)[:, None] * inv[None, :]
    cs = np.cos(ang).astype(np.float64).T
    sn = np.sin(ang).astype(np.float64).T
    cosd = np.concatenate([cs, cs], 0)
    sind = np.concatenate([-sn, sn], 0)
    rt0 = np.zeros((4, 128, 4, T), f32)
    for h in range(4):
        lg = math.log1p(-2.0 ** (-5.0 - h))
        xi = np.exp((ci + 1.0) * lg)[None, :]
        kk = (1.0 / xi) * (128.0 ** -0.5)
        rt0[h, :, 0] = cosd * xi
        rt0[h, :, 1] = sind * xi
        rt0[h, :, 2] = cosd * kk
        rt0[h, :, 3] = sind * kk
    half = 32
    inv1 = (f32(150000.0) ** (-np.arange(half, dtype=f32) / f32(half))).astype(f32)
    ang1 = pos.astype(f32)[:, None] * inv1[None, :]
    c1 = np.cos(ang1).astype(f32).T
    s1 = np.sin(ang1).astype(f32).T
    rt1 = np.zeros((128, 2, T), f32)
    rt1[:, 0] = np.concatenate([c1, c1, c1, c1], 0)
    rt1[:, 1] = np.concatenate([-s1, s1, -s1, s1], 0)
    return cbf, bsel, rt0, rt1


def _prep_shared(inp):
    f32 = np.float32
    g = lambda k: np.asarray(inp[k], f32)
    w_in = g("w_in_e")[0]
    slots = []
    slots.append(_slot(w_in[:, 0:1024]))
    slots.append(_slot(w_in[:, 1024:2048]))
    w_out = g("w_out_e")[0]
    slots.append(_slot(w_out[0:1024]))
    slots.append(_slot(w_out[1024:2048]))
    sw = np.concatenate([np.arange(64, 128), np.arange(0, 64)])
    for h in range(4):
        q = w_in[:, 2048 + h * 128:2048 + (h + 1) * 128]
        k = w_in[:, 2560 + h * 128:2560 + (h + 1) * 128]
        v = w_in[:, 3072 + h * 256:3072 + (h + 1) * 256]
        gt = w_in[:, 4096 + h * 256:4096 + (h + 1) * 256]
        slots.append(_slot(np.concatenate([q, q[:, sw], k, k[:, sw], v, gt], 1)))
    def cross_slots(l):
        return [_slot(g("w_mk")[l]), _slot(g("w_mv")[l]), _slot(g("w_mq")[l]), _slot(g("w_mo")[l])]
    def mlp_slots(l):
        up, dn = g("w_up")[l], g("w_down")[l]
        return [np.concatenate([_slot(up[:, fb * 512:(fb + 1) * 512]), _slot(dn[fb * 512:(fb + 1) * 512, :])], 1)
                for fb in range(8)]
    slots += cross_slots(0) + mlp_slots(0)
    wqkv = g("w_qkv_o")[0]
    bqkv = g("b_qkv_o")[0]
    sw64 = np.concatenate([np.arange(32, 64), np.arange(0, 32)])
    qcols = np.concatenate([np.arange(h * 64, (h + 1) * 64) for h in PERM_HEADS])
    qscols = np.concatenate([h * 64 + sw64 for h in PERM_HEADS])
    kcols = 1024 + np.arange(256)
    kscols = 1024 + np.concatenate([h * 64 + sw64 for h in range(4)])
    vcols = 1280 + np.arange(256)
    slots.append(_pad_slot(_slot(wqkv[:, np.concatenate([kcols, kscols, vcols])])))
    slots.append(_slot(wqkv[:, qcols]))
    slots.append(_slot(wqkv[:, qscols]))
    slots.append(_slot(g("w_out_o")[0][qcols, :]))
    slots += cross_slots(1)
    slots += mlp_slots(1)
    assert len(slots) == 36
    wslots = np.stack(slots, 0)
    cf = np.zeros((128, 109), f32)
    cf[:, 108] = EPS
    gl = [g("g_mix")[0], g("g_mix")[1], g("g_cross")[0], g("g_cross")[1], g("g_mem")[0], g("g_mem")[1],
          g("g_ffn")[0], g("g_ffn")[1], g("g_final")]
    for i, v in enumerate(gl):
        cf[:, i * 8:(i + 1) * 8] = v.reshape(8, 128).T
    cf[:, 72:80] = bqkv[qcols].reshape(8, 128).T
    cf[:, 80:88] = bqkv[qscols].reshape(8, 128).T
    cf[:, 88:90] = bqkv[kcols].reshape(2, 128).T
    cf[:, 90:92] = bqkv[kscols].reshape(2, 128).T
    cf[:, 92:100] = g("b_out_o")[0].reshape(8, 128).T
    sk = g("sinks")[0]
    for c in range(8):
        cf[0:64, 100 + c] = sk[PERM_HEADS[2 * c]]
        cf[64:128, 100 + c] = sk[PERM_HEADS[2 * c + 1]]
    crow = np.zeros((1, 1408), f32)
    bs = g("b_spatial")[0]
    crow[0, 0:512] = bs.reshape(-1)
    crow[0, 512:1024] = np.tile(bs[:, 0:8], (1, 16)).reshape(-1)
    crow[0, 1024:1280] = bqkv[vcols]
    crow[0, 1280:1408] = 1.0
    lngb = np.stack([np.broadcast_to(g("sgu_ln_g")[0], (128, 1024)), np.broadcast_to(g("sgu_ln_b")[0], (128, 1024))], 0)
    ws = g("w_spatial")[0]
    wst = np.zeros((128, 8, 128), f32)
    for gg in range(4):
        wst[:, gg, :] = ws[gg].T
        for b in range(16):
            wst[b * 8:(b + 1) * 8, 4 + gg, b * 8:(b + 1) * 8] = ws[gg, 0:8, 0:8].T
    return wslots, cf, crow, np.ascontiguousarray(lngb), wst


def make_in_maps(inp, cores=range(NCORES)):
    f32 = np.float32
    cbf, bsel, rt0, rt1 = _const_tables()
    wslots, cf, crow, lngb, wst = _prep_shared(inp)
    in_maps = []
    for c in cores:
        bs = slice(16 * c, 16 * c + 16)
        xs = np.asarray(inp["x_sample"][bs], f32).reshape(128, D)
        xT = np.ascontiguousarray(np.concatenate([np.asarray(inp["x_prompt"][c], f32), xs], 0).T)
        cmk = np.asarray(inp["cache_mem_k"][:, bs], f32).reshape(2, 16, 256, D)
        cmv = np.asarray(inp["cache_mem_v"][:, bs], f32).reshape(2, 16, 256, D)
        cwk = np.asarray(inp["cache_win_k"][0, bs], f32).reshape(16, 128, 256)
        cwv = np.asarray(inp["cache_win_v"][0, bs], f32).reshape(16, 128, 256)
        cwkT = np.ascontiguousarray(cwk.reshape(16, 128, 2, 128).transpose(0, 3, 2, 1))
        in_maps.append({
            "xT": xT,
            "memT": np.ascontiguousarray(np.asarray(inp["mem_prompt"][c], f32).T),
            "wslots": wslots,
            "cbf": cbf, "cf": cf, "crow": crow, "lngb": lngb, "wst": wst, "rt0": rt0, "rt1": rt1, "bsel": bsel,
            "state": np.ascontiguousarray(np.asarray(inp["state_ret"][0, bs], f32)),
            "cmkT": np.ascontiguousarray(cmk.transpose(0, 1, 3, 2)),
            "cmv": np.ascontiguousarray(cmv),
            "cwkT": cwkT, "cwv": np.ascontiguousarray(cwv),
            "cwk_raw": np.ascontiguousarray(cwk), "cwv_raw": np.ascontiguousarray(cwv),
        })
    return in_maps


def kernel(**inp):
    f32 = np.float32
    nc = build_program()
    in_maps = make_in_maps(inp)
    res = run_bass_kernel_spmd(nc, in_maps, core_ids=list(range(NCORES)))
    R = res.results
    y_p = np.stack([R[c]["yT"][:, :TP].T for c in range(NCORES)], 0).astype(f32)
    y_s = np.concatenate([R[c]["yT"][:, TP:].T.reshape(16, 8, D) for c in range(NCORES)], 0).astype(f32)
    mem_k = np.stack([R[c]["memk"] for c in range(NCORES)], 1).reshape(2, 8, 256, 4, 256).astype(f32)
    mem_v = np.stack([R[c]["memv"] for c in range(NCORES)], 1).reshape(2, 8, 256, 4, 256).astype(f32)
    ret_p = np.stack([R[c]["retp"] for c in range(NCORES)], 0)[None].astype(f32)
    ret_s = np.concatenate([R[c]["rets"] for c in range(NCORES)], 0)[None].astype(f32)
    sgu_v = np.concatenate([R[c]["sguv"].reshape(16, 8, 4, 256) for c in range(NCORES)], 0)[None].astype(f32)

    def kT_to_tm(a):
        return a.reshape(2, 64, 2, 128).transpose(3, 2, 0, 1).reshape(128, 4, 64)

    wk_p = np.stack([kT_to_tm(R[c]["wkT"][:, 0, :].reshape(128, 2, 128)) for c in range(NCORES)], 0)[None].astype(f32)
    wv_p = np.stack([R[c]["wvp"].reshape(128, 4, 64) for c in range(NCORES)], 0)[None].astype(f32)
    wk_s_l, wv_s_l = [], []
    for c in range(NCORES):
        knew = kT_to_tm(R[c]["wkT"][:, 1, :].reshape(128, 2, 128)).reshape(16, 8, 4, 64)
        vnew = R[c]["wvs"].reshape(16, 8, 4, 64)
        wk_s_l.append(np.concatenate([R[c]["wk_old"].reshape(16, 120, 4, 64), knew], 1))
        wv_s_l.append(np.concatenate([R[c]["wv_old"].reshape(16, 120, 4, 64), vnew], 1))
    wk_s = np.concatenate(wk_s_l, 0)[None].astype(f32)
    wv_s = np.concatenate(wv_s_l, 0)[None].astype(f32)
    return (y_p, y_s, mem_k, mem_v, ret_p, ret_s, sgu_v, wk_p, wv_p, wk_s, wv_s)
```

```python
import math
from contextlib import ExitStack
import numpy as np
import concourse.bass as bass
import concourse.mybir as mybir
from concourse.bass_utils import run_bass_kernel_spmd

F32 = mybir.dt.float32
BF16 = mybir.dt.bfloat16
AF = mybir.ActivationFunctionType
ALU = mybir.AluOpType

NCORES = 8
D = 1024
TP = 2048
TS = 128
T = TP + TS
BLKS = [(0, 512), (512, 512), (1024, 512), (1536, 512), (2048, 128)]
BLK2 = [(i * 256, 256) for i in range(8)] + [(2048, 128)]
EPS = 1e-6
SLOT = 8192
NRING = 3
LIM = 8000
PERM_HEADS = []
for _c in range(8):
    PERM_HEADS += ([_c, 4 + _c] if _c < 4 else [4 + _c, 8 + _c])


class Tok:
    __slots__ = ("w", "r", "const")

    def __init__(self, const=False):
        self.w = None
        self.r = {}
        self.const = const


class SemC:
    def __init__(self, h, owner=None, base=0):
        self.h = h
        self.count = 0
        self.owner = owner
        self.base = base


class Eng:
    def __init__(self, name, h):
        self.name = name
        self.h = h
        self.sems = []
        self.n = 0
        self.seen = {}


class KB:
    def __init__(self, nc):
        self.nc = nc
        self.es = ExitStack()
        self.eng = {n: Eng(n, h) for n, h in [("pe", nc.tensor), ("act", nc.scalar), ("dve", nc.vector),
                                               ("pool", nc.gpsimd), ("sp", nc.sync)]}
        self.nsem = 0
        self.rings = {q: [self.newsem() for _ in range(16)] for q in ("sp", "pool")}
        self.ri = {"sp": 0, "pool": 0}
        self.bar = self.newsem()
        self.psum = []
        self.pi = 0
        for i in range(8):
            t = self.es.enter_context(nc.psum_tensor(f"ps{i}", [128, 512], F32))
            self.psum.append((t, Tok()))

    def newsem(self, owner=None, base=0):
        self.nsem += 1
        h = self.es.enter_context(self.nc.semaphore(f"s{self.nsem}"))
        return SemC(h, owner, base)

    def sb(self, scope, name, shape, dt):
        self.nsb = getattr(self, "nsb", 0) + 1
        return scope.enter_context(self.nc.sbuf_tensor(f"sb{self.nsb}_{name}", shape, dt))

    def ps(self, hold=False):
        held = getattr(self, "held", None)
        if held is None:
            held = self.held = set()
        assert len(held) < 8, "all PSUM banks held"
        while (self.pi % 8) in held:
            self.pi += 1
        i = self.pi % 8
        if hold:
            held.add(i)
        self.pi += 1
        return self.psum[i]

    def ps_release_all(self):
        self.held = set()

    def pfree(self, t):
        for i, (tt_, _) in enumerate(self.psum):
            if tt_ is t:
                self.held.discard(i)

    def pipeline(self, gens, depth):
        gens = list(gens)
        active = []
        nxt = 0
        while nxt < len(gens) or active:
            while len(active) < depth and nxt < len(gens):
                active.append(gens[nxt])
                nxt += 1
            for g in list(active):
                try:
                    next(g)
                except StopIteration:
                    active.remove(g)

    def tick(self, e):
        ep = e.n // LIM
        if ep >= len(e.sems):
            e.sems.append(self.newsem(owner=e, base=ep * LIM))
        s = e.sems[ep]
        v = e.n % LIM + 1
        e.n += 1
        return s, v

    def sync(self, e, deps):
        for d in deps:
            if d is None:
                continue
            s, v = d
            if s.owner is e:
                if e.name == "pe":
                    continue
                if s.base + v < e.n - 1:
                    continue
            if e.seen.get(s, 0) >= v:
                continue
            e.h.wait_ge(s.h, v)
            e.seen[s] = v

    def _deps(self, reads, writes):
        deps = []
        for t in reads:
            deps.append(t.w)
        for t in writes:
            deps.append(t.w)
            deps.extend(t.r.items())
        return deps

    def _update(self, d, reads, writes):
        for t in reads:
            if not t.const:
                if t.r.get(d[0], 0) < d[1]:
                    t.r[d[0]] = d[1]
        for t in writes:
            t.w = d
            t.r = {}

    def op(self, en, reads, writes, fn):
        e = self.eng[en]
        self.sync(e, self._deps(reads, writes))
        ins = fn(e.h)
        d = self.tick(e)
        ins.then_inc(d[0].h, 1)
        self._update(d, reads, writes)

    def dma(self, q, out, in_, reads, writes):
        e = self.eng[q]
        ring = self.rings[q]
        s = ring[self.ri[q] % len(ring)]
        self.ri[q] += 1
        deps = self._deps(reads, writes)
        if s.count:
            deps.append((s, s.count))
        self.sync(e, deps)
        e.h.dma_start(out=out, in_=in_).then_inc(s.h, 16)
        s.count += 16
        self._update((s, s.count), reads, writes)

    def barrier(self):
        sp = self.eng["sp"]
        deps = []
        for q in self.rings:
            for s in self.rings[q]:
                if s.count:
                    deps.append((s, s.count))
        for n in ("pe", "act", "dve"):
            e = self.eng[n]
            if e.n:
                s = e.sems[(e.n - 1) // LIM]
                deps.append((s, (e.n - 1) % LIM + 1))
        self.sync(sp, deps)
        sp.h.sem_inc(self.bar.h, 1)
        self.bar.count += 1
        for n in ("pe", "act", "dve", "pool"):
            self.sync(self.eng[n], [(self.bar, self.bar.count)])

    def mm(self, out, pairs, reads, writes):
        def fn(h):
            ins = None
            n = len(pairs)
            for i, (l, r) in enumerate(pairs):
                ins = h.matmul(out, l, r, start=(i == 0), stop=(i == n - 1))
            return ins
        self.op("pe", reads, writes, fn)

    def act(self, out, in_, func, reads, writes, **kw):
        self.op("act", reads, writes, lambda h: h.activation(out=out, in_=in_, func=func, **kw))

    def tt(self, out, a, b, op, reads, writes, en="dve"):
        self.op(en, reads, writes, lambda h: h.tensor_tensor(out=out, in0=a, in1=b, op=op))

    def ts(self, out, a, s1, s2, op0, op1, reads, writes, en="dve"):
        self.op(en, reads, writes,
                lambda h: h.tensor_scalar(out=out, in0=a, scalar1=s1, scalar2=s2, op0=op0, op1=op1))

    def stt(self, out, a, s, b, op0, op1, reads, writes, en="dve"):
        self.op(en, reads, writes,
                lambda h: h.scalar_tensor_tensor(out=out, in0=a, scalar=s, in1=b, op0=op0, op1=op1))

    def recip(self, out, in_, reads, writes):
        self.op("dve", reads, writes, lambda h: h.reciprocal(out=out, in_=in_))

    def copy(self, out, in_, reads, writes, en="dve"):
        if en == "act":
            self.act(out, in_, AF.Copy, reads, writes)
        else:
            self.op(en, reads, writes, lambda h: h.tensor_copy(out=out, in_=in_))


def build_program():
    nc = bass.Bass("TRN2", target_bir_lowering=False)
    kb = KB(nc)
    es = kb.es

    def din(name, shape):
        return nc.dram_tensor(name, list(shape), F32, kind="ExternalInput").ap()

    def dout(name, shape):
        return nc.dram_tensor(name, list(shape), F32, kind="ExternalOutput").ap()

    d_xT = din("xT", [D, T])
    d_memT = din("memT", [D, 256])
    d_w = din("wslots", [36, 128, SLOT])
    d_cbf = din("cbf", [128, 1792])
    d_cf = din("cf", [128, 109])
    d_crow = din("crow", [1, 1408])
    d_ln = din("lngb", [2, 128, 1024])
    d_wst = din("wst", [128, 8, 128])
    d_rt0 = din("rt0", [4, 128, 4, T])
    d_rt1 = din("rt1", [128, 2, T])
    d_state = din("state", [16, 4, 128, 256])
    d_cmkT = din("cmkT", [2, 16, D, 256])
    d_cmv = din("cmv", [2, 16, 256, D])
    d_cwkT = din("cwkT", [16, 128, 2, 128])
    d_cwv = din("cwv", [16, 128, 256])
    d_cwk_raw = din("cwk_raw", [16, 128, 256])
    d_cwv_raw = din("cwv_raw", [16, 128, 256])

    o_yT = dout("yT", [D, T])
    o_memk = dout("memk", [2, 256, D])
    o_memv = dout("memv", [2, 256, D])
    o_retp = dout("retp", [4, 128, 256])
    o_rets = dout("rets", [16, 4, 128, 256])
    o_sguv = dout("sguv", [128, 1024])
    o_wkT = dout("wkT", [128, 2, 256])
    o_wvp = dout("wvp", [128, 256])
    o_wvs = dout("wvs", [128, 256])
    o_wk_old = dout("wk_old", [16, 120, 256])
    o_wv_old = dout("wv_old", [16, 120, 256])
    import os
    DBG = bool(os.environ.get("KDBG"))
    if DBG:
        o_dbg = dout("dbg", [8, 128, 512])

    def dbgdump(i, ap, n, toks):
        if DBG:
            kb.dma("pool", o_dbg[i, :, 0:n], ap, toks, [])

    X = kb.sb(es, "X", [128, 8, T], F32)
    H = kb.sb(es, "H", [128, 8, T], BF16)
    WR = [kb.sb(es, f"WR{i}", [128, SLOT], BF16) for i in range(NRING)]
    cbf = kb.sb(es, "cbf", [128, 1792], BF16)
    cf = kb.sb(es, "cf", [128, 109], F32)
    crow = kb.sb(es, "crow", [1, 1408], BF16)
    Xt = [Tok() for _ in BLKS]
    Ht = [Tok() for _ in BLKS]
    WRt = [Tok() for _ in range(NRING)]
    Ct = Tok(const=True)
    sqt, rst = Tok(), Tok()

    ident = cbf[:, 0:128]
    o1024 = cbf[:, 128:256]
    o256 = cbf[:, 256:384]
    o1 = cbf[:, 384:512]
    olo = cbf[:, 512:640]
    ohi = cbf[:, 640:768]
    M1 = cbf[:, 768:896]
    M2 = cbf[:, 896:1024]
    mP = cbf[:, 1024:1280]
    mP0 = cbf[:, 1280:1536]
    mS = cbf[:, 1536:1792]

    def gv(i, kc):
        return cf[:, i * 8 + kc:i * 8 + kc + 1]
    BQ, BQS, BK, BKS, BO, ESK, EPSC = 72, 80, 88, 90, 92, 100, 108
    bsP = crow[0:1, 0:512]
    bsS = crow[0:1, 512:1024]
    bvrow = crow[0:1, 1024:1280]
    onerow = crow[0:1, 1280:1408]

    xT_v = d_xT.rearrange("(kc p) t -> p kc t", p=128)
    for bi, (c0, n) in enumerate(BLKS):
        kb.dma("sp", X[:, :, c0:c0 + n], xT_v[:, :, c0:c0 + n], [], [Xt[bi]])
    cft = Tok()
    kb.dma("sp", cf[:], d_cf, [], [cft])
    kb.dma("pool", cbf[:], d_cbf, [], [Ct])
    kb.dma("pool", crow[:], d_crow, [], [Ct])
    bsel = kb.sb(es, "bsel", [128, 16], BF16)
    d_bsel = din("bsel", [128, 16])
    kb.dma("pool", bsel[:], d_bsel, [], [Ct])
    kb.act(cf[:, ESK:ESK + 8], cf[:, ESK:ESK + 8], AF.Exp, [cft], [cft])
    kb.barrier()
    Ct.w = None

    def window_passthrough(scope):
        pas = kb.sb(scope, "pas", [120, 16, 256], F32)
        past = Tok()
        for src, dst in ((d_cwk_raw, o_wk_old), (d_cwv_raw, o_wv_old)):
            kb.dma("sp", pas[:], src[:, 8:128, :].rearrange("b s e -> s b e"), [], [past])
            kb.dma("sp", dst.rearrange("b s e -> s b e"), pas[:], [past], [past])

    wstate = {"next": 0, "free": list(range(NRING)), "loaded": {}}

    def wprefetch():
        while wstate["free"] and wstate["next"] < 36:
            k = wstate["next"]
            ph = wstate["free"].pop(0)
            kb.dma("pool", WR[ph][:], d_w[k], [], [WRt[ph]])
            wstate["loaded"][k] = ph
            wstate["next"] += 1

    def wuse(k):
        while k not in wstate["loaded"]:
            assert wstate["free"], "weight ring exhausted"
            wprefetch()
        ph = wstate["loaded"][k]
        return WR[ph], WRt[ph]

    def wrelease(k):
        ph = wstate["loaded"].pop(k)
        wstate["free"].append(ph)
        wprefetch()

    def w3(W, off, kc, n):
        return W[:, off:off + kc * n].rearrange("p (k n) -> p k n", k=kc)

    def rmsnorm(Xs, Xtok, c0, n, gi, Hs, Htok, hc0, sq, rs):
        kb.act(sq[:, :, 0:n], Xs[:, :, c0:c0 + n], AF.Square, [Xtok], [sqt])
        pt, ptk = kb.ps()
        kb.mm(pt[:, 0:n], [(o1024, sq[:, kc, 0:n]) for kc in range(8)], [sqt, Ct], [ptk])
        kb.act(rs[:, 0:n], pt[:, 0:n], AF.Ln, [ptk, Ct], [rst], bias=cf[:, EPSC:EPSC + 1], scale=1.0)
        kb.act(rs[:, 0:n], rs[:, 0:n], AF.Exp, [rst], [rst], scale=-0.5)
        for kc in range(8):
            kb.stt(Hs[:, kc, hc0:hc0 + n], Xs[:, kc, c0:c0 + n], gv(gi, kc), rs[:, 0:n],
                   ALU.mult, ALU.mult, [Xtok, rst, Ct], [Htok])

    def norm_all(gi):
        with ExitStack() as scn:
            sq = kb.sb(scn, "sq", [128, 8, 512], BF16)
            rs = kb.sb(scn, "rs", [128, 512], F32)
            for bi, (c0, n) in enumerate(BLKS):
                rmsnorm(X, Xt[bi], c0, n, gi, H, Ht[bi], c0, sq, rs)
            kb.barrier()

    def xadd(oc, c0, n, pt, ptk, bi):
        kb.tt(X[:, oc, c0:c0 + n], X[:, oc, c0:c0 + n], pt[:, 0:n], ALU.add, [ptk, Xt[bi]], [Xt[bi]])

    def layer0_mixer():
        norm_all(0)
        wprefetch()
        with ExitStack() as sc:
            lng = kb.sb(sc, "lng", [128, 1024], F32)
            lnb = kb.sb(sc, "lnb", [128, 1024], F32)
            wst = kb.sb(sc, "wst", [128, 8, 128], BF16)
            GU = kb.sb(sc, "GU", [128, 8, 512], BF16)
            AO = kb.sb(sc, "AO", [128, 8, 512], BF16)
            zv2 = [kb.sb(sc, f"zv{i}", [128, 1024], F32) for i in range(2)]
            vnb2 = [kb.sb(sc, f"vnb{i}", [128, 1024], BF16) for i in range(2)]
            st62 = [kb.sb(sc, f"st6{i}", [128, 2, 6], F32) for i in range(2)]
            mv2 = [kb.sb(sc, f"mv{i}", [128, 4], F32) for i in range(2)]
            zvt2, vnbt2, stt2, mvt2 = [[Tok(), Tok()] for _ in range(4)]
            lt, wstt, GUt, AOt = [Tok() for _ in range(4)]
            kb.dma("sp", lng[:], d_ln[0], [], [lt])
            kb.dma("sp", lnb[:], d_ln[1], [], [lt])
            kb.dma("pool", wst[:], d_wst, [], [wstt])
            for g in range(4):
                kb.tt(wst[:, g, :], wst[:, g, :], M1, ALU.mult, [wstt, Ct], [wstt])
                kb.tt(wst[:, 4 + g, :], wst[:, 4 + g, :], M2, ALU.mult, [wstt, Ct], [wstt])
            Wu, Wut = wuse(0)
            Wv, Wvt = wuse(1)
            Wo, Wot = wuse(2)
            Wu3, Wv3, Wo3 = w3(Wu, 0, 8, 1024), w3(Wv, 0, 8, 1024), w3(Wo, 0, 8, 1024)
            def uproj(bi):
                c0, n = BLKS[bi]
                for oc in range(8):
                    pt, ptk = kb.ps()
                    kb.mm(pt[:, 0:n], [(Wu3[:, kc, oc * 128:(oc + 1) * 128], H[:, kc, c0:c0 + n]) for kc in range(8)],
                          [Wut, Ht[bi]], [ptk])
                    kb.act(GU[:, oc, 0:n], pt[:, 0:n], AF.Gelu_apprx_tanh, [ptk], [GUt])

            uproj(0)
            for bi, (c0, n) in enumerate(BLKS):
                smp = bi == 4
                def sgu_chunk(bi, c0, ch, smp):
                    t0 = c0 + ch * 128
                    db = ch % 2
                    zvb, zvtb, vnbb, vnbtb = zv2[db], zvt2[db], vnb2[db], vnbt2[db]
                    st6b, sttb, mvb, mvtb = st62[db], stt2[db], mv2[db], mvt2[db]
                    pv = []
                    for hf in range(2):
                        pt, ptk = kb.ps(hold=True)
                        pv.append((pt, ptk))
                        kb.mm(pt[:, :], [(H[:, kc, t0:t0 + 128], Wv3[:, kc, hf * 512:(hf + 1) * 512]) for kc in range(8)],
                              [Wvt, Ht[bi]], [ptk])
                    yield
                    for hf in range(2):
                        pt, ptk = pv[hf]
                        kb.act(zvb[:, hf * 512:(hf + 1) * 512], pt[:, :], AF.Gelu_apprx_tanh, [ptk], [zvtb])
                        kb.pfree(pt)
                    for hf in range(2):
                        kb.op("dve", [zvtb], [sttb],
                              lambda h, hf=hf: h.bn_stats(out=st6b[:, hf, :], in_=zvb[:, hf * 512:(hf + 1) * 512]))
                    kb.op("dve", [sttb], [mvtb],
                          lambda h: h.bn_aggr(out=mvb[:, 0:2], in_=st6b[:].rearrange("p a b -> p (a b)")))
                    kb.act(mvb[:, 2:3], mvb[:, 1:2], AF.Sqrt, [mvtb], [mvtb], bias=EPS, scale=1.0)
                    kb.recip(mvb[:, 2:3], mvb[:, 2:3], [mvtb], [mvtb])
                    kb.stt(mvb[:, 3:4], mvb[:, 0:1], -1.0, mvb[:, 2:3], ALU.mult, ALU.mult, [mvtb], [mvtb])
                    kb.ts(zvb[:], zvb[:], mvb[:, 2:3], mvb[:, 3:4], ALU.mult, ALU.add, [zvtb, mvtb], [zvtb])
                    kb.tt(zvb[:], zvb[:], lng[:], ALU.mult, [zvtb, lt], [zvtb])
                    if smp:
                        kb.tt(zvb[:], zvb[:], lnb[:], ALU.add, [zvtb, lt], [zvtb])
                        kb.dma("sp", o_sguv, zvb[:], [zvtb], [])
                        kb.copy(vnbb[:], zvb[:], [zvtb], [vnbtb])
                    else:
                        kb.tt(vnbb[:], zvb[:], lnb[:], ALU.add, [zvtb, lt], [vnbtb])
                    yield
                    pA, pAt = kb.ps(hold=True)
                    pB, pBt = kb.ps(hold=True)
                    for oc in range(8):
                        g = oc // 2
                        pp, ppt = (pA, pAt) if oc < 4 else (pB, pBt)
                        wsel = wst[:, (4 + g) if smp else g, :]
                        brow = (bsS if smp else bsP)[0:1, g * 128:(g + 1) * 128]
                        kb.mm(pp[:, (oc % 4) * 128:(oc % 4 + 1) * 128],
                              [(vnbb[:, oc * 128:(oc + 1) * 128], wsel), (onerow, brow)],
                              [vnbtb, wstt, Ct], [ppt])
                    yield
                    kb.tt(AO[:, 0:4, ch * 128:(ch + 1) * 128], GU[:, 0:4, ch * 128:(ch + 1) * 128],
                          pA[:, :].rearrange("p (a b) -> p a b", a=4), ALU.mult, [GUt, pAt], [AOt])
                    kb.tt(AO[:, 4:8, ch * 128:(ch + 1) * 128], GU[:, 4:8, ch * 128:(ch + 1) * 128],
                          pB[:, :].rearrange("p (a b) -> p a b", a=4), ALU.mult, [GUt, pBt], [AOt])
                    kb.pfree(pA)
                    kb.pfree(pB)

                kb.pipeline([sgu_chunk(bi, c0, ch, smp) for ch in range(n // 128)], 2)
                if bi + 1 < len(BLKS):
                    uproj(bi + 1)
                for oc in range(8):
                    pt, ptk = kb.ps()
                    kb.mm(pt[:, 0:n], [(Wo3[:, kc, oc * 128:(oc + 1) * 128], AO[:, kc, 0:n]) for kc in range(8)],
                          [Wot, AOt], [ptk])
                    xadd(oc, c0, n, pt, ptk, bi)
            kb.barrier()
        wrelease(0)
        wrelease(1)
        wrelease(2)
        import os
        if os.environ.get("KSKIPB"):
            for k in range(3, 8):
                wuse(k)
                wrelease(k)
            return
        Wob, Wobt = wuse(3)
        Wob3 = w3(Wob, 0, 8, 1024)
        with ExitStack() as sc:
            tab = kb.sb(sc, "tab", [128, 4, 512], F32)
            QK = kb.sb(sc, "QK", [128, 2, 512], BF16)
            t1 = kb.sb(sc, "t1", [128, 512], F32)
            t2 = kb.sb(sc, "t2", [128, 512], F32)
            SG = kb.sb(sc, "SG", [128, 2, 512], BF16)
            Vt = kb.sb(sc, "Vt", [128, 4, 256], BF16)
            S32 = kb.sb(sc, "S32", [128, 256], F32)
            st32 = kb.sb(sc, "st32", [128, 8, 256], F32)
            stbf = kb.sb(sc, "stbf", [128, 8, 256], BF16)
            Vblk = kb.sb(sc, "Vblk", [128, 8, 256], BF16)
            (tabt, QKt, t1t, t2t, SGt, Vtt, scTt, kTMt, S32t, Sbft, osqt, r2t, BOt, st32t, stbft,
             Vblkt) = [Tok() for _ in range(16)]
            scT4 = kb.sb(sc, "scT4", [128, 4, 128], BF16)
            kTM4 = kb.sb(sc, "kTM4", [128, 4, 128], BF16)
            Sb5 = kb.sb(sc, "Sb5", [128, 5, 256], BF16)
            osq4 = kb.sb(sc, "osq4", [128, 4, 2, 128], BF16)
            osq4t = [Tok() for _ in range(4)]
            osq2 = osq4
            BO2 = kb.sb(sc, "BO2", [128, 2, 2, 512], BF16)
            scT4t, kTM4t = [Tok() for _ in range(4)], [Tok() for _ in range(4)]
            Sb5t = [Tok() for _ in range(5)]
            osq2t, r22t, BO2t = [Tok(), Tok()], [Tok(), Tok()], [Tok(), Tok()]
            r24 = kb.sb(sc, "r24", [128, 512], F32)
            SGr = kb.sb(sc, "SGr", [128, 2, 512], BF16)
            r24t, SGrt = Tok(), Tok()
            scT, kTM, osq, r2, BO = scT4[:, 0, :], kTM4[:, 0, :], osq4[:, 0], r24[:, 0:128], BO2[:, 0]
            scTt, kTMt, osqt, r2t, BOt = scT4t[0], kTM4t[0], osq4t[0], r24t, BO2t[0]
            Sbf, Sbft = Sb5[:, 0, :], Sb5t[0]

            st32q, stbfq, Vblkq = [Tok(), Tok()], [Tok(), Tok()], [Tok(), Tok()]

            def load_state(hd, hq):
                qb2 = hq % 2
                src = d_state[hq * 4:(hq + 1) * 4, hd].rearrange("b p e -> p b e")
                kb.dma("sp", st32[:, qb2 * 4:qb2 * 4 + 4, :], src, [], [st32q[qb2]])
                kb.dma("pool", stbf[:, qb2 * 4:qb2 * 4 + 4, :], src, [], [stbfq[qb2]])

            def outproj(hd, bi, c0, n, BOb, BObt):
                for oc in range(8):
                    pt, ptk = kb.ps()
                    kb.mm(pt[:, 0:n], [(Wob3[:, 2 * hd + ec, oc * 128:(oc + 1) * 128], BOb[:, ec, 0:n]) for ec in range(2)],
                          [Wobt, BObt], [ptk])
                    xadd(oc, c0, n, pt, ptk, bi)

            for hd in range(4):
                lg = math.log1p(-2.0 ** (-5.0 - hd))
                gP = math.exp(128.0 * lg)
                gS = math.exp(8.0 * lg)
                Wh, Wht = wuse(4 + hd)
                Wh3 = w3(Wh, 0, 8, 1024)
                kb.op("dve", [], [S32t], lambda h: h.memset(S32[:], 0.0))
                kb.op("dve", [], [Sb5t[0]], lambda h: h.memset(Sb5[:, 0, :], 0.0))
                pending = None
                load_state(hd, 0)
                load_state(hd, 1)

                def p0_qk(bi, hd=hd, Wh3=Wh3, Wht=Wht):
                    c0, n = BLKS[bi]
                    kb.dma("sp", tab[:, :, 0:n], d_rt0[hd, :, :, c0:c0 + n], [], [tabt])
                    for qk in range(2):
                        pa, pat = kb.ps()
                        pb, pbt = kb.ps()
                        kb.mm(pa[:, 0:n], [(Wh3[:, kc, qk * 256:qk * 256 + 128], H[:, kc, c0:c0 + n]) for kc in range(8)],
                              [Wht, Ht[bi]], [pat])
                        kb.mm(pb[:, 0:n], [(Wh3[:, kc, qk * 256 + 128:qk * 256 + 256], H[:, kc, c0:c0 + n]) for kc in range(8)],
                              [Wht, Ht[bi]], [pbt])
                        kb.tt(t1[:, 0:n], pa[:, 0:n], tab[:, 2 * qk, 0:n], ALU.mult, [pat, tabt], [t1t])
                        kb.tt(t2[:, 0:n], pb[:, 0:n], tab[:, 2 * qk + 1, 0:n], ALU.mult, [pbt, tabt], [t2t])
                        kb.tt(QK[:, qk, 0:n], t1[:, 0:n], t2[:, 0:n], ALU.add, [t1t, t2t], [QKt])

                def p0_v(bi, Wh3=Wh3, Wht=Wht):
                    c0, n = BLKS[bi]
                    for ch in range(n // 128):
                        t0 = c0 + ch * 128
                        pt, ptk = kb.ps()
                        kb.mm(pt[:, 0:256], [(H[:, kc, t0:t0 + 128], Wh3[:, kc, 512:768]) for kc in range(8)],
                              [Wht, Ht[bi]], [ptk])
                        kb.copy(Vt[:, ch, :], pt[:, 0:256], [ptk], [Vtt], en="act")

                for bi, (c0, n) in enumerate(BLKS):
                    smp = bi == 4
                    p0_qk(bi)
                    p0_v(bi)

                    def gate_proj():
                        for ec in range(2):
                            pt, ptk = kb.ps()
                            kb.mm(pt[:, 0:n], [(Wh3[:, kc, 768 + ec * 128:768 + (ec + 1) * 128], H[:, kc, c0:c0 + n]) for kc in range(8)],
                                  [Wht, Ht[bi]], [ptk])
                            kb.act(SG[:, ec, 0:n], pt[:, 0:n], AF.Silu, [ptk], [SGt])

                    if not smp:
                        bb = bi % 2
                        BOb, BObt = BO2[:, bb], BO2t[bb]
                        nch = 4
                        for ch in range(nch):
                            cs = slice(ch * 128, (ch + 1) * 128)
                            pt, ptk = kb.ps()
                            kb.mm(pt[:, 0:128], [(QK[:, 1, cs], QK[:, 0, cs])], [QKt], [ptk])
                            kb.tt(scT4[:, ch, :], pt[:, 0:128], M1, ALU.mult, [ptk, Ct], [scT4t[ch]])
                            pk, pkt = kb.ps()
                            kb.mm(pk[:, 0:128], [(QK[:, 1, cs], ident)], [QKt, Ct], [pkt])
                            kb.copy(kTM4[:, ch, :], pk[:, 0:128], [pkt], [kTM4t[ch]], en="act")
                        gate_proj()
                        puA, puAt = kb.ps(hold=True)
                        puB, puBt = kb.ps(hold=True)
                        PU = [(puA, puAt, 0), (puA, puAt, 256), (puB, puBt, 0), (puB, puBt, 256)]
                        for ch in range(nch):
                            pu, put, o = PU[ch]
                            kb.mm(pu[:, o:o + 256], [(kTM4[:, ch, :], Vt[:, ch, :])], [kTM4t[ch], Vtt], [put])
                        if pending is not None:
                            outproj(*pending)
                            pending = None
                        for ch in range(nch):
                            g = bi * 4 + ch
                            pu, put, o = PU[ch]
                            kb.stt(S32[:], S32[:], gP, pu[:, o:o + 256], ALU.mult, ALU.add, [put, S32t], [S32t])
                            kb.act(Sb5[:, (g + 1) % 5, :], S32[:], AF.Copy, [S32t], [Sb5t[(g + 1) % 5]], scale=gP)
                        kb.pfree(puA)
                        kb.pfree(puB)
                        POs = []
                        for ch in range(nch):
                            g = bi * 4 + ch
                            cs = slice(ch * 128, (ch + 1) * 128)
                            po, pot = kb.ps(hold=True)
                            POs.append((po, pot))
                            for ec in range(2):
                                kb.mm(po[:, ec * 128:(ec + 1) * 128],
                                      [(Vt[:, ch, ec * 128:(ec + 1) * 128], scT4[:, ch, :]),
                                       (Sb5[:, g % 5, ec * 128:(ec + 1) * 128], QK[:, 0, cs])],
                                      [Vtt, scT4t[ch], Sb5t[g % 5], QKt], [pot])
                        for ch in range(nch):
                            po, pot = POs[ch]
                            kb.act(osq4[:, ch].rearrange("p a b -> p (a b)"), po[:, 0:256], AF.Square, [pot], [osq4t[ch]])
                        pn, pnt = kb.ps(hold=True)
                        for ch in range(nch):
                            kb.mm(pn[:, ch * 128:(ch + 1) * 128], [(o256, osq4[:, ch, ec, :]) for ec in range(2)], [osq4t[ch], Ct], [pnt])
                        kb.act(r24[:], pn[:, :], AF.Ln, [pnt], [r24t], bias=cf[:, EPSC:EPSC + 1], scale=1.0)
                        kb.pfree(pn)
                        kb.act(r24[:], r24[:], AF.Exp, [r24t], [r24t], scale=-0.5)
                        kb.tt(SGr[:], SG[:], r24[:].unsqueeze(1).to_broadcast([128, 2, 512]), ALU.mult, [SGt, r24t], [SGrt])
                        for ch in range(nch):
                            po, pot = POs[ch]
                            cs = slice(ch * 128, (ch + 1) * 128)
                            kb.tt(BOb[:, :, cs], po[:, 0:256].rearrange("p (a b) -> p a b", a=2), SGr[:, :, cs], ALU.mult,
                                  [pot, SGrt], [BObt])
                            kb.pfree(po)
                        pending = (hd, bi, c0, n, BOb, BObt)
                        if bi == 3:
                            outproj(*pending)
                            pending = None
                            kb.act(S32[:], S32[:], AF.Copy, [S32t], [S32t], scale=gP)
                            kb.dma("sp", o_retp[hd], S32[:], [S32t], [S32t])
                        continue
                    gate_proj()
                    for ch in range(n // 128):
                        cs = slice(ch * 128, (ch + 1) * 128)
                        pt, ptk = kb.ps()
                        kb.mm(pt[:, 0:128], [(QK[:, 1, cs], QK[:, 0, cs])], [QKt], [ptk])
                        kb.tt(scT[:], pt[:, 0:128], M2 if smp else M1, ALU.mult, [ptk, Ct], [scTt])
                        pk, pkt = kb.ps()
                        kb.mm(pk[:, 0:128], [(QK[:, 1, cs], ident)], [QKt, Ct], [pkt])
                        kb.copy(kTM[:], pk[:, 0:128], [pkt], [kTMt], en="act")
                        if not smp:
                            po, pot = kb.ps()
                            PO = [(po, pot, 0), (po, pot, 128)]
                        else:
                            poA, poAt = kb.ps(hold=True)
                            poB, poBt = kb.ps(hold=True)
                            PO = [(poA, poAt, 0), (poB, poBt, 0)]
                        if not smp:
                            for ec in range(2):
                                kb.mm(po[:, ec * 128:(ec + 1) * 128],
                                      [(Vt[:, ch, ec * 128:(ec + 1) * 128], scT[:]),
                                       (Sbf[:, ec * 128:(ec + 1) * 128], QK[:, 0, cs])],
                                      [Vtt, scTt, Sbft, QKt], [pot])
                        else:
                            for hq in range(4):
                                qb2 = hq % 2
                                s32q = st32[:, qb2 * 4:qb2 * 4 + 4, :]
                                sbfq = stbf[:, qb2 * 4:qb2 * 4 + 4, :]
                                vbq = Vblk[:, qb2 * 4:qb2 * 4 + 4, :]
                                if hq == 0:
                                    for ec in range(2):
                                        kb.mm(PO[ec][0][:, 0:128],
                                              [(Vt[:, ch, ec * 128:(ec + 1) * 128], scT[:])],
                                              [Vtt, scTt], [PO[ec][1]])
                                for b4 in range(4):
                                    b = hq * 4 + b4
                                    for ec in range(2):
                                        kb.op("pe", [stbfq[qb2], QKt], [PO[ec][1]],
                                              lambda h, b=b, b4=b4, ec=ec, PO=PO, sbfq=sbfq: h.matmul(
                                                  PO[ec][0][:, b * 8:b * 8 + 8],
                                                  sbfq[:, b4, ec * 128:(ec + 1) * 128],
                                                  QK[:, 0, b * 8:b * 8 + 8], start=False, stop=True,
                                                  skip_group_check=True))
                                kb.op("dve", [Vtt, Ct], [Vblkq[qb2]],
                                      lambda h, hq=hq, vbq=vbq: h.tensor_tensor(
                                          out=vbq,
                                          in0=Vt[:, ch:ch + 1, :].to_broadcast([128, 4, 256]),
                                          in1=bsel[:, hq * 4:(hq + 1) * 4].unsqueeze(2).to_broadcast([128, 4, 256]), op=ALU.mult))
                                for b2 in range(2):
                                    pu, put = kb.ps()
                                    kb.mm(pu[:, :], [(kTM[:], vbq[:, 2 * b2:2 * b2 + 2, :].rearrange("p a b -> p (a b)"))],
                                          [kTMt, Vblkq[qb2]], [put])
                                    kb.tt(s32q[:, 2 * b2:2 * b2 + 2, :].rearrange("p a b -> p (a b)"),
                                          s32q[:, 2 * b2:2 * b2 + 2, :].rearrange("p a b -> p (a b)"),
                                          pu[:, :], ALU.add, [put, st32q[qb2]], [st32q[qb2]])
                                kb.act(s32q, s32q, AF.Copy, [st32q[qb2]], [st32q[qb2]], scale=gS)
                                kb.dma("sp", o_rets[hq * 4:(hq + 1) * 4, hd].rearrange("b p e -> p b e"), s32q,
                                       [st32q[qb2]], [st32q[qb2]])
                                if hq + 2 < 4:
                                    load_state(hd, hq + 2)
                        if not smp:
                            kb.act(osq[:].rearrange("p a b -> p (a b)"), po[:, 0:256], AF.Square, [pot], [osqt])
                        else:
                            for ec in range(2):
                                kb.act(osq[:, ec, :], PO[ec][0][:, 0:128], AF.Square, [PO[ec][1]], [osqt])
                        pn, pnt = kb.ps()
                        kb.mm(pn[:, 0:128], [(o256, osq[:, ec, :]) for ec in range(2)], [osqt, Ct], [pnt])
                        kb.act(r2[:], pn[:, 0:128], AF.Sqrt, [pnt], [r2t], bias=EPS, scale=1.0)
                        kb.recip(r2[:], r2[:], [r2t], [r2t])
                        for ec in range(2):
                            kb.tt(t1[:, 0:128], PO[ec][0][:, PO[ec][2]:PO[ec][2] + 128], r2[:], ALU.mult, [PO[ec][1], r2t], [t1t])
                            kb.tt(BO[:, ec, cs], t1[:, 0:128], SG[:, ec, cs], ALU.mult, [t1t, SGt], [BOt])
                        kb.ps_release_all()
                        if not smp:
                            pu, put = kb.ps()
                            kb.mm(pu[:, 0:256], [(kTM[:], Vt[:, ch, :])], [kTMt, Vtt], [put])
                            kb.stt(S32[:], S32[:], gP, pu[:, 0:256], ALU.mult, ALU.add, [put, S32t], [S32t])
                            kb.act(Sbf[:], S32[:], AF.Copy, [S32t], [Sbft], scale=gP)
                    if smp and hd == 0:
                        dbgdump(0, QK[:, 0, 0:128], 128, [QKt])
                        dbgdump(1, QK[:, 1, 0:128], 128, [QKt])
                        dbgdump(2, scT[:], 128, [scTt])
                        dbgdump(3, BO[:, 0, 0:128], 128, [BOt])
                        dbgdump(4, BO[:, 1, 0:128], 128, [BOt])
                        dbgdump(5, Vt[:, 0, :], 256, [Vtt])
                        dbgdump(6, SG[:, 0, 0:128], 128, [SGt])
                        dbgdump(7, r2[:], 128, [r2t])
                    for oc in range(8):
                        pt, ptk = kb.ps()
                        kb.mm(pt[:, 0:n], [(Wob3[:, 2 * hd + ec, oc * 128:(oc + 1) * 128], BO[:, ec, 0:n]) for ec in range(2)],
                              [Wobt, BOt], [ptk])
                        xadd(oc, c0, n, pt, ptk, bi)
                wrelease(4 + hd)
            kb.barrier()
        wrelease(3)

    def cross(l):
        s_mk, s_mv, s_mq, s_mo = (8, 9, 10, 11) if l == 0 else (24, 25, 26, 27)
        norm_all(2 + l)
        with ExitStack() as sc:
            KT = kb.sb(sc, "KT", [128, 8, 256], BF16)
            Vm = kb.sb(sc, "Vm", [128, 2, 1024], BF16)
            KTt, Vmt = Tok(), Tok()
            with ExitStack() as sc1:
                MT = kb.sb(sc1, "MT", [128, 8, 256], F32)
                Mh = kb.sb(sc1, "Mh", [128, 8, 256], BF16)
                kvo = kb.sb(sc1, "kvo", [128, 1024], F32)
                MTt, Mht, kvot = Tok(), Tok(), Tok()
                kb.dma("sp", MT[:], d_memT.rearrange("(kc p) m -> p kc m", p=128), [], [MTt])
                sqm = kb.sb(sc1, "sqm", [128, 8, 256], BF16)
                rsm = kb.sb(sc1, "rsm", [128, 256], F32)
                rmsnorm(MT, MTt, 0, 256, 4 + l, Mh, Mht, 0, sqm, rsm)
                Wk, Wkt = wuse(s_mk)
                Wk3 = w3(Wk, 0, 8, 1024)
                for oc in range(8):
                    pt, ptk = kb.ps()
                    kb.mm(pt[:, 0:256], [(Wk3[:, kc, oc * 128:(oc + 1) * 128], Mh[:, kc, :]) for kc in range(8)],
                          [Wkt, Mht], [ptk])
                    kb.copy(KT[:, oc, :], pt[:, 0:256], [ptk], [KTt], en="act")
                for which, (sl, dst) in enumerate([(s_mk, o_memk), (s_mv, o_memv)]):
                    Wx, Wxt = wuse(sl)
                    Wx3 = w3(Wx, 0, 8, 1024)
                    for mc in range(2):
                        for hf in range(2):
                            pt, ptk = kb.ps()
                            kb.mm(pt[:, :], [(Mh[:, kc, mc * 128:(mc + 1) * 128], Wx3[:, kc, hf * 512:(hf + 1) * 512]) for kc in range(8)],
                                  [Wxt, Mht], [ptk])
                            kb.copy(kvo[:, hf * 512:(hf + 1) * 512], pt[:, :], [ptk], [kvot], en="act")
                            if which == 1:
                                kb.copy(Vm[:, mc, hf * 512:(hf + 1) * 512], kvo[:, hf * 512:(hf + 1) * 512], [kvot], [Vmt], en="dve")
                        kb.dma("sp", dst[l, mc * 128:(mc + 1) * 128, :], kvo[:], [kvot], [kvot])
                    wrelease(sl)
                kb.barrier()
            with ExitStack() as sc2:
                QTh = kb.sb(sc2, "QTh", [128, 2, 2, 512], BF16)
                E = kb.sb(sc2, "E", [128, 2, 2, 512], BF16)
                rden = kb.sb(sc2, "rden", [128, 512], F32)
                AT = kb.sb(sc2, "AT", [128, 8, 512], BF16)
                KbT = kb.sb(sc2, "KbT", [128, 2, 8, 256], BF16)
                Vb = kb.sb(sc2, "Vb", [128, 2, 2, 1024], BF16)
                QTs = kb.sb(sc2, "QTs", [128, 8, 128], BF16)
                Eb = kb.sb(sc2, "Eb", [128, 64], BF16)
                rd = kb.sb(sc2, "rd", [128, 32], F32)
                QTht, Et = [Tok(), Tok()], [Tok(), Tok()]
                rdent, ATt, QTst, Ebt, rdt = [Tok() for _ in range(5)]
                KbTt, Vbt = [Tok(), Tok()], [Tok(), Tok()]
                Wq, Wqt = wuse(s_mq)
                Wo, Wot = wuse(s_mo)
                Wq3, Wo3 = w3(Wq, 0, 8, 1024), w3(Wo, 0, 8, 1024)
                ATs = kb.sb(sc2, "ATs", [128, 8, 128], BF16)
                ATst = Tok()

                def outproj(ATx, ATxt, bi, c0, n):
                    for oc in range(8):
                        pt, ptk = kb.ps()
                        kb.mm(pt[:, 0:n], [(Wo3[:, kc, oc * 128:(oc + 1) * 128], ATx[:, kc, 0:n]) for kc in range(8)],
                              [Wot, ATxt], [ptk])
                        xadd(oc, c0, n, pt, ptk, bi)

                def load_batch(b):
                    pb = b % 2
                    kb.dma("pool", KbT[:, pb], d_cmkT[l, b].rearrange("(kc p) m -> p kc m", p=128), [], [KbTt[pb]])
                    kb.dma("pool", Vb[:, pb], d_cmv[l, b].rearrange("(mc p) e -> p mc e", p=128), [], [Vbt[pb]])

                def head_A(bi, c0, n, hd):
                    hb = hd % 2
                    for dc in range(2):
                        pt, ptk = kb.ps()
                        kb.mm(pt[:, 0:n], [(Wq3[:, kc, (2 * hd + dc) * 128:(2 * hd + dc + 1) * 128], H[:, kc, c0:c0 + n]) for kc in range(8)],
                              [Wqt, Ht[bi]], [ptk])
                        kb.copy(QTh[:, hb, dc, 0:n], pt[:, 0:n], [ptk], [QTht[hb]], en="act")

                def head_B(bi, c0, n, hd):
                    hb = hd % 2
                    for mc in range(2):
                        pt, ptk = kb.ps()
                        kb.mm(pt[:, 0:n], [(KT[:, 2 * hd + dc, mc * 128:(mc + 1) * 128], QTh[:, hb, dc, 0:n]) for dc in range(2)],
                              [KTt, QTht[hb]], [ptk])
                        kb.act(E[:, hb, mc, 0:n], pt[:, 0:n], AF.Exp, [ptk], [Et[hb]], scale=1.0 / 16.0)

                def head_C(bi, c0, n, hd):
                    hb = hd % 2
                    pd, pdt = kb.ps()
                    kb.mm(pd[:, 0:n], [(o1, E[:, hb, mc, 0:n]) for mc in range(2)], [Et[hb], Ct], [pdt])
                    kb.act(rden[:, 0:n], pd[:, 0:n], AF.Ln, [pdt], [rdent])
                    kb.act(rden[:, 0:n], rden[:, 0:n], AF.Exp, [rdent], [rdent], scale=-1.0)
                    for ec in range(2):
                        pt, ptk = kb.ps()
                        kb.mm(pt[:, 0:n], [(Vm[:, mc, hd * 256 + ec * 128:hd * 256 + (ec + 1) * 128], E[:, hb, mc, 0:n]) for mc in range(2)],
                              [Vmt, Et[hb]], [ptk])
                        kb.tt(AT[:, 2 * hd + ec, 0:n], pt[:, 0:n], rden[:, 0:n], ALU.mult, [ptk, rdent], [ATt])

                def sample_S1(b):
                    pb = b % 2
                    pS, pSt = kb.ps()
                    for hd in range(4):
                        for mc in range(2):
                            r0 = (hd * 2 + mc) * 8
                            kb.mm(pS[:, r0:r0 + 8],
                                  [(KbT[:, pb, 2 * hd + dc, mc * 128:(mc + 1) * 128], QTs[:, 2 * hd + dc, b * 8:b * 8 + 8]) for dc in range(2)],
                                  [KbTt[pb], QTst], [pSt])
                    kb.act(Eb[:], pS[:, 0:64], AF.Exp, [pSt], [Ebt], scale=1.0 / 16.0)

                def sample_S2(b):
                    pb = b % 2
                    pO, pOt = kb.ps()
                    for hd in range(4):
                        for ec in range(2):
                            r0 = (hd * 2 + ec) * 8
                            kb.mm(pO[:, r0:r0 + 8],
                                  [(Vb[:, pb, mc, hd * 256 + ec * 128:hd * 256 + (ec + 1) * 128], Eb[:, (hd * 2 + mc) * 8:(hd * 2 + mc) * 8 + 8]) for mc in range(2)],
                                  [Vbt[pb], Ebt], [pOt])
                        kb.mm(pO[:, 64 + hd * 8:64 + hd * 8 + 8],
                              [(o1, Eb[:, (hd * 2 + mc) * 8:(hd * 2 + mc) * 8 + 8]) for mc in range(2)],
                              [Ebt, Ct], [pOt])
                    kb.act(rd[:], pO[:, 64:96], AF.Ln, [pOt], [rdt])
                    kb.act(rd[:], rd[:], AF.Exp, [rdt], [rdt], scale=-1.0)
                    for ec in range(2):
                        kb.tt(ATs[:, :, b * 8:b * 8 + 8].rearrange("p (h e) c -> p h e c", e=2)[:, :, ec, :],
                              pO[:, 0:64].rearrange("p (h e c) -> p h e c", e=2, c=8)[:, :, ec, :],
                              rd[:].rearrange("p (h c) -> p h c", c=8), ALU.mult, [pOt, rdt], [ATst])

                sc0, sn = BLKS[4]
                for oc in range(8):
                    pt, ptk = kb.ps()
                    kb.mm(pt[:, 0:sn], [(Wq3[:, kc, oc * 128:(oc + 1) * 128], H[:, kc, sc0:sc0 + sn]) for kc in range(8)],
                          [Wqt, Ht[4]], [ptk])
                    kb.copy(QTs[:, oc, :], pt[:, 0:sn], [ptk], [QTst], en="act")
                load_batch(0)
                load_batch(1)
                def unit(u):
                    bi, hd = u // 4, u % 4
                    c0, n = BLKS[bi]
                    return bi, c0, n, hd
                head_A(*unit(0))
                head_B(*unit(0))
                for u in range(16):
                    bi, c0, n, hd = unit(u)
                    if u + 1 < 16:
                        head_A(*unit(u + 1))
                    sample_S1(u)
                    head_C(bi, c0, n, hd)
                    if u + 1 < 16:
                        head_B(*unit(u + 1))
                    if hd == 3:
                        outproj(AT, ATt, bi, c0, n)
                    sample_S2(u)
                    if u + 2 < 16:
                        load_batch(u + 2)
                outproj(ATs, ATst, 4, sc0, sn)
                kb.barrier()
            wrelease(s_mq)
            wrelease(s_mo)

    def mlp(l):
        base = 12 if l == 0 else 28
        with ExitStack() as sc:
            sqn = kb.sb(sc, "sqn", [128, 8, 512], BF16)
            rsn = kb.sb(sc, "rsn", [128, 512], F32)
            for bi_, (c0_, n_) in enumerate(BLKS):
                rmsnorm(X, Xt[bi_], c0_, n_, 6 + l, H, Ht[bi_], c0_, sqn, rsn)
            if l == 1:
                yo = kb.sb(sc, "yo", [128, 8, 512], F32)
                yot = Tok()
                yT_v = o_yT.rearrange("(kc p) t -> p kc t", p=128)
            r32 = kb.sb(sc, "r32", [128, 2, 512], F32)
            hid = kb.sb(sc, "hid", [128, 2, 4, 512], BF16)
            r32t, hidt = [Tok(), Tok()], [Tok(), Tok()]
            if l == 0:
                window_passthrough(sc)
            its = [(fb, bi) for fb in range(8) for bi in range(len(BLKS))]
            wcache = {}

            def getw(fb):
                if fb not in wcache:
                    Wf, Wft = wuse(base + fb)
                    wcache[fb] = (w3(Wf, 0, 8, 512), w3(Wf, 4096, 4, 1024), Wft)
                return wcache[fb]

            def up(i):
                fb, bi = its[i]
                c0, n = BLKS[bi]
                Wup, Wdn, Wft = getw(fb)
                hbuf = i % 2
                for hc in range(4):
                    pt, ptk = kb.ps()
                    kb.mm(pt[:, 0:n], [(Wup[:, kc, hc * 128:(hc + 1) * 128], H[:, kc, c0:c0 + n]) for kc in range(8)],
                          [Wft, Ht[bi]], [ptk])
                    rb = hc % 2
                    kb.act(r32[:, rb, 0:n], pt[:, 0:n], AF.Relu, [ptk], [r32t[rb]])
                    kb.tt(hid[:, hbuf, hc, 0:n], r32[:, rb, 0:n], r32[:, rb, 0:n], ALU.mult, [r32t[rb]], [hidt[hbuf]])

            def down(i):
                fb, bi = its[i]
                c0, n = BLKS[bi]
                Wup, Wdn, Wft = getw(fb)
                hbuf = i % 2
                for oc in range(8):
                    pt, ptk = kb.ps()
                    kb.mm(pt[:, 0:n], [(Wdn[:, hc, oc * 128:(oc + 1) * 128], hid[:, hbuf, hc, 0:n]) for hc in range(4)],
                          [Wft, hidt[hbuf]], [ptk])
                    xadd(oc, c0, n, pt, ptk, bi)
                if l == 1 and fb == 7:
                    rmsnorm(X, Xt[bi], c0, n, 8, yo, yot, 0, sqn, rsn)
                    kb.dma("sp", yT_v[:, :, c0:c0 + n], yo[:, :, 0:n], [yot], [yot])
                if bi == len(BLKS) - 1:
                    wrelease(base + fb)

            up(0)
            for i in range(len(its)):
                if i + 1 < len(its):
                    up(i + 1)
                down(i)
            kb.barrier()

    def layer1_mixer():
        norm_all(1)
        with ExitStack() as sc:
            KTa = kb.sb(sc, "KTa", [128, 2, T], BF16)
            Va = kb.sb(sc, "Va", [128, 17, 256], BF16)
            tb = kb.sb(sc, "tb", [128, 2, 256], F32)
            t1 = kb.sb(sc, "t1", [128, 256], F32)
            t2 = kb.sb(sc, "t2", [128, 256], F32)
            scp1 = ExitStack()
            k32 = kb.sb(scp1, "k32", [128, 2, 128], F32)
            v32 = kb.sb(scp1, "v32", [128, 256], F32)
            KTat = [Tok() for _ in BLKS]
            Vat = [Tok() for _ in BLKS]
            tbt, t1t, t2t, k32t, v32t = [Tok() for _ in range(5)]
            Wkv, Wkvt = wuse(20)
            Wkv3 = w3(Wkv, 0, 8, 768)
            for (c0, n) in BLK2:
                bi = min(c0 // 512, 4)
                kb.dma("sp", tb[:, :, 0:n], d_rt1[:, :, c0:c0 + n], [], [tbt])
                for kc in range(2):
                    pa, pat = kb.ps()
                    pb, pbt = kb.ps()
                    kb.mm(pa[:, 0:n], [(Wkv3[:, k, kc * 128:(kc + 1) * 128], H[:, k, c0:c0 + n]) for k in range(8)],
                          [Wkvt, Ht[bi]], [pat])
                    kb.mm(pb[:, 0:n], [(Wkv3[:, k, 256 + kc * 128:256 + (kc + 1) * 128], H[:, k, c0:c0 + n]) for k in range(8)],
                          [Wkvt, Ht[bi]], [pbt])
                    kb.stt(t1[:, 0:n], pa[:, 0:n], cf[:, BK + kc:BK + kc + 1], tb[:, 0, 0:n], ALU.add, ALU.mult, [pat, tbt, Ct], [t1t])
                    kb.stt(t2[:, 0:n], pb[:, 0:n], cf[:, BKS + kc:BKS + kc + 1], tb[:, 1, 0:n], ALU.add, ALU.mult, [pbt, tbt, Ct], [t2t])
                    kb.tt(KTa[:, kc, c0:c0 + n], t1[:, 0:n], t2[:, 0:n], ALU.add, [t1t, t2t], [KTat[bi]])
                    if c0 >= 1792:
                        lo = n - 128
                        kb.tt(k32[:, kc, :], t1[:, lo:n], t2[:, lo:n], ALU.add, [t1t, t2t], [k32t])
                if c0 >= 1792:
                    kb.dma("sp", o_wkT[:, (c0 - 1792) // 256, :].rearrange("p (k t) -> p k t", k=2), k32[:], [k32t], [k32t])
                for ch in range(n // 128):
                    t0 = c0 + ch * 128
                    ci = t0 // 128
                    pt, ptk = kb.ps()
                    kb.mm(pt[:, 0:256], [(H[:, k, t0:t0 + 128], Wkv3[:, k, 512:768]) for k in range(8)] + [(onerow, bvrow)],
                          [Wkvt, Ht[bi], Ct], [ptk])
                    kb.copy(Va[:, ci, :], pt[:, 0:256], [ptk], [Vat[bi]], en="act")
                    if ci == 15 or ci == 16:
                        kb.copy(v32[:], pt[:, 0:256], [ptk, Vat[bi]], [v32t], en="dve")
                        kb.dma("sp", o_wvp if ci == 15 else o_wvs, v32[:], [v32t], [v32t])
            wrelease(20)
            kb.barrier()
            scp1.close()
            QTc = kb.sb(sc, "QTc", [128, 4, 2, 256], BF16)
            Ee = kb.sb(sc, "Ee", [128, 5, 512], BF16)
            rr = kb.sb(sc, "rr", [128, 128], F32)
            AT = kb.sb(sc, "AT", [128, 8, 256], BF16)
            KcT = kb.sb(sc, "KcT", [128, 16, 256], BF16)
            Vc = kb.sb(sc, "Vc", [128, 16, 256], BF16)
            QTct, Eet = [Tok(), Tok(), Tok(), Tok()], [Tok() for _ in range(5)]
            kb.op("dve", [], QTct, lambda h: h.memset(QTc[:], 0.0))
            rrt, ATt, KcTt, Vct = [Tok() for _ in range(4)]
            kb.dma("pool", KcT[:], d_cwkT.rearrange("b p k s -> p b (k s)"), [], [KcTt])
            kb.dma("pool", Vc[:], d_cwv.rearrange("b s e -> s b e"), [], [Vct])
            Wq, Wqt = wuse(21)
            Wqs, Wqst = wuse(22)
            Wo, Wot = wuse(23)
            Wq3, Wqs3, Wo3 = w3(Wq, 0, 8, 1024), w3(Wqs, 0, 8, 1024), w3(Wo, 0, 8, 1024)
            def pair_gen(c0, n, bi, c, smp):
                kcx = c // 4
                qb = c % 4
                pa, pat = kb.ps(hold=True)
                pb, pbt = kb.ps(hold=True)
                kb.mm(pa[:, 0:n], [(Wq3[:, k, c * 128:(c + 1) * 128], H[:, k, c0:c0 + n]) for k in range(8)],
                      [Wqt, Ht[bi]], [pat])
                kb.mm(pb[:, 0:n], [(Wqs3[:, k, c * 128:(c + 1) * 128], H[:, k, c0:c0 + n]) for k in range(8)],
                      [Wqst, Ht[bi]], [pbt])
                yield
                kb.stt(t1[:, 0:n], pa[:, 0:n], cf[:, BQ + c:BQ + c + 1], tb[:, 0, 0:n], ALU.add, ALU.mult, [pat, tbt, Ct], [t1t])
                kb.stt(t2[:, 0:n], pb[:, 0:n], cf[:, BQS + c:BQS + c + 1], tb[:, 1, 0:n], ALU.add, ALU.mult, [pbt, tbt, Ct], [t2t])
                kb.pfree(pa)
                kb.pfree(pb)
                kb.tt(QTc[0:64, qb, 0, 0:n], t1[0:64, 0:n], t2[0:64, 0:n], ALU.add, [t1t, t2t], [QTct[qb]])
                kb.tt(QTc[64:128, qb, 1, 0:n], t1[64:128, 0:n], t2[64:128, 0:n], ALU.add, [t1t, t2t], [QTct[qb]])
                for ch in range(n // 128):
                    t0 = c0 + ch * 128
                    ci = t0 // 128
                    cs = slice(ch * 128, (ch + 1) * 128)
                    eb = (c * 2 + ch) % 5
                    pci = max(ci - 1, 0)
                    pbi = min(pci // 4, 4)
                    mask = mS if smp else (mP0 if ci == 0 else mP)
                    pS, pSt = kb.ps(hold=True)
                    rd_toks = [QTct[qb], KTat[bi], Ct] + ([KcTt] if smp else [KTat[pbi]])

                    def emit_scores(h, pS=pS, qb=qb, kcx=kcx, pci=pci, t0=t0, cs=cs, mask=mask, smp=smp):
                        ins = None
                        for e2 in range(2):
                            h.matmul(pS[:, e2 * 256:(e2 + 1) * 256], ident, mask, start=True, stop=False, skip_group_check=True)
                            if not smp:
                                h.matmul(pS[:, e2 * 256:e2 * 256 + 128], KTa[:, kcx, pci * 128:(pci + 1) * 128],
                                         QTc[:, qb, e2, cs], start=False, stop=False, skip_group_check=True)
                            else:
                                for b in range(16):
                                    h.matmul(pS[:, e2 * 256 + b * 8:e2 * 256 + b * 8 + 8], KcT[:, b, kcx * 128:(kcx + 1) * 128],
                                             QTc[:, qb, e2, b * 8:b * 8 + 8], start=False, stop=False, skip_group_check=True)
                            ins = h.matmul(pS[:, e2 * 256 + 128:e2 * 256 + 256], KTa[:, kcx, t0:t0 + 128],
                                           QTc[:, qb, e2, cs], start=False, stop=True, skip_group_check=True)
                        return ins
                    kb.op("pe", rd_toks, [pSt], emit_scores)
                    yield
                    kb.act(Ee[:, eb, :], pS[:, :], AF.Exp, [pSt], [Eet[eb]], scale=0.125)
                    kb.pfree(pS)
                    pO, pOt = kb.ps(hold=True)
                    for e2 in range(2):
                        if not smp:
                            kb.mm(pO[:, e2 * 128:(e2 + 1) * 128],
                                  [(Va[:, pci, kcx * 128:(kcx + 1) * 128], Ee[:, eb, e2 * 256:e2 * 256 + 128]),
                                   (Va[:, ci, kcx * 128:(kcx + 1) * 128], Ee[:, eb, e2 * 256 + 128:e2 * 256 + 256])],
                                  [Vat[pbi], Vat[bi], Eet[eb]], [pOt])
                        else:
                            kb.mm(pO[:, e2 * 128:(e2 + 1) * 128],
                                  [(Va[:, ci, kcx * 128:(kcx + 1) * 128], Ee[:, eb, e2 * 256 + 128:e2 * 256 + 256])],
                                  [Vat[bi], Eet[eb]], [pOt])
                            for b in range(16):
                                kb.op("pe", [Vct, Eet[eb]], [pOt],
                                      lambda h, b=b, e2=e2, eb=eb, kcx=kcx, pO=pO: h.matmul(
                                          pO[:, e2 * 128 + b * 8:e2 * 128 + b * 8 + 8],
                                          Vc[:, b, kcx * 128:(kcx + 1) * 128],
                                          Ee[:, eb, e2 * 256 + b * 8:e2 * 256 + b * 8 + 8],
                                          start=False, stop=True, skip_group_check=True))
                    kb.mm(pO[:, 256:384],
                          [(olo, Ee[:, eb, 0:128]), (olo, Ee[:, eb, 128:256]),
                           (ohi, Ee[:, eb, 256:384]), (ohi, Ee[:, eb, 384:512])],
                          [Eet[eb], Ct], [pOt])
                    yield
                    kb.act(rr[:], pO[:, 256:384], AF.Ln, [pOt, Ct], [rrt], bias=cf[:, ESK + c:ESK + c + 1], scale=1.0)
                    kb.act(rr[:], rr[:], AF.Exp, [rrt], [rrt], scale=-1.0)
                    for e2 in range(2):
                        rows = slice(e2 * 64, (e2 + 1) * 64)
                        kb.tt(AT[rows, c, cs], pO[rows, e2 * 128:(e2 + 1) * 128], rr[rows, :], ALU.mult, [pOt, rrt], [ATt])
                    kb.pfree(pO)

            for (c0, n) in BLK2:
                bi = min(c0 // 512, 4)
                smp = bi == 4
                kb.dma("sp", tb[:, :, 0:n], d_rt1[:, :, c0:c0 + n], [], [tbt])
                kb.pipeline([pair_gen(c0, n, bi, c, smp) for c in range(8)], 4)
                for oc in range(8):
                    pt, ptk = kb.ps()
                    kb.mm(pt[:, 0:n], [(Wo3[:, k, oc * 128:(oc + 1) * 128], AT[:, k, 0:n]) for k in range(8)],
                          [Wot, ATt], [ptk])
                    kb.stt(X[:, oc, c0:c0 + n], pt[:, 0:n], cf[:, BO + oc:BO + oc + 1], X[:, oc, c0:c0 + n],
                           ALU.add, ALU.add, [ptk, Xt[bi], Ct], [Xt[bi]])
            kb.barrier()
        wrelease(21)
        wrelease(22)
        wrelease(23)

    import os
    KST = int(os.environ.get("KSTAGES", "99"))
    if KST >= 1:
        layer0_mixer()
    if KST >= 2:
        cross(0)
    if KST >= 3:
        mlp(0)
    if KST >= 4:
        layer1_mixer()
    if KST >= 5:
        cross(1)
        mlp(1)

    with ExitStack() as sc:
        if KST < 5:
            yo = kb.sb(sc, "yo", [128, 8, 512], F32)
            sq = kb.sb(sc, "sq", [128, 8, 512], BF16)
            rs = kb.sb(sc, "rs", [128, 512], F32)
            yot = Tok()
            yT_v = o_yT.rearrange("(kc p) t -> p kc t", p=128)
            for bi, (c0, n) in enumerate(BLKS):
                rmsnorm(X, Xt[bi], c0, n, 8, yo, yot, 0, sq, rs)
                kb.dma("sp", yT_v[:, :, c0:c0 + n], yo[:, :, 0:n], [yot], [yot])
        sp = kb.eng["sp"]
        deps = []
        for q in kb.rings:
            for s in kb.rings[q]:
                if s.count:
                    deps.append((s, s.count))
        kb.sync(sp, deps)
        kb.barrier()
    es.close()
    return nc


def _slot(w):
    K, N = w.shape
    return np.ascontiguousarray(w.reshape(K // 128, 128, N).transpose(1, 0, 2).reshape(128, -1))


def _pad_slot(a):
    out = np.zeros((128, SLOT), np.float32)
    out[:, :a.shape[1]] = a
    return out


def _const_tables():
    f32 = np.float32
    cbf = np.zeros((128, 1792), f32)
    cbf[:, 0:128] = np.eye(128)
    cbf[:, 128:256] = 1.0 / 1024.0
    cbf[:, 256:384] = 1.0 / 256.0
    cbf[:, 384:512] = 1.0
    cbf[:, 512:576] = 1.0
    cbf[:, 704:768] = 1.0
    j = np.arange(128)[:, None]
    i = np.arange(128)[None, :]
    A = (j <= i).astype(f32)
    cbf[:, 768:896] = A
    cbf[:, 896:1024] = ((j // 8 == i // 8) & (j <= i)).astype(f32)
    cbf[:, 1024:1152] = 1.0 - A
    cbf[:, 1152:1280] = A
    cbf[:, 1408:1536] = A
    cbf[:, 1536:1664] = (j > (i % 8)).astype(f32)
    cbf[:, 1664:1792] = cbf[:, 896:1024]
    cbf[:, 1024:1792] = (cbf[:, 1024:1792] - 1.0) * 30000.0
    bsel = np.zeros((128, 16), f32)
    for b in range(16):
        bsel[b * 8:(b + 1) * 8, b] = 1.0
    pos = np.concatenate([np.arange(TP), np.tile(16384 + np.arange(8), 16)]).astype(np.int32)
    ci = np.concatenate([np.arange(TP) % 128, np.tile(np.arange(8), 16)]).astype(np.float64)
    half = 64
    inv = (f32(10000.0) ** (-np.arange(half, dtype=f32) / f32(half))).astype(f32)
    ang = pos.astype(f32)[:, None] * inv[None, :]
    cs = np.cos(ang).astype(np.float64).T
    sn = np.sin(ang).astype(np.float64).T
    cosd = np.concatenate([cs, cs], 0)
    sind = np.concatenate([-sn, sn], 0)
    rt0 = np.zeros((4, 128, 4, T), f32)
    for h in range(4):
        lg = math.log1p(-2.0 ** (-5.0 - h))
        xi = np.exp((ci + 1.0) * lg)[None, :]
        kk = (1.0 / xi) * (128.0 ** -0.5)
        rt0[h, :, 0] = cosd * xi
        rt0[h, :, 1] = sind * xi
        rt0[h, :, 2] = cosd * kk
        rt0[h, :, 3] = sind * kk
    half = 32
    inv1 = (f32(150000.0) ** (-np.arange(half, dtype=f32) / f32(half))).astype(f32)
    ang1 = pos.astype(f32)[:, None] * inv1[None, :]
    c1 = np.cos(ang1).astype(f32).T
    s1 = np.sin(ang1).astype(f32).T
    rt1 = np.zeros((128, 2, T), f32)
    rt1[:, 0] = np.concatenate([c1, c1, c1, c1], 0)
    rt1[:, 1] = np.concatenate([-s1, s1, -s1, s1], 0)
    return cbf, bsel, rt0, rt1


def _prep_shared(inp):
    f32 = np.float32
    g = lambda k: np.asarray(inp[k], f32)
    w_in = g("w_in_e")[0]
    slots = []
    slots.append(_slot(w_in[:, 0:1024]))
    slots.append(_slot(w_in[:, 1024:2048]))
    w_out = g("w_out_e")[0]
    slots.append(_slot(w_out[0:1024]))
    slots.append(_slot(w_out[1024:2048]))
    sw = np.concatenate([np.arange(64, 128), np.arange(0, 64)])
    for h in range(4):
        q = w_in[:, 2048 + h * 128:2048 + (h + 1) * 128]
        k = w_in[:, 2560 + h * 128:2560 + (h + 1) * 128]
        v = w_in[:, 3072 + h * 256:3072 + (h + 1) * 256]
        gt = w_in[:, 4096 + h * 256:4096 + (h + 1) * 256]
        slots.append(_slot(np.concatenate([q, q[:, sw], k, k[:, sw], v, gt], 1)))
    def cross_slots(l):
        return [_slot(g("w_mk")[l]), _slot(g("w_mv")[l]), _slot(g("w_mq")[l]), _slot(g("w_mo")[l])]
    def mlp_slots(l):
        up, dn = g("w_up")[l], g("w_down")[l]
        return [np.concatenate([_slot(up[:, fb * 512:(fb + 1) * 512]), _slot(dn[fb * 512:(fb + 1) * 512, :])], 1)
                for fb in range(8)]
    slots += cross_slots(0) + mlp_slots(0)
    wqkv = g("w_qkv_o")[0]
    bqkv = g("b_qkv_o")[0]
    sw64 = np.concatenate([np.arange(32, 64), np.arange(0, 32)])
    qcols = np.concatenate([np.arange(h * 64, (h + 1) * 64) for h in PERM_HEADS])
    qscols = np.concatenate([h * 64 + sw64 for h in PERM_HEADS])
    kcols = 1024 + np.arange(256)
    kscols = 1024 + np.concatenate([h * 64 + sw64 for h in range(4)])
    vcols = 1280 + np.arange(256)
    slots.append(_pad_slot(_slot(wqkv[:, np.concatenate([kcols, kscols, vcols])])))
    slots.append(_slot(wqkv[:, qcols]))
    slots.append(_slot(wqkv[:, qscols]))
    slots.append(_slot(g("w_out_o")[0][qcols, :]))
    slots += cross_slots(1)
    slots += mlp_slots(1)
    assert len(slots) == 36
    wslots = np.stack(slots, 0)
    cf = np.zeros((128, 109), f32)
    cf[:, 108] = EPS
    gl = [g("g_mix")[0], g("g_mix")[1], g("g_cross")[0], g("g_cross")[1], g("g_mem")[0], g("g_mem")[1],
          g("g_ffn")[0], g("g_ffn")[1], g("g_final")]
    for i, v in enumerate(gl):
        cf[:, i * 8:(i + 1) * 8] = v.reshape(8, 128).T
    cf[:, 72:80] = bqkv[qcols].reshape(8, 128).T
    cf[:, 80:88] = bqkv[qscols].reshape(8, 128).T
    cf[:, 88:90] = bqkv[kcols].reshape(2, 128).T
    cf[:, 90:92] = bqkv[kscols].reshape(2, 128).T
    cf[:, 92:100] = g("b_out_o")[0].reshape(8, 128).T
    sk = g("sinks")[0]
    for c in range(8):
        cf[0:64, 100 + c] = sk[PERM_HEADS[2 * c]]
        cf[64:128, 100 + c] = sk[PERM_HEADS[2 * c + 1]]
    crow = np.zeros((1, 1408), f32)
    bs = g("b_spatial")[0]
    crow[0, 0:512] = bs.reshape(-1)
    crow[0, 512:1024] = np.tile(bs[:, 0:8], (1, 16)).reshape(-1)
    crow[0, 1024:1280] = bqkv[vcols]
    crow[0, 1280:1408] = 1.0
    lngb = np.stack([np.broadcast_to(g("sgu_ln_g")[0], (128, 1024)), np.broadcast_to(g("sgu_ln_b")[0], (128, 1024))], 0)
    ws = g("w_spatial")[0]
    wst = np.zeros((128, 8, 128), f32)
    for gg in range(4):
        wst[:, gg, :] = ws[gg].T
        for b in range(16):
            wst[b * 8:(b + 1) * 8, 4 + gg, b * 8:(b + 1) * 8] = ws[gg, 0:8, 0:8].T
    return wslots, cf, crow, np.ascontiguousarray(lngb), wst


def make_in_maps(inp, cores=range(NCORES)):
    f32 = np.float32
    cbf, bsel, rt0, rt1 = _const_tables()
    wslots, cf, crow, lngb, wst = _prep_shared(inp)
    in_maps = []
    for c in cores:
        bs = slice(16 * c, 16 * c + 16)
        xs = np.asarray(inp["x_sample"][bs], f32).reshape(128, D)
        xT = np.ascontiguousarray(np.concatenate([np.asarray(inp["x_prompt"][c], f32), xs], 0).T)
        cmk = np.asarray(inp["cache_mem_k"][:, bs], f32).reshape(2, 16, 256, D)
        cmv = np.asarray(inp["cache_mem_v"][:, bs], f32).reshape(2, 16, 256, D)
        cwk = np.asarray(inp["cache_win_k"][0, bs], f32).reshape(16, 128, 256)
        cwv = np.asarray(inp["cache_win_v"][0, bs], f32).reshape(16, 128, 256)
        cwkT = np.ascontiguousarray(cwk.reshape(16, 128, 2, 128).transpose(0, 3, 2, 1))
        in_maps.append({
            "xT": xT,
            "memT": np.ascontiguousarray(np.asarray(inp["mem_prompt"][c], f32).T),
            "wslots": wslots,
            "cbf": cbf, "cf": cf, "crow": crow, "lngb": lngb, "wst": wst, "rt0": rt0, "rt1": rt1, "bsel": bsel,
            "state": np.ascontiguousarray(np.asarray(inp["state_ret"][0, bs], f32)),
            "cmkT": np.ascontiguousarray(cmk.transpose(0, 1, 3, 2)),
            "cmv": np.ascontiguousarray(cmv),
            "cwkT": cwkT, "cwv": np.ascontiguousarray(cwv),
            "cwk_raw": np.ascontiguousarray(cwk), "cwv_raw": np.ascontiguousarray(cwv),
        })
    return in_maps


def kernel(**inp):
    f32 = np.float32
    nc = build_program()
    in_maps = make_in_maps(inp)
    res = run_bass_kernel_spmd(nc, in_maps, core_ids=list(range(NCORES)))
    R = res.results
    y_p = np.stack([R[c]["yT"][:, :TP].T for c in range(NCORES)], 0).astype(f32)
    y_s = np.concatenate([R[c]["yT"][:, TP:].T.reshape(16, 8, D) for c in range(NCORES)], 0).astype(f32)
    mem_k = np.stack([R[c]["memk"] for c in range(NCORES)], 1).reshape(2, 8, 256, 4, 256).astype(f32)
    mem_v = np.stack([R[c]["memv"] for c in range(NCORES)], 1).reshape(2, 8, 256, 4, 256).astype(f32)
    ret_p = np.stack([R[c]["retp"] for c in range(NCORES)], 0)[None].astype(f32)
    ret_s = np.concatenate([R[c]["rets"] for c in range(NCORES)], 0)[None].astype(f32)
    sgu_v = np.concatenate([R[c]["sguv"].reshape(16, 8, 4, 256) for c in range(NCORES)], 0)[None].astype(f32)

    def kT_to_tm(a):
        return a.reshape(2, 64, 2, 128).transpose(3, 2, 0, 1).reshape(128, 4, 64)

    wk_p = np.stack([kT_to_tm(R[c]["wkT"][:, 0, :].reshape(128, 2, 128)) for c in range(NCORES)], 0)[None].astype(f32)
    wv_p = np.stack([R[c]["wvp"].reshape(128, 4, 64) for c in range(NCORES)], 0)[None].astype(f32)
    wk_s_l, wv_s_l = [], []
    for c in range(NCORES):
        knew = kT_to_tm(R[c]["wkT"][:, 1, :].reshape(128, 2, 128)).reshape(16, 8, 4, 64)
        vnew = R[c]["wvs"].reshape(16, 8, 4, 64)
        wk_s_l.append(np.concatenate([R[c]["wk_old"].reshape(16, 120, 4, 64), knew], 1))
        wv_s_l.append(np.concatenate([R[c]["wv_old"].reshape(16, 120, 4, 64), vnew], 1))
    wk_s = np.concatenate(wk_s_l, 0)[None].astype(f32)
    wv_s = np.concatenate(wv_s_l, 0)[None].astype(f32)
    return (y_p, y_s, mem_k, mem_v, ret_p, ret_s, sgu_v, wk_p, wv_p, wk_s, wv_s)
```

```python
import math
from contextlib import ExitStack
import numpy as np
import concourse.bass as bass
import concourse.mybir as mybir
from concourse.bass_utils import run_bass_kernel_spmd

F32 = mybir.dt.float32
BF16 = mybir.dt.bfloat16
AF = mybir.ActivationFunctionType
ALU = mybir.AluOpType

NCORES = 8
D = 1024
TP = 2048
TS = 128
T = TP + TS
BLKS = [(0, 512), (512, 512), (1024, 512), (1536, 512), (2048, 128)]
BLK2 = [(i * 256, 256) for i in range(8)] + [(2048, 128)]
EPS = 1e-6
SLOT = 8192
NRING = 3
LIM = 8000
PERM_HEADS = []
for _c in range(8):
    PERM_HEADS += ([_c, 4 + _c] if _c < 4 else [4 + _c, 8 + _c])


class Tok:
    __slots__ = ("w", "r", "const")

    def __init__(self, const=False):
        self.w = None
        self.r = {}
        self.const = const


class SemC:
    def __init__(self, h, owner=None, base=0):
        self.h = h
        self.count = 0
        self.owner = owner
        self.base = base


class Eng:
    def __init__(self, name, h):
        self.name = name
        self.h = h
        self.sems = []
        self.n = 0
        self.seen = {}


class KB:
    def __init__(self, nc):
        self.nc = nc
        self.es = ExitStack()
        self.eng = {n: Eng(n, h) for n, h in [("pe", nc.tensor), ("act", nc.scalar), ("dve", nc.vector),
                                               ("pool", nc.gpsimd), ("sp", nc.sync)]}
        self.nsem = 0
        self.rings = {q: [self.newsem() for _ in range(16)] for q in ("sp", "pool")}
        self.ri = {"sp": 0, "pool": 0}
        self.bar = self.newsem()
        self.psum = []
        self.pi = 0
        for i in range(8):
            t = self.es.enter_context(nc.psum_tensor(f"ps{i}", [128, 512], F32))
            self.psum.append((t, Tok()))

    def newsem(self, owner=None, base=0):
        self.nsem += 1
        h = self.es.enter_context(self.nc.semaphore(f"s{self.nsem}"))
        return SemC(h, owner, base)

    def sb(self, scope, name, shape, dt):
        self.nsb = getattr(self, "nsb", 0) + 1
        return scope.enter_context(self.nc.sbuf_tensor(f"sb{self.nsb}_{name}", shape, dt))

    def ps(self, hold=False):
        held = getattr(self, "held", None)
        if held is None:
            held = self.held = set()
        assert len(held) < 8, "all PSUM banks held"
        while (self.pi % 8) in held:
            self.pi += 1
        i = self.pi % 8
        if hold:
            held.add(i)
        self.pi += 1
        return self.psum[i]

    def ps_release_all(self):
        self.held = set()

    def pfree(self, t):
        for i, (tt_, _) in enumerate(self.psum):
            if tt_ is t:
                self.held.discard(i)

    def pipeline(self, gens, depth):
        gens = list(gens)
        active = []
        nxt = 0
        while nxt < len(gens) or active:
            while len(active) < depth and nxt < len(gens):
                active.append(gens[nxt])
                nxt += 1
            for g in list(active):
                try:
                    next(g)
                except StopIteration:
                    active.remove(g)

    def tick(self, e):
        ep = e.n // LIM
        if ep >= len(e.sems):
            e.sems.append(self.newsem(owner=e, base=ep * LIM))
        s = e.sems[ep]
        v = e.n % LIM + 1
        e.n += 1
        return s, v

    def sync(self, e, deps):
        for d in deps:
            if d is None:
                continue
            s, v = d
            if s.owner is e:
                if e.name == "pe":
                    continue
                if s.base + v < e.n - 1:
                    continue
            if e.seen.get(s, 0) >= v:
                continue
            e.h.wait_ge(s.h, v)
            e.seen[s] = v

    def _deps(self, reads, writes):
        deps = []
        for t in reads:
            deps.append(t.w)
        for t in writes:
            deps.append(t.w)
            deps.extend(t.r.items())
        return deps

    def _update(self, d, reads, writes):
        for t in reads:
            if not t.const:
                if t.r.get(d[0], 0) < d[1]:
                    t.r[d[0]] = d[1]
        for t in writes:
            t.w = d
            t.r = {}

    def op(self, en, reads, writes, fn):
        e = self.eng[en]
        self.sync(e, self._deps(reads, writes))
        ins = fn(e.h)
        d = self.tick(e)
        ins.then_inc(d[0].h, 1)
        self._update(d, reads, writes)

    def dma(self, q, out, in_, reads, writes):
        e = self.eng[q]
        ring = self.rings[q]
        s = ring[self.ri[q] % len(ring)]
        self.ri[q] += 1
        deps = self._deps(reads, writes)
        if s.count:
            deps.append((s, s.count))
        self.sync(e, deps)
        e.h.dma_start(out=out, in_=in_).then_inc(s.h, 16)
        s.count += 16
        self._update((s, s.count), reads, writes)

    def barrier(self):
        sp = self.eng["sp"]
        deps = []
        for q in self.rings:
            for s in self.rings[q]:
                if s.count:
                    deps.append((s, s.count))
        for n in ("pe", "act", "dve"):
            e = self.eng[n]
            if e.n:
                s = e.sems[(e.n - 1) // LIM]
                deps.append((s, (e.n - 1) % LIM + 1))
        self.sync(sp, deps)
        sp.h.sem_inc(self.bar.h, 1)
        self.bar.count += 1
        for n in ("pe", "act", "dve", "pool"):
            self.sync(self.eng[n], [(self.bar, self.bar.count)])

    def mm(self, out, pairs, reads, writes):
        def fn(h):
            ins = None
            n = len(pairs)
            for i, (l, r) in enumerate(pairs):
                ins = h.matmul(out, l, r, start=(i == 0), stop=(i == n - 1))
            return ins
        self.op("pe", reads, writes, fn)

    def act(self, out, in_, func, reads, writes, **kw):
        self.op("act", reads, writes, lambda h: h.activation(out=out, in_=in_, func=func, **kw))

    def tt(self, out, a, b, op, reads, writes, en="dve"):
        self.op(en, reads, writes, lambda h: h.tensor_tensor(out=out, in0=a, in1=b, op=op))

    def ts(self, out, a, s1, s2, op0, op1, reads, writes, en="dve"):
        self.op(en, reads, writes,
                lambda h: h.tensor_scalar(out=out, in0=a, scalar1=s1, scalar2=s2, op0=op0, op1=op1))

    def stt(self, out, a, s, b, op0, op1, reads, writes, en="dve"):
        self.op(en, reads, writes,
                lambda h: h.scalar_tensor_tensor(out=out, in0=a, scalar=s, in1=b, op0=op0, op1=op1))

    def recip(self, out, in_, reads, writes):
        self.op("dve", reads, writes, lambda h: h.reciprocal(out=out, in_=in_))

    def copy(self, out, in_, reads, writes, en="dve"):
        if en == "act":
            self.act(out, in_, AF.Copy, reads, writes)
        else:
            self.op(en, reads, writes, lambda h: h.tensor_copy(out=out, in_=in_))


def build_program():
    nc = bass.Bass("TRN2", target_bir_lowering=False)
    kb = KB(nc)
    es = kb.es

    def din(name, shape):
        return nc.dram_tensor(name, list(shape), F32, kind="ExternalInput").ap()

    def dout(name, shape):
        return nc.dram_tensor(name, list(shape), F32, kind="ExternalOutput").ap()

    d_xT = din("xT", [D, T])
    d_memT = din("memT", [D, 256])
    d_w = din("wslots", [36, 128, SLOT])
    d_cbf = din("cbf", [128, 1792])
    d_cf = din("cf", [128, 109])
    d_crow = din("crow", [1, 1408])
    d_ln = din("lngb", [2, 128, 1024])
    d_wst = din("wst", [128, 8, 128])
    d_rt0 = din("rt0", [4, 128, 4, T])
    d_rt1 = din("rt1", [128, 2, T])
    d_state = din("state", [16, 4, 128, 256])
    d_cmkT = din("cmkT", [2, 16, D, 256])
    d_cmv = din("cmv", [2, 16, 256, D])
    d_cwkT = din("cwkT", [16, 128, 2, 128])
    d_cwv = din("cwv", [16, 128, 256])
    d_cwk_raw = din("cwk_raw", [16, 128, 256])
    d_cwv_raw = din("cwv_raw", [16, 128, 256])

    o_yT = dout("yT", [D, T])
    o_memk = dout("memk", [2, 256, D])
    o_memv = dout("memv", [2, 256, D])
    o_retp = dout("retp", [4, 128, 256])
    o_rets = dout("rets", [16, 4, 128, 256])
    o_sguv = dout("sguv", [128, 1024])
    o_wkT = dout("wkT", [128, 2, 256])
    o_wvp = dout("wvp", [128, 256])
    o_wvs = dout("wvs", [128, 256])
    o_wk_old = dout("wk_old", [16, 120, 256])
    o_wv_old = dout("wv_old", [16, 120, 256])
    import os
    DBG = bool(os.environ.get("KDBG"))
    if DBG:
        o_dbg = dout("dbg", [8, 128, 512])

    def dbgdump(i, ap, n, toks):
        if DBG:
            kb.dma("pool", o_dbg[i, :, 0:n], ap, toks, [])

    X = kb.sb(es, "X", [128, 8, T], F32)
    H = kb.sb(es, "H", [128, 8, T], BF16)
    WR = [kb.sb(es, f"WR{i}", [128, SLOT], BF16) for i in range(NRING)]
    cbf = kb.sb(es, "cbf", [128, 1792], BF16)
    cf = kb.sb(es, "cf", [128, 109], F32)
    crow = kb.sb(es, "crow", [1, 1408], BF16)
    Xt = [Tok() for _ in BLKS]
    Ht = [Tok() for _ in BLKS]
    WRt = [Tok() for _ in range(NRING)]
    Ct = Tok(const=True)
    sqt, rst = Tok(), Tok()

    ident = cbf[:, 0:128]
    o1024 = cbf[:, 128:256]
    o256 = cbf[:, 256:384]
    o1 = cbf[:, 384:512]
    olo = cbf[:, 512:640]
    ohi = cbf[:, 640:768]
    M1 = cbf[:, 768:896]
    M2 = cbf[:, 896:1024]
    mP = cbf[:, 1024:1280]
    mP0 = cbf[:, 1280:1536]
    mS = cbf[:, 1536:1792]

    def gv(i, kc):
        return cf[:, i * 8 + kc:i * 8 + kc + 1]
    BQ, BQS, BK, BKS, BO, ESK, EPSC = 72, 80, 88, 90, 92, 100, 108
    bsP = crow[0:1, 0:512]
    bsS = crow[0:1, 512:1024]
    bvrow = crow[0:1, 1024:1280]
    onerow = crow[0:1, 1280:1408]

    xT_v = d_xT.rearrange("(kc p) t -> p kc t", p=128)
    for bi, (c0, n) in enumerate(BLKS):
        kb.dma("sp", X[:, :, c0:c0 + n], xT_v[:, :, c0:c0 + n], [], [Xt[bi]])
    cft = Tok()
    kb.dma("sp", cf[:], d_cf, [], [cft])
    kb.dma("pool", cbf[:], d_cbf, [], [Ct])
    kb.dma("pool", crow[:], d_crow, [], [Ct])
    bsel = kb.sb(es, "bsel", [128, 16], BF16)
    d_bsel = din("bsel", [128, 16])
    kb.dma("pool", bsel[:], d_bsel, [], [Ct])
    kb.act(cf[:, ESK:ESK + 8], cf[:, ESK:ESK + 8], AF.Exp, [cft], [cft])
    kb.barrier()
    Ct.w = None

    def window_passthrough(scope):
        pas = kb.sb(scope, "pas", [120, 16, 256], F32)
        past = Tok()
        for src, dst in ((d_cwk_raw, o_wk_old), (d_cwv_raw, o_wv_old)):
            kb.dma("sp", pas[:], src[:, 8:128, :].rearrange("b s e -> s b e"), [], [past])
            kb.dma("sp", dst.rearrange("b s e -> s b e"), pas[:], [past], [past])

    wstate = {"next": 0, "free": list(range(NRING)), "loaded": {}}

    def wprefetch():
        while wstate["free"] and wstate["next"] < 36:
            k = wstate["next"]
            ph = wstate["free"].pop(0)
            kb.dma("pool", WR[ph][:], d_w[k], [], [WRt[ph]])
            wstate["loaded"][k] = ph
            wstate["next"] += 1

    def wuse(k):
        while k not in wstate["loaded"]:
            assert wstate["free"], "weight ring exhausted"
            wprefetch()
        ph = wstate["loaded"][k]
        return WR[ph], WRt[ph]

    def wrelease(k):
        ph = wstate["loaded"].pop(k)
        wstate["free"].append(ph)
        wprefetch()

    def w3(W, off, kc, n):
        return W[:, off:off + kc * n].rearrange("p (k n) -> p k n", k=kc)

    def rmsnorm(Xs, Xtok, c0, n, gi, Hs, Htok, hc0, sq, rs):
        kb.act(sq[:, :, 0:n], Xs[:, :, c0:c0 + n], AF.Square, [Xtok], [sqt])
        pt, ptk = kb.ps()
        kb.mm(pt[:, 0:n], [(o1024, sq[:, kc, 0:n]) for kc in range(8)], [sqt, Ct], [ptk])
        kb.act(rs[:, 0:n], pt[:, 0:n], AF.Ln, [ptk, Ct], [rst], bias=cf[:, EPSC:EPSC + 1], scale=1.0)
        kb.act(rs[:, 0:n], rs[:, 0:n], AF.Exp, [rst], [rst], scale=-0.5)
        for kc in range(8):
            kb.stt(Hs[:, kc, hc0:hc0 + n], Xs[:, kc, c0:c0 + n], gv(gi, kc), rs[:, 0:n],
                   ALU.mult, ALU.mult, [Xtok, rst, Ct], [Htok])

    def norm_all(gi):
        with ExitStack() as scn:
            sq = kb.sb(scn, "sq", [128, 8, 512], BF16)
            rs = kb.sb(scn, "rs", [128, 512], F32)
            for bi, (c0, n) in enumerate(BLKS):
                rmsnorm(X, Xt[bi], c0, n, gi, H, Ht[bi], c0, sq, rs)
            kb.barrier()

    def xadd(oc, c0, n, pt, ptk, bi):
        kb.tt(X[:, oc, c0:c0 + n], X[:, oc, c0:c0 + n], pt[:, 0:n], ALU.add, [ptk, Xt[bi]], [Xt[bi]])

    def layer0_mixer():
        wprefetch()
        norm_all(0)
        with ExitStack() as sc:
            lng = kb.sb(sc, "lng", [128, 1024], F32)
            lnb = kb.sb(sc, "lnb", [128, 1024], F32)
            wst = kb.sb(sc, "wst", [128, 8, 128], BF16)
            GU = kb.sb(sc, "GU", [128, 8, 512], BF16)
            AO = kb.sb(sc, "AO", [128, 8, 512], BF16)
            zv2 = [kb.sb(sc, f"zv{i}", [128, 1024], F32) for i in range(2)]
            vnb2 = [kb.sb(sc, f"vnb{i}", [128, 1024], BF16) for i in range(2)]
            st62 = [kb.sb(sc, f"st6{i}", [128, 2, 6], F32) for i in range(2)]
            mv2 = [kb.sb(sc, f"mv{i}", [128, 4], F32) for i in range(2)]
            zvt2, vnbt2, stt2, mvt2 = [[Tok(), Tok()] for _ in range(4)]
            lt, wstt, GUt, AOt = [Tok() for _ in range(4)]
            kb.dma("sp", lng[:], d_ln[0], [], [lt])
            kb.dma("sp", lnb[:], d_ln[1], [], [lt])
            kb.dma("pool", wst[:], d_wst, [], [wstt])
            for g in range(4):
                kb.tt(wst[:, g, :], wst[:, g, :], M1, ALU.mult, [wstt, Ct], [wstt])
                kb.tt(wst[:, 4 + g, :], wst[:, 4 + g, :], M2, ALU.mult, [wstt, Ct], [wstt])
            Wu, Wut = wuse(0)
            Wv, Wvt = wuse(1)
            Wo, Wot = wuse(2)
            Wu3, Wv3, Wo3 = w3(Wu, 0, 8, 1024), w3(Wv, 0, 8, 1024), w3(Wo, 0, 8, 1024)
            def uproj(bi):
                c0, n = BLKS[bi]
                for oc in range(8):
                    pt, ptk = kb.ps()
                    kb.mm(pt[:, 0:n], [(Wu3[:, kc, oc * 128:(oc + 1) * 128], H[:, kc, c0:c0 + n]) for kc in range(8)],
                          [Wut, Ht[bi]], [ptk])
                    kb.act(GU[:, oc, 0:n], pt[:, 0:n], AF.Gelu_apprx_tanh, [ptk], [GUt])

            uproj(0)
            for bi, (c0, n) in enumerate(BLKS):
                smp = bi == 4
                def sgu_chunk(bi, c0, ch, smp):
                    t0 = c0 + ch * 128
                    db = ch % 2
                    zvb, zvtb, vnbb, vnbtb = zv2[db], zvt2[db], vnb2[db], vnbt2[db]
                    st6b, sttb, mvb, mvtb = st62[db], stt2[db], mv2[db], mvt2[db]
                    pv = []
                    for hf in range(2):
                        pt, ptk = kb.ps(hold=True)
                        pv.append((pt, ptk))
                        kb.mm(pt[:, :], [(H[:, kc, t0:t0 + 128], Wv3[:, kc, hf * 512:(hf + 1) * 512]) for kc in range(8)],
                              [Wvt, Ht[bi]], [ptk])
                    yield
                    for hf in range(2):
                        pt, ptk = pv[hf]
                        kb.act(zvb[:, hf * 512:(hf + 1) * 512], pt[:, :], AF.Gelu_apprx_tanh, [ptk], [zvtb])
                        kb.pfree(pt)
                    for hf in range(2):
                        kb.op("dve", [zvtb], [sttb],
                              lambda h, hf=hf: h.bn_stats(out=st6b[:, hf, :], in_=zvb[:, hf * 512:(hf + 1) * 512]))
                    kb.op("dve", [sttb], [mvtb],
                          lambda h: h.bn_aggr(out=mvb[:, 0:2], in_=st6b[:].rearrange("p a b -> p (a b)")))
                    kb.act(mvb[:, 2:3], mvb[:, 1:2], AF.Sqrt, [mvtb], [mvtb], bias=EPS, scale=1.0)
                    kb.recip(mvb[:, 2:3], mvb[:, 2:3], [mvtb], [mvtb])
                    kb.stt(mvb[:, 3:4], mvb[:, 0:1], -1.0, mvb[:, 2:3], ALU.mult, ALU.mult, [mvtb], [mvtb])
                    kb.ts(zvb[:], zvb[:], mvb[:, 2:3], mvb[:, 3:4], ALU.mult, ALU.add, [zvtb, mvtb], [zvtb])
                    kb.tt(zvb[:], zvb[:], lng[:], ALU.mult, [zvtb, lt], [zvtb])
                    if smp:
                        kb.tt(zvb[:], zvb[:], lnb[:], ALU.add, [zvtb, lt], [zvtb])
                        kb.dma("sp", o_sguv, zvb[:], [zvtb], [])
                        kb.copy(vnbb[:], zvb[:], [zvtb], [vnbtb])
                    else:
                        kb.tt(vnbb[:], zvb[:], lnb[:], ALU.add, [zvtb, lt], [vnbtb])
                    yield
                    pA, pAt = kb.ps(hold=True)
                    pB, pBt = kb.ps(hold=True)
                    for oc in range(8):
                        g = oc // 2
                        pp, ppt = (pA, pAt) if oc < 4 else (pB, pBt)
                        wsel = wst[:, (4 + g) if smp else g, :]
                        brow = (bsS if smp else bsP)[0:1, g * 128:(g + 1) * 128]
                        kb.mm(pp[:, (oc % 4) * 128:(oc % 4 + 1) * 128],
                              [(vnbb[:, oc * 128:(oc + 1) * 128], wsel), (onerow, brow)],
                              [vnbtb, wstt, Ct], [ppt])
                    yield
                    kb.tt(AO[:, 0:4, ch * 128:(ch + 1) * 128], GU[:, 0:4, ch * 128:(ch + 1) * 128],
                          pA[:, :].rearrange("p (a b) -> p a b", a=4), ALU.mult, [GUt, pAt], [AOt])
                    kb.tt(AO[:, 4:8, ch * 128:(ch + 1) * 128], GU[:, 4:8, ch * 128:(ch + 1) * 128],
                          pB[:, :].rearrange("p (a b) -> p a b", a=4), ALU.mult, [GUt, pBt], [AOt])
                    kb.pfree(pA)
                    kb.pfree(pB)

                kb.pipeline([sgu_chunk(bi, c0, ch, smp) for ch in range(n // 128)], 2)
                if bi + 1 < len(BLKS):
                    uproj(bi + 1)
                    if bi + 1 == len(BLKS) - 1:
                        wrelease(0)
                else:
                    wrelease(1)
                for oc in range(8):
                    pt, ptk = kb.ps()
                    kb.mm(pt[:, 0:n], [(Wo3[:, kc, oc * 128:(oc + 1) * 128], AO[:, kc, 0:n]) for kc in range(8)],
                          [Wot, AOt], [ptk])
                    xadd(oc, c0, n, pt, ptk, bi)
            kb.barrier()
        wrelease(2)
        import os
        if os.environ.get("KSKIPB"):
            for k in range(3, 8):
                wuse(k)
                wrelease(k)
            return
        Wob, Wobt = wuse(3)
        Wob3 = w3(Wob, 0, 8, 1024)
        with ExitStack() as sc:
            tab = kb.sb(sc, "tab", [128, 4, 512], F32)
            QK = kb.sb(sc, "QK", [128, 2, 512], BF16)
            t1 = kb.sb(sc, "t1", [128, 512], F32)
            t2 = kb.sb(sc, "t2", [128, 512], F32)
            SG = kb.sb(sc, "SG", [128, 2, 512], BF16)
            Vt = kb.sb(sc, "Vt", [128, 4, 256], BF16)
            S32 = kb.sb(sc, "S32", [128, 256], F32)
            st32 = kb.sb(sc, "st32", [128, 8, 256], F32)
            stbf = kb.sb(sc, "stbf", [128, 8, 256], BF16)
            Vblk = kb.sb(sc, "Vblk", [128, 8, 256], BF16)
            (tabt, QKt, t1t, t2t, SGt, Vtt, scTt, kTMt, S32t, Sbft, osqt, r2t, BOt, st32t, stbft,
             Vblkt) = [Tok() for _ in range(16)]
            scT4 = kb.sb(sc, "scT4", [128, 4, 128], BF16)
            kTM4 = kb.sb(sc, "kTM4", [128, 4, 128], BF16)
            Sb5 = kb.sb(sc, "Sb5", [128, 5, 256], BF16)
            osq4 = kb.sb(sc, "osq4", [128, 4, 2, 128], BF16)
            osq4t = [Tok() for _ in range(4)]
            osq2 = osq4
            BO2 = kb.sb(sc, "BO2", [128, 2, 2, 512], BF16)
            scT4t, kTM4t = [Tok() for _ in range(4)], [Tok() for _ in range(4)]
            Sb5t = [Tok() for _ in range(5)]
            osq2t, r22t, BO2t = [Tok(), Tok()], [Tok(), Tok()], [Tok(), Tok()]
            r24 = kb.sb(sc, "r24", [128, 512], F32)
            SGr = kb.sb(sc, "SGr", [128, 2, 512], BF16)
            r24t, SGrt = Tok(), Tok()
            scT, kTM, osq, r2, BO = scT4[:, 0, :], kTM4[:, 0, :], osq4[:, 0], r24[:, 0:128], BO2[:, 0]
            scTt, kTMt, osqt, r2t, BOt = scT4t[0], kTM4t[0], osq4t[0], r24t, BO2t[0]
            Sbf, Sbft = Sb5[:, 0, :], Sb5t[0]

            st32q, stbfq, Vblkq = [Tok(), Tok()], [Tok(), Tok()], [Tok(), Tok()]

            def load_state(hd, hq):
                qb2 = hq % 2
                src = d_state[hq * 4:(hq + 1) * 4, hd].rearrange("b p e -> p b e")
                kb.dma("sp", st32[:, qb2 * 4:qb2 * 4 + 4, :], src, [], [st32q[qb2]])
                kb.dma("pool", stbf[:, qb2 * 4:qb2 * 4 + 4, :], src, [], [stbfq[qb2]])

            def outproj(hd, bi, c0, n, BOb, BObt):
                for oc in range(8):
                    pt, ptk = kb.ps()
                    kb.mm(pt[:, 0:n], [(Wob3[:, 2 * hd + ec, oc * 128:(oc + 1) * 128], BOb[:, ec, 0:n]) for ec in range(2)],
                          [Wobt, BObt], [ptk])
                    xadd(oc, c0, n, pt, ptk, bi)

            for hd in range(4):
                lg = math.log1p(-2.0 ** (-5.0 - hd))
                gP = math.exp(128.0 * lg)
                gS = math.exp(8.0 * lg)
                Wh, Wht = wuse(4 + hd)
                Wh3 = w3(Wh, 0, 8, 1024)
                kb.op("dve", [], [S32t], lambda h: h.memset(S32[:], 0.0))
                kb.op("dve", [], [Sb5t[0]], lambda h: h.memset(Sb5[:, 0, :], 0.0))
                pending = None
                load_state(hd, 0)
                load_state(hd, 1)

                def p0_qk(bi, hd=hd, Wh3=Wh3, Wht=Wht):
                    c0, n = BLKS[bi]
                    kb.dma("sp", tab[:, :, 0:n], d_rt0[hd, :, :, c0:c0 + n], [], [tabt])
                    for qk in range(2):
                        pa, pat = kb.ps()
                        pb, pbt = kb.ps()
                        kb.mm(pa[:, 0:n], [(Wh3[:, kc, qk * 256:qk * 256 + 128], H[:, kc, c0:c0 + n]) for kc in range(8)],
                              [Wht, Ht[bi]], [pat])
                        kb.mm(pb[:, 0:n], [(Wh3[:, kc, qk * 256 + 128:qk * 256 + 256], H[:, kc, c0:c0 + n]) for kc in range(8)],
                              [Wht, Ht[bi]], [pbt])
                        kb.tt(t1[:, 0:n], pa[:, 0:n], tab[:, 2 * qk, 0:n], ALU.mult, [pat, tabt], [t1t])
                        kb.tt(t2[:, 0:n], pb[:, 0:n], tab[:, 2 * qk + 1, 0:n], ALU.mult, [pbt, tabt], [t2t])
                        kb.tt(QK[:, qk, 0:n], t1[:, 0:n], t2[:, 0:n], ALU.add, [t1t, t2t], [QKt])

                def p0_v(bi, Wh3=Wh3, Wht=Wht):
                    c0, n = BLKS[bi]
                    for ch in range(n // 128):
                        t0 = c0 + ch * 128
                        pt, ptk = kb.ps()
                        kb.mm(pt[:, 0:256], [(H[:, kc, t0:t0 + 128], Wh3[:, kc, 512:768]) for kc in range(8)],
                              [Wht, Ht[bi]], [ptk])
                        kb.copy(Vt[:, ch, :], pt[:, 0:256], [ptk], [Vtt], en="act")

                for bi, (c0, n) in enumerate(BLKS):
                    smp = bi == 4
                    p0_qk(bi)
                    p0_v(bi)

                    def gate_proj():
                        for ec in range(2):
                            pt, ptk = kb.ps()
                            kb.mm(pt[:, 0:n], [(Wh3[:, kc, 768 + ec * 128:768 + (ec + 1) * 128], H[:, kc, c0:c0 + n]) for kc in range(8)],
                                  [Wht, Ht[bi]], [ptk])
                            kb.act(SG[:, ec, 0:n], pt[:, 0:n], AF.Silu, [ptk], [SGt])

                    if not smp:
                        bb = bi % 2
                        BOb, BObt = BO2[:, bb], BO2t[bb]
                        nch = 4
                        for ch in range(nch):
                            cs = slice(ch * 128, (ch + 1) * 128)
                            pt, ptk = kb.ps()
                            kb.mm(pt[:, 0:128], [(QK[:, 1, cs], QK[:, 0, cs])], [QKt], [ptk])
                            kb.tt(scT4[:, ch, :], pt[:, 0:128], M1, ALU.mult, [ptk, Ct], [scT4t[ch]])
                            pk, pkt = kb.ps()
                            kb.mm(pk[:, 0:128], [(QK[:, 1, cs], ident)], [QKt, Ct], [pkt])
                            kb.copy(kTM4[:, ch, :], pk[:, 0:128], [pkt], [kTM4t[ch]], en="act")
                        gate_proj()
                        puA, puAt = kb.ps(hold=True)
                        puB, puBt = kb.ps(hold=True)
                        PU = [(puA, puAt, 0), (puA, puAt, 256), (puB, puBt, 0), (puB, puBt, 256)]
                        for ch in range(nch):
                            pu, put, o = PU[ch]
                            kb.mm(pu[:, o:o + 256], [(kTM4[:, ch, :], Vt[:, ch, :])], [kTM4t[ch], Vtt], [put])
                        if pending is not None:
                            outproj(*pending)
                            pending = None
                        for ch in range(nch):
                            g = bi * 4 + ch
                            pu, put, o = PU[ch]
                            kb.stt(S32[:], S32[:], gP, pu[:, o:o + 256], ALU.mult, ALU.add, [put, S32t], [S32t])
                            kb.act(Sb5[:, (g + 1) % 5, :], S32[:], AF.Copy, [S32t], [Sb5t[(g + 1) % 5]], scale=gP)
                        kb.pfree(puA)
                        kb.pfree(puB)
                        POs = []
                        for ch in range(nch):
                            g = bi * 4 + ch
                            cs = slice(ch * 128, (ch + 1) * 128)
                            po, pot = kb.ps(hold=True)
                            POs.append((po, pot))
                            for ec in range(2):
                                kb.mm(po[:, ec * 128:(ec + 1) * 128],
                                      [(Vt[:, ch, ec * 128:(ec + 1) * 128], scT4[:, ch, :]),
                                       (Sb5[:, g % 5, ec * 128:(ec + 1) * 128], QK[:, 0, cs])],
                                      [Vtt, scT4t[ch], Sb5t[g % 5], QKt], [pot])
                        for ch in range(nch):
                            po, pot = POs[ch]
                            kb.act(osq4[:, ch].rearrange("p a b -> p (a b)"), po[:, 0:256], AF.Square, [pot], [osq4t[ch]])
                        pn, pnt = kb.ps(hold=True)
                        for ch in range(nch):
                            kb.mm(pn[:, ch * 128:(ch + 1) * 128], [(o256, osq4[:, ch, ec, :]) for ec in range(2)], [osq4t[ch], Ct], [pnt])
                        kb.act(r24[:], pn[:, :], AF.Ln, [pnt], [r24t], bias=cf[:, EPSC:EPSC + 1], scale=1.0)
                        kb.pfree(pn)
                        kb.act(r24[:], r24[:], AF.Exp, [r24t], [r24t], scale=-0.5)
                        kb.tt(SGr[:], SG[:], r24[:].unsqueeze(1).to_broadcast([128, 2, 512]), ALU.mult, [SGt, r24t], [SGrt])
                        for ch in range(nch):
                            po, pot = POs[ch]
                            cs = slice(ch * 128, (ch + 1) * 128)
                            kb.tt(BOb[:, :, cs], po[:, 0:256].rearrange("p (a b) -> p a b", a=2), SGr[:, :, cs], ALU.mult,
                                  [pot, SGrt], [BObt])
                            kb.pfree(po)
                        pending = (hd, bi, c0, n, BOb, BObt)
                        if bi == 3:
                            outproj(*pending)
                            pending = None
                            kb.act(S32[:], S32[:], AF.Copy, [S32t], [S32t], scale=gP)
                            kb.dma("sp", o_retp[hd], S32[:], [S32t], [S32t])
                        continue
                    gate_proj()
                    for ch in range(n // 128):
                        cs = slice(ch * 128, (ch + 1) * 128)
                        pt, ptk = kb.ps()
                        kb.mm(pt[:, 0:128], [(QK[:, 1, cs], QK[:, 0, cs])], [QKt], [ptk])
                        kb.tt(scT[:], pt[:, 0:128], M2 if smp else M1, ALU.mult, [ptk, Ct], [scTt])
                        pk, pkt = kb.ps()
                        kb.mm(pk[:, 0:128], [(QK[:, 1, cs], ident)], [QKt, Ct], [pkt])
                        kb.copy(kTM[:], pk[:, 0:128], [pkt], [kTMt], en="act")
                        if not smp:
                            po, pot = kb.ps()
                            PO = [(po, pot, 0), (po, pot, 128)]
                        else:
                            poA, poAt = kb.ps(hold=True)
                            poB, poBt = kb.ps(hold=True)
                            PO = [(poA, poAt, 0), (poB, poBt, 0)]
                        if not smp:
                            for ec in range(2):
                                kb.mm(po[:, ec * 128:(ec + 1) * 128],
                                      [(Vt[:, ch, ec * 128:(ec + 1) * 128], scT[:]),
                                       (Sbf[:, ec * 128:(ec + 1) * 128], QK[:, 0, cs])],
                                      [Vtt, scTt, Sbft, QKt], [pot])
                        else:
                            for hq in range(4):
                                qb2 = hq % 2
                                s32q = st32[:, qb2 * 4:qb2 * 4 + 4, :]
                                sbfq = stbf[:, qb2 * 4:qb2 * 4 + 4, :]
                                vbq = Vblk[:, qb2 * 4:qb2 * 4 + 4, :]
                                if hq == 0:
                                    for ec in range(2):
                                        kb.mm(PO[ec][0][:, 0:128],
                                              [(Vt[:, ch, ec * 128:(ec + 1) * 128], scT[:])],
                                              [Vtt, scTt], [PO[ec][1]])
                                for b4 in range(4):
                                    b = hq * 4 + b4
                                    for ec in range(2):
                                        kb.op("pe", [stbfq[qb2], QKt], [PO[ec][1]],
                                              lambda h, b=b, b4=b4, ec=ec, PO=PO, sbfq=sbfq: h.matmul(
                                                  PO[ec][0][:, b * 8:b * 8 + 8],
                                                  sbfq[:, b4, ec * 128:(ec + 1) * 128],
                                                  QK[:, 0, b * 8:b * 8 + 8], start=False, stop=True,
                                                  skip_group_check=True))
                                kb.op("dve", [Vtt, Ct], [Vblkq[qb2]],
                                      lambda h, hq=hq, vbq=vbq: h.tensor_tensor(
                                          out=vbq,
                                          in0=Vt[:, ch:ch + 1, :].to_broadcast([128, 4, 256]),
                                          in1=bsel[:, hq * 4:(hq + 1) * 4].unsqueeze(2).to_broadcast([128, 4, 256]), op=ALU.mult))
                                for b2 in range(2):
                                    pu, put = kb.ps()
                                    kb.mm(pu[:, :], [(kTM[:], vbq[:, 2 * b2:2 * b2 + 2, :].rearrange("p a b -> p (a b)"))],
                                          [kTMt, Vblkq[qb2]], [put])
                                    kb.tt(s32q[:, 2 * b2:2 * b2 + 2, :].rearrange("p a b -> p (a b)"),
                                          s32q[:, 2 * b2:2 * b2 + 2, :].rearrange("p a b -> p (a b)"),
                                          pu[:, :], ALU.add, [put, st32q[qb2]], [st32q[qb2]])
                                kb.act(s32q, s32q, AF.Copy, [st32q[qb2]], [st32q[qb2]], scale=gS)
                                kb.dma("sp", o_rets[hq * 4:(hq + 1) * 4, hd].rearrange("b p e -> p b e"), s32q,
                                       [st32q[qb2]], [st32q[qb2]])
                                if hq + 2 < 4:
                                    load_state(hd, hq + 2)
                        if not smp:
                            kb.act(osq[:].rearrange("p a b -> p (a b)"), po[:, 0:256], AF.Square, [pot], [osqt])
                        else:
                            for ec in range(2):
                                kb.act(osq[:, ec, :], PO[ec][0][:, 0:128], AF.Square, [PO[ec][1]], [osqt])
                        pn, pnt = kb.ps()
                        kb.mm(pn[:, 0:128], [(o256, osq[:, ec, :]) for ec in range(2)], [osqt, Ct], [pnt])
                        kb.act(r2[:], pn[:, 0:128], AF.Sqrt, [pnt], [r2t], bias=EPS, scale=1.0)
                        kb.recip(r2[:], r2[:], [r2t], [r2t])
                        for ec in range(2):
                            kb.tt(t1[:, 0:128], PO[ec][0][:, PO[ec][2]:PO[ec][2] + 128], r2[:], ALU.mult, [PO[ec][1], r2t], [t1t])
                            kb.tt(BO[:, ec, cs], t1[:, 0:128], SG[:, ec, cs], ALU.mult, [t1t, SGt], [BOt])
                        kb.ps_release_all()
                        if not smp:
                            pu, put = kb.ps()
                            kb.mm(pu[:, 0:256], [(kTM[:], Vt[:, ch, :])], [kTMt, Vtt], [put])
                            kb.stt(S32[:], S32[:], gP, pu[:, 0:256], ALU.mult, ALU.add, [put, S32t], [S32t])
                            kb.act(Sbf[:], S32[:], AF.Copy, [S32t], [Sbft], scale=gP)
                    if smp and hd == 0:
                        dbgdump(0, QK[:, 0, 0:128], 128, [QKt])
                        dbgdump(1, QK[:, 1, 0:128], 128, [QKt])
                        dbgdump(2, scT[:], 128, [scTt])
                        dbgdump(3, BO[:, 0, 0:128], 128, [BOt])
                        dbgdump(4, BO[:, 1, 0:128], 128, [BOt])
                        dbgdump(5, Vt[:, 0, :], 256, [Vtt])
                        dbgdump(6, SG[:, 0, 0:128], 128, [SGt])
                        dbgdump(7, r2[:], 128, [r2t])
                    for oc in range(8):
                        pt, ptk = kb.ps()
                        kb.mm(pt[:, 0:n], [(Wob3[:, 2 * hd + ec, oc * 128:(oc + 1) * 128], BO[:, ec, 0:n]) for ec in range(2)],
                              [Wobt, BOt], [ptk])
                        xadd(oc, c0, n, pt, ptk, bi)
                wrelease(4 + hd)
            kb.barrier()
        wrelease(3)

    def cross(l):
        s_mk, s_mv, s_mq, s_mo = (8, 9, 10, 11) if l == 0 else (24, 25, 26, 27)
        norm_all(2 + l)
        with ExitStack() as sc:
            KT = kb.sb(sc, "KT", [128, 8, 256], BF16)
            Vm = kb.sb(sc, "Vm", [128, 2, 1024], BF16)
            KTt, Vmt = Tok(), Tok()
            with ExitStack() as sc1:
                MT = kb.sb(sc1, "MT", [128, 8, 256], F32)
                Mh = kb.sb(sc1, "Mh", [128, 8, 256], BF16)
                kvo = kb.sb(sc1, "kvo", [128, 1024], F32)
                MTt, Mht, kvot = Tok(), Tok(), Tok()
                kb.dma("sp", MT[:], d_memT.rearrange("(kc p) m -> p kc m", p=128), [], [MTt])
                sqm = kb.sb(sc1, "sqm", [128, 8, 256], BF16)
                rsm = kb.sb(sc1, "rsm", [128, 256], F32)
                rmsnorm(MT, MTt, 0, 256, 4 + l, Mh, Mht, 0, sqm, rsm)
                Wk, Wkt = wuse(s_mk)
                Wk3 = w3(Wk, 0, 8, 1024)
                for oc in range(8):
                    pt, ptk = kb.ps()
                    kb.mm(pt[:, 0:256], [(Wk3[:, kc, oc * 128:(oc + 1) * 128], Mh[:, kc, :]) for kc in range(8)],
                          [Wkt, Mht], [ptk])
                    kb.copy(KT[:, oc, :], pt[:, 0:256], [ptk], [KTt], en="act")
                for which, (sl, dst) in enumerate([(s_mk, o_memk), (s_mv, o_memv)]):
                    Wx, Wxt = wuse(sl)
                    Wx3 = w3(Wx, 0, 8, 1024)
                    for mc in range(2):
                        for hf in range(2):
                            pt, ptk = kb.ps()
                            kb.mm(pt[:, :], [(Mh[:, kc, mc * 128:(mc + 1) * 128], Wx3[:, kc, hf * 512:(hf + 1) * 512]) for kc in range(8)],
                                  [Wxt, Mht], [ptk])
                            kb.copy(kvo[:, hf * 512:(hf + 1) * 512], pt[:, :], [ptk], [kvot], en="act")
                            if which == 1:
                                kb.copy(Vm[:, mc, hf * 512:(hf + 1) * 512], kvo[:, hf * 512:(hf + 1) * 512], [kvot], [Vmt], en="dve")
                        kb.dma("sp", dst[l, mc * 128:(mc + 1) * 128, :], kvo[:], [kvot], [kvot])
                    wrelease(sl)
                kb.barrier()
            with ExitStack() as sc2:
                QTh = kb.sb(sc2, "QTh", [128, 2, 2, 512], BF16)
                E = kb.sb(sc2, "E", [128, 2, 2, 512], BF16)
                rden = kb.sb(sc2, "rden", [128, 512], F32)
                AT = kb.sb(sc2, "AT", [128, 8, 512], BF16)
                KbT = kb.sb(sc2, "KbT", [128, 2, 8, 256], BF16)
                Vb = kb.sb(sc2, "Vb", [128, 2, 2, 1024], BF16)
                QTs = kb.sb(sc2, "QTs", [128, 8, 128], BF16)
                Eb = kb.sb(sc2, "Eb", [128, 64], BF16)
                rd = kb.sb(sc2, "rd", [128, 32], F32)
                QTht, Et = [Tok(), Tok()], [Tok(), Tok()]
                rdent, ATt, QTst, Ebt, rdt = [Tok() for _ in range(5)]
                KbTt, Vbt = [Tok(), Tok()], [Tok(), Tok()]
                Wq, Wqt = wuse(s_mq)
                Wo, Wot = wuse(s_mo)
                Wq3, Wo3 = w3(Wq, 0, 8, 1024), w3(Wo, 0, 8, 1024)
                ATs = kb.sb(sc2, "ATs", [128, 8, 128], BF16)
                ATst = Tok()

                def outproj(ATx, ATxt, bi, c0, n):
                    for oc in range(8):
                        pt, ptk = kb.ps()
                        kb.mm(pt[:, 0:n], [(Wo3[:, kc, oc * 128:(oc + 1) * 128], ATx[:, kc, 0:n]) for kc in range(8)],
                              [Wot, ATxt], [ptk])
                        xadd(oc, c0, n, pt, ptk, bi)

                def load_batch(b):
                    pb = b % 2
                    kb.dma("pool", KbT[:, pb], d_cmkT[l, b].rearrange("(kc p) m -> p kc m", p=128), [], [KbTt[pb]])
                    kb.dma("pool", Vb[:, pb], d_cmv[l, b].rearrange("(mc p) e -> p mc e", p=128), [], [Vbt[pb]])

                def head_A(bi, c0, n, hd):
                    hb = hd % 2
                    for dc in range(2):
                        pt, ptk = kb.ps()
                        kb.mm(pt[:, 0:n], [(Wq3[:, kc, (2 * hd + dc) * 128:(2 * hd + dc + 1) * 128], H[:, kc, c0:c0 + n]) for kc in range(8)],
                              [Wqt, Ht[bi]], [ptk])
                        kb.copy(QTh[:, hb, dc, 0:n], pt[:, 0:n], [ptk], [QTht[hb]], en="act")

                def head_B(bi, c0, n, hd):
                    hb = hd % 2
                    for mc in range(2):
                        pt, ptk = kb.ps()
                        kb.mm(pt[:, 0:n], [(KT[:, 2 * hd + dc, mc * 128:(mc + 1) * 128], QTh[:, hb, dc, 0:n]) for dc in range(2)],
                              [KTt, QTht[hb]], [ptk])
                        kb.act(E[:, hb, mc, 0:n], pt[:, 0:n], AF.Exp, [ptk], [Et[hb]], scale=1.0 / 16.0)

                def head_C(bi, c0, n, hd):
                    hb = hd % 2
                    pd, pdt = kb.ps()
                    kb.mm(pd[:, 0:n], [(o1, E[:, hb, mc, 0:n]) for mc in range(2)], [Et[hb], Ct], [pdt])
                    kb.act(rden[:, 0:n], pd[:, 0:n], AF.Ln, [pdt], [rdent])
                    kb.act(rden[:, 0:n], rden[:, 0:n], AF.Exp, [rdent], [rdent], scale=-1.0)
                    for ec in range(2):
                        pt, ptk = kb.ps()
                        kb.mm(pt[:, 0:n], [(Vm[:, mc, hd * 256 + ec * 128:hd * 256 + (ec + 1) * 128], E[:, hb, mc, 0:n]) for mc in range(2)],
                              [Vmt, Et[hb]], [ptk])
                        kb.tt(AT[:, 2 * hd + ec, 0:n], pt[:, 0:n], rden[:, 0:n], ALU.mult, [ptk, rdent], [ATt])

                def sample_S1(b):
                    pb = b % 2
                    pS, pSt = kb.ps()
                    for hd in range(4):
                        for mc in range(2):
                            r0 = (hd * 2 + mc) * 8
                            kb.mm(pS[:, r0:r0 + 8],
                                  [(KbT[:, pb, 2 * hd + dc, mc * 128:(mc + 1) * 128], QTs[:, 2 * hd + dc, b * 8:b * 8 + 8]) for dc in range(2)],
                                  [KbTt[pb], QTst], [pSt])
                    kb.act(Eb[:], pS[:, 0:64], AF.Exp, [pSt], [Ebt], scale=1.0 / 16.0)

                def sample_S2(b):
                    pb = b % 2
                    pO, pOt = kb.ps()
                    for hd in range(4):
                        for ec in range(2):
                            r0 = (hd * 2 + ec) * 8
                            kb.mm(pO[:, r0:r0 + 8],
                                  [(Vb[:, pb, mc, hd * 256 + ec * 128:hd * 256 + (ec + 1) * 128], Eb[:, (hd * 2 + mc) * 8:(hd * 2 + mc) * 8 + 8]) for mc in range(2)],
                                  [Vbt[pb], Ebt], [pOt])
                        kb.mm(pO[:, 64 + hd * 8:64 + hd * 8 + 8],
                              [(o1, Eb[:, (hd * 2 + mc) * 8:(hd * 2 + mc) * 8 + 8]) for mc in range(2)],
                              [Ebt, Ct], [pOt])
                    kb.act(rd[:], pO[:, 64:96], AF.Ln, [pOt], [rdt])
                    kb.act(rd[:], rd[:], AF.Exp, [rdt], [rdt], scale=-1.0)
                    for ec in range(2):
                        kb.tt(ATs[:, :, b * 8:b * 8 + 8].rearrange("p (h e) c -> p h e c", e=2)[:, :, ec, :],
                              pO[:, 0:64].rearrange("p (h e c) -> p h e c", e=2, c=8)[:, :, ec, :],
                              rd[:].rearrange("p (h c) -> p h c", c=8), ALU.mult, [pOt, rdt], [ATst])

                sc0, sn = BLKS[4]
                for oc in range(8):
                    pt, ptk = kb.ps()
                    kb.mm(pt[:, 0:sn], [(Wq3[:, kc, oc * 128:(oc + 1) * 128], H[:, kc, sc0:sc0 + sn]) for kc in range(8)],
                          [Wqt, Ht[4]], [ptk])
                    kb.copy(QTs[:, oc, :], pt[:, 0:sn], [ptk], [QTst], en="act")
                load_batch(0)
                load_batch(1)
                def unit(u):
                    bi, hd = u // 4, u % 4
                    c0, n = BLKS[bi]
                    return bi, c0, n, hd
                head_A(*unit(0))
                head_B(*unit(0))
                for u in range(16):
                    bi, c0, n, hd = unit(u)
                    if u + 1 < 16:
                        head_A(*unit(u + 1))
                    sample_S1(u)
                    head_C(bi, c0, n, hd)
                    if u + 1 < 16:
                        head_B(*unit(u + 1))
                    if hd == 3:
                        outproj(AT, ATt, bi, c0, n)
                    sample_S2(u)
                    if u + 2 < 16:
                        load_batch(u + 2)
                outproj(ATs, ATst, 4, sc0, sn)
                kb.barrier()
            wrelease(s_mq)
            wrelease(s_mo)

    def mlp(l):
        base = 12 if l == 0 else 28
        with ExitStack() as sc:
            sqn = kb.sb(sc, "sqn", [128, 8, 512], BF16)
            rsn = kb.sb(sc, "rsn", [128, 512], F32)
            for bi_, (c0_, n_) in enumerate(BLKS):
                rmsnorm(X, Xt[bi_], c0_, n_, 6 + l, H, Ht[bi_], c0_, sqn, rsn)
            if l == 1:
                yo = kb.sb(sc, "yo", [128, 8, 512], F32)
                yot = Tok()
                yT_v = o_yT.rearrange("(kc p) t -> p kc t", p=128)
            r32 = kb.sb(sc, "r32", [128, 2, 512], F32)
            hid = kb.sb(sc, "hid", [128, 2, 4, 512], BF16)
            r32t, hidt = [Tok(), Tok()], [Tok(), Tok()]
            if l == 0:
                window_passthrough(sc)
            its = [(fb, bi) for fb in range(8) for bi in range(len(BLKS))]
            wcache = {}

            def getw(fb):
                if fb not in wcache:
                    Wf, Wft = wuse(base + fb)
                    wcache[fb] = (w3(Wf, 0, 8, 512), w3(Wf, 4096, 4, 1024), Wft)
                return wcache[fb]

            def up(i):
                fb, bi = its[i]
                c0, n = BLKS[bi]
                Wup, Wdn, Wft = getw(fb)
                hbuf = i % 2
                for hc in range(4):
                    pt, ptk = kb.ps()
                    kb.mm(pt[:, 0:n], [(Wup[:, kc, hc * 128:(hc + 1) * 128], H[:, kc, c0:c0 + n]) for kc in range(8)],
                          [Wft, Ht[bi]], [ptk])
                    rb = hc % 2
                    kb.act(r32[:, rb, 0:n], pt[:, 0:n], AF.Relu, [ptk], [r32t[rb]])
                    kb.tt(hid[:, hbuf, hc, 0:n], r32[:, rb, 0:n], r32[:, rb, 0:n], ALU.mult, [r32t[rb]], [hidt[hbuf]])

            def down(i):
                fb, bi = its[i]
                c0, n = BLKS[bi]
                Wup, Wdn, Wft = getw(fb)
                hbuf = i % 2
                for oc in range(8):
                    pt, ptk = kb.ps()
                    kb.mm(pt[:, 0:n], [(Wdn[:, hc, oc * 128:(oc + 1) * 128], hid[:, hbuf, hc, 0:n]) for hc in range(4)],
                          [Wft, hidt[hbuf]], [ptk])
                    xadd(oc, c0, n, pt, ptk, bi)
                if l == 1 and fb == 7:
                    rmsnorm(X, Xt[bi], c0, n, 8, yo, yot, 0, sqn, rsn)
                    kb.dma("sp", yT_v[:, :, c0:c0 + n], yo[:, :, 0:n], [yot], [yot])
                if bi == len(BLKS) - 1:
                    wrelease(base + fb)

            up(0)
            for i in range(len(its)):
                if i + 1 < len(its):
                    up(i + 1)
                down(i)
            kb.barrier()

    def layer1_mixer():
        norm_all(1)
        with ExitStack() as sc:
            KTa = kb.sb(sc, "KTa", [128, 2, T], BF16)
            Va = kb.sb(sc, "Va", [128, 17, 256], BF16)
            tb = kb.sb(sc, "tb", [128, 2, 256], F32)
            t1 = kb.sb(sc, "t1", [128, 256], F32)
            t2 = kb.sb(sc, "t2", [128, 256], F32)
            scp1 = ExitStack()
            k32 = kb.sb(scp1, "k32", [128, 2, 128], F32)
            v32 = kb.sb(scp1, "v32", [128, 256], F32)
            KTat = [Tok() for _ in BLKS]
            Vat = [Tok() for _ in BLKS]
            tbt, t1t, t2t, k32t, v32t = [Tok() for _ in range(5)]
            Wkv, Wkvt = wuse(20)
            Wkv3 = w3(Wkv, 0, 8, 768)
            for (c0, n) in BLK2:
                bi = min(c0 // 512, 4)
                kb.dma("sp", tb[:, :, 0:n], d_rt1[:, :, c0:c0 + n], [], [tbt])
                for kc in range(2):
                    pa, pat = kb.ps()
                    pb, pbt = kb.ps()
                    kb.mm(pa[:, 0:n], [(Wkv3[:, k, kc * 128:(kc + 1) * 128], H[:, k, c0:c0 + n]) for k in range(8)],
                          [Wkvt, Ht[bi]], [pat])
                    kb.mm(pb[:, 0:n], [(Wkv3[:, k, 256 + kc * 128:256 + (kc + 1) * 128], H[:, k, c0:c0 + n]) for k in range(8)],
                          [Wkvt, Ht[bi]], [pbt])
                    kb.stt(t1[:, 0:n], pa[:, 0:n], cf[:, BK + kc:BK + kc + 1], tb[:, 0, 0:n], ALU.add, ALU.mult, [pat, tbt, Ct], [t1t])
                    kb.stt(t2[:, 0:n], pb[:, 0:n], cf[:, BKS + kc:BKS + kc + 1], tb[:, 1, 0:n], ALU.add, ALU.mult, [pbt, tbt, Ct], [t2t])
                    kb.tt(KTa[:, kc, c0:c0 + n], t1[:, 0:n], t2[:, 0:n], ALU.add, [t1t, t2t], [KTat[bi]])
                    if c0 >= 1792:
                        lo = n - 128
                        kb.tt(k32[:, kc, :], t1[:, lo:n], t2[:, lo:n], ALU.add, [t1t, t2t], [k32t])
                if c0 >= 1792:
                    kb.dma("sp", o_wkT[:, (c0 - 1792) // 256, :].rearrange("p (k t) -> p k t", k=2), k32[:], [k32t], [k32t])
                for ch in range(n // 128):
                    t0 = c0 + ch * 128
                    ci = t0 // 128
                    pt, ptk = kb.ps()
                    kb.mm(pt[:, 0:256], [(H[:, k, t0:t0 + 128], Wkv3[:, k, 512:768]) for k in range(8)] + [(onerow, bvrow)],
                          [Wkvt, Ht[bi], Ct], [ptk])
                    kb.copy(Va[:, ci, :], pt[:, 0:256], [ptk], [Vat[bi]], en="act")
                    if ci == 15 or ci == 16:
                        kb.copy(v32[:], pt[:, 0:256], [ptk, Vat[bi]], [v32t], en="dve")
                        kb.dma("sp", o_wvp if ci == 15 else o_wvs, v32[:], [v32t], [v32t])
            wrelease(20)
            kb.barrier()
            scp1.close()
            QTc = kb.sb(sc, "QTc", [128, 4, 2, 256], BF16)
            Ee = kb.sb(sc, "Ee", [128, 5, 512], BF16)
            rr = kb.sb(sc, "rr", [128, 128], F32)
            AT = kb.sb(sc, "AT", [128, 8, 256], BF16)
            KcT = kb.sb(sc, "KcT", [128, 16, 256], BF16)
            Vc = kb.sb(sc, "Vc", [128, 16, 256], BF16)
            QTct, Eet = [Tok(), Tok(), Tok(), Tok()], [Tok() for _ in range(5)]
            kb.op("dve", [], QTct, lambda h: h.memset(QTc[:], 0.0))
            rrt, ATt, KcTt, Vct = [Tok() for _ in range(4)]
            kb.dma("pool", KcT[:], d_cwkT.rearrange("b p k s -> p b (k s)"), [], [KcTt])
            kb.dma("pool", Vc[:], d_cwv.rearrange("b s e -> s b e"), [], [Vct])
            Wq, Wqt = wuse(21)
            Wqs, Wqst = wuse(22)
            Wo, Wot = wuse(23)
            Wq3, Wqs3, Wo3 = w3(Wq, 0, 8, 1024), w3(Wqs, 0, 8, 1024), w3(Wo, 0, 8, 1024)
            def pair_gen(c0, n, bi, c, smp):
                kcx = c // 4
                qb = c % 4
                pa, pat = kb.ps(hold=True)
                pb, pbt = kb.ps(hold=True)
                kb.mm(pa[:, 0:n], [(Wq3[:, k, c * 128:(c + 1) * 128], H[:, k, c0:c0 + n]) for k in range(8)],
                      [Wqt, Ht[bi]], [pat])
                kb.mm(pb[:, 0:n], [(Wqs3[:, k, c * 128:(c + 1) * 128], H[:, k, c0:c0 + n]) for k in range(8)],
                      [Wqst, Ht[bi]], [pbt])
                yield
                kb.stt(t1[:, 0:n], pa[:, 0:n], cf[:, BQ + c:BQ + c + 1], tb[:, 0, 0:n], ALU.add, ALU.mult, [pat, tbt, Ct], [t1t])
                kb.stt(t2[:, 0:n], pb[:, 0:n], cf[:, BQS + c:BQS + c + 1], tb[:, 1, 0:n], ALU.add, ALU.mult, [pbt, tbt, Ct], [t2t])
                kb.pfree(pa)
                kb.pfree(pb)
                kb.tt(QTc[0:64, qb, 0, 0:n], t1[0:64, 0:n], t2[0:64, 0:n], ALU.add, [t1t, t2t], [QTct[qb]])
                kb.tt(QTc[64:128, qb, 1, 0:n], t1[64:128, 0:n], t2[64:128, 0:n], ALU.add, [t1t, t2t], [QTct[qb]])
                for ch in range(n // 128):
                    t0 = c0 + ch * 128
                    ci = t0 // 128
                    cs = slice(ch * 128, (ch + 1) * 128)
                    eb = (c * 2 + ch) % 5
                    pci = max(ci - 1, 0)
                    pbi = min(pci // 4, 4)
                    mask = mS if smp else (mP0 if ci == 0 else mP)
                    pS, pSt = kb.ps(hold=True)
                    rd_toks = [QTct[qb], KTat[bi], Ct] + ([KcTt] if smp else [KTat[pbi]])

                    def emit_scores(h, pS=pS, qb=qb, kcx=kcx, pci=pci, t0=t0, cs=cs, mask=mask, smp=smp):
                        ins = None
                        for e2 in range(2):
                            h.matmul(pS[:, e2 * 256:(e2 + 1) * 256], ident, mask, start=True, stop=False, skip_group_check=True)
                            if not smp:
                                h.matmul(pS[:, e2 * 256:e2 * 256 + 128], KTa[:, kcx, pci * 128:(pci + 1) * 128],
                                         QTc[:, qb, e2, cs], start=False, stop=False, skip_group_check=True)
                            else:
                                for b in range(16):
                                    h.matmul(pS[:, e2 * 256 + b * 8:e2 * 256 + b * 8 + 8], KcT[:, b, kcx * 128:(kcx + 1) * 128],
                                             QTc[:, qb, e2, b * 8:b * 8 + 8], start=False, stop=False, skip_group_check=True)
                            ins = h.matmul(pS[:, e2 * 256 + 128:e2 * 256 + 256], KTa[:, kcx, t0:t0 + 128],
                                           QTc[:, qb, e2, cs], start=False, stop=True, skip_group_check=True)
                        return ins
                    kb.op("pe", rd_toks, [pSt], emit_scores)
                    yield
                    kb.act(Ee[:, eb, :], pS[:, :], AF.Exp, [pSt], [Eet[eb]], scale=0.125)
                    kb.pfree(pS)
                    pO, pOt = kb.ps(hold=True)
                    for e2 in range(2):
                        if not smp:
                            kb.mm(pO[:, e2 * 128:(e2 + 1) * 128],
                                  [(Va[:, pci, kcx * 128:(kcx + 1) * 128], Ee[:, eb, e2 * 256:e2 * 256 + 128]),
                                   (Va[:, ci, kcx * 128:(kcx + 1) * 128], Ee[:, eb, e2 * 256 + 128:e2 * 256 + 256])],
                                  [Vat[pbi], Vat[bi], Eet[eb]], [pOt])
                        else:
                            kb.mm(pO[:, e2 * 128:(e2 + 1) * 128],
                                  [(Va[:, ci, kcx * 128:(kcx + 1) * 128], Ee[:, eb, e2 * 256 + 128:e2 * 256 + 256])],
                                  [Vat[bi], Eet[eb]], [pOt])
                            for b in range(16):
                                kb.op("pe", [Vct, Eet[eb]], [pOt],
                                      lambda h, b=b, e2=e2, eb=eb, kcx=kcx, pO=pO: h.matmul(
                                          pO[:, e2 * 128 + b * 8:e2 * 128 + b * 8 + 8],
                                          Vc[:, b, kcx * 128:(kcx + 1) * 128],
                                          Ee[:, eb, e2 * 256 + b * 8:e2 * 256 + b * 8 + 8],
                                          start=False, stop=True, skip_group_check=True))
                    kb.mm(pO[:, 256:384],
                          [(olo, Ee[:, eb, 0:128]), (olo, Ee[:, eb, 128:256]),
                           (ohi, Ee[:, eb, 256:384]), (ohi, Ee[:, eb, 384:512])],
                          [Eet[eb], Ct], [pOt])
                    yield
                    kb.act(rr[:], pO[:, 256:384], AF.Ln, [pOt, Ct], [rrt], bias=cf[:, ESK + c:ESK + c + 1], scale=1.0)
                    kb.act(rr[:], rr[:], AF.Exp, [rrt], [rrt], scale=-1.0)
                    for e2 in range(2):
                        rows = slice(e2 * 64, (e2 + 1) * 64)
                        kb.tt(AT[rows, c, cs], pO[rows, e2 * 128:(e2 + 1) * 128], rr[rows, :], ALU.mult, [pOt, rrt], [ATt])
                    kb.pfree(pO)

            for (c0, n) in BLK2:
                bi = min(c0 // 512, 4)
                smp = bi == 4
                kb.dma("sp", tb[:, :, 0:n], d_rt1[:, :, c0:c0 + n], [], [tbt])
                kb.pipeline([pair_gen(c0, n, bi, c, smp) for c in range(8)], 4)
                for oc in range(8):
                    pt, ptk = kb.ps()
                    kb.mm(pt[:, 0:n], [(Wo3[:, k, oc * 128:(oc + 1) * 128], AT[:, k, 0:n]) for k in range(8)],
                          [Wot, ATt], [ptk])
                    kb.stt(X[:, oc, c0:c0 + n], pt[:, 0:n], cf[:, BO + oc:BO + oc + 1], X[:, oc, c0:c0 + n],
                           ALU.add, ALU.add, [ptk, Xt[bi], Ct], [Xt[bi]])
            kb.barrier()
        wrelease(21)
        wrelease(22)
        wrelease(23)

    import os
    KST = int(os.environ.get("KSTAGES", "99"))
    if KST >= 1:
        layer0_mixer()
    if KST >= 2:
        cross(0)
    if KST >= 3:
        mlp(0)
    if KST >= 4:
        layer1_mixer()
    if KST >= 5:
        cross(1)
        mlp(1)

    with ExitStack() as sc:
        if KST < 5:
            yo = kb.sb(sc, "yo", [128, 8, 512], F32)
            sq = kb.sb(sc, "sq", [128, 8, 512], BF16)
            rs = kb.sb(sc, "rs", [128, 512], F32)
            yot = Tok()
            yT_v = o_yT.rearrange("(kc p) t -> p kc t", p=128)
            for bi, (c0, n) in enumerate(BLKS):
                rmsnorm(X, Xt[bi], c0, n, 8, yo, yot, 0, sq, rs)
                kb.dma("sp", yT_v[:, :, c0:c0 + n], yo[:, :, 0:n], [yot], [yot])
        sp = kb.eng["sp"]
        deps = []
        for q in kb.rings:
            for s in kb.rings[q]:
                if s.count:
                    deps.append((s, s.count))
        kb.sync(sp, deps)
        kb.barrier()
    es.close()
    return nc


def _slot(w):
    K, N = w.shape
    return np.ascontiguousarray(w.reshape(K // 128, 128, N).transpose(1, 0, 2).reshape(128, -1))


def _pad_slot(a):
    out = np.zeros((128, SLOT), np.float32)
    out[:, :a.shape[1]] = a
    return out


def _const_tables():
    f32 = np.float32
    cbf = np.zeros((128, 1792), f32)
    cbf[:, 0:128] = np.eye(128)
    cbf[:, 128:256] = 1.0 / 1024.0
    cbf[:, 256:384] = 1.0 / 256.0
    cbf[:, 384:512] = 1.0
    cbf[:, 512:576] = 1.0
    cbf[:, 704:768] = 1.0
    j = np.arange(128)[:, None]
    i = np.arange(128)[None, :]
    A = (j <= i).astype(f32)
    cbf[:, 768:896] = A
    cbf[:, 896:1024] = ((j // 8 == i // 8) & (j <= i)).astype(f32)
    cbf[:, 1024:1152] = 1.0 - A
    cbf[:, 1152:1280] = A
    cbf[:, 1408:1536] = A
    cbf[:, 1536:1664] = (j > (i % 8)).astype(f32)
    cbf[:, 1664:1792] = cbf[:, 896:1024]
    cbf[:, 1024:1792] = (cbf[:, 1024:1792] - 1.0) * 30000.0
    bsel = np.zeros((128, 16), f32)
    for b in range(16):
        bsel[b * 8:(b + 1) * 8, b] = 1.0
    pos = np.concatenate([np.arange(TP), np.tile(16384 + np.arange(8), 16)]).astype(np.int32)
    ci = np.concatenate([np.arange(TP) % 128, np.tile(np.arange(8), 16)]).astype(np.float64)
    half = 64
    inv = (f32(10000.0) ** (-np.arange(half, dtype=f32) / f32(half))).astype(f32)
    ang = pos.astype(f32)[:, None] * inv[None, :]
    cs = np.cos(ang).astype(np.float64).T
    sn = np.sin(ang).astype(np.float64).T
    cosd = np.concatenate([cs, cs], 0)
    sind = np.concatenate([-sn, sn], 0)
    rt0 = np.zeros((4, 128, 4, T), f32)
    for h in range(4):
        lg = math.log1p(-2.0 ** (-5.0 - h))
        xi = np.exp((ci + 1.0) * lg)[None, :]
        kk = (1.0 / xi) * (128.0 ** -0.5)
        rt0[h, :, 0] = cosd * xi
        rt0[h, :, 1] = sind * xi
        rt0[h, :, 2] = cosd * kk
        rt0[h, :, 3] = sind * kk
    half = 32
    inv1 = (f32(150000.0) ** (-np.arange(half, dtype=f32) / f32(half))).astype(f32)
    ang1 = pos.astype(f32)[:, None] * inv1[None, :]
    c1 = np.cos(ang1).astype(f32).T
    s1 = np.sin(ang1).astype(f32).T
    rt1 = np.zeros((128, 2, T), f32)
    rt1[:, 0] = np.concatenate([c1, c1, c1, c1], 0)
    rt1[:, 1] = np.concatenate([-s1, s1, -s1, s1], 0)
    return cbf, bsel, rt0, rt1


def _prep_shared(inp):
    f32 = np.float32
    g = lambda k: np.asarray(inp[k], f32)
    w_in = g("w_in_e")[0]
    slots = []
    slots.append(_slot(w_in[:, 0:1024]))
    slots.append(_slot(w_in[:, 1024:2048]))
    w_out = g("w_out_e")[0]
    slots.append(_slot(w_out[0:1024]))
    slots.append(_slot(w_out[1024:2048]))
    sw = np.concatenate([np.arange(64, 128), np.arange(0, 64)])
    for h in range(4):
        q = w_in[:, 2048 + h * 128:2048 + (h + 1) * 128]
        k = w_in[:, 2560 + h * 128:2560 + (h + 1) * 128]
        v = w_in[:, 3072 + h * 256:3072 + (h + 1) * 256]
        gt = w_in[:, 4096 + h * 256:4096 + (h + 1) * 256]
        slots.append(_slot(np.concatenate([q, q[:, sw], k, k[:, sw], v, gt], 1)))
    def cross_slots(l):
        return [_slot(g("w_mk")[l]), _slot(g("w_mv")[l]), _slot(g("w_mq")[l]), _slot(g("w_mo")[l])]
    def mlp_slots(l):
        up, dn = g("w_up")[l], g("w_down")[l]
        return [np.concatenate([_slot(up[:, fb * 512:(fb + 1) * 512]), _slot(dn[fb * 512:(fb + 1) * 512, :])], 1)
                for fb in range(8)]
    slots += cross_slots(0) + mlp_slots(0)
    wqkv = g("w_qkv_o")[0]
    bqkv = g("b_qkv_o")[0]
    sw64 = np.concatenate([np.arange(32, 64), np.arange(0, 32)])
    qcols = np.concatenate([np.arange(h * 64, (h + 1) * 64) for h in PERM_HEADS])
    qscols = np.concatenate([h * 64 + sw64 for h in PERM_HEADS])
    kcols = 1024 + np.arange(256)
    kscols = 1024 + np.concatenate([h * 64 + sw64 for h in range(4)])
    vcols = 1280 + np.arange(256)
    slots.append(_pad_slot(_slot(wqkv[:, np.concatenate([kcols, kscols, vcols])])))
    slots.append(_slot(wqkv[:, qcols]))
    slots.append(_slot(wqkv[:, qscols]))
    slots.append(_slot(g("w_out_o")[0][qcols, :]))
    slots += cross_slots(1)
    slots += mlp_slots(1)
    assert len(slots) == 36
    wslots = np.stack(slots, 0)
    cf = np.zeros((128, 109), f32)
    cf[:, 108] = EPS
    gl = [g("g_mix")[0], g("g_mix")[1], g("g_cross")[0], g("g_cross")[1], g("g_mem")[0], g("g_mem")[1],
          g("g_ffn")[0], g("g_ffn")[1], g("g_final")]
    for i, v in enumerate(gl):
        cf[:, i * 8:(i + 1) * 8] = v.reshape(8, 128).T
    cf[:, 72:80] = bqkv[qcols].reshape(8, 128).T
    cf[:, 80:88] = bqkv[qscols].reshape(8, 128).T
    cf[:, 88:90] = bqkv[kcols].reshape(2, 128).T
    cf[:, 90:92] = bqkv[kscols].reshape(2, 128).T
    cf[:, 92:100] = g("b_out_o")[0].reshape(8, 128).T
    sk = g("sinks")[0]
    for c in range(8):
        cf[0:64, 100 + c] = sk[PERM_HEADS[2 * c]]
        cf[64:128, 100 + c] = sk[PERM_HEADS[2 * c + 1]]
    crow = np.zeros((1, 1408), f32)
    bs = g("b_spatial")[0]
    crow[0, 0:512] = bs.reshape(-1)
    crow[0, 512:1024] = np.tile(bs[:, 0:8], (1, 16)).reshape(-1)
    crow[0, 1024:1280] = bqkv[vcols]
    crow[0, 1280:1408] = 1.0
    lngb = np.stack([np.broadcast_to(g("sgu_ln_g")[0], (128, 1024)), np.broadcast_to(g("sgu_ln_b")[0], (128, 1024))], 0)
    ws = g("w_spatial")[0]
    wst = np.zeros((128, 8, 128), f32)
    for gg in range(4):
        wst[:, gg, :] = ws[gg].T
        for b in range(16):
            wst[b * 8:(b + 1) * 8, 4 + gg, b * 8:(b + 1) * 8] = ws[gg, 0:8, 0:8].T
    return wslots, cf, crow, np.ascontiguousarray(lngb), wst


def make_in_maps(inp, cores=range(NCORES)):
    f32 = np.float32
    cbf, bsel, rt0, rt1 = _const_tables()
    wslots, cf, crow, lngb, wst = _prep_shared(inp)
    in_maps = []
    for c in cores:
        bs = slice(16 * c, 16 * c + 16)
        xs = np.asarray(inp["x_sample"][bs], f32).reshape(128, D)
        xT = np.ascontiguousarray(np.concatenate([np.asarray(inp["x_prompt"][c], f32), xs], 0).T)
        cmk = np.asarray(inp["cache_mem_k"][:, bs], f32).reshape(2, 16, 256, D)
        cmv = np.asarray(inp["cache_mem_v"][:, bs], f32).reshape(2, 16, 256, D)
        cwk = np.asarray(inp["cache_win_k"][0, bs], f32).reshape(16, 128, 256)
        cwv = np.asarray(inp["cache_win_v"][0, bs], f32).reshape(16, 128, 256)
        cwkT = np.ascontiguousarray(cwk.reshape(16, 128, 2, 128).transpose(0, 3, 2, 1))
        in_maps.append({
            "xT": xT,
            "memT": np.ascontiguousarray(np.asarray(inp["mem_prompt"][c], f32).T),
            "wslots": wslots,
            "cbf": cbf, "cf": cf, "crow": crow, "lngb": lngb, "wst": wst, "rt0": rt0, "rt1": rt1, "bsel": bsel,
            "state": np.ascontiguousarray(np.asarray(inp["state_ret"][0, bs], f32)),
            "cmkT": np.ascontiguousarray(cmk.transpose(0, 1, 3, 2)),
            "cmv": np.ascontiguousarray(cmv),
            "cwkT": cwkT, "cwv": np.ascontiguousarray(cwv),
            "cwk_raw": np.ascontiguousarray(cwk), "cwv_raw": np.ascontiguousarray(cwv),
        })
    return in_maps


def kernel(**inp):
    f32 = np.float32
    nc = build_program()
    in_maps = make_in_maps(inp)
    res = run_bass_kernel_spmd(nc, in_maps, core_ids=list(range(NCORES)))
    R = res.results
    y_p = np.stack([R[c]["yT"][:, :TP].T for c in range(NCORES)], 0).astype(f32)
    y_s = np.concatenate([R[c]["yT"][:, TP:].T.reshape(16, 8, D) for c in range(NCORES)], 0).astype(f32)
    mem_k = np.stack([R[c]["memk"] for c in range(NCORES)], 1).reshape(2, 8, 256, 4, 256).astype(f32)
    mem_v = np.stack([R[c]["memv"] for c in range(NCORES)], 1).reshape(2, 8, 256, 4, 256).astype(f32)
    ret_p = np.stack([R[c]["retp"] for c in range(NCORES)], 0)[None].astype(f32)
    ret_s = np.concatenate([R[c]["rets"] for c in range(NCORES)], 0)[None].astype(f32)
    sgu_v = np.concatenate([R[c]["sguv"].reshape(16, 8, 4, 256) for c in range(NCORES)], 0)[None].astype(f32)

    def kT_to_tm(a):
        return a.reshape(2, 64, 2, 128).transpose(3, 2, 0, 1).reshape(128, 4, 64)

    wk_p = np.stack([kT_to_tm(R[c]["wkT"][:, 0, :].reshape(128, 2, 128)) for c in range(NCORES)], 0)[None].astype(f32)
    wv_p = np.stack([R[c]["wvp"].reshape(128, 4, 64) for c in range(NCORES)], 0)[None].astype(f32)
    wk_s_l, wv_s_l = [], []
    for c in range(NCORES):
        knew = kT_to_tm(R[c]["wkT"][:, 1, :].reshape(128, 2, 128)).reshape(16, 8, 4, 64)
        vnew = R[c]["wvs"].reshape(16, 8, 4, 64)
        wk_s_l.append(np.concatenate([R[c]["wk_old"].reshape(16, 120, 4, 64), knew], 1))
        wv_s_l.append(np.concatenate([R[c]["wv_old"].reshape(16, 120, 4, 64), vnew], 1))
    wk_s = np.concatenate(wk_s_l, 0)[None].astype(f32)
    wv_s = np.concatenate(wv_s_l, 0)[None].astype(f32)
    return (y_p, y_s, mem_k, mem_v, ret_p, ret_s, sgu_v, wk_p, wv_p, wk_s, wv_s)
```

```python
import math
from contextlib import ExitStack
import numpy as np
import concourse.bass as bass
import concourse.mybir as mybir
from concourse.bass_utils import run_bass_kernel_spmd

F32 = mybir.dt.float32
BF16 = mybir.dt.bfloat16
AF = mybir.ActivationFunctionType
ALU = mybir.AluOpType

NCORES = 8
D = 1024
TP = 2048
TS = 128
T = TP + TS
BLKS = [(0, 512), (512, 512), (1024, 512), (1536, 512), (2048, 128)]
BLK2 = [(i * 256, 256) for i in range(8)] + [(2048, 128)]
EPS = 1e-6
SLOT = 8192
NRING = 3
LIM = 8000
PERM_HEADS = []
for _c in range(8):
    PERM_HEADS += ([_c, 4 + _c] if _c < 4 else [4 + _c, 8 + _c])


class Tok:
    __slots__ = ("w", "r", "const")

    def __init__(self, const=False):
        self.w = None
        self.r = {}
        self.const = const


class SemC:
    def __init__(self, h, owner=None, base=0):
        self.h = h
        self.count = 0
        self.owner = owner
        self.base = base


class Eng:
    def __init__(self, name, h):
        self.name = name
        self.h = h
        self.sems = []
        self.n = 0
        self.seen = {}


class KB:
    def __init__(self, nc):
        self.nc = nc
        self.es = ExitStack()
        self.eng = {n: Eng(n, h) for n, h in [("pe", nc.tensor), ("act", nc.scalar), ("dve", nc.vector),
                                               ("pool", nc.gpsimd), ("sp", nc.sync)]}
        self.nsem = 0
        self.rings = {q: [self.newsem() for _ in range(16)] for q in ("sp", "pool")}
        self.ri = {"sp": 0, "pool": 0}
        self.bar = self.newsem()
        self.psum = []
        self.pi = 0
        for i in range(8):
            t = self.es.enter_context(nc.psum_tensor(f"ps{i}", [128, 512], F32))
            self.psum.append((t, Tok()))

    def newsem(self, owner=None, base=0):
        self.nsem += 1
        h = self.es.enter_context(self.nc.semaphore(f"s{self.nsem}"))
        return SemC(h, owner, base)

    def sb(self, scope, name, shape, dt):
        self.nsb = getattr(self, "nsb", 0) + 1
        return scope.enter_context(self.nc.sbuf_tensor(f"sb{self.nsb}_{name}", shape, dt))

    def ps(self, hold=False):
        held = getattr(self, "held", None)
        if held is None:
            held = self.held = set()
        assert len(held) < 8, "all PSUM banks held"
        while (self.pi % 8) in held:
            self.pi += 1
        i = self.pi % 8
        if hold:
            held.add(i)
        self.pi += 1
        return self.psum[i]

    def ps_release_all(self):
        self.held = set()

    def pfree(self, t):
        for i, (tt_, _) in enumerate(self.psum):
            if tt_ is t:
                self.held.discard(i)

    def pipeline(self, gens, depth):
        gens = list(gens)
        active = []
        nxt = 0
        while nxt < len(gens) or active:
            while len(active) < depth and nxt < len(gens):
                active.append(gens[nxt])
                nxt += 1
            for g in list(active):
                try:
                    next(g)
                except StopIteration:
                    active.remove(g)

    def tick(self, e):
        ep = e.n // LIM
        if ep >= len(e.sems):
            e.sems.append(self.newsem(owner=e, base=ep * LIM))
        s = e.sems[ep]
        v = e.n % LIM + 1
        e.n += 1
        return s, v

    def sync(self, e, deps):
        for d in deps:
            if d is None:
                continue
            s, v = d
            if s.owner is e:
                if e.name == "pe":
                    continue
                if s.base + v < e.n - 1:
                    continue
            if e.seen.get(s, 0) >= v:
                continue
            e.h.wait_ge(s.h, v)
            e.seen[s] = v

    def _deps(self, reads, writes):
        deps = []
        for t in reads:
            deps.append(t.w)
        for t in writes:
            deps.append(t.w)
            deps.extend(t.r.items())
        return deps

    def _update(self, d, reads, writes):
        for t in reads:
            if not t.const:
                if t.r.get(d[0], 0) < d[1]:
                    t.r[d[0]] = d[1]
        for t in writes:
            t.w = d
            t.r = {}

    def op(self, en, reads, writes, fn):
        e = self.eng[en]
        self.sync(e, self._deps(reads, writes))
        ins = fn(e.h)
        d = self.tick(e)
        ins.then_inc(d[0].h, 1)
        self._update(d, reads, writes)

    def dma(self, q, out, in_, reads, writes):
        e = self.eng[q]
        ring = self.rings[q]
        s = ring[self.ri[q] % len(ring)]
        self.ri[q] += 1
        deps = self._deps(reads, writes)
        if s.count:
            deps.append((s, s.count))
        self.sync(e, deps)
        e.h.dma_start(out=out, in_=in_).then_inc(s.h, 16)
        s.count += 16
        self._update((s, s.count), reads, writes)

    def barrier(self):
        sp = self.eng["sp"]
        deps = []
        for q in self.rings:
            for s in self.rings[q]:
                if s.count:
                    deps.append((s, s.count))
        for n in ("pe", "act", "dve"):
            e = self.eng[n]
            if e.n:
                s = e.sems[(e.n - 1) // LIM]
                deps.append((s, (e.n - 1) % LIM + 1))
        self.sync(sp, deps)
        sp.h.sem_inc(self.bar.h, 1)
        self.bar.count += 1
        for n in ("pe", "act", "dve", "pool"):
            self.sync(self.eng[n], [(self.bar, self.bar.count)])

    def mm(self, out, pairs, reads, writes):
        def fn(h):
            ins = None
            n = len(pairs)
            for i, (l, r) in enumerate(pairs):
                ins = h.matmul(out, l, r, start=(i == 0), stop=(i == n - 1))
            return ins
        self.op("pe", reads, writes, fn)

    def act(self, out, in_, func, reads, writes, **kw):
        self.op("act", reads, writes, lambda h: h.activation(out=out, in_=in_, func=func, **kw))

    def tt(self, out, a, b, op, reads, writes, en="dve"):
        self.op(en, reads, writes, lambda h: h.tensor_tensor(out=out, in0=a, in1=b, op=op))

    def ts(self, out, a, s1, s2, op0, op1, reads, writes, en="dve"):
        self.op(en, reads, writes,
                lambda h: h.tensor_scalar(out=out, in0=a, scalar1=s1, scalar2=s2, op0=op0, op1=op1))

    def stt(self, out, a, s, b, op0, op1, reads, writes, en="dve"):
        self.op(en, reads, writes,
                lambda h: h.scalar_tensor_tensor(out=out, in0=a, scalar=s, in1=b, op0=op0, op1=op1))

    def recip(self, out, in_, reads, writes):
        self.op("dve", reads, writes, lambda h: h.reciprocal(out=out, in_=in_))

    def copy(self, out, in_, reads, writes, en="dve"):
        if en == "act":
            self.act(out, in_, AF.Copy, reads, writes)
        else:
            self.op(en, reads, writes, lambda h: h.tensor_copy(out=out, in_=in_))


def build_program():
    nc = bass.Bass("TRN2", target_bir_lowering=False)
    kb = KB(nc)
    es = kb.es

    def din(name, shape):
        return nc.dram_tensor(name, list(shape), F32, kind="ExternalInput").ap()

    def dout(name, shape):
        return nc.dram_tensor(name, list(shape), F32, kind="ExternalOutput").ap()

    d_xT = din("xT", [D, T])
    d_memT = din("memT", [D, 256])
    d_w = din("wslots", [36, 128, SLOT])
    d_cbf = din("cbf", [128, 1792])
    d_cf = din("cf", [128, 109])
    d_crow = din("crow", [1, 1408])
    d_ln = din("lngb", [2, 128, 1024])
    d_wst = din("wst", [128, 8, 128])
    d_rt0 = din("rt0", [4, 128, 4, T])
    d_rt1 = din("rt1", [128, 2, T])
    d_state = din("state", [16, 4, 128, 256])
    d_cmkT = din("cmkT", [2, 16, D, 256])
    d_cmv = din("cmv", [2, 16, 256, D])
    d_cwkT = din("cwkT", [16, 128, 2, 128])
    d_cwv = din("cwv", [16, 128, 256])
    d_cwk_raw = din("cwk_raw", [16, 128, 256])
    d_cwv_raw = din("cwv_raw", [16, 128, 256])

    o_yT = dout("yT", [D, T])
    o_memk = dout("memk", [2, 256, D])
    o_memv = dout("memv", [2, 256, D])
    o_retp = dout("retp", [4, 128, 256])
    o_rets = dout("rets", [16, 4, 128, 256])
    o_sguv = dout("sguv", [128, 1024])
    o_wkT = dout("wkT", [128, 2, 256])
    o_wvp = dout("wvp", [128, 256])
    o_wvs = dout("wvs", [128, 256])
    o_wk_old = dout("wk_old", [16, 120, 256])
    o_wv_old = dout("wv_old", [16, 120, 256])
    import os
    DBG = bool(os.environ.get("KDBG"))
    if DBG:
        o_dbg = dout("dbg", [8, 128, 512])

    def dbgdump(i, ap, n, toks):
        if DBG:
            kb.dma("pool", o_dbg[i, :, 0:n], ap, toks, [])

    X = kb.sb(es, "X", [128, 8, T], F32)
    H = kb.sb(es, "H", [128, 8, T], BF16)
    WR = [kb.sb(es, f"WR{i}", [128, SLOT], BF16) for i in range(NRING)]
    cbf = kb.sb(es, "cbf", [128, 1792], BF16)
    cf = kb.sb(es, "cf", [128, 109], F32)
    crow = kb.sb(es, "crow", [1, 1408], BF16)
    Xt = [Tok() for _ in BLKS]
    Ht = [Tok() for _ in BLKS]
    WRt = [Tok() for _ in range(NRING)]
    Ct = Tok(const=True)
    sqt, rst = Tok(), Tok()

    ident = cbf[:, 0:128]
    o1024 = cbf[:, 128:256]
    o256 = cbf[:, 256:384]
    o1 = cbf[:, 384:512]
    olo = cbf[:, 512:640]
    ohi = cbf[:, 640:768]
    M1 = cbf[:, 768:896]
    M2 = cbf[:, 896:1024]
    mP = cbf[:, 1024:1280]
    mP0 = cbf[:, 1280:1536]
    mS = cbf[:, 1536:1792]

    def gv(i, kc):
        return cf[:, i * 8 + kc:i * 8 + kc + 1]
    BQ, BQS, BK, BKS, BO, ESK, EPSC = 72, 80, 88, 90, 92, 100, 108
    bsP = crow[0:1, 0:512]
    bsS = crow[0:1, 512:1024]
    bvrow = crow[0:1, 1024:1280]
    onerow = crow[0:1, 1280:1408]

    xT_v = d_xT.rearrange("(kc p) t -> p kc t", p=128)
    for bi, (c0, n) in enumerate(BLKS):
        kb.dma("sp", X[:, :, c0:c0 + n], xT_v[:, :, c0:c0 + n], [], [Xt[bi]])
    cft = Tok()
    kb.dma("sp", cf[:], d_cf, [], [cft])
    kb.dma("pool", cbf[:], d_cbf, [], [Ct])
    kb.dma("pool", crow[:], d_crow, [], [Ct])
    bsel = kb.sb(es, "bsel", [128, 16], BF16)
    d_bsel = din("bsel", [128, 16])
    kb.dma("pool", bsel[:], d_bsel, [], [Ct])
    kb.act(cf[:, ESK:ESK + 8], cf[:, ESK:ESK + 8], AF.Exp, [cft], [cft])
    kb.barrier()
    Ct.w = None

    def window_passthrough(scope):
        pas = kb.sb(scope, "pas", [120, 16, 256], F32)
        past = Tok()
        for src, dst in ((d_cwk_raw, o_wk_old), (d_cwv_raw, o_wv_old)):
            kb.dma("sp", pas[:], src[:, 8:128, :].rearrange("b s e -> s b e"), [], [past])
            kb.dma("sp", dst.rearrange("b s e -> s b e"), pas[:], [past], [past])

    wstate = {"next": 0, "free": list(range(NRING)), "loaded": {}}

    def wprefetch():
        while wstate["free"] and wstate["next"] < 36:
            k = wstate["next"]
            ph = wstate["free"].pop(0)
            kb.dma("pool", WR[ph][:], d_w[k], [], [WRt[ph]])
            wstate["loaded"][k] = ph
            wstate["next"] += 1

    def wuse(k):
        while k not in wstate["loaded"]:
            assert wstate["free"], "weight ring exhausted"
            wprefetch()
        ph = wstate["loaded"][k]
        return WR[ph], WRt[ph]

    def wrelease(k):
        ph = wstate["loaded"].pop(k)
        wstate["free"].append(ph)
        wprefetch()

    def w3(W, off, kc, n):
        return W[:, off:off + kc * n].rearrange("p (k n) -> p k n", k=kc)

    def rmsnorm(Xs, Xtok, c0, n, gi, Hs, Htok, hc0, sq, rs):
        kb.act(sq[:, :, 0:n], Xs[:, :, c0:c0 + n], AF.Square, [Xtok], [sqt])
        pt, ptk = kb.ps()
        kb.mm(pt[:, 0:n], [(o1024, sq[:, kc, 0:n]) for kc in range(8)], [sqt, Ct], [ptk])
        kb.act(rs[:, 0:n], pt[:, 0:n], AF.Ln, [ptk, Ct], [rst], bias=cf[:, EPSC:EPSC + 1], scale=1.0)
        kb.act(rs[:, 0:n], rs[:, 0:n], AF.Exp, [rst], [rst], scale=-0.5)
        for kc in range(8):
            kb.stt(Hs[:, kc, hc0:hc0 + n], Xs[:, kc, c0:c0 + n], gv(gi, kc), rs[:, 0:n],
                   ALU.mult, ALU.mult, [Xtok, rst, Ct], [Htok])

    def norm_all(gi):
        with ExitStack() as scn:
            sq = kb.sb(scn, "sq", [128, 8, 512], BF16)
            rs = kb.sb(scn, "rs", [128, 512], F32)
            for bi, (c0, n) in enumerate(BLKS):
                rmsnorm(X, Xt[bi], c0, n, gi, H, Ht[bi], c0, sq, rs)
            kb.barrier()

    def xadd(oc, c0, n, pt, ptk, bi):
        kb.tt(X[:, oc, c0:c0 + n], X[:, oc, c0:c0 + n], pt[:, 0:n], ALU.add, [ptk, Xt[bi]], [Xt[bi]])

    def layer0_mixer():
        wprefetch()
        norm_all(0)
        with ExitStack() as sc:
            lng = kb.sb(sc, "lng", [128, 1024], F32)
            lnb = kb.sb(sc, "lnb", [128, 1024], F32)
            wst = kb.sb(sc, "wst", [128, 8, 128], BF16)
            GU = kb.sb(sc, "GU", [128, 8, 512], BF16)
            AO = kb.sb(sc, "AO", [128, 8, 512], BF16)
            zv2 = [kb.sb(sc, f"zv{i}", [128, 1024], F32) for i in range(2)]
            vnb2 = [kb.sb(sc, f"vnb{i}", [128, 1024], BF16) for i in range(2)]
            st62 = [kb.sb(sc, f"st6{i}", [128, 2, 6], F32) for i in range(2)]
            mv2 = [kb.sb(sc, f"mv{i}", [128, 4], F32) for i in range(2)]
            zvt2, vnbt2, stt2, mvt2 = [[Tok(), Tok()] for _ in range(4)]
            lt, wstt, GUt, AOt = [Tok() for _ in range(4)]
            kb.dma("sp", lng[:], d_ln[0], [], [lt])
            kb.dma("sp", lnb[:], d_ln[1], [], [lt])
            kb.dma("pool", wst[:], d_wst, [], [wstt])
            for g in range(4):
                kb.tt(wst[:, g, :], wst[:, g, :], M1, ALU.mult, [wstt, Ct], [wstt])
                kb.tt(wst[:, 4 + g, :], wst[:, 4 + g, :], M2, ALU.mult, [wstt, Ct], [wstt])
            Wu, Wut = wuse(0)
            Wv, Wvt = wuse(1)
            Wo, Wot = wuse(2)
            Wu3, Wv3, Wo3 = w3(Wu, 0, 8, 1024), w3(Wv, 0, 8, 1024), w3(Wo, 0, 8, 1024)
            def uproj(bi):
                c0, n = BLKS[bi]
                for oc in range(8):
                    pt, ptk = kb.ps()
                    kb.mm(pt[:, 0:n], [(Wu3[:, kc, oc * 128:(oc + 1) * 128], H[:, kc, c0:c0 + n]) for kc in range(8)],
                          [Wut, Ht[bi]], [ptk])
                    kb.act(GU[:, oc, 0:n], pt[:, 0:n], AF.Gelu_apprx_tanh, [ptk], [GUt])

            uproj(0)
            for bi, (c0, n) in enumerate(BLKS):
                smp = bi == 4
                def sgu_chunk(bi, c0, ch, smp):
                    t0 = c0 + ch * 128
                    db = ch % 2
                    zvb, zvtb, vnbb, vnbtb = zv2[db], zvt2[db], vnb2[db], vnbt2[db]
                    st6b, sttb, mvb, mvtb = st62[db], stt2[db], mv2[db], mvt2[db]
                    pv = []
                    for hf in range(2):
                        pt, ptk = kb.ps(hold=True)
                        pv.append((pt, ptk))
                        kb.mm(pt[:, :], [(H[:, kc, t0:t0 + 128], Wv3[:, kc, hf * 512:(hf + 1) * 512]) for kc in range(8)],
                              [Wvt, Ht[bi]], [ptk])
                    yield
                    for hf in range(2):
                        pt, ptk = pv[hf]
                        kb.act(zvb[:, hf * 512:(hf + 1) * 512], pt[:, :], AF.Gelu_apprx_tanh, [ptk], [zvtb])
                        kb.pfree(pt)
                    for hf in range(2):
                        kb.op("dve", [zvtb], [sttb],
                              lambda h, hf=hf: h.bn_stats(out=st6b[:, hf, :], in_=zvb[:, hf * 512:(hf + 1) * 512]))
                    kb.op("dve", [sttb], [mvtb],
                          lambda h: h.bn_aggr(out=mvb[:, 0:2], in_=st6b[:].rearrange("p a b -> p (a b)")))
                    kb.act(mvb[:, 2:3], mvb[:, 1:2], AF.Sqrt, [mvtb], [mvtb], bias=EPS, scale=1.0)
                    kb.recip(mvb[:, 2:3], mvb[:, 2:3], [mvtb], [mvtb])
                    kb.stt(mvb[:, 3:4], mvb[:, 0:1], -1.0, mvb[:, 2:3], ALU.mult, ALU.mult, [mvtb], [mvtb])
                    kb.ts(zvb[:], zvb[:], mvb[:, 2:3], mvb[:, 3:4], ALU.mult, ALU.add, [zvtb, mvtb], [zvtb])
                    kb.tt(zvb[:], zvb[:], lng[:], ALU.mult, [zvtb, lt], [zvtb])
                    if smp:
                        kb.tt(zvb[:], zvb[:], lnb[:], ALU.add, [zvtb, lt], [zvtb])
                        kb.dma("sp", o_sguv, zvb[:], [zvtb], [])
                        kb.copy(vnbb[:], zvb[:], [zvtb], [vnbtb])
                    else:
                        kb.tt(vnbb[:], zvb[:], lnb[:], ALU.add, [zvtb, lt], [vnbtb])
                    yield
                    pA, pAt = kb.ps(hold=True)
                    pB, pBt = kb.ps(hold=True)
                    for oc in range(8):
                        g = oc // 2
                        pp, ppt = (pA, pAt) if oc < 4 else (pB, pBt)
                        wsel = wst[:, (4 + g) if smp else g, :]
                        brow = (bsS if smp else bsP)[0:1, g * 128:(g + 1) * 128]
                        kb.mm(pp[:, (oc % 4) * 128:(oc % 4 + 1) * 128],
                              [(vnbb[:, oc * 128:(oc + 1) * 128], wsel), (onerow, brow)],
                              [vnbtb, wstt, Ct], [ppt])
                    yield
                    kb.tt(AO[:, 0:4, ch * 128:(ch + 1) * 128], GU[:, 0:4, ch * 128:(ch + 1) * 128],
                          pA[:, :].rearrange("p (a b) -> p a b", a=4), ALU.mult, [GUt, pAt], [AOt])
                    kb.tt(AO[:, 4:8, ch * 128:(ch + 1) * 128], GU[:, 4:8, ch * 128:(ch + 1) * 128],
                          pB[:, :].rearrange("p (a b) -> p a b", a=4), ALU.mult, [GUt, pBt], [AOt])
                    kb.pfree(pA)
                    kb.pfree(pB)

                kb.pipeline([sgu_chunk(bi, c0, ch, smp) for ch in range(n // 128)], 2)
                if bi + 1 < len(BLKS):
                    uproj(bi + 1)
                    if bi + 1 == len(BLKS) - 1:
                        wrelease(0)
                else:
                    wrelease(1)
                for oc in range(8):
                    pt, ptk = kb.ps()
                    kb.mm(pt[:, 0:n], [(Wo3[:, kc, oc * 128:(oc + 1) * 128], AO[:, kc, 0:n]) for kc in range(8)],
                          [Wot, AOt], [ptk])
                    xadd(oc, c0, n, pt, ptk, bi)
            kb.barrier()
        wrelease(2)
        import os
        if os.environ.get("KSKIPB"):
            for k in range(3, 8):
                wuse(k)
                wrelease(k)
            return
        Wob, Wobt = wuse(3)
        Wob3 = w3(Wob, 0, 8, 1024)
        with ExitStack() as sc:
            tab = kb.sb(sc, "tab", [128, 4, 512], F32)
            QK = kb.sb(sc, "QK", [128, 2, 512], BF16)
            t1 = kb.sb(sc, "t1", [128, 512], F32)
            t2 = kb.sb(sc, "t2", [128, 512], F32)
            SG = kb.sb(sc, "SG", [128, 2, 512], BF16)
            Vt = kb.sb(sc, "Vt", [128, 4, 256], BF16)
            S32 = kb.sb(sc, "S32", [128, 256], F32)
            st32 = kb.sb(sc, "st32", [128, 8, 256], F32)
            stbf = kb.sb(sc, "stbf", [128, 8, 256], BF16)
            Vblk = kb.sb(sc, "Vblk", [128, 8, 256], BF16)
            (tabt, QKt, t1t, t2t, SGt, Vtt, scTt, kTMt, S32t, Sbft, osqt, r2t, BOt, st32t, stbft,
             Vblkt) = [Tok() for _ in range(16)]
            scT4 = kb.sb(sc, "scT4", [128, 4, 128], BF16)
            kTM4 = kb.sb(sc, "kTM4", [128, 4, 128], BF16)
            Sb5 = kb.sb(sc, "Sb5", [128, 5, 256], BF16)
            osq4 = kb.sb(sc, "osq4", [128, 4, 2, 128], BF16)
            osq4t = [Tok() for _ in range(4)]
            osq2 = osq4
            BO2 = kb.sb(sc, "BO2", [128, 2, 2, 512], BF16)
            scT4t, kTM4t = [Tok() for _ in range(4)], [Tok() for _ in range(4)]
            Sb5t = [Tok() for _ in range(5)]
            osq2t, r22t, BO2t = [Tok(), Tok()], [Tok(), Tok()], [Tok(), Tok()]
            r24 = kb.sb(sc, "r24", [128, 512], F32)
            SGr = kb.sb(sc, "SGr", [128, 2, 512], BF16)
            r24t, SGrt = Tok(), Tok()
            scT, kTM, osq, r2, BO = scT4[:, 0, :], kTM4[:, 0, :], osq4[:, 0], r24[:, 0:128], BO2[:, 0]
            scTt, kTMt, osqt, r2t, BOt = scT4t[0], kTM4t[0], osq4t[0], r24t, BO2t[0]
            Sbf, Sbft = Sb5[:, 0, :], Sb5t[0]

            st32q, stbfq, Vblkq = [Tok(), Tok()], [Tok(), Tok()], [Tok(), Tok()]

            def load_state(hd, hq):
                qb2 = hq % 2
                src = d_state[hq * 4:(hq + 1) * 4, hd].rearrange("b p e -> p b e")
                kb.dma("sp", st32[:, qb2 * 4:qb2 * 4 + 4, :], src, [], [st32q[qb2]])
                kb.dma("pool", stbf[:, qb2 * 4:qb2 * 4 + 4, :], src, [], [stbfq[qb2]])

            def outproj(hd, bi, c0, n, BOb, BObt):
                for oc in range(8):
                    pt, ptk = kb.ps()
                    kb.mm(pt[:, 0:n], [(Wob3[:, 2 * hd + ec, oc * 128:(oc + 1) * 128], BOb[:, ec, 0:n]) for ec in range(2)],
                          [Wobt, BObt], [ptk])
                    xadd(oc, c0, n, pt, ptk, bi)

            for hd in range(4):
                lg = math.log1p(-2.0 ** (-5.0 - hd))
                gP = math.exp(128.0 * lg)
                gS = math.exp(8.0 * lg)
                Wh, Wht = wuse(4 + hd)
                Wh3 = w3(Wh, 0, 8, 1024)
                kb.op("dve", [], [S32t], lambda h: h.memset(S32[:], 0.0))
                kb.op("dve", [], [Sb5t[0]], lambda h: h.memset(Sb5[:, 0, :], 0.0))
                pending = None
                load_state(hd, 0)
                load_state(hd, 1)

                def p0_qk(bi, hd=hd, Wh3=Wh3, Wht=Wht):
                    c0, n = BLKS[bi]
                    kb.dma("sp", tab[:, :, 0:n], d_rt0[hd, :, :, c0:c0 + n], [], [tabt])
                    for qk in range(2):
                        pa, pat = kb.ps()
                        pb, pbt = kb.ps()
                        kb.mm(pa[:, 0:n], [(Wh3[:, kc, qk * 256:qk * 256 + 128], H[:, kc, c0:c0 + n]) for kc in range(8)],
                              [Wht, Ht[bi]], [pat])
                        kb.mm(pb[:, 0:n], [(Wh3[:, kc, qk * 256 + 128:qk * 256 + 256], H[:, kc, c0:c0 + n]) for kc in range(8)],
                              [Wht, Ht[bi]], [pbt])
                        kb.tt(t1[:, 0:n], pa[:, 0:n], tab[:, 2 * qk, 0:n], ALU.mult, [pat, tabt], [t1t])
                        kb.tt(t2[:, 0:n], pb[:, 0:n], tab[:, 2 * qk + 1, 0:n], ALU.mult, [pbt, tabt], [t2t])
                        kb.tt(QK[:, qk, 0:n], t1[:, 0:n], t2[:, 0:n], ALU.add, [t1t, t2t], [QKt])

                def p0_v(bi, Wh3=Wh3, Wht=Wht):
                    c0, n = BLKS[bi]
                    for ch in range(n // 128):
                        t0 = c0 + ch * 128
                        pt, ptk = kb.ps()
                        kb.mm(pt[:, 0:256], [(H[:, kc, t0:t0 + 128], Wh3[:, kc, 512:768]) for kc in range(8)],
                              [Wht, Ht[bi]], [ptk])
                        kb.copy(Vt[:, ch, :], pt[:, 0:256], [ptk], [Vtt], en="act")

                for bi, (c0, n) in enumerate(BLKS):
                    smp = bi == 4
                    p0_qk(bi)
                    p0_v(bi)

                    def gate_proj():
                        for ec in range(2):
                            pt, ptk = kb.ps()
                            kb.mm(pt[:, 0:n], [(Wh3[:, kc, 768 + ec * 128:768 + (ec + 1) * 128], H[:, kc, c0:c0 + n]) for kc in range(8)],
                                  [Wht, Ht[bi]], [ptk])
                            kb.act(SG[:, ec, 0:n], pt[:, 0:n], AF.Silu, [ptk], [SGt])

                    if not smp:
                        bb = bi % 2
                        BOb, BObt = BO2[:, bb], BO2t[bb]
                        nch = 4
                        ps1, ps1t = kb.ps()
                        pk1, pk1t = kb.ps()
                        for ch in range(nch):
                            cs = slice(ch * 128, (ch + 1) * 128)
                            kb.mm(pk1[:, cs], [(QK[:, 1, cs], ident)], [QKt, Ct], [pk1t])
                        kb.copy(kTM4[:].rearrange("p c t -> p (c t)"), pk1[:, :], [pk1t], kTM4t, en="act")
                        for ch in range(nch):
                            cs = slice(ch * 128, (ch + 1) * 128)
                            kb.mm(ps1[:, cs], [(QK[:, 1, cs], QK[:, 0, cs])], [QKt], [ps1t])
                        kb.tt(scT4[:], ps1[:, :].rearrange("p (c t) -> p c t", c=4),
                              M1.unsqueeze(1).to_broadcast([128, 4, 128]), ALU.mult, [ps1t, Ct], scT4t)
                        gate_proj()
                        puA, puAt = kb.ps(hold=True)
                        puB, puBt = kb.ps(hold=True)
                        PU = [(puA, puAt, 0), (puA, puAt, 256), (puB, puBt, 0), (puB, puBt, 256)]
                        for ch in range(nch):
                            pu, put, o = PU[ch]
                            kb.mm(pu[:, o:o + 256], [(kTM4[:, ch, :], Vt[:, ch, :])], [kTM4t[ch], Vtt], [put])
                        if pending is not None:
                            outproj(*pending)
                            pending = None
                        for ch in range(nch):
                            g = bi * 4 + ch
                            pu, put, o = PU[ch]
                            kb.stt(S32[:], S32[:], gP, pu[:, o:o + 256], ALU.mult, ALU.add, [put, S32t], [S32t])
                            kb.act(Sb5[:, (g + 1) % 5, :], S32[:], AF.Copy, [S32t], [Sb5t[(g + 1) % 5]], scale=gP)
                        kb.pfree(puA)
                        kb.pfree(puB)
                        POs = []
                        for ch in range(nch):
                            g = bi * 4 + ch
                            cs = slice(ch * 128, (ch + 1) * 128)
                            po, pot = kb.ps(hold=True)
                            POs.append((po, pot))
                            for ec in range(2):
                                kb.mm(po[:, ec * 128:(ec + 1) * 128],
                                      [(Vt[:, ch, ec * 128:(ec + 1) * 128], scT4[:, ch, :]),
                                       (Sb5[:, g % 5, ec * 128:(ec + 1) * 128], QK[:, 0, cs])],
                                      [Vtt, scT4t[ch], Sb5t[g % 5], QKt], [pot])
                        for ch in range(nch):
                            po, pot = POs[ch]
                            kb.act(osq4[:, ch].rearrange("p a b -> p (a b)"), po[:, 0:256], AF.Square, [pot], [osq4t[ch]])
                        pn, pnt = kb.ps(hold=True)
                        for ch in range(nch):
                            kb.mm(pn[:, ch * 128:(ch + 1) * 128], [(o256, osq4[:, ch, ec, :]) for ec in range(2)], [osq4t[ch], Ct], [pnt])
                        kb.act(r24[:], pn[:, :], AF.Ln, [pnt], [r24t], bias=cf[:, EPSC:EPSC + 1], scale=1.0)
                        kb.pfree(pn)
                        kb.act(r24[:], r24[:], AF.Exp, [r24t], [r24t], scale=-0.5)
                        kb.tt(SGr[:], SG[:], r24[:].unsqueeze(1).to_broadcast([128, 2, 512]), ALU.mult, [SGt, r24t], [SGrt])
                        for ch in range(nch):
                            po, pot = POs[ch]
                            cs = slice(ch * 128, (ch + 1) * 128)
                            kb.tt(BOb[:, :, cs], po[:, 0:256].rearrange("p (a b) -> p a b", a=2), SGr[:, :, cs], ALU.mult,
                                  [pot, SGrt], [BObt])
                            kb.pfree(po)
                        pending = (hd, bi, c0, n, BOb, BObt)
                        if bi == 3:
                            outproj(*pending)
                            pending = None
                            kb.act(S32[:], S32[:], AF.Copy, [S32t], [S32t], scale=gP)
                            kb.dma("sp", o_retp[hd], S32[:], [S32t], [S32t])
                        continue
                    gate_proj()
                    for ch in range(n // 128):
                        cs = slice(ch * 128, (ch + 1) * 128)
                        pt, ptk = kb.ps()
                        kb.mm(pt[:, 0:128], [(QK[:, 1, cs], QK[:, 0, cs])], [QKt], [ptk])
                        kb.tt(scT[:], pt[:, 0:128], M2 if smp else M1, ALU.mult, [ptk, Ct], [scTt])
                        pk, pkt = kb.ps()
                        kb.mm(pk[:, 0:128], [(QK[:, 1, cs], ident)], [QKt, Ct], [pkt])
                        kb.copy(kTM[:], pk[:, 0:128], [pkt], [kTMt], en="act")
                        if not smp:
                            po, pot = kb.ps()
                            PO = [(po, pot, 0), (po, pot, 128)]
                        else:
                            poA, poAt = kb.ps(hold=True)
                            poB, poBt = kb.ps(hold=True)
                            PO = [(poA, poAt, 0), (poB, poBt, 0)]
                        if not smp:
                            for ec in range(2):
                                kb.mm(po[:, ec * 128:(ec + 1) * 128],
                                      [(Vt[:, ch, ec * 128:(ec + 1) * 128], scT[:]),
                                       (Sbf[:, ec * 128:(ec + 1) * 128], QK[:, 0, cs])],
                                      [Vtt, scTt, Sbft, QKt], [pot])
                        else:
                            for hq in range(4):
                                qb2 = hq % 2
                                s32q = st32[:, qb2 * 4:qb2 * 4 + 4, :]
                                sbfq = stbf[:, qb2 * 4:qb2 * 4 + 4, :]
                                vbq = Vblk[:, qb2 * 4:qb2 * 4 + 4, :]
                                if hq == 0:
                                    for ec in range(2):
                                        kb.mm(PO[ec][0][:, 0:128],
                                              [(Vt[:, ch, ec * 128:(ec + 1) * 128], scT[:])],
                                              [Vtt, scTt], [PO[ec][1]])
                                for b4 in range(4):
                                    b = hq * 4 + b4
                                    for ec in range(2):
                                        kb.op("pe", [stbfq[qb2], QKt], [PO[ec][1]],
                                              lambda h, b=b, b4=b4, ec=ec, PO=PO, sbfq=sbfq: h.matmul(
                                                  PO[ec][0][:, b * 8:b * 8 + 8],
                                                  sbfq[:, b4, ec * 128:(ec + 1) * 128],
                                                  QK[:, 0, b * 8:b * 8 + 8], start=False, stop=True,
                                                  skip_group_check=True))
                                kb.op("dve", [Vtt, Ct], [Vblkq[qb2]],
                                      lambda h, hq=hq, vbq=vbq: h.tensor_tensor(
                                          out=vbq,
                                          in0=Vt[:, ch:ch + 1, :].to_broadcast([128, 4, 256]),
                                          in1=bsel[:, hq * 4:(hq + 1) * 4].unsqueeze(2).to_broadcast([128, 4, 256]), op=ALU.mult))
                                for b2 in range(2):
                                    pu, put = kb.ps()
                                    kb.mm(pu[:, :], [(kTM[:], vbq[:, 2 * b2:2 * b2 + 2, :].rearrange("p a b -> p (a b)"))],
                                          [kTMt, Vblkq[qb2]], [put])
                                    kb.tt(s32q[:, 2 * b2:2 * b2 + 2, :].rearrange("p a b -> p (a b)"),
                                          s32q[:, 2 * b2:2 * b2 + 2, :].rearrange("p a b -> p (a b)"),
                                          pu[:, :], ALU.add, [put, st32q[qb2]], [st32q[qb2]])
                                kb.act(s32q, s32q, AF.Copy, [st32q[qb2]], [st32q[qb2]], scale=gS)
                                kb.dma("sp", o_rets[hq * 4:(hq + 1) * 4, hd].rearrange("b p e -> p b e"), s32q,
                                       [st32q[qb2]], [st32q[qb2]])
                                if hq + 2 < 4:
                                    load_state(hd, hq + 2)
                        if not smp:
                            kb.act(osq[:].rearrange("p a b -> p (a b)"), po[:, 0:256], AF.Square, [pot], [osqt])
                        else:
                            for ec in range(2):
                                kb.act(osq[:, ec, :], PO[ec][0][:, 0:128], AF.Square, [PO[ec][1]], [osqt])
                        pn, pnt = kb.ps()
                        kb.mm(pn[:, 0:128], [(o256, osq[:, ec, :]) for ec in range(2)], [osqt, Ct], [pnt])
                        kb.act(r2[:], pn[:, 0:128], AF.Sqrt, [pnt], [r2t], bias=EPS, scale=1.0)
                        kb.recip(r2[:], r2[:], [r2t], [r2t])
                        for ec in range(2):
                            kb.tt(t1[:, 0:128], PO[ec][0][:, PO[ec][2]:PO[ec][2] + 128], r2[:], ALU.mult, [PO[ec][1], r2t], [t1t])
                            kb.tt(BO[:, ec, cs], t1[:, 0:128], SG[:, ec, cs], ALU.mult, [t1t, SGt], [BOt])
                        kb.ps_release_all()
                        if not smp:
                            pu, put = kb.ps()
                            kb.mm(pu[:, 0:256], [(kTM[:], Vt[:, ch, :])], [kTMt, Vtt], [put])
                            kb.stt(S32[:], S32[:], gP, pu[:, 0:256], ALU.mult, ALU.add, [put, S32t], [S32t])
                            kb.act(Sbf[:], S32[:], AF.Copy, [S32t], [Sbft], scale=gP)
                    if smp and hd == 0:
                        dbgdump(0, QK[:, 0, 0:128], 128, [QKt])
                        dbgdump(1, QK[:, 1, 0:128], 128, [QKt])
                        dbgdump(2, scT[:], 128, [scTt])
                        dbgdump(3, BO[:, 0, 0:128], 128, [BOt])
                        dbgdump(4, BO[:, 1, 0:128], 128, [BOt])
                        dbgdump(5, Vt[:, 0, :], 256, [Vtt])
                        dbgdump(6, SG[:, 0, 0:128], 128, [SGt])
                        dbgdump(7, r2[:], 128, [r2t])
                    for oc in range(8):
                        pt, ptk = kb.ps()
                        kb.mm(pt[:, 0:n], [(Wob3[:, 2 * hd + ec, oc * 128:(oc + 1) * 128], BO[:, ec, 0:n]) for ec in range(2)],
                              [Wobt, BOt], [ptk])
                        xadd(oc, c0, n, pt, ptk, bi)
                wrelease(4 + hd)
            kb.barrier()
        wrelease(3)

    def cross(l):
        s_mk, s_mv, s_mq, s_mo = (8, 9, 10, 11) if l == 0 else (24, 25, 26, 27)
        norm_all(2 + l)
        with ExitStack() as sc:
            KT = kb.sb(sc, "KT", [128, 8, 256], BF16)
            Vm = kb.sb(sc, "Vm", [128, 2, 1024], BF16)
            KTt, Vmt = Tok(), Tok()
            with ExitStack() as sc1:
                MT = kb.sb(sc1, "MT", [128, 8, 256], F32)
                Mh = kb.sb(sc1, "Mh", [128, 8, 256], BF16)
                kvo = kb.sb(sc1, "kvo", [128, 1024], F32)
                MTt, Mht, kvot = Tok(), Tok(), Tok()
                kb.dma("sp", MT[:], d_memT.rearrange("(kc p) m -> p kc m", p=128), [], [MTt])
                sqm = kb.sb(sc1, "sqm", [128, 8, 256], BF16)
                rsm = kb.sb(sc1, "rsm", [128, 256], F32)
                rmsnorm(MT, MTt, 0, 256, 4 + l, Mh, Mht, 0, sqm, rsm)
                Wk, Wkt = wuse(s_mk)
                Wk3 = w3(Wk, 0, 8, 1024)
                for oc in range(8):
                    pt, ptk = kb.ps()
                    kb.mm(pt[:, 0:256], [(Wk3[:, kc, oc * 128:(oc + 1) * 128], Mh[:, kc, :]) for kc in range(8)],
                          [Wkt, Mht], [ptk])
                    kb.copy(KT[:, oc, :], pt[:, 0:256], [ptk], [KTt], en="act")
                for which, (sl, dst) in enumerate([(s_mk, o_memk), (s_mv, o_memv)]):
                    Wx, Wxt = wuse(sl)
                    Wx3 = w3(Wx, 0, 8, 1024)
                    for mc in range(2):
                        for hf in range(2):
                            pt, ptk = kb.ps()
                            kb.mm(pt[:, :], [(Mh[:, kc, mc * 128:(mc + 1) * 128], Wx3[:, kc, hf * 512:(hf + 1) * 512]) for kc in range(8)],
                                  [Wxt, Mht], [ptk])
                            kb.copy(kvo[:, hf * 512:(hf + 1) * 512], pt[:, :], [ptk], [kvot], en="act")
                            if which == 1:
                                kb.copy(Vm[:, mc, hf * 512:(hf + 1) * 512], kvo[:, hf * 512:(hf + 1) * 512], [kvot], [Vmt], en="dve")
                        kb.dma("sp", dst[l, mc * 128:(mc + 1) * 128, :], kvo[:], [kvot], [kvot])
                    wrelease(sl)
                kb.barrier()
            with ExitStack() as sc2:
                QTh = kb.sb(sc2, "QTh", [128, 2, 2, 512], BF16)
                E = kb.sb(sc2, "E", [128, 2, 2, 512], BF16)
                rden = kb.sb(sc2, "rden", [128, 512], F32)
                AT = kb.sb(sc2, "AT", [128, 8, 512], BF16)
                KbT = kb.sb(sc2, "KbT", [128, 2, 8, 256], BF16)
                Vb = kb.sb(sc2, "Vb", [128, 2, 2, 1024], BF16)
                QTs = kb.sb(sc2, "QTs", [128, 8, 128], BF16)
                Eb = kb.sb(sc2, "Eb", [128, 64], BF16)
                rd = kb.sb(sc2, "rd", [128, 32], F32)
                QTht, Et = [Tok(), Tok()], [Tok(), Tok()]
                rdent, ATt, QTst, Ebt, rdt = [Tok() for _ in range(5)]
                KbTt, Vbt = [Tok(), Tok()], [Tok(), Tok()]
                Wq, Wqt = wuse(s_mq)
                Wo, Wot = wuse(s_mo)
                Wq3, Wo3 = w3(Wq, 0, 8, 1024), w3(Wo, 0, 8, 1024)
                ATs = kb.sb(sc2, "ATs", [128, 8, 128], BF16)
                ATst = Tok()

                def outproj(ATx, ATxt, bi, c0, n):
                    for oc in range(8):
                        pt, ptk = kb.ps()
                        kb.mm(pt[:, 0:n], [(Wo3[:, kc, oc * 128:(oc + 1) * 128], ATx[:, kc, 0:n]) for kc in range(8)],
                              [Wot, ATxt], [ptk])
                        xadd(oc, c0, n, pt, ptk, bi)

                def load_batch(b):
                    pb = b % 2
                    kb.dma("pool", KbT[:, pb], d_cmkT[l, b].rearrange("(kc p) m -> p kc m", p=128), [], [KbTt[pb]])
                    kb.dma("pool", Vb[:, pb], d_cmv[l, b].rearrange("(mc p) e -> p mc e", p=128), [], [Vbt[pb]])

                def head_A(bi, c0, n, hd):
                    hb = hd % 2
                    for dc in range(2):
                        pt, ptk = kb.ps()
                        kb.mm(pt[:, 0:n], [(Wq3[:, kc, (2 * hd + dc) * 128:(2 * hd + dc + 1) * 128], H[:, kc, c0:c0 + n]) for kc in range(8)],
                              [Wqt, Ht[bi]], [ptk])
                        kb.copy(QTh[:, hb, dc, 0:n], pt[:, 0:n], [ptk], [QTht[hb]], en="act")

                def head_B(bi, c0, n, hd):
                    hb = hd % 2
                    for mc in range(2):
                        pt, ptk = kb.ps()
                        kb.mm(pt[:, 0:n], [(KT[:, 2 * hd + dc, mc * 128:(mc + 1) * 128], QTh[:, hb, dc, 0:n]) for dc in range(2)],
                              [KTt, QTht[hb]], [ptk])
                        kb.act(E[:, hb, mc, 0:n], pt[:, 0:n], AF.Exp, [ptk], [Et[hb]], scale=1.0 / 16.0)

                def head_C(bi, c0, n, hd):
                    hb = hd % 2
                    pd, pdt = kb.ps()
                    kb.mm(pd[:, 0:n], [(o1, E[:, hb, mc, 0:n]) for mc in range(2)], [Et[hb], Ct], [pdt])
                    kb.act(rden[:, 0:n], pd[:, 0:n], AF.Ln, [pdt], [rdent])
                    kb.act(rden[:, 0:n], rden[:, 0:n], AF.Exp, [rdent], [rdent], scale=-1.0)
                    for ec in range(2):
                        pt, ptk = kb.ps()
                        kb.mm(pt[:, 0:n], [(Vm[:, mc, hd * 256 + ec * 128:hd * 256 + (ec + 1) * 128], E[:, hb, mc, 0:n]) for mc in range(2)],
                              [Vmt, Et[hb]], [ptk])
                        kb.tt(AT[:, 2 * hd + ec, 0:n], pt[:, 0:n], rden[:, 0:n], ALU.mult, [ptk, rdent], [ATt])

                def sample_S1(b):
                    pb = b % 2
                    pS, pSt = kb.ps()
                    for hd in range(4):
                        for mc in range(2):
                            r0 = (hd * 2 + mc) * 8
                            kb.mm(pS[:, r0:r0 + 8],
                                  [(KbT[:, pb, 2 * hd + dc, mc * 128:(mc + 1) * 128], QTs[:, 2 * hd + dc, b * 8:b * 8 + 8]) for dc in range(2)],
                                  [KbTt[pb], QTst], [pSt])
                    kb.act(Eb[:], pS[:, 0:64], AF.Exp, [pSt], [Ebt], scale=1.0 / 16.0)

                def sample_S2(b):
                    pb = b % 2
                    pO, pOt = kb.ps()
                    for hd in range(4):
                        for ec in range(2):
                            r0 = (hd * 2 + ec) * 8
                            kb.mm(pO[:, r0:r0 + 8],
                                  [(Vb[:, pb, mc, hd * 256 + ec * 128:hd * 256 + (ec + 1) * 128], Eb[:, (hd * 2 + mc) * 8:(hd * 2 + mc) * 8 + 8]) for mc in range(2)],
                                  [Vbt[pb], Ebt], [pOt])
                        kb.mm(pO[:, 64 + hd * 8:64 + hd * 8 + 8],
                              [(o1, Eb[:, (hd * 2 + mc) * 8:(hd * 2 + mc) * 8 + 8]) for mc in range(2)],
                              [Ebt, Ct], [pOt])
                    kb.act(rd[:], pO[:, 64:96], AF.Ln, [pOt], [rdt])
                    kb.act(rd[:], rd[:], AF.Exp, [rdt], [rdt], scale=-1.0)
                    for ec in range(2):
                        kb.tt(ATs[:, :, b * 8:b * 8 + 8].rearrange("p (h e) c -> p h e c", e=2)[:, :, ec, :],
                              pO[:, 0:64].rearrange("p (h e c) -> p h e c", e=2, c=8)[:, :, ec, :],
                              rd[:].rearrange("p (h c) -> p h c", c=8), ALU.mult, [pOt, rdt], [ATst])

                sc0, sn = BLKS[4]
                for oc in range(8):
                    pt, ptk = kb.ps()
                    kb.mm(pt[:, 0:sn], [(Wq3[:, kc, oc * 128:(oc + 1) * 128], H[:, kc, sc0:sc0 + sn]) for kc in range(8)],
                          [Wqt, Ht[4]], [ptk])
                    kb.copy(QTs[:, oc, :], pt[:, 0:sn], [ptk], [QTst], en="act")
                load_batch(0)
                load_batch(1)
                def unit(u):
                    bi, hd = u // 4, u % 4
                    c0, n = BLKS[bi]
                    return bi, c0, n, hd
                head_A(*unit(0))
                head_B(*unit(0))
                for u in range(16):
                    bi, c0, n, hd = unit(u)
                    if u + 1 < 16:
                        head_A(*unit(u + 1))
                    sample_S1(u)
                    head_C(bi, c0, n, hd)
                    if u + 1 < 16:
                        head_B(*unit(u + 1))
                    if hd == 3:
                        outproj(AT, ATt, bi, c0, n)
                    sample_S2(u)
                    if u + 2 < 16:
                        load_batch(u + 2)
                outproj(ATs, ATst, 4, sc0, sn)
                kb.barrier()
            wrelease(s_mq)
            wrelease(s_mo)

    def mlp(l):
        base = 12 if l == 0 else 28
        with ExitStack() as sc:
            sqn = kb.sb(sc, "sqn", [128, 8, 512], BF16)
            rsn = kb.sb(sc, "rsn", [128, 512], F32)
            for bi_, (c0_, n_) in enumerate(BLKS):
                rmsnorm(X, Xt[bi_], c0_, n_, 6 + l, H, Ht[bi_], c0_, sqn, rsn)
            if l == 1:
                yo = kb.sb(sc, "yo", [128, 8, 512], F32)
                yot = Tok()
                yT_v = o_yT.rearrange("(kc p) t -> p kc t", p=128)
            r32 = kb.sb(sc, "r32", [128, 2, 512], F32)
            hid = kb.sb(sc, "hid", [128, 2, 4, 512], BF16)
            r32t, hidt = [Tok(), Tok()], [Tok(), Tok()]
            if l == 0:
                window_passthrough(sc)
            its = [(fb, bi) for fb in range(8) for bi in range(len(BLKS))]
            wcache = {}

            def getw(fb):
                if fb not in wcache:
                    Wf, Wft = wuse(base + fb)
                    wcache[fb] = (w3(Wf, 0, 8, 512), w3(Wf, 4096, 4, 1024), Wft)
                return wcache[fb]

            def up(i):
                fb, bi = its[i]
                c0, n = BLKS[bi]
                Wup, Wdn, Wft = getw(fb)
                hbuf = i % 2
                for hc in range(4):
                    pt, ptk = kb.ps()
                    kb.mm(pt[:, 0:n], [(Wup[:, kc, hc * 128:(hc + 1) * 128], H[:, kc, c0:c0 + n]) for kc in range(8)],
                          [Wft, Ht[bi]], [ptk])
                    rb = hc % 2
                    kb.act(r32[:, rb, 0:n], pt[:, 0:n], AF.Relu, [ptk], [r32t[rb]])
                    kb.tt(hid[:, hbuf, hc, 0:n], r32[:, rb, 0:n], r32[:, rb, 0:n], ALU.mult, [r32t[rb]], [hidt[hbuf]])

            def down(i):
                fb, bi = its[i]
                c0, n = BLKS[bi]
                Wup, Wdn, Wft = getw(fb)
                hbuf = i % 2
                for oc in range(8):
                    pt, ptk = kb.ps()
                    kb.mm(pt[:, 0:n], [(Wdn[:, hc, oc * 128:(oc + 1) * 128], hid[:, hbuf, hc, 0:n]) for hc in range(4)],
                          [Wft, hidt[hbuf]], [ptk])
                    xadd(oc, c0, n, pt, ptk, bi)
                if l == 1 and fb == 7:
                    rmsnorm(X, Xt[bi], c0, n, 8, yo, yot, 0, sqn, rsn)
                    kb.dma("sp", yT_v[:, :, c0:c0 + n], yo[:, :, 0:n], [yot], [yot])
                if bi == len(BLKS) - 1:
                    wrelease(base + fb)

            up(0)
            for i in range(len(its)):
                if i + 1 < len(its):
                    up(i + 1)
                down(i)
            kb.barrier()

    def layer1_mixer():
        norm_all(1)
        with ExitStack() as sc:
            KTa = kb.sb(sc, "KTa", [128, 2, T], BF16)
            Va = kb.sb(sc, "Va", [128, 17, 256], BF16)
            tb = kb.sb(sc, "tb", [128, 2, 256], F32)
            t1 = kb.sb(sc, "t1", [128, 256], F32)
            t2 = kb.sb(sc, "t2", [128, 256], F32)
            scp1 = ExitStack()
            k32 = kb.sb(scp1, "k32", [128, 2, 128], F32)
            v32 = kb.sb(scp1, "v32", [128, 256], F32)
            KTat = [Tok() for _ in BLKS]
            Vat = [Tok() for _ in BLKS]
            tbt, t1t, t2t, k32t, v32t = [Tok() for _ in range(5)]
            Wkv, Wkvt = wuse(20)
            Wkv3 = w3(Wkv, 0, 8, 768)
            for (c0, n) in BLK2:
                bi = min(c0 // 512, 4)
                kb.dma("sp", tb[:, :, 0:n], d_rt1[:, :, c0:c0 + n], [], [tbt])
                for kc in range(2):
                    pa, pat = kb.ps()
                    pb, pbt = kb.ps()
                    kb.mm(pa[:, 0:n], [(Wkv3[:, k, kc * 128:(kc + 1) * 128], H[:, k, c0:c0 + n]) for k in range(8)],
                          [Wkvt, Ht[bi]], [pat])
                    kb.mm(pb[:, 0:n], [(Wkv3[:, k, 256 + kc * 128:256 + (kc + 1) * 128], H[:, k, c0:c0 + n]) for k in range(8)],
                          [Wkvt, Ht[bi]], [pbt])
                    kb.stt(t1[:, 0:n], pa[:, 0:n], cf[:, BK + kc:BK + kc + 1], tb[:, 0, 0:n], ALU.add, ALU.mult, [pat, tbt, Ct], [t1t])
                    kb.stt(t2[:, 0:n], pb[:, 0:n], cf[:, BKS + kc:BKS + kc + 1], tb[:, 1, 0:n], ALU.add, ALU.mult, [pbt, tbt, Ct], [t2t])
                    kb.tt(KTa[:, kc, c0:c0 + n], t1[:, 0:n], t2[:, 0:n], ALU.add, [t1t, t2t], [KTat[bi]])
                    if c0 >= 1792:
                        lo = n - 128
                        kb.tt(k32[:, kc, :], t1[:, lo:n], t2[:, lo:n], ALU.add, [t1t, t2t], [k32t])
                if c0 >= 1792:
                    kb.dma("sp", o_wkT[:, (c0 - 1792) // 256, :].rearrange("p (k t) -> p k t", k=2), k32[:], [k32t], [k32t])
                for ch in range(n // 128):
                    t0 = c0 + ch * 128
                    ci = t0 // 128
                    pt, ptk = kb.ps()
                    kb.mm(pt[:, 0:256], [(H[:, k, t0:t0 + 128], Wkv3[:, k, 512:768]) for k in range(8)] + [(onerow, bvrow)],
                          [Wkvt, Ht[bi], Ct], [ptk])
                    kb.copy(Va[:, ci, :], pt[:, 0:256], [ptk], [Vat[bi]], en="act")
                    if ci == 15 or ci == 16:
                        kb.copy(v32[:], pt[:, 0:256], [ptk, Vat[bi]], [v32t], en="dve")
                        kb.dma("sp", o_wvp if ci == 15 else o_wvs, v32[:], [v32t], [v32t])
            wrelease(20)
            kb.barrier()
            scp1.close()
            QTc = kb.sb(sc, "QTc", [128, 4, 2, 256], BF16)
            Ee = kb.sb(sc, "Ee", [128, 5, 512], BF16)
            rr = kb.sb(sc, "rr", [128, 128], F32)
            AT = kb.sb(sc, "AT", [128, 8, 256], BF16)
            KcT = kb.sb(sc, "KcT", [128, 16, 256], BF16)
            Vc = kb.sb(sc, "Vc", [128, 16, 256], BF16)
            QTct, Eet = [Tok(), Tok(), Tok(), Tok()], [Tok() for _ in range(5)]
            kb.op("dve", [], QTct, lambda h: h.memset(QTc[:], 0.0))
            rrt, ATt, KcTt, Vct = [Tok() for _ in range(4)]
            kb.dma("pool", KcT[:], d_cwkT.rearrange("b p k s -> p b (k s)"), [], [KcTt])
            kb.dma("pool", Vc[:], d_cwv.rearrange("b s e -> s b e"), [], [Vct])
            Wq, Wqt = wuse(21)
            Wqs, Wqst = wuse(22)
            Wo, Wot = wuse(23)
            Wq3, Wqs3, Wo3 = w3(Wq, 0, 8, 1024), w3(Wqs, 0, 8, 1024), w3(Wo, 0, 8, 1024)
            def pair_gen(c0, n, bi, c, smp):
                kcx = c // 4
                qb = c % 4
                pa, pat = kb.ps(hold=True)
                pb, pbt = kb.ps(hold=True)
                kb.mm(pa[:, 0:n], [(Wq3[:, k, c * 128:(c + 1) * 128], H[:, k, c0:c0 + n]) for k in range(8)],
                      [Wqt, Ht[bi]], [pat])
                kb.mm(pb[:, 0:n], [(Wqs3[:, k, c * 128:(c + 1) * 128], H[:, k, c0:c0 + n]) for k in range(8)],
                      [Wqst, Ht[bi]], [pbt])
                yield
                kb.stt(t1[:, 0:n], pa[:, 0:n], cf[:, BQ + c:BQ + c + 1], tb[:, 0, 0:n], ALU.add, ALU.mult, [pat, tbt, Ct], [t1t])
                kb.stt(t2[:, 0:n], pb[:, 0:n], cf[:, BQS + c:BQS + c + 1], tb[:, 1, 0:n], ALU.add, ALU.mult, [pbt, tbt, Ct], [t2t])
                kb.pfree(pa)
                kb.pfree(pb)
                kb.tt(QTc[0:64, qb, 0, 0:n], t1[0:64, 0:n], t2[0:64, 0:n], ALU.add, [t1t, t2t], [QTct[qb]])
                kb.tt(QTc[64:128, qb, 1, 0:n], t1[64:128, 0:n], t2[64:128, 0:n], ALU.add, [t1t, t2t], [QTct[qb]])
                for ch in range(n // 128):
                    t0 = c0 + ch * 128
                    ci = t0 // 128
                    cs = slice(ch * 128, (ch + 1) * 128)
                    eb = (c * 2 + ch) % 5
                    pci = max(ci - 1, 0)
                    pbi = min(pci // 4, 4)
                    mask = mS if smp else (mP0 if ci == 0 else mP)
                    pS, pSt = kb.ps(hold=True)
                    rd_toks = [QTct[qb], KTat[bi], Ct] + ([KcTt] if smp else [KTat[pbi]])

                    def emit_scores(h, pS=pS, qb=qb, kcx=kcx, pci=pci, t0=t0, cs=cs, mask=mask, smp=smp):
                        ins = None
                        for e2 in range(2):
                            h.matmul(pS[:, e2 * 256:(e2 + 1) * 256], ident, mask, start=True, stop=False, skip_group_check=True)
                            if not smp:
                                h.matmul(pS[:, e2 * 256:e2 * 256 + 128], KTa[:, kcx, pci * 128:(pci + 1) * 128],
                                         QTc[:, qb, e2, cs], start=False, stop=False, skip_group_check=True)
                            else:
                                for b in range(16):
                                    h.matmul(pS[:, e2 * 256 + b * 8:e2 * 256 + b * 8 + 8], KcT[:, b, kcx * 128:(kcx + 1) * 128],
                                             QTc[:, qb, e2, b * 8:b * 8 + 8], start=False, stop=False, skip_group_check=True)
                            ins = h.matmul(pS[:, e2 * 256 + 128:e2 * 256 + 256], KTa[:, kcx, t0:t0 + 128],
                                           QTc[:, qb, e2, cs], start=False, stop=True, skip_group_check=True)
                        return ins
                    kb.op("pe", rd_toks, [pSt], emit_scores)
                    yield
                    kb.act(Ee[:, eb, :], pS[:, :], AF.Exp, [pSt], [Eet[eb]], scale=0.125)
                    kb.pfree(pS)
                    pO, pOt = kb.ps(hold=True)
                    for e2 in range(2):
                        if not smp:
                            kb.mm(pO[:, e2 * 128:(e2 + 1) * 128],
                                  [(Va[:, pci, kcx * 128:(kcx + 1) * 128], Ee[:, eb, e2 * 256:e2 * 256 + 128]),
                                   (Va[:, ci, kcx * 128:(kcx + 1) * 128], Ee[:, eb, e2 * 256 + 128:e2 * 256 + 256])],
                                  [Vat[pbi], Vat[bi], Eet[eb]], [pOt])
                        else:
                            kb.mm(pO[:, e2 * 128:(e2 + 1) * 128],
                                  [(Va[:, ci, kcx * 128:(kcx + 1) * 128], Ee[:, eb, e2 * 256 + 128:e2 * 256 + 256])],
                                  [Vat[bi], Eet[eb]], [pOt])
                            for b in range(16):
                                kb.op("pe", [Vct, Eet[eb]], [pOt],
                                      lambda h, b=b, e2=e2, eb=eb, kcx=kcx, pO=pO: h.matmul(
                                          pO[:, e2 * 128 + b * 8:e2 * 128 + b * 8 + 8],
                                          Vc[:, b, kcx * 128:(kcx + 1) * 128],
                                          Ee[:, eb, e2 * 256 + b * 8:e2 * 256 + b * 8 + 8],
                                          start=False, stop=True, skip_group_check=True))
                    kb.mm(pO[:, 256:384],
                          [(olo, Ee[:, eb, 0:128]), (olo, Ee[:, eb, 128:256]),
                           (ohi, Ee[:, eb, 256:384]), (ohi, Ee[:, eb, 384:512])],
                          [Eet[eb], Ct], [pOt])
                    yield
                    kb.act(rr[:], pO[:, 256:384], AF.Ln, [pOt, Ct], [rrt], bias=cf[:, ESK + c:ESK + c + 1], scale=1.0)
                    kb.act(rr[:], rr[:], AF.Exp, [rrt], [rrt], scale=-1.0)
                    for e2 in range(2):
                        rows = slice(e2 * 64, (e2 + 1) * 64)
                        kb.tt(AT[rows, c, cs], pO[rows, e2 * 128:(e2 + 1) * 128], rr[rows, :], ALU.mult, [pOt, rrt], [ATt])
                    kb.pfree(pO)

            for (c0, n) in BLK2:
                bi = min(c0 // 512, 4)
                smp = bi == 4
                kb.dma("sp", tb[:, :, 0:n], d_rt1[:, :, c0:c0 + n], [], [tbt])
                kb.pipeline([pair_gen(c0, n, bi, c, smp) for c in range(8)], 4)
                for oc in range(8):
                    pt, ptk = kb.ps()
                    kb.mm(pt[:, 0:n], [(Wo3[:, k, oc * 128:(oc + 1) * 128], AT[:, k, 0:n]) for k in range(8)],
                          [Wot, ATt], [ptk])
                    kb.stt(X[:, oc, c0:c0 + n], pt[:, 0:n], cf[:, BO + oc:BO + oc + 1], X[:, oc, c0:c0 + n],
                           ALU.add, ALU.add, [ptk, Xt[bi], Ct], [Xt[bi]])
            kb.barrier()
        wrelease(21)
        wrelease(22)
        wrelease(23)

    import os
    KST = int(os.environ.get("KSTAGES", "99"))
    if KST >= 1:
        layer0_mixer()
    if KST >= 2:
        cross(0)
    if KST >= 3:
        mlp(0)
    if KST >= 4:
        layer1_mixer()
    if KST >= 5:
        cross(1)
        mlp(1)

    with ExitStack() as sc:
        if KST < 5:
            yo = kb.sb(sc, "yo", [128, 8, 512], F32)
            sq = kb.sb(sc, "sq", [128, 8, 512], BF16)
            rs = kb.sb(sc, "rs", [128, 512], F32)
            yot = Tok()
            yT_v = o_yT.rearrange("(kc p) t -> p kc t", p=128)
            for bi, (c0, n) in enumerate(BLKS):
                rmsnorm(X, Xt[bi], c0, n, 8, yo, yot, 0, sq, rs)
                kb.dma("sp", yT_v[:, :, c0:c0 + n], yo[:, :, 0:n], [yot], [yot])
        sp = kb.eng["sp"]
        deps = []
        for q in kb.rings:
            for s in kb.rings[q]:
                if s.count:
                    deps.append((s, s.count))
        kb.sync(sp, deps)
        kb.barrier()
    es.close()
    return nc


def _slot(w):
    K, N = w.shape
    return np.ascontiguousarray(w.reshape(K // 128, 128, N).transpose(1, 0, 2).reshape(128, -1))


def _pad_slot(a):
    out = np.zeros((128, SLOT), np.float32)
    out[:, :a.shape[1]] = a
    return out


def _const_tables():
    f32 = np.float32
    cbf = np.zeros((128, 1792), f32)
    cbf[:, 0:128] = np.eye(128)
    cbf[:, 128:256] = 1.0 / 1024.0
    cbf[:, 256:384] = 1.0 / 256.0
    cbf[:, 384:512] = 1.0
    cbf[:, 512:576] = 1.0
    cbf[:, 704:768] = 1.0
    j = np.arange(128)[:, None]
    i = np.arange(128)[None, :]
    A = (j <= i).astype(f32)
    cbf[:, 768:896] = A
    cbf[:, 896:1024] = ((j // 8 == i // 8) & (j <= i)).astype(f32)
    cbf[:, 1024:1152] = 1.0 - A
    cbf[:, 1152:1280] = A
    cbf[:, 1408:1536] = A
    cbf[:, 1536:1664] = (j > (i % 8)).astype(f32)
    cbf[:, 1664:1792] = cbf[:, 896:1024]
    cbf[:, 1024:1792] = (cbf[:, 1024:1792] - 1.0) * 30000.0
    bsel = np.zeros((128, 16), f32)
    for b in range(16):
        bsel[b * 8:(b + 1) * 8, b] = 1.0
    pos = np.concatenate([np.arange(TP), np.tile(16384 + np.arange(8), 16)]).astype(np.int32)
    ci = np.concatenate([np.arange(TP) % 128, np.tile(np.arange(8), 16)]).astype(np.float64)
    half = 64
    inv = (f32(10000.0) ** (-np.arange(half, dtype=f32) / f32(half))).astype(f32)
    ang = pos.astype(f32)[:, None] * inv[None, :]
    cs = np.cos(ang).astype(np.float64).T
    sn = np.sin(ang).astype(np.float64).T
    cosd = np.concatenate([cs, cs], 0)
    sind = np.concatenate([-sn, sn], 0)
    rt0 = np.zeros((4, 128, 4, T), f32)
    for h in range(4):
        lg = math.log1p(-2.0 ** (-5.0 - h))
        xi = np.exp((ci + 1.0) * lg)[None, :]
        kk = (1.0 / xi) * (128.0 ** -0.5)
        rt0[h, :, 0] = cosd * xi
        rt0[h, :, 1] = sind * xi
        rt0[h, :, 2] = cosd * kk
        rt0[h, :, 3] = sind * kk
    half = 32
    inv1 = (f32(150000.0) ** (-np.arange(half, dtype=f32) / f32(half))).astype(f32)
    ang1 = pos.astype(f32)[:, None] * inv1[None, :]
    c1 = np.cos(ang1).astype(f32).T
    s1 = np.sin(ang1).astype(f32).T
    rt1 = np.zeros((128, 2, T), f32)
    rt1[:, 0] = np.concatenate([c1, c1, c1, c1], 0)
    rt1[:, 1] = np.concatenate([-s1, s1, -s1, s1], 0)
    return cbf, bsel, rt0, rt1


def _prep_shared(inp):
    f32 = np.float32
    g = lambda k: np.asarray(inp[k], f32)
    w_in = g("w_in_e")[0]
    slots = []
    slots.append(_slot(w_in[:, 0:1024]))
    slots.append(_slot(w_in[:, 1024:2048]))
    w_out = g("w_out_e")[0]
    slots.append(_slot(w_out[0:1024]))
    slots.append(_slot(w_out[1024:2048]))
    sw = np.concatenate([np.arange(64, 128), np.arange(0, 64)])
    for h in range(4):
        q = w_in[:, 2048 + h * 128:2048 + (h + 1) * 128]
        k = w_in[:, 2560 + h * 128:2560 + (h + 1) * 128]
        v = w_in[:, 3072 + h * 256:3072 + (h + 1) * 256]
        gt = w_in[:, 4096 + h * 256:4096 + (h + 1) * 256]
        slots.append(_slot(np.concatenate([q, q[:, sw], k, k[:, sw], v, gt], 1)))
    def cross_slots(l):
        return [_slot(g("w_mk")[l]), _slot(g("w_mv")[l]), _slot(g("w_mq")[l]), _slot(g("w_mo")[l])]
    def mlp_slots(l):
        up, dn = g("w_up")[l], g("w_down")[l]
        return [np.concatenate([_slot(up[:, fb * 512:(fb + 1) * 512]), _slot(dn[fb * 512:(fb + 1) * 512, :])], 1)
                for fb in range(8)]
    slots += cross_slots(0) + mlp_slots(0)
    wqkv = g("w_qkv_o")[0]
    bqkv = g("b_qkv_o")[0]
    sw64 = np.concatenate([np.arange(32, 64), np.arange(0, 32)])
    qcols = np.concatenate([np.arange(h * 64, (h + 1) * 64) for h in PERM_HEADS])
    qscols = np.concatenate([h * 64 + sw64 for h in PERM_HEADS])
    kcols = 1024 + np.arange(256)
    kscols = 1024 + np.concatenate([h * 64 + sw64 for h in range(4)])
    vcols = 1280 + np.arange(256)
    slots.append(_pad_slot(_slot(wqkv[:, np.concatenate([kcols, kscols, vcols])])))
    slots.append(_slot(wqkv[:, qcols]))
    slots.append(_slot(wqkv[:, qscols]))
    slots.append(_slot(g("w_out_o")[0][qcols, :]))
    slots += cross_slots(1)
    slots += mlp_slots(1)
    assert len(slots) == 36
    wslots = np.stack(slots, 0)
    cf = np.zeros((128, 109), f32)
    cf[:, 108] = EPS
    gl = [g("g_mix")[0], g("g_mix")[1], g("g_cross")[0], g("g_cross")[1], g("g_mem")[0], g("g_mem")[1],
          g("g_ffn")[0], g("g_ffn")[1], g("g_final")]
    for i, v in enumerate(gl):
        cf[:, i * 8:(i + 1) * 8] = v.reshape(8, 128).T
    cf[:, 72:80] = bqkv[qcols].reshape(8, 128).T
    cf[:, 80:88] = bqkv[qscols].reshape(8, 128).T
    cf[:, 88:90] = bqkv[kcols].reshape(2, 128).T
    cf[:, 90:92] = bqkv[kscols].reshape(2, 128).T
    cf[:, 92:100] = g("b_out_o")[0].reshape(8, 128).T
    sk = g("sinks")[0]
    for c in range(8):
        cf[0:64, 100 + c] = sk[PERM_HEADS[2 * c]]
        cf[64:128, 100 + c] = sk[PERM_HEADS[2 * c + 1]]
    crow = np.zeros((1, 1408), f32)
    bs = g("b_spatial")[0]
    crow[0, 0:512] = bs.reshape(-1)
    crow[0, 512:1024] = np.tile(bs[:, 0:8], (1, 16)).reshape(-1)
    crow[0, 1024:1280] = bqkv[vcols]
    crow[0, 1280:1408] = 1.0
    lngb = np.stack([np.broadcast_to(g("sgu_ln_g")[0], (128, 1024)), np.broadcast_to(g("sgu_ln_b")[0], (128, 1024))], 0)
    ws = g("w_spatial")[0]
    wst = np.zeros((128, 8, 128), f32)
    for gg in range(4):
        wst[:, gg, :] = ws[gg].T
        for b in range(16):
            wst[b * 8:(b + 1) * 8, 4 + gg, b * 8:(b + 1) * 8] = ws[gg, 0:8, 0:8].T
    return wslots, cf, crow, np.ascontiguousarray(lngb), wst


def make_in_maps(inp, cores=range(NCORES)):
    f32 = np.float32
    cbf, bsel, rt0, rt1 = _const_tables()
    wslots, cf, crow, lngb, wst = _prep_shared(inp)
    in_maps = []
    for c in cores:
        bs = slice(16 * c, 16 * c + 16)
        xs = np.asarray(inp["x_sample"][bs], f32).reshape(128, D)
        xT = np.ascontiguousarray(np.concatenate([np.asarray(inp["x_prompt"][c], f32), xs], 0).T)
        cmk = np.asarray(inp["cache_mem_k"][:, bs], f32).reshape(2, 16, 256, D)
        cmv = np.asarray(inp["cache_mem_v"][:, bs], f32).reshape(2, 16, 256, D)
        cwk = np.asarray(inp["cache_win_k"][0, bs], f32).reshape(16, 128, 256)
        cwv = np.asarray(inp["cache_win_v"][0, bs], f32).reshape(16, 128, 256)
        cwkT = np.ascontiguousarray(cwk.reshape(16, 128, 2, 128).transpose(0, 3, 2, 1))
        in_maps.append({
            "xT": xT,
            "memT": np.ascontiguousarray(np.asarray(inp["mem_prompt"][c], f32).T),
            "wslots": wslots,
            "cbf": cbf, "cf": cf, "crow": crow, "lngb": lngb, "wst": wst, "rt0": rt0, "rt1": rt1, "bsel": bsel,
            "state": np.ascontiguousarray(np.asarray(inp["state_ret"][0, bs], f32)),
            "cmkT": np.ascontiguousarray(cmk.transpose(0, 1, 3, 2)),
            "cmv": np.ascontiguousarray(cmv),
            "cwkT": cwkT, "cwv": np.ascontiguousarray(cwv),
            "cwk_raw": np.ascontiguousarray(cwk), "cwv_raw": np.ascontiguousarray(cwv),
        })
    return in_maps


def kernel(**inp):
    f32 = np.float32
    nc = build_program()
    in_maps = make_in_maps(inp)
    res = run_bass_kernel_spmd(nc, in_maps, core_ids=list(range(NCORES)))
    R = res.results
    y_p = np.stack([R[c]["yT"][:, :TP].T for c in range(NCORES)], 0).astype(f32)
    y_s = np.concatenate([R[c]["yT"][:, TP:].T.reshape(16, 8, D) for c in range(NCORES)], 0).astype(f32)
    mem_k = np.stack([R[c]["memk"] for c in range(NCORES)], 1).reshape(2, 8, 256, 4, 256).astype(f32)
    mem_v = np.stack([R[c]["memv"] for c in range(NCORES)], 1).reshape(2, 8, 256, 4, 256).astype(f32)
    ret_p = np.stack([R[c]["retp"] for c in range(NCORES)], 0)[None].astype(f32)
    ret_s = np.concatenate([R[c]["rets"] for c in range(NCORES)], 0)[None].astype(f32)
    sgu_v = np.concatenate([R[c]["sguv"].reshape(16, 8, 4, 256) for c in range(NCORES)], 0)[None].astype(f32)

    def kT_to_tm(a):
        return a.reshape(2, 64, 2, 128).transpose(3, 2, 0, 1).reshape(128, 4, 64)

    wk_p = np.stack([kT_to_tm(R[c]["wkT"][:, 0, :].reshape(128, 2, 128)) for c in range(NCORES)], 0)[None].astype(f32)
    wv_p = np.stack([R[c]["wvp"].reshape(128, 4, 64) for c in range(NCORES)], 0)[None].astype(f32)
    wk_s_l, wv_s_l = [], []
    for c in range(NCORES):
        knew = kT_to_tm(R[c]["wkT"][:, 1, :].reshape(128, 2, 128)).reshape(16, 8, 4, 64)
        vnew = R[c]["wvs"].reshape(16, 8, 4, 64)
        wk_s_l.append(np.concatenate([R[c]["wk_old"].reshape(16, 120, 4, 64), knew], 1))
        wv_s_l.append(np.concatenate([R[c]["wv_old"].reshape(16, 120, 4, 64), vnew], 1))
    wk_s = np.concatenate(wk_s_l, 0)[None].astype(f32)
    wv_s = np.concatenate(wv_s_l, 0)[None].astype(f32)
    return (y_p, y_s, mem_k, mem_v, ret_p, ret_s, sgu_v, wk_p, wv_p, wk_s, wv_s)
```

```python
import math
from contextlib import ExitStack
import numpy as np
import concourse.bass as bass
import concourse.mybir as mybir
from concourse.bass_utils import run_bass_kernel_spmd

F32 = mybir.dt.float32
BF16 = mybir.dt.bfloat16
AF = mybir.ActivationFunctionType
ALU = mybir.AluOpType

NCORES = 8
D = 1024
TP = 2048
TS = 128
T = TP + TS
BLKS = [(0, 512), (512, 512), (1024, 512), (1536, 512), (2048, 128)]
BLK2 = [(i * 256, 256) for i in range(8)] + [(2048, 128)]
EPS = 1e-6
SLOT = 8192
NRING = 3
LIM = 8000
PERM_HEADS = []
for _c in range(8):
    PERM_HEADS += ([_c, 4 + _c] if _c < 4 else [4 + _c, 8 + _c])


class Tok:
    __slots__ = ("w", "r", "const")

    def __init__(self, const=False):
        self.w = None
        self.r = {}
        self.const = const


class SemC:
    def __init__(self, h, owner=None, base=0):
        self.h = h
        self.count = 0
        self.owner = owner
        self.base = base


class Eng:
    def __init__(self, name, h):
        self.name = name
        self.h = h
        self.sems = []
        self.n = 0
        self.seen = {}


class KB:
    def __init__(self, nc):
        self.nc = nc
        self.es = ExitStack()
        self.eng = {n: Eng(n, h) for n, h in [("pe", nc.tensor), ("act", nc.scalar), ("dve", nc.vector),
                                               ("pool", nc.gpsimd), ("sp", nc.sync)]}
        self.nsem = 0
        self.rings = {q: [self.newsem() for _ in range(16)] for q in ("sp", "pool")}
        self.ri = {"sp": 0, "pool": 0}
        self.bar = self.newsem()
        self.psum = []
        self.pi = 0
        for i in range(8):
            t = self.es.enter_context(nc.psum_tensor(f"ps{i}", [128, 512], F32))
            self.psum.append((t, Tok()))

    def newsem(self, owner=None, base=0):
        self.nsem += 1
        h = self.es.enter_context(self.nc.semaphore(f"s{self.nsem}"))
        return SemC(h, owner, base)

    def sb(self, scope, name, shape, dt):
        self.nsb = getattr(self, "nsb", 0) + 1
        return scope.enter_context(self.nc.sbuf_tensor(f"sb{self.nsb}_{name}", shape, dt))

    def ps(self, hold=False):
        held = getattr(self, "held", None)
        if held is None:
            held = self.held = set()
        assert len(held) < 8, "all PSUM banks held"
        while (self.pi % 8) in held:
            self.pi += 1
        i = self.pi % 8
        if hold:
            held.add(i)
        self.pi += 1
        return self.psum[i]

    def ps_release_all(self):
        self.held = set()

    def pfree(self, t):
        for i, (tt_, _) in enumerate(self.psum):
            if tt_ is t:
                self.held.discard(i)

    def pipeline(self, gens, depth):
        gens = list(gens)
        active = []
        nxt = 0
        while nxt < len(gens) or active:
            while len(active) < depth and nxt < len(gens):
                active.append(gens[nxt])
                nxt += 1
            for g in list(active):
                try:
                    next(g)
                except StopIteration:
                    active.remove(g)

    def tick(self, e):
        ep = e.n // LIM
        if ep >= len(e.sems):
            e.sems.append(self.newsem(owner=e, base=ep * LIM))
        s = e.sems[ep]
        v = e.n % LIM + 1
        e.n += 1
        return s, v

    def sync(self, e, deps):
        for d in deps:
            if d is None:
                continue
            s, v = d
            if s.owner is e:
                if e.name == "pe":
                    continue
                if s.base + v < e.n - 1:
                    continue
            if e.seen.get(s, 0) >= v:
                continue
            e.h.wait_ge(s.h, v)
            e.seen[s] = v

    def _deps(self, reads, writes):
        deps = []
        for t in reads:
            deps.append(t.w)
        for t in writes:
            deps.append(t.w)
            deps.extend(t.r.items())
        return deps

    def _update(self, d, reads, writes):
        for t in reads:
            if not t.const:
                if t.r.get(d[0], 0) < d[1]:
                    t.r[d[0]] = d[1]
        for t in writes:
            t.w = d
            t.r = {}

    def op(self, en, reads, writes, fn):
        e = self.eng[en]
        self.sync(e, self._deps(reads, writes))
        ins = fn(e.h)
        d = self.tick(e)
        ins.then_inc(d[0].h, 1)
        self._update(d, reads, writes)

    def dma(self, q, out, in_, reads, writes):
        e = self.eng[q]
        ring = self.rings[q]
        s = ring[self.ri[q] % len(ring)]
        self.ri[q] += 1
        deps = self._deps(reads, writes)
        if s.count:
            deps.append((s, s.count))
        self.sync(e, deps)
        e.h.dma_start(out=out, in_=in_).then_inc(s.h, 16)
        s.count += 16
        self._update((s, s.count), reads, writes)

    def barrier(self):
        sp = self.eng["sp"]
        deps = []
        for q in self.rings:
            for s in self.rings[q]:
                if s.count:
                    deps.append((s, s.count))
        for n in ("pe", "act", "dve"):
            e = self.eng[n]
            if e.n:
                s = e.sems[(e.n - 1) // LIM]
                deps.append((s, (e.n - 1) % LIM + 1))
        self.sync(sp, deps)
        sp.h.sem_inc(self.bar.h, 1)
        self.bar.count += 1
        for n in ("pe", "act", "dve", "pool"):
            self.sync(self.eng[n], [(self.bar, self.bar.count)])

    def mm(self, out, pairs, reads, writes):
        def fn(h):
            ins = None
            n = len(pairs)
            for i, (l, r) in enumerate(pairs):
                ins = h.matmul(out, l, r, start=(i == 0), stop=(i == n - 1))
            return ins
        self.op("pe", reads, writes, fn)

    def act(self, out, in_, func, reads, writes, **kw):
        self.op("act", reads, writes, lambda h: h.activation(out=out, in_=in_, func=func, **kw))

    def tt(self, out, a, b, op, reads, writes, en="dve"):
        self.op(en, reads, writes, lambda h: h.tensor_tensor(out=out, in0=a, in1=b, op=op))

    def ts(self, out, a, s1, s2, op0, op1, reads, writes, en="dve"):
        self.op(en, reads, writes,
                lambda h: h.tensor_scalar(out=out, in0=a, scalar1=s1, scalar2=s2, op0=op0, op1=op1))

    def stt(self, out, a, s, b, op0, op1, reads, writes, en="dve"):
        self.op(en, reads, writes,
                lambda h: h.scalar_tensor_tensor(out=out, in0=a, scalar=s, in1=b, op0=op0, op1=op1))

    def recip(self, out, in_, reads, writes):
        self.op("dve", reads, writes, lambda h: h.reciprocal(out=out, in_=in_))

    def copy(self, out, in_, reads, writes, en="dve"):
        if en == "act":
            self.act(out, in_, AF.Copy, reads, writes)
        else:
            self.op(en, reads, writes, lambda h: h.tensor_copy(out=out, in_=in_))


def build_program():
    nc = bass.Bass("TRN2", target_bir_lowering=False)
    kb = KB(nc)
    es = kb.es

    def din(name, shape):
        return nc.dram_tensor(name, list(shape), F32, kind="ExternalInput").ap()

    def dout(name, shape):
        return nc.dram_tensor(name, list(shape), F32, kind="ExternalOutput").ap()

    d_xT = din("xT", [D, T])
    d_memT = din("memT", [D, 256])
    d_w = din("wslots", [36, 128, SLOT])
    d_cbf = din("cbf", [128, 1792])
    d_cf = din("cf", [128, 109])
    d_crow = din("crow", [1, 1408])
    d_ln = din("lngb", [2, 128, 1024])
    d_wst = din("wst", [128, 8, 128])
    d_rt0 = din("rt0", [4, 128, 4, T])
    d_rt1 = din("rt1", [128, 2, T])
    d_state = din("state", [16, 4, 128, 256])
    d_cmkT = din("cmkT", [2, 16, D, 256])
    d_cmv = din("cmv", [2, 16, 256, D])
    d_cwkT = din("cwkT", [16, 128, 2, 128])
    d_cwv = din("cwv", [16, 128, 256])
    d_cwk_raw = din("cwk_raw", [16, 128, 256])
    d_cwv_raw = din("cwv_raw", [16, 128, 256])

    o_yT = dout("yT", [D, T])
    o_memk = dout("memk", [2, 256, D])
    o_memv = dout("memv", [2, 256, D])
    o_retp = dout("retp", [4, 128, 256])
    o_rets = dout("rets", [16, 4, 128, 256])
    o_sguv = dout("sguv", [128, 1024])
    o_wkT = dout("wkT", [128, 2, 256])
    o_wvp = dout("wvp", [128, 256])
    o_wvs = dout("wvs", [128, 256])
    o_wk_old = dout("wk_old", [16, 120, 256])
    o_wv_old = dout("wv_old", [16, 120, 256])
    import os
    DBG = bool(os.environ.get("KDBG"))
    if DBG:
        o_dbg = dout("dbg", [8, 128, 512])

    def dbgdump(i, ap, n, toks):
        if DBG:
            kb.dma("pool", o_dbg[i, :, 0:n], ap, toks, [])

    X = kb.sb(es, "X", [128, 8, T], F32)
    H = kb.sb(es, "H", [128, 8, T], BF16)
    WR = [kb.sb(es, f"WR{i}", [128, SLOT], BF16) for i in range(NRING)]
    cbf = kb.sb(es, "cbf", [128, 1792], BF16)
    cf = kb.sb(es, "cf", [128, 109], F32)
    crow = kb.sb(es, "crow", [1, 1408], BF16)
    Xt = [Tok() for _ in BLKS]
    Ht = [Tok() for _ in BLKS]
    WRt = [Tok() for _ in range(NRING)]
    Ct = Tok(const=True)
    sqt, rst = Tok(), Tok()

    ident = cbf[:, 0:128]
    o1024 = cbf[:, 128:256]
    o256 = cbf[:, 256:384]
    o1 = cbf[:, 384:512]
    olo = cbf[:, 512:640]
    ohi = cbf[:, 640:768]
    M1 = cbf[:, 768:896]
    M2 = cbf[:, 896:1024]
    mP = cbf[:, 1024:1280]
    mP0 = cbf[:, 1280:1536]
    mS = cbf[:, 1536:1792]

    def gv(i, kc):
        return cf[:, i * 8 + kc:i * 8 + kc + 1]
    BQ, BQS, BK, BKS, BO, ESK, EPSC = 72, 80, 88, 90, 92, 100, 108
    bsP = crow[0:1, 0:512]
    bsS = crow[0:1, 512:1024]
    bvrow = crow[0:1, 1024:1280]
    onerow = crow[0:1, 1280:1408]

    xT_v = d_xT.rearrange("(kc p) t -> p kc t", p=128)
    for bi, (c0, n) in enumerate(BLKS):
        kb.dma("sp", X[:, :, c0:c0 + n], xT_v[:, :, c0:c0 + n], [], [Xt[bi]])
    cft = Tok()
    kb.dma("sp", cf[:], d_cf, [], [cft])
    kb.dma("pool", cbf[:], d_cbf, [], [Ct])
    kb.dma("pool", crow[:], d_crow, [], [Ct])
    bsel = kb.sb(es, "bsel", [128, 16], BF16)
    d_bsel = din("bsel", [128, 16])
    kb.dma("pool", bsel[:], d_bsel, [], [Ct])
    kb.act(cf[:, ESK:ESK + 8], cf[:, ESK:ESK + 8], AF.Exp, [cft], [cft])
    kb.barrier()
    Ct.w = None

    def window_passthrough(scope):
        pas = kb.sb(scope, "pas", [120, 16, 256], F32)
        past = Tok()
        for src, dst in ((d_cwk_raw, o_wk_old), (d_cwv_raw, o_wv_old)):
            kb.dma("sp", pas[:], src[:, 8:128, :].rearrange("b s e -> s b e"), [], [past])
            kb.dma("sp", dst.rearrange("b s e -> s b e"), pas[:], [past], [past])

    wstate = {"next": 0, "free": list(range(NRING)), "loaded": {}}

    def wprefetch():
        while wstate["free"] and wstate["next"] < 36:
            k = wstate["next"]
            ph = wstate["free"].pop(0)
            kb.dma("pool", WR[ph][:], d_w[k], [], [WRt[ph]])
            wstate["loaded"][k] = ph
            wstate["next"] += 1

    def wuse(k):
        while k not in wstate["loaded"]:
            assert wstate["free"], "weight ring exhausted"
            wprefetch()
        ph = wstate["loaded"][k]
        return WR[ph], WRt[ph]

    def wrelease(k):
        ph = wstate["loaded"].pop(k)
        wstate["free"].append(ph)
        wprefetch()

    def w3(W, off, kc, n):
        return W[:, off:off + kc * n].rearrange("p (k n) -> p k n", k=kc)

    def rmsnorm(Xs, Xtok, c0, n, gi, Hs, Htok, hc0, sq, rs):
        kb.act(sq[:, :, 0:n], Xs[:, :, c0:c0 + n], AF.Square, [Xtok], [sqt])
        pt, ptk = kb.ps()
        kb.mm(pt[:, 0:n], [(o1024, sq[:, kc, 0:n]) for kc in range(8)], [sqt, Ct], [ptk])
        kb.act(rs[:, 0:n], pt[:, 0:n], AF.Ln, [ptk, Ct], [rst], bias=cf[:, EPSC:EPSC + 1], scale=1.0)
        kb.act(rs[:, 0:n], rs[:, 0:n], AF.Exp, [rst], [rst], scale=-0.5)
        for kc in range(8):
            kb.stt(Hs[:, kc, hc0:hc0 + n], Xs[:, kc, c0:c0 + n], gv(gi, kc), rs[:, 0:n],
                   ALU.mult, ALU.mult, [Xtok, rst, Ct], [Htok])

    def norm_all(gi):
        with ExitStack() as scn:
            sq = kb.sb(scn, "sq", [128, 8, 512], BF16)
            rs = kb.sb(scn, "rs", [128, 512], F32)
            for bi, (c0, n) in enumerate(BLKS):
                rmsnorm(X, Xt[bi], c0, n, gi, H, Ht[bi], c0, sq, rs)
            kb.barrier()

    def xadd(oc, c0, n, pt, ptk, bi):
        kb.tt(X[:, oc, c0:c0 + n], X[:, oc, c0:c0 + n], pt[:, 0:n], ALU.add, [ptk, Xt[bi]], [Xt[bi]])

    def layer0_mixer():
        wprefetch()
        norm_all(0)
        with ExitStack() as sc:
            lng = kb.sb(sc, "lng", [128, 1024], F32)
            lnb = kb.sb(sc, "lnb", [128, 1024], F32)
            wst = kb.sb(sc, "wst", [128, 8, 128], BF16)
            GU = kb.sb(sc, "GU", [128, 8, 512], BF16)
            AO = kb.sb(sc, "AO", [128, 8, 512], BF16)
            zv2 = [kb.sb(sc, f"zv{i}", [128, 1024], F32) for i in range(2)]
            vnb2 = [kb.sb(sc, f"vnb{i}", [128, 1024], BF16) for i in range(2)]
            st62 = [kb.sb(sc, f"st6{i}", [128, 2, 6], F32) for i in range(2)]
            mv2 = [kb.sb(sc, f"mv{i}", [128, 4], F32) for i in range(2)]
            zvt2, vnbt2, stt2, mvt2 = [[Tok(), Tok()] for _ in range(4)]
            lt, wstt, GUt, AOt = [Tok() for _ in range(4)]
            kb.dma("sp", lng[:], d_ln[0], [], [lt])
            kb.dma("sp", lnb[:], d_ln[1], [], [lt])
            kb.dma("pool", wst[:], d_wst, [], [wstt])
            for g in range(4):
                kb.tt(wst[:, g, :], wst[:, g, :], M1, ALU.mult, [wstt, Ct], [wstt])
                kb.tt(wst[:, 4 + g, :], wst[:, 4 + g, :], M2, ALU.mult, [wstt, Ct], [wstt])
            Wu, Wut = wuse(0)
            Wv, Wvt = wuse(1)
            Wo, Wot = wuse(2)
            Wu3, Wv3, Wo3 = w3(Wu, 0, 8, 1024), w3(Wv, 0, 8, 1024), w3(Wo, 0, 8, 1024)
            def uproj(bi):
                c0, n = BLKS[bi]
                for oc in range(8):
                    pt, ptk = kb.ps()
                    kb.mm(pt[:, 0:n], [(Wu3[:, kc, oc * 128:(oc + 1) * 128], H[:, kc, c0:c0 + n]) for kc in range(8)],
                          [Wut, Ht[bi]], [ptk])
                    kb.act(GU[:, oc, 0:n], pt[:, 0:n], AF.Gelu_apprx_tanh, [ptk], [GUt])

            uproj(0)
            for bi, (c0, n) in enumerate(BLKS):
                smp = bi == 4
                def sgu_chunk(bi, c0, ch, smp):
                    t0 = c0 + ch * 128
                    db = ch % 2
                    zvb, zvtb, vnbb, vnbtb = zv2[db], zvt2[db], vnb2[db], vnbt2[db]
                    st6b, sttb, mvb, mvtb = st62[db], stt2[db], mv2[db], mvt2[db]
                    pv = []
                    for hf in range(2):
                        pt, ptk = kb.ps(hold=True)
                        pv.append((pt, ptk))
                        kb.mm(pt[:, :], [(H[:, kc, t0:t0 + 128], Wv3[:, kc, hf * 512:(hf + 1) * 512]) for kc in range(8)],
                              [Wvt, Ht[bi]], [ptk])
                    yield
                    for hf in range(2):
                        pt, ptk = pv[hf]
                        kb.act(zvb[:, hf * 512:(hf + 1) * 512], pt[:, :], AF.Gelu_apprx_tanh, [ptk], [zvtb])
                        kb.pfree(pt)
                    for hf in range(2):
                        kb.op("dve", [zvtb], [sttb],
                              lambda h, hf=hf: h.bn_stats(out=st6b[:, hf, :], in_=zvb[:, hf * 512:(hf + 1) * 512]))
                    kb.op("dve", [sttb], [mvtb],
                          lambda h: h.bn_aggr(out=mvb[:, 0:2], in_=st6b[:].rearrange("p a b -> p (a b)")))
                    kb.act(mvb[:, 2:3], mvb[:, 1:2], AF.Sqrt, [mvtb], [mvtb], bias=EPS, scale=1.0)
                    kb.recip(mvb[:, 2:3], mvb[:, 2:3], [mvtb], [mvtb])
                    kb.stt(mvb[:, 3:4], mvb[:, 0:1], -1.0, mvb[:, 2:3], ALU.mult, ALU.mult, [mvtb], [mvtb])
                    kb.ts(zvb[:], zvb[:], mvb[:, 2:3], mvb[:, 3:4], ALU.mult, ALU.add, [zvtb, mvtb], [zvtb])
                    kb.tt(zvb[:], zvb[:], lng[:], ALU.mult, [zvtb, lt], [zvtb])
                    if smp:
                        kb.tt(zvb[:], zvb[:], lnb[:], ALU.add, [zvtb, lt], [zvtb])
                        kb.dma("sp", o_sguv, zvb[:], [zvtb], [])
                        kb.copy(vnbb[:], zvb[:], [zvtb], [vnbtb])
                    else:
                        kb.tt(vnbb[:], zvb[:], lnb[:], ALU.add, [zvtb, lt], [vnbtb])
                    yield
                    pA, pAt = kb.ps(hold=True)
                    pB, pBt = kb.ps(hold=True)
                    for oc in range(8):
                        g = oc // 2
                        pp, ppt = (pA, pAt) if oc < 4 else (pB, pBt)
                        wsel = wst[:, (4 + g) if smp else g, :]
                        brow = (bsS if smp else bsP)[0:1, g * 128:(g + 1) * 128]
                        kb.mm(pp[:, (oc % 4) * 128:(oc % 4 + 1) * 128],
                              [(vnbb[:, oc * 128:(oc + 1) * 128], wsel), (onerow, brow)],
                              [vnbtb, wstt, Ct], [ppt])
                    yield
                    kb.tt(AO[:, 0:4, ch * 128:(ch + 1) * 128], GU[:, 0:4, ch * 128:(ch + 1) * 128],
                          pA[:, :].rearrange("p (a b) -> p a b", a=4), ALU.mult, [GUt, pAt], [AOt])
                    kb.tt(AO[:, 4:8, ch * 128:(ch + 1) * 128], GU[:, 4:8, ch * 128:(ch + 1) * 128],
                          pB[:, :].rearrange("p (a b) -> p a b", a=4), ALU.mult, [GUt, pBt], [AOt])
                    kb.pfree(pA)
                    kb.pfree(pB)

                kb.pipeline([sgu_chunk(bi, c0, ch, smp) for ch in range(n // 128)], 2)
                if bi + 1 < len(BLKS):
                    uproj(bi + 1)
                    if bi + 1 == len(BLKS) - 1:
                        wrelease(0)
                else:
                    wrelease(1)
                for oc in range(8):
                    pt, ptk = kb.ps()
                    kb.mm(pt[:, 0:n], [(Wo3[:, kc, oc * 128:(oc + 1) * 128], AO[:, kc, 0:n]) for kc in range(8)],
                          [Wot, AOt], [ptk])
                    xadd(oc, c0, n, pt, ptk, bi)
            kb.barrier()
        wrelease(2)
        import os
        if os.environ.get("KSKIPB"):
            for k in range(3, 8):
                wuse(k)
                wrelease(k)
            return
        Wob, Wobt = wuse(3)
        Wob3 = w3(Wob, 0, 8, 1024)
        with ExitStack() as sc:
            tab = kb.sb(sc, "tab", [128, 4, 512], F32)
            QK = kb.sb(sc, "QK", [128, 2, 512], BF16)
            t1 = kb.sb(sc, "t1", [128, 512], F32)
            t2 = kb.sb(sc, "t2", [128, 512], F32)
            SG = kb.sb(sc, "SG", [128, 2, 512], BF16)
            Vt = kb.sb(sc, "Vt", [128, 4, 256], BF16)
            S32 = kb.sb(sc, "S32", [128, 256], F32)
            st32 = kb.sb(sc, "st32", [128, 8, 256], F32)
            stbf = kb.sb(sc, "stbf", [128, 8, 256], BF16)
            Vblk = kb.sb(sc, "Vblk", [128, 8, 256], BF16)
            (tabt, QKt, t1t, t2t, SGt, Vtt, scTt, kTMt, S32t, Sbft, osqt, r2t, BOt, st32t, stbft,
             Vblkt) = [Tok() for _ in range(16)]
            scT4 = kb.sb(sc, "scT4", [128, 4, 128], BF16)
            kTM4 = kb.sb(sc, "kTM4", [128, 4, 128], BF16)
            Sb5 = kb.sb(sc, "Sb5", [128, 5, 256], BF16)
            osq4 = kb.sb(sc, "osq4", [128, 4, 2, 128], BF16)
            osq4t = [Tok() for _ in range(4)]
            osq2 = osq4
            BO2 = kb.sb(sc, "BO2", [128, 2, 2, 512], BF16)
            scT4t, kTM4t = [Tok() for _ in range(4)], [Tok() for _ in range(4)]
            Sb5t = [Tok() for _ in range(5)]
            osq2t, r22t, BO2t = [Tok(), Tok()], [Tok(), Tok()], [Tok(), Tok()]
            r24 = kb.sb(sc, "r24", [128, 512], F32)
            SGr = kb.sb(sc, "SGr", [128, 2, 512], BF16)
            r24t, SGrt = Tok(), Tok()
            scT, kTM, osq, r2, BO = scT4[:, 0, :], kTM4[:, 0, :], osq4[:, 0], r24[:, 0:128], BO2[:, 0]
            scTt, kTMt, osqt, r2t, BOt = scT4t[0], kTM4t[0], osq4t[0], r24t, BO2t[0]
            Sbf, Sbft = Sb5[:, 0, :], Sb5t[0]

            st32q, stbfq, Vblkq = [Tok(), Tok()], [Tok(), Tok()], [Tok(), Tok()]

            def load_state(hd, hq):
                qb2 = hq % 2
                src = d_state[hq * 4:(hq + 1) * 4, hd].rearrange("b p e -> p b e")
                kb.dma("sp", st32[:, qb2 * 4:qb2 * 4 + 4, :], src, [], [st32q[qb2]])
                kb.dma("pool", stbf[:, qb2 * 4:qb2 * 4 + 4, :], src, [], [stbfq[qb2]])

            def outproj(hd, bi, c0, n, BOb, BObt):
                for oc in range(8):
                    pt, ptk = kb.ps()
                    kb.mm(pt[:, 0:n], [(Wob3[:, 2 * hd + ec, oc * 128:(oc + 1) * 128], BOb[:, ec, 0:n]) for ec in range(2)],
                          [Wobt, BObt], [ptk])
                    xadd(oc, c0, n, pt, ptk, bi)

            for hd in range(4):
                lg = math.log1p(-2.0 ** (-5.0 - hd))
                gP = math.exp(128.0 * lg)
                gS = math.exp(8.0 * lg)
                Wh, Wht = wuse(4 + hd)
                Wh3 = w3(Wh, 0, 8, 1024)
                kb.op("dve", [], [S32t], lambda h: h.memset(S32[:], 0.0))
                kb.op("dve", [], [Sb5t[0]], lambda h: h.memset(Sb5[:, 0, :], 0.0))
                pending = None
                load_state(hd, 0)
                load_state(hd, 1)

                def p0_qk(bi, hd=hd, Wh3=Wh3, Wht=Wht):
                    c0, n = BLKS[bi]
                    kb.dma("sp", tab[:, :, 0:n], d_rt0[hd, :, :, c0:c0 + n], [], [tabt])
                    for qk in range(2):
                        pa, pat = kb.ps()
                        pb, pbt = kb.ps()
                        kb.mm(pa[:, 0:n], [(Wh3[:, kc, qk * 256:qk * 256 + 128], H[:, kc, c0:c0 + n]) for kc in range(8)],
                              [Wht, Ht[bi]], [pat])
                        kb.mm(pb[:, 0:n], [(Wh3[:, kc, qk * 256 + 128:qk * 256 + 256], H[:, kc, c0:c0 + n]) for kc in range(8)],
                              [Wht, Ht[bi]], [pbt])
                        kb.tt(t1[:, 0:n], pa[:, 0:n], tab[:, 2 * qk, 0:n], ALU.mult, [pat, tabt], [t1t])
                        kb.tt(t2[:, 0:n], pb[:, 0:n], tab[:, 2 * qk + 1, 0:n], ALU.mult, [pbt, tabt], [t2t])
                        kb.tt(QK[:, qk, 0:n], t1[:, 0:n], t2[:, 0:n], ALU.add, [t1t, t2t], [QKt])

                def p0_v(bi, Wh3=Wh3, Wht=Wht):
                    c0, n = BLKS[bi]
                    nchk = n // 128
                    for c2 in range(0, nchk, 2):
                        pt, ptk = kb.ps()
                        w2 = min(2, nchk - c2)
                        for k2 in range(w2):
                            t0 = c0 + (c2 + k2) * 128
                            kb.mm(pt[:, k2 * 256:(k2 + 1) * 256], [(H[:, kc, t0:t0 + 128], Wh3[:, kc, 512:768]) for kc in range(8)],
                                  [Wht, Ht[bi]], [ptk])
                        kb.copy(Vt[:, c2:c2 + w2, :].rearrange("p c e -> p (c e)"), pt[:, 0:w2 * 256], [ptk], [Vtt], en="act")

                for bi, (c0, n) in enumerate(BLKS):
                    smp = bi == 4
                    p0_qk(bi)
                    p0_v(bi)

                    def gate_proj():
                        for ec in range(2):
                            pt, ptk = kb.ps()
                            kb.mm(pt[:, 0:n], [(Wh3[:, kc, 768 + ec * 128:768 + (ec + 1) * 128], H[:, kc, c0:c0 + n]) for kc in range(8)],
                                  [Wht, Ht[bi]], [ptk])
                            kb.act(SG[:, ec, 0:n], pt[:, 0:n], AF.Silu, [ptk], [SGt])

                    if not smp:
                        bb = bi % 2
                        BOb, BObt = BO2[:, bb], BO2t[bb]
                        nch = 4
                        ps1, ps1t = kb.ps()
                        pk1, pk1t = kb.ps()
                        for ch in range(nch):
                            cs = slice(ch * 128, (ch + 1) * 128)
                            kb.mm(pk1[:, cs], [(QK[:, 1, cs], ident)], [QKt, Ct], [pk1t])
                        kb.copy(kTM4[:].rearrange("p c t -> p (c t)"), pk1[:, :], [pk1t], kTM4t, en="act")
                        for ch in range(nch):
                            cs = slice(ch * 128, (ch + 1) * 128)
                            kb.mm(ps1[:, cs], [(QK[:, 1, cs], QK[:, 0, cs])], [QKt], [ps1t])
                        kb.tt(scT4[:], ps1[:, :].rearrange("p (c t) -> p c t", c=4),
                              M1.unsqueeze(1).to_broadcast([128, 4, 128]), ALU.mult, [ps1t, Ct], scT4t)
                        gate_proj()
                        puA, puAt = kb.ps(hold=True)
                        puB, puBt = kb.ps(hold=True)
                        PU = [(puA, puAt, 0), (puA, puAt, 256), (puB, puBt, 0), (puB, puBt, 256)]
                        for ch in range(nch):
                            pu, put, o = PU[ch]
                            kb.mm(pu[:, o:o + 256], [(kTM4[:, ch, :], Vt[:, ch, :])], [kTM4t[ch], Vtt], [put])
                        if pending is not None:
                            outproj(*pending)
                            pending = None
                        for ch in range(nch):
                            g = bi * 4 + ch
                            pu, put, o = PU[ch]
                            kb.stt(S32[:], S32[:], gP, pu[:, o:o + 256], ALU.mult, ALU.add, [put, S32t], [S32t])
                            kb.act(Sb5[:, (g + 1) % 5, :], S32[:], AF.Copy, [S32t], [Sb5t[(g + 1) % 5]], scale=gP)
                        kb.pfree(puA)
                        kb.pfree(puB)
                        POs = []
                        for ch in range(nch):
                            g = bi * 4 + ch
                            cs = slice(ch * 128, (ch + 1) * 128)
                            po, pot = kb.ps(hold=True)
                            POs.append((po, pot))
                            for ec in range(2):
                                kb.mm(po[:, ec * 128:(ec + 1) * 128],
                                      [(Vt[:, ch, ec * 128:(ec + 1) * 128], scT4[:, ch, :]),
                                       (Sb5[:, g % 5, ec * 128:(ec + 1) * 128], QK[:, 0, cs])],
                                      [Vtt, scT4t[ch], Sb5t[g % 5], QKt], [pot])
                        for ch in range(nch):
                            po, pot = POs[ch]
                            kb.act(osq4[:, ch].rearrange("p a b -> p (a b)"), po[:, 0:256], AF.Square, [pot], [osq4t[ch]])
                        pn, pnt = kb.ps(hold=True)
                        for ch in range(nch):
                            kb.mm(pn[:, ch * 128:(ch + 1) * 128], [(o256, osq4[:, ch, ec, :]) for ec in range(2)], [osq4t[ch], Ct], [pnt])
                        kb.act(r24[:], pn[:, :], AF.Ln, [pnt], [r24t], bias=cf[:, EPSC:EPSC + 1], scale=1.0)
                        kb.pfree(pn)
                        kb.act(r24[:], r24[:], AF.Exp, [r24t], [r24t], scale=-0.5)
                        kb.tt(SGr[:], SG[:], r24[:].unsqueeze(1).to_broadcast([128, 2, 512]), ALU.mult, [SGt, r24t], [SGrt])
                        for ch in range(nch):
                            po, pot = POs[ch]
                            cs = slice(ch * 128, (ch + 1) * 128)
                            kb.tt(BOb[:, :, cs], po[:, 0:256].rearrange("p (a b) -> p a b", a=2), SGr[:, :, cs], ALU.mult,
                                  [pot, SGrt], [BObt])
                            kb.pfree(po)
                        pending = (hd, bi, c0, n, BOb, BObt)
                        if bi == 3:
                            outproj(*pending)
                            pending = None
                            kb.act(S32[:], S32[:], AF.Copy, [S32t], [S32t], scale=gP)
                            kb.dma("sp", o_retp[hd], S32[:], [S32t], [S32t])
                        continue
                    gate_proj()
                    for ch in range(n // 128):
                        cs = slice(ch * 128, (ch + 1) * 128)
                        pt, ptk = kb.ps()
                        kb.mm(pt[:, 0:128], [(QK[:, 1, cs], QK[:, 0, cs])], [QKt], [ptk])
                        kb.tt(scT[:], pt[:, 0:128], M2 if smp else M1, ALU.mult, [ptk, Ct], [scTt])
                        pk, pkt = kb.ps()
                        kb.mm(pk[:, 0:128], [(QK[:, 1, cs], ident)], [QKt, Ct], [pkt])
                        kb.copy(kTM[:], pk[:, 0:128], [pkt], [kTMt], en="act")
                        if not smp:
                            po, pot = kb.ps()
                            PO = [(po, pot, 0), (po, pot, 128)]
                        else:
                            poA, poAt = kb.ps(hold=True)
                            poB, poBt = kb.ps(hold=True)
                            PO = [(poA, poAt, 0), (poB, poBt, 0)]
                        if not smp:
                            for ec in range(2):
                                kb.mm(po[:, ec * 128:(ec + 1) * 128],
                                      [(Vt[:, ch, ec * 128:(ec + 1) * 128], scT[:]),
                                       (Sbf[:, ec * 128:(ec + 1) * 128], QK[:, 0, cs])],
                                      [Vtt, scTt, Sbft, QKt], [pot])
                        else:
                            for hq in range(4):
                                qb2 = hq % 2
                                s32q = st32[:, qb2 * 4:qb2 * 4 + 4, :]
                                sbfq = stbf[:, qb2 * 4:qb2 * 4 + 4, :]
                                vbq = Vblk[:, qb2 * 4:qb2 * 4 + 4, :]
                                if hq == 0:
                                    for ec in range(2):
                                        kb.mm(PO[ec][0][:, 0:128],
                                              [(Vt[:, ch, ec * 128:(ec + 1) * 128], scT[:])],
                                              [Vtt, scTt], [PO[ec][1]])
                                for b4 in range(4):
                                    b = hq * 4 + b4
                                    for ec in range(2):
                                        kb.op("pe", [stbfq[qb2], QKt], [PO[ec][1]],
                                              lambda h, b=b, b4=b4, ec=ec, PO=PO, sbfq=sbfq: h.matmul(
                                                  PO[ec][0][:, b * 8:b * 8 + 8],
                                                  sbfq[:, b4, ec * 128:(ec + 1) * 128],
                                                  QK[:, 0, b * 8:b * 8 + 8], start=False, stop=True,
                                                  skip_group_check=True))
                                kb.op("dve", [Vtt, Ct], [Vblkq[qb2]],
                                      lambda h, hq=hq, vbq=vbq: h.tensor_tensor(
                                          out=vbq,
                                          in0=Vt[:, ch:ch + 1, :].to_broadcast([128, 4, 256]),
                                          in1=bsel[:, hq * 4:(hq + 1) * 4].unsqueeze(2).to_broadcast([128, 4, 256]), op=ALU.mult))
                                for b2 in range(2):
                                    pu, put = kb.ps()
                                    kb.mm(pu[:, :], [(kTM[:], vbq[:, 2 * b2:2 * b2 + 2, :].rearrange("p a b -> p (a b)"))],
                                          [kTMt, Vblkq[qb2]], [put])
                                    kb.tt(s32q[:, 2 * b2:2 * b2 + 2, :].rearrange("p a b -> p (a b)"),
                                          s32q[:, 2 * b2:2 * b2 + 2, :].rearrange("p a b -> p (a b)"),
                                          pu[:, :], ALU.add, [put, st32q[qb2]], [st32q[qb2]])
                                kb.act(s32q, s32q, AF.Copy, [st32q[qb2]], [st32q[qb2]], scale=gS)
                                kb.dma("sp", o_rets[hq * 4:(hq + 1) * 4, hd].rearrange("b p e -> p b e"), s32q,
                                       [st32q[qb2]], [st32q[qb2]])
                                if hq + 2 < 4:
                                    load_state(hd, hq + 2)
                        if not smp:
                            kb.act(osq[:].rearrange("p a b -> p (a b)"), po[:, 0:256], AF.Square, [pot], [osqt])
                        else:
                            for ec in range(2):
                                kb.act(osq[:, ec, :], PO[ec][0][:, 0:128], AF.Square, [PO[ec][1]], [osqt])
                        pn, pnt = kb.ps()
                        kb.mm(pn[:, 0:128], [(o256, osq[:, ec, :]) for ec in range(2)], [osqt, Ct], [pnt])
                        kb.act(r2[:], pn[:, 0:128], AF.Sqrt, [pnt], [r2t], bias=EPS, scale=1.0)
                        kb.recip(r2[:], r2[:], [r2t], [r2t])
                        for ec in range(2):
                            kb.tt(t1[:, 0:128], PO[ec][0][:, PO[ec][2]:PO[ec][2] + 128], r2[:], ALU.mult, [PO[ec][1], r2t], [t1t])
                            kb.tt(BO[:, ec, cs], t1[:, 0:128], SG[:, ec, cs], ALU.mult, [t1t, SGt], [BOt])
                        kb.ps_release_all()
                        if not smp:
                            pu, put = kb.ps()
                            kb.mm(pu[:, 0:256], [(kTM[:], Vt[:, ch, :])], [kTMt, Vtt], [put])
                            kb.stt(S32[:], S32[:], gP, pu[:, 0:256], ALU.mult, ALU.add, [put, S32t], [S32t])
                            kb.act(Sbf[:], S32[:], AF.Copy, [S32t], [Sbft], scale=gP)
                    if smp and hd == 0:
                        dbgdump(0, QK[:, 0, 0:128], 128, [QKt])
                        dbgdump(1, QK[:, 1, 0:128], 128, [QKt])
                        dbgdump(2, scT[:], 128, [scTt])
                        dbgdump(3, BO[:, 0, 0:128], 128, [BOt])
                        dbgdump(4, BO[:, 1, 0:128], 128, [BOt])
                        dbgdump(5, Vt[:, 0, :], 256, [Vtt])
                        dbgdump(6, SG[:, 0, 0:128], 128, [SGt])
                        dbgdump(7, r2[:], 128, [r2t])
                    for oc in range(8):
                        pt, ptk = kb.ps()
                        kb.mm(pt[:, 0:n], [(Wob3[:, 2 * hd + ec, oc * 128:(oc + 1) * 128], BO[:, ec, 0:n]) for ec in range(2)],
                              [Wobt, BOt], [ptk])
                        xadd(oc, c0, n, pt, ptk, bi)
                wrelease(4 + hd)
            kb.barrier()
        wrelease(3)

    def cross(l):
        s_mk, s_mv, s_mq, s_mo = (8, 9, 10, 11) if l == 0 else (24, 25, 26, 27)
        norm_all(2 + l)
        with ExitStack() as sc:
            KT = kb.sb(sc, "KT", [128, 8, 256], BF16)
            Vm = kb.sb(sc, "Vm", [128, 2, 1024], BF16)
            KTt, Vmt = Tok(), Tok()
            with ExitStack() as sc1:
                MT = kb.sb(sc1, "MT", [128, 8, 256], F32)
                Mh = kb.sb(sc1, "Mh", [128, 8, 256], BF16)
                kvo = kb.sb(sc1, "kvo", [128, 1024], F32)
                MTt, Mht, kvot = Tok(), Tok(), Tok()
                kb.dma("sp", MT[:], d_memT.rearrange("(kc p) m -> p kc m", p=128), [], [MTt])
                sqm = kb.sb(sc1, "sqm", [128, 8, 256], BF16)
                rsm = kb.sb(sc1, "rsm", [128, 256], F32)
                rmsnorm(MT, MTt, 0, 256, 4 + l, Mh, Mht, 0, sqm, rsm)
                Wk, Wkt = wuse(s_mk)
                Wk3 = w3(Wk, 0, 8, 1024)
                for oc in range(8):
                    pt, ptk = kb.ps()
                    kb.mm(pt[:, 0:256], [(Wk3[:, kc, oc * 128:(oc + 1) * 128], Mh[:, kc, :]) for kc in range(8)],
                          [Wkt, Mht], [ptk])
                    kb.copy(KT[:, oc, :], pt[:, 0:256], [ptk], [KTt], en="act")
                for which, (sl, dst) in enumerate([(s_mk, o_memk), (s_mv, o_memv)]):
                    Wx, Wxt = wuse(sl)
                    Wx3 = w3(Wx, 0, 8, 1024)
                    for mc in range(2):
                        for hf in range(2):
                            pt, ptk = kb.ps()
                            kb.mm(pt[:, :], [(Mh[:, kc, mc * 128:(mc + 1) * 128], Wx3[:, kc, hf * 512:(hf + 1) * 512]) for kc in range(8)],
                                  [Wxt, Mht], [ptk])
                            kb.copy(kvo[:, hf * 512:(hf + 1) * 512], pt[:, :], [ptk], [kvot], en="act")
                            if which == 1:
                                kb.copy(Vm[:, mc, hf * 512:(hf + 1) * 512], kvo[:, hf * 512:(hf + 1) * 512], [kvot], [Vmt], en="dve")
                        kb.dma("sp", dst[l, mc * 128:(mc + 1) * 128, :], kvo[:], [kvot], [kvot])
                    wrelease(sl)
                kb.barrier()
            with ExitStack() as sc2:
                QTh = kb.sb(sc2, "QTh", [128, 2, 2, 512], BF16)
                E = kb.sb(sc2, "E", [128, 2, 2, 512], BF16)
                rden = kb.sb(sc2, "rden", [128, 512], F32)
                AT = kb.sb(sc2, "AT", [128, 8, 512], BF16)
                KbT = kb.sb(sc2, "KbT", [128, 2, 8, 256], BF16)
                Vb = kb.sb(sc2, "Vb", [128, 2, 2, 1024], BF16)
                QTs = kb.sb(sc2, "QTs", [128, 8, 128], BF16)
                Eb = kb.sb(sc2, "Eb", [128, 64], BF16)
                rd = kb.sb(sc2, "rd", [128, 32], F32)
                QTht, Et = [Tok(), Tok()], [Tok(), Tok()]
                rdent, ATt, QTst, Ebt, rdt = [Tok() for _ in range(5)]
                KbTt, Vbt = [Tok(), Tok()], [Tok(), Tok()]
                Wq, Wqt = wuse(s_mq)
                Wo, Wot = wuse(s_mo)
                Wq3, Wo3 = w3(Wq, 0, 8, 1024), w3(Wo, 0, 8, 1024)
                ATs = kb.sb(sc2, "ATs", [128, 8, 128], BF16)
                ATst = Tok()

                def outproj(ATx, ATxt, bi, c0, n):
                    for oc in range(8):
                        pt, ptk = kb.ps()
                        kb.mm(pt[:, 0:n], [(Wo3[:, kc, oc * 128:(oc + 1) * 128], ATx[:, kc, 0:n]) for kc in range(8)],
                              [Wot, ATxt], [ptk])
                        xadd(oc, c0, n, pt, ptk, bi)

                def load_batch(b):
                    pb = b % 2
                    kb.dma("pool", KbT[:, pb], d_cmkT[l, b].rearrange("(kc p) m -> p kc m", p=128), [], [KbTt[pb]])
                    kb.dma("pool", Vb[:, pb], d_cmv[l, b].rearrange("(mc p) e -> p mc e", p=128), [], [Vbt[pb]])

                def head_A(bi, c0, n, hd):
                    hb = hd % 2
                    for dc in range(2):
                        pt, ptk = kb.ps()
                        kb.mm(pt[:, 0:n], [(Wq3[:, kc, (2 * hd + dc) * 128:(2 * hd + dc + 1) * 128], H[:, kc, c0:c0 + n]) for kc in range(8)],
                              [Wqt, Ht[bi]], [ptk])
                        kb.copy(QTh[:, hb, dc, 0:n], pt[:, 0:n], [ptk], [QTht[hb]], en="act")

                def head_B(bi, c0, n, hd):
                    hb = hd % 2
                    for mc in range(2):
                        pt, ptk = kb.ps()
                        kb.mm(pt[:, 0:n], [(KT[:, 2 * hd + dc, mc * 128:(mc + 1) * 128], QTh[:, hb, dc, 0:n]) for dc in range(2)],
                              [KTt, QTht[hb]], [ptk])
                        kb.act(E[:, hb, mc, 0:n], pt[:, 0:n], AF.Exp, [ptk], [Et[hb]], scale=1.0 / 16.0)

                def head_C(bi, c0, n, hd):
                    hb = hd % 2
                    pd, pdt = kb.ps()
                    kb.mm(pd[:, 0:n], [(o1, E[:, hb, mc, 0:n]) for mc in range(2)], [Et[hb], Ct], [pdt])
                    kb.act(rden[:, 0:n], pd[:, 0:n], AF.Ln, [pdt], [rdent])
                    kb.act(rden[:, 0:n], rden[:, 0:n], AF.Exp, [rdent], [rdent], scale=-1.0)
                    for ec in range(2):
                        pt, ptk = kb.ps()
                        kb.mm(pt[:, 0:n], [(Vm[:, mc, hd * 256 + ec * 128:hd * 256 + (ec + 1) * 128], E[:, hb, mc, 0:n]) for mc in range(2)],
                              [Vmt, Et[hb]], [ptk])
                        kb.tt(AT[:, 2 * hd + ec, 0:n], pt[:, 0:n], rden[:, 0:n], ALU.mult, [ptk, rdent], [ATt])

                def sample_S1(b):
                    pb = b % 2
                    pS, pSt = kb.ps()
                    for hd in range(4):
                        for mc in range(2):
                            r0 = (hd * 2 + mc) * 8
                            kb.mm(pS[:, r0:r0 + 8],
                                  [(KbT[:, pb, 2 * hd + dc, mc * 128:(mc + 1) * 128], QTs[:, 2 * hd + dc, b * 8:b * 8 + 8]) for dc in range(2)],
                                  [KbTt[pb], QTst], [pSt])
                    kb.act(Eb[:], pS[:, 0:64], AF.Exp, [pSt], [Ebt], scale=1.0 / 16.0)

                def sample_S2(b):
                    pb = b % 2
                    pO, pOt = kb.ps()
                    for hd in range(4):
                        for ec in range(2):
                            r0 = (hd * 2 + ec) * 8
                            kb.mm(pO[:, r0:r0 + 8],
                                  [(Vb[:, pb, mc, hd * 256 + ec * 128:hd * 256 + (ec + 1) * 128], Eb[:, (hd * 2 + mc) * 8:(hd * 2 + mc) * 8 + 8]) for mc in range(2)],
                                  [Vbt[pb], Ebt], [pOt])
                        kb.mm(pO[:, 64 + hd * 8:64 + hd * 8 + 8],
                              [(o1, Eb[:, (hd * 2 + mc) * 8:(hd * 2 + mc) * 8 + 8]) for mc in range(2)],
                              [Ebt, Ct], [pOt])
                    kb.act(rd[:], pO[:, 64:96], AF.Ln, [pOt], [rdt])
                    kb.act(rd[:], rd[:], AF.Exp, [rdt], [rdt], scale=-1.0)
                    for ec in range(2):
                        kb.tt(ATs[:, :, b * 8:b * 8 + 8].rearrange("p (h e) c -> p h e c", e=2)[:, :, ec, :],
                              pO[:, 0:64].rearrange("p (h e c) -> p h e c", e=2, c=8)[:, :, ec, :],
                              rd[:].rearrange("p (h c) -> p h c", c=8), ALU.mult, [pOt, rdt], [ATst])

                sc0, sn = BLKS[4]
                for oc in range(8):
                    pt, ptk = kb.ps()
                    kb.mm(pt[:, 0:sn], [(Wq3[:, kc, oc * 128:(oc + 1) * 128], H[:, kc, sc0:sc0 + sn]) for kc in range(8)],
                          [Wqt, Ht[4]], [ptk])
                    kb.copy(QTs[:, oc, :], pt[:, 0:sn], [ptk], [QTst], en="act")
                load_batch(0)
                load_batch(1)
                def unit(u):
                    bi, hd = u // 4, u % 4
                    c0, n = BLKS[bi]
                    return bi, c0, n, hd
                head_A(*unit(0))
                head_B(*unit(0))
                for u in range(16):
                    bi, c0, n, hd = unit(u)
                    if u + 1 < 16:
                        head_A(*unit(u + 1))
                    sample_S1(u)
                    head_C(bi, c0, n, hd)
                    if u + 1 < 16:
                        head_B(*unit(u + 1))
                    if hd == 3:
                        outproj(AT, ATt, bi, c0, n)
                    sample_S2(u)
                    if u + 2 < 16:
                        load_batch(u + 2)
                outproj(ATs, ATst, 4, sc0, sn)
                kb.barrier()
            wrelease(s_mq)
            wrelease(s_mo)

    def mlp(l):
        base = 12 if l == 0 else 28
        with ExitStack() as sc:
            sqn = kb.sb(sc, "sqn", [128, 8, 512], BF16)
            rsn = kb.sb(sc, "rsn", [128, 512], F32)
            for bi_, (c0_, n_) in enumerate(BLKS):
                rmsnorm(X, Xt[bi_], c0_, n_, 6 + l, H, Ht[bi_], c0_, sqn, rsn)
            if l == 1:
                yo = kb.sb(sc, "yo", [128, 8, 512], F32)
                yot = Tok()
                yT_v = o_yT.rearrange("(kc p) t -> p kc t", p=128)
            r32 = kb.sb(sc, "r32", [128, 2, 512], F32)
            hid = kb.sb(sc, "hid", [128, 2, 4, 512], BF16)
            r32t, hidt = [Tok(), Tok()], [Tok(), Tok()]
            if l == 0:
                window_passthrough(sc)
            its = [(fb, bi) for fb in range(8) for bi in range(len(BLKS))]
            wcache = {}

            def getw(fb):
                if fb not in wcache:
                    Wf, Wft = wuse(base + fb)
                    wcache[fb] = (w3(Wf, 0, 8, 512), w3(Wf, 4096, 4, 1024), Wft)
                return wcache[fb]

            def up(i):
                fb, bi = its[i]
                c0, n = BLKS[bi]
                Wup, Wdn, Wft = getw(fb)
                hbuf = i % 2
                for hc in range(4):
                    pt, ptk = kb.ps()
                    kb.mm(pt[:, 0:n], [(Wup[:, kc, hc * 128:(hc + 1) * 128], H[:, kc, c0:c0 + n]) for kc in range(8)],
                          [Wft, Ht[bi]], [ptk])
                    rb = hc % 2
                    kb.act(r32[:, rb, 0:n], pt[:, 0:n], AF.Relu, [ptk], [r32t[rb]])
                    kb.tt(hid[:, hbuf, hc, 0:n], r32[:, rb, 0:n], r32[:, rb, 0:n], ALU.mult, [r32t[rb]], [hidt[hbuf]])

            def down(i):
                fb, bi = its[i]
                c0, n = BLKS[bi]
                Wup, Wdn, Wft = getw(fb)
                hbuf = i % 2
                for oc in range(8):
                    pt, ptk = kb.ps()
                    kb.mm(pt[:, 0:n], [(Wdn[:, hc, oc * 128:(oc + 1) * 128], hid[:, hbuf, hc, 0:n]) for hc in range(4)],
                          [Wft, hidt[hbuf]], [ptk])
                    xadd(oc, c0, n, pt, ptk, bi)
                if l == 1 and fb == 7:
                    rmsnorm(X, Xt[bi], c0, n, 8, yo, yot, 0, sqn, rsn)
                    kb.dma("sp", yT_v[:, :, c0:c0 + n], yo[:, :, 0:n], [yot], [yot])
                if bi == len(BLKS) - 1:
                    wrelease(base + fb)

            up(0)
            for i in range(len(its)):
                if i + 1 < len(its):
                    up(i + 1)
                down(i)
            kb.barrier()

    def layer1_mixer():
        norm_all(1)
        with ExitStack() as sc:
            KTa = kb.sb(sc, "KTa", [128, 2, T], BF16)
            Va = kb.sb(sc, "Va", [128, 17, 256], BF16)
            tb = kb.sb(sc, "tb", [128, 2, 256], F32)
            t1 = kb.sb(sc, "t1", [128, 256], F32)
            t2 = kb.sb(sc, "t2", [128, 256], F32)
            scp1 = ExitStack()
            k32 = kb.sb(scp1, "k32", [128, 2, 128], F32)
            v32 = kb.sb(scp1, "v32", [128, 256], F32)
            KTat = [Tok() for _ in BLKS]
            Vat = [Tok() for _ in BLKS]
            tbt, t1t, t2t, k32t, v32t = [Tok() for _ in range(5)]
            Wkv, Wkvt = wuse(20)
            Wkv3 = w3(Wkv, 0, 8, 768)
            for (c0, n) in BLK2:
                bi = min(c0 // 512, 4)
                kb.dma("sp", tb[:, :, 0:n], d_rt1[:, :, c0:c0 + n], [], [tbt])
                for kc in range(2):
                    pa, pat = kb.ps()
                    pb, pbt = kb.ps()
                    kb.mm(pa[:, 0:n], [(Wkv3[:, k, kc * 128:(kc + 1) * 128], H[:, k, c0:c0 + n]) for k in range(8)],
                          [Wkvt, Ht[bi]], [pat])
                    kb.mm(pb[:, 0:n], [(Wkv3[:, k, 256 + kc * 128:256 + (kc + 1) * 128], H[:, k, c0:c0 + n]) for k in range(8)],
                          [Wkvt, Ht[bi]], [pbt])
                    kb.stt(t1[:, 0:n], pa[:, 0:n], cf[:, BK + kc:BK + kc + 1], tb[:, 0, 0:n], ALU.add, ALU.mult, [pat, tbt, Ct], [t1t])
                    kb.stt(t2[:, 0:n], pb[:, 0:n], cf[:, BKS + kc:BKS + kc + 1], tb[:, 1, 0:n], ALU.add, ALU.mult, [pbt, tbt, Ct], [t2t])
                    kb.tt(KTa[:, kc, c0:c0 + n], t1[:, 0:n], t2[:, 0:n], ALU.add, [t1t, t2t], [KTat[bi]])
                    if c0 >= 1792:
                        lo = n - 128
                        kb.tt(k32[:, kc, :], t1[:, lo:n], t2[:, lo:n], ALU.add, [t1t, t2t], [k32t])
                if c0 >= 1792:
                    kb.dma("sp", o_wkT[:, (c0 - 1792) // 256, :].rearrange("p (k t) -> p k t", k=2), k32[:], [k32t], [k32t])
                for ch in range(n // 128):
                    t0 = c0 + ch * 128
                    ci = t0 // 128
                    pt, ptk = kb.ps()
                    kb.mm(pt[:, 0:256], [(H[:, k, t0:t0 + 128], Wkv3[:, k, 512:768]) for k in range(8)] + [(onerow, bvrow)],
                          [Wkvt, Ht[bi], Ct], [ptk])
                    kb.copy(Va[:, ci, :], pt[:, 0:256], [ptk], [Vat[bi]], en="act")
                    if ci == 15 or ci == 16:
                        kb.copy(v32[:], pt[:, 0:256], [ptk, Vat[bi]], [v32t], en="dve")
                        kb.dma("sp", o_wvp if ci == 15 else o_wvs, v32[:], [v32t], [v32t])
            wrelease(20)
            kb.barrier()
            scp1.close()
            QTc = kb.sb(sc, "QTc", [128, 4, 2, 256], BF16)
            Ee = kb.sb(sc, "Ee", [128, 5, 512], BF16)
            rr = kb.sb(sc, "rr", [128, 128], F32)
            AT = kb.sb(sc, "AT", [128, 8, 256], BF16)
            KcT = kb.sb(sc, "KcT", [128, 16, 256], BF16)
            Vc = kb.sb(sc, "Vc", [128, 16, 256], BF16)
            QTct, Eet = [Tok(), Tok(), Tok(), Tok()], [Tok() for _ in range(5)]
            kb.op("dve", [], QTct, lambda h: h.memset(QTc[:], 0.0))
            rrt, ATt, KcTt, Vct = [Tok() for _ in range(4)]
            kb.dma("pool", KcT[:], d_cwkT.rearrange("b p k s -> p b (k s)"), [], [KcTt])
            kb.dma("pool", Vc[:], d_cwv.rearrange("b s e -> s b e"), [], [Vct])
            Wq, Wqt = wuse(21)
            Wqs, Wqst = wuse(22)
            Wo, Wot = wuse(23)
            Wq3, Wqs3, Wo3 = w3(Wq, 0, 8, 1024), w3(Wqs, 0, 8, 1024), w3(Wo, 0, 8, 1024)
            def pair_gen(c0, n, bi, c, smp):
                kcx = c // 4
                qb = c % 4
                pa, pat = kb.ps(hold=True)
                pb, pbt = kb.ps(hold=True)
                kb.mm(pa[:, 0:n], [(Wq3[:, k, c * 128:(c + 1) * 128], H[:, k, c0:c0 + n]) for k in range(8)],
                      [Wqt, Ht[bi]], [pat])
                kb.mm(pb[:, 0:n], [(Wqs3[:, k, c * 128:(c + 1) * 128], H[:, k, c0:c0 + n]) for k in range(8)],
                      [Wqst, Ht[bi]], [pbt])
                yield
                kb.stt(t1[:, 0:n], pa[:, 0:n], cf[:, BQ + c:BQ + c + 1], tb[:, 0, 0:n], ALU.add, ALU.mult, [pat, tbt, Ct], [t1t])
                kb.stt(t2[:, 0:n], pb[:, 0:n], cf[:, BQS + c:BQS + c + 1], tb[:, 1, 0:n], ALU.add, ALU.mult, [pbt, tbt, Ct], [t2t])
                kb.pfree(pa)
                kb.pfree(pb)
                kb.tt(QTc[0:64, qb, 0, 0:n], t1[0:64, 0:n], t2[0:64, 0:n], ALU.add, [t1t, t2t], [QTct[qb]])
                kb.tt(QTc[64:128, qb, 1, 0:n], t1[64:128, 0:n], t2[64:128, 0:n], ALU.add, [t1t, t2t], [QTct[qb]])
                for ch in range(n // 128):
                    t0 = c0 + ch * 128
                    ci = t0 // 128
                    cs = slice(ch * 128, (ch + 1) * 128)
                    eb = (c * 2 + ch) % 5
                    pci = max(ci - 1, 0)
                    pbi = min(pci // 4, 4)
                    mask = mS if smp else (mP0 if ci == 0 else mP)
                    pS, pSt = kb.ps(hold=True)
                    rd_toks = [QTct[qb], KTat[bi], Ct] + ([KcTt] if smp else [KTat[pbi]])

                    def emit_scores(h, pS=pS, qb=qb, kcx=kcx, pci=pci, t0=t0, cs=cs, mask=mask, smp=smp):
                        ins = None
                        for e2 in range(2):
                            h.matmul(pS[:, e2 * 256:(e2 + 1) * 256], ident, mask, start=True, stop=False, skip_group_check=True)
                            if not smp:
                                h.matmul(pS[:, e2 * 256:e2 * 256 + 128], KTa[:, kcx, pci * 128:(pci + 1) * 128],
                                         QTc[:, qb, e2, cs], start=False, stop=False, skip_group_check=True)
                            else:
                                for b in range(16):
                                    h.matmul(pS[:, e2 * 256 + b * 8:e2 * 256 + b * 8 + 8], KcT[:, b, kcx * 128:(kcx + 1) * 128],
                                             QTc[:, qb, e2, b * 8:b * 8 + 8], start=False, stop=False, skip_group_check=True)
                            ins = h.matmul(pS[:, e2 * 256 + 128:e2 * 256 + 256], KTa[:, kcx, t0:t0 + 128],
                                           QTc[:, qb, e2, cs], start=False, stop=True, skip_group_check=True)
                        return ins
                    kb.op("pe", rd_toks, [pSt], emit_scores)
                    yield
                    kb.act(Ee[:, eb, :], pS[:, :], AF.Exp, [pSt], [Eet[eb]], scale=0.125)
                    kb.pfree(pS)
                    pO, pOt = kb.ps(hold=True)
                    for e2 in range(2):
                        if not smp:
                            kb.mm(pO[:, e2 * 128:(e2 + 1) * 128],
                                  [(Va[:, pci, kcx * 128:(kcx + 1) * 128], Ee[:, eb, e2 * 256:e2 * 256 + 128]),
                                   (Va[:, ci, kcx * 128:(kcx + 1) * 128], Ee[:, eb, e2 * 256 + 128:e2 * 256 + 256])],
                                  [Vat[pbi], Vat[bi], Eet[eb]], [pOt])
                        else:
                            kb.mm(pO[:, e2 * 128:(e2 + 1) * 128],
                                  [(Va[:, ci, kcx * 128:(kcx + 1) * 128], Ee[:, eb, e2 * 256 + 128:e2 * 256 + 256])],
                                  [Vat[bi], Eet[eb]], [pOt])
                            for b in range(16):
                                kb.op("pe", [Vct, Eet[eb]], [pOt],
                                      lambda h, b=b, e2=e2, eb=eb, kcx=kcx, pO=pO: h.matmul(
                                          pO[:, e2 * 128 + b * 8:e2 * 128 + b * 8 + 8],
                                          Vc[:, b, kcx * 128:(kcx + 1) * 128],
                                          Ee[:, eb, e2 * 256 + b * 8:e2 * 256 + b * 8 + 8],
                                          start=False, stop=True, skip_group_check=True))
                    kb.mm(pO[:, 256:384],
                          [(olo, Ee[:, eb, 0:128]), (olo, Ee[:, eb, 128:256]),
                           (ohi, Ee[:, eb, 256:384]), (ohi, Ee[:, eb, 384:512])],
                          [Eet[eb], Ct], [pOt])
                    yield
                    kb.act(rr[:], pO[:, 256:384], AF.Ln, [pOt, Ct], [rrt], bias=cf[:, ESK + c:ESK + c + 1], scale=1.0)
                    kb.act(rr[:], rr[:], AF.Exp, [rrt], [rrt], scale=-1.0)
                    for e2 in range(2):
                        rows = slice(e2 * 64, (e2 + 1) * 64)
                        kb.tt(AT[rows, c, cs], pO[rows, e2 * 128:(e2 + 1) * 128], rr[rows, :], ALU.mult, [pOt, rrt], [ATt])
                    kb.pfree(pO)

            for (c0, n) in BLK2:
                bi = min(c0 // 512, 4)
                smp = bi == 4
                kb.dma("sp", tb[:, :, 0:n], d_rt1[:, :, c0:c0 + n], [], [tbt])
                kb.pipeline([pair_gen(c0, n, bi, c, smp) for c in range(8)], 4)
                for oc in range(8):
                    pt, ptk = kb.ps()
                    kb.mm(pt[:, 0:n], [(Wo3[:, k, oc * 128:(oc + 1) * 128], AT[:, k, 0:n]) for k in range(8)],
                          [Wot, ATt], [ptk])
                    kb.stt(X[:, oc, c0:c0 + n], pt[:, 0:n], cf[:, BO + oc:BO + oc + 1], X[:, oc, c0:c0 + n],
                           ALU.add, ALU.add, [ptk, Xt[bi], Ct], [Xt[bi]])
            kb.barrier()
        wrelease(21)
        wrelease(22)
        wrelease(23)

    import os
    KST = int(os.environ.get("KSTAGES", "99"))
    if KST >= 1:
        layer0_mixer()
    if KST >= 2:
        cross(0)
    if KST >= 3:
        mlp(0)
    if KST >= 4:
        layer1_mixer()
    if KST >= 5:
        cross(1)
        mlp(1)

    with ExitStack() as sc:
        if KST < 5:
            yo = kb.sb(sc, "yo", [128, 8, 512], F32)
            sq = kb.sb(sc, "sq", [128, 8, 512], BF16)
            rs = kb.sb(sc, "rs", [128, 512], F32)
            yot = Tok()
            yT_v = o_yT.rearrange("(kc p) t -> p kc t", p=128)
            for bi, (c0, n) in enumerate(BLKS):
                rmsnorm(X, Xt[bi], c0, n, 8, yo, yot, 0, sq, rs)
                kb.dma("sp", yT_v[:, :, c0:c0 + n], yo[:, :, 0:n], [yot], [yot])
        sp = kb.eng["sp"]
        deps = []
        for q in kb.rings:
            for s in kb.rings[q]:
                if s.count:
                    deps.append((s, s.count))
        kb.sync(sp, deps)
        kb.barrier()
    es.close()
    return nc


def _slot(w):
    K, N = w.shape
    return np.ascontiguousarray(w.reshape(K // 128, 128, N).transpose(1, 0, 2).reshape(128, -1))


def _pad_slot(a):
    out = np.zeros((128, SLOT), np.float32)
    out[:, :a.shape[1]] = a
    return out


def _const_tables():
    f32 = np.float32
    cbf = np.zeros((128, 1792), f32)
    cbf[:, 0:128] = np.eye(128)
    cbf[:, 128:256] = 1.0 / 1024.0
    cbf[:, 256:384] = 1.0 / 256.0
    cbf[:, 384:512] = 1.0
    cbf[:, 512:576] = 1.0
    cbf[:, 704:768] = 1.0
    j = np.arange(128)[:, None]
    i = np.arange(128)[None, :]
    A = (j <= i).astype(f32)
    cbf[:, 768:896] = A
    cbf[:, 896:1024] = ((j // 8 == i // 8) & (j <= i)).astype(f32)
    cbf[:, 1024:1152] = 1.0 - A
    cbf[:, 1152:1280] = A
    cbf[:, 1408:1536] = A
    cbf[:, 1536:1664] = (j > (i % 8)).astype(f32)
    cbf[:, 1664:1792] = cbf[:, 896:1024]
    cbf[:, 1024:1792] = (cbf[:, 1024:1792] - 1.0) * 30000.0
    bsel = np.zeros((128, 16), f32)
    for b in range(16):
        bsel[b * 8:(b + 1) * 8, b] = 1.0
    pos = np.concatenate([np.arange(TP), np.tile(16384 + np.arange(8), 16)]).astype(np.int32)
    ci = np.concatenate([np.arange(TP) % 128, np.tile(np.arange(8), 16)]).astype(np.float64)
    half = 64
    inv = (f32(10000.0) ** (-np.arange(half, dtype=f32) / f32(half))).astype(f32)
    ang = pos.astype(f32)[:, None] * inv[None, :]
    cs = np.cos(ang).astype(np.float64).T
    sn = np.sin(ang).astype(np.float64).T
    cosd = np.concatenate([cs, cs], 0)
    sind = np.concatenate([-sn, sn], 0)
    rt0 = np.zeros((4, 128, 4, T), f32)
    for h in range(4):
        lg = math.log1p(-2.0 ** (-5.0 - h))
        xi = np.exp((ci + 1.0) * lg)[None, :]
        kk = (1.0 / xi) * (128.0 ** -0.5)
        rt0[h, :, 0] = cosd * xi
        rt0[h, :, 1] = sind * xi
        rt0[h, :, 2] = cosd * kk
        rt0[h, :, 3] = sind * kk
    half = 32
    inv1 = (f32(150000.0) ** (-np.arange(half, dtype=f32) / f32(half))).astype(f32)
    ang1 = pos.astype(f32)[:, None] * inv1[None, :]
    c1 = np.cos(ang1).astype(f32).T
    s1 = np.sin(ang1).astype(f32).T
    rt1 = np.zeros((128, 2, T), f32)
    rt1[:, 0] = np.concatenate([c1, c1, c1, c1], 0)
    rt1[:, 1] = np.concatenate([-s1, s1, -s1, s1], 0)
    return cbf, bsel, rt0, rt1


def _prep_shared(inp):
    f32 = np.float32
    g = lambda k: np.asarray(inp[k], f32)
    w_in = g("w_in_e")[0]
    slots = []
    slots.append(_slot(w_in[:, 0:1024]))
    slots.append(_slot(w_in[:, 1024:2048]))
    w_out = g("w_out_e")[0]
    slots.append(_slot(w_out[0:1024]))
    slots.append(_slot(w_out[1024:2048]))
    sw = np.concatenate([np.arange(64, 128), np.arange(0, 64)])
    for h in range(4):
        q = w_in[:, 2048 + h * 128:2048 + (h + 1) * 128]
        k = w_in[:, 2560 + h * 128:2560 + (h + 1) * 128]
        v = w_in[:, 3072 + h * 256:3072 + (h + 1) * 256]
        gt = w_in[:, 4096 + h * 256:4096 + (h + 1) * 256]
        slots.append(_slot(np.concatenate([q, q[:, sw], k, k[:, sw], v, gt], 1)))
    def cross_slots(l):
        return [_slot(g("w_mk")[l]), _slot(g("w_mv")[l]), _slot(g("w_mq")[l]), _slot(g("w_mo")[l])]
    def mlp_slots(l):
        up, dn = g("w_up")[l], g("w_down")[l]
        return [np.concatenate([_slot(up[:, fb * 512:(fb + 1) * 512]), _slot(dn[fb * 512:(fb + 1) * 512, :])], 1)
                for fb in range(8)]
    slots += cross_slots(0) + mlp_slots(0)
    wqkv = g("w_qkv_o")[0]
    bqkv = g("b_qkv_o")[0]
    sw64 = np.concatenate([np.arange(32, 64), np.arange(0, 32)])
    qcols = np.concatenate([np.arange(h * 64, (h + 1) * 64) for h in PERM_HEADS])
    qscols = np.concatenate([h * 64 + sw64 for h in PERM_HEADS])
    kcols = 1024 + np.arange(256)
    kscols = 1024 + np.concatenate([h * 64 + sw64 for h in range(4)])
    vcols = 1280 + np.arange(256)
    slots.append(_pad_slot(_slot(wqkv[:, np.concatenate([kcols, kscols, vcols])])))
    slots.append(_slot(wqkv[:, qcols]))
    slots.append(_slot(wqkv[:, qscols]))
    slots.append(_slot(g("w_out_o")[0][qcols, :]))
    slots += cross_slots(1)
    slots += mlp_slots(1)
    assert len(slots) == 36
    wslots = np.stack(slots, 0)
    cf = np.zeros((128, 109), f32)
    cf[:, 108] = EPS
    gl = [g("g_mix")[0], g("g_mix")[1], g("g_cross")[0], g("g_cross")[1], g("g_mem")[0], g("g_mem")[1],
          g("g_ffn")[0], g("g_ffn")[1], g("g_final")]
    for i, v in enumerate(gl):
        cf[:, i * 8:(i + 1) * 8] = v.reshape(8, 128).T
    cf[:, 72:80] = bqkv[qcols].reshape(8, 128).T
    cf[:, 80:88] = bqkv[qscols].reshape(8, 128).T
    cf[:, 88:90] = bqkv[kcols].reshape(2, 128).T
    cf[:, 90:92] = bqkv[kscols].reshape(2, 128).T
    cf[:, 92:100] = g("b_out_o")[0].reshape(8, 128).T
    sk = g("sinks")[0]
    for c in range(8):
        cf[0:64, 100 + c] = sk[PERM_HEADS[2 * c]]
        cf[64:128, 100 + c] = sk[PERM_HEADS[2 * c + 1]]
    crow = np.zeros((1, 1408), f32)
    bs = g("b_spatial")[0]
    crow[0, 0:512] = bs.reshape(-1)
    crow[0, 512:1024] = np.tile(bs[:, 0:8], (1, 16)).reshape(-1)
    crow[0, 1024:1280] = bqkv[vcols]
    crow[0, 1280:1408] = 1.0
    lngb = np.stack([np.broadcast_to(g("sgu_ln_g")[0], (128, 1024)), np.broadcast_to(g("sgu_ln_b")[0], (128, 1024))], 0)
    ws = g("w_spatial")[0]
    wst = np.zeros((128, 8, 128), f32)
    for gg in range(4):
        wst[:, gg, :] = ws[gg].T
        for b in range(16):
            wst[b * 8:(b + 1) * 8, 4 + gg, b * 8:(b + 1) * 8] = ws[gg, 0:8, 0:8].T
    return wslots, cf, crow, np.ascontiguousarray(lngb), wst


def make_in_maps(inp, cores=range(NCORES)):
    f32 = np.float32
    cbf, bsel, rt0, rt1 = _const_tables()
    wslots, cf, crow, lngb, wst = _prep_shared(inp)
    in_maps = []
    for c in cores:
        bs = slice(16 * c, 16 * c + 16)
        xs = np.asarray(inp["x_sample"][bs], f32).reshape(128, D)
        xT = np.ascontiguousarray(np.concatenate([np.asarray(inp["x_prompt"][c], f32), xs], 0).T)
        cmk = np.asarray(inp["cache_mem_k"][:, bs], f32).reshape(2, 16, 256, D)
        cmv = np.asarray(inp["cache_mem_v"][:, bs], f32).reshape(2, 16, 256, D)
        cwk = np.asarray(inp["cache_win_k"][0, bs], f32).reshape(16, 128, 256)
        cwv = np.asarray(inp["cache_win_v"][0, bs], f32).reshape(16, 128, 256)
        cwkT = np.ascontiguousarray(cwk.reshape(16, 128, 2, 128).transpose(0, 3, 2, 1))
        in_maps.append({
            "xT": xT,
            "memT": np.ascontiguousarray(np.asarray(inp["mem_prompt"][c], f32).T),
            "wslots": wslots,
            "cbf": cbf, "cf": cf, "crow": crow, "lngb": lngb, "wst": wst, "rt0": rt0, "rt1": rt1, "bsel": bsel,
            "state": np.ascontiguousarray(np.asarray(inp["state_ret"][0, bs], f32)),
            "cmkT": np.ascontiguousarray(cmk.transpose(0, 1, 3, 2)),
            "cmv": np.ascontiguousarray(cmv),
            "cwkT": cwkT, "cwv": np.ascontiguousarray(cwv),
            "cwk_raw": np.ascontiguousarray(cwk), "cwv_raw": np.ascontiguousarray(cwv),
        })
    return in_maps


def kernel(**inp):
    f32 = np.float32
    nc = build_program()
    in_maps = make_in_maps(inp)
    res = run_bass_kernel_spmd(nc, in_maps, core_ids=list(range(NCORES)))
    R = res.results
    y_p = np.stack([R[c]["yT"][:, :TP].T for c in range(NCORES)], 0).astype(f32)
    y_s = np.concatenate([R[c]["yT"][:, TP:].T.reshape(16, 8, D) for c in range(NCORES)], 0).astype(f32)
    mem_k = np.stack([R[c]["memk"] for c in range(NCORES)], 1).reshape(2, 8, 256, 4, 256).astype(f32)
    mem_v = np.stack([R[c]["memv"] for c in range(NCORES)], 1).reshape(2, 8, 256, 4, 256).astype(f32)
    ret_p = np.stack([R[c]["retp"] for c in range(NCORES)], 0)[None].astype(f32)
    ret_s = np.concatenate([R[c]["rets"] for c in range(NCORES)], 0)[None].astype(f32)
    sgu_v = np.concatenate([R[c]["sguv"].reshape(16, 8, 4, 256) for c in range(NCORES)], 0)[None].astype(f32)

    def kT_to_tm(a):
        return a.reshape(2, 64, 2, 128).transpose(3, 2, 0, 1).reshape(128, 4, 64)

    wk_p = np.stack([kT_to_tm(R[c]["wkT"][:, 0, :].reshape(128, 2, 128)) for c in range(NCORES)], 0)[None].astype(f32)
    wv_p = np.stack([R[c]["wvp"].reshape(128, 4, 64) for c in range(NCORES)], 0)[None].astype(f32)
    wk_s_l, wv_s_l = [], []
    for c in range(NCORES):
        knew = kT_to_tm(R[c]["wkT"][:, 1, :].reshape(128, 2, 128)).reshape(16, 8, 4, 64)
        vnew = R[c]["wvs"].reshape(16, 8, 4, 64)
        wk_s_l.append(np.concatenate([R[c]["wk_old"].reshape(16, 120, 4, 64), knew], 1))
        wv_s_l.append(np.concatenate([R[c]["wv_old"].reshape(16, 120, 4, 64), vnew], 1))
    wk_s = np.concatenate(wk_s_l, 0)[None].astype(f32)
    wv_s = np.concatenate(wv_s_l, 0)[None].astype(f32)
    return (y_p, y_s, mem_k, mem_v, ret_p, ret_s, sgu_v, wk_p, wv_p, wk_s, wv_s)
```
